# Optimizing a Trainium2 kernel written in Bass

```python
import jax, jax.numpy as jnp
from jax import lax
import numpy as np

D_MODEL = 1024
BATCH = 1
SEQ = 16384
DEPTH = 2
DEC_BATCH = 16
DEC_SEQ = 2048
PAST_LEN = 128

D_RG = D_MODEL
RG_HEADS = 8
RG_HEAD_DIM = D_RG // RG_HEADS
RG_CONV = 4
RG_PAD_L = 2
RG_PAD_R = RG_CONV - 1 - RG_PAD_L
RG_C = 8.0
D_SG = D_MODEL // 2
SG_CHUNK = 128
SG_GROUPS = 4
SG_GROUP_DIM = D_SG // SG_GROUPS
D_CC = D_MODEL // 2
CC_KERNEL = 31
CC_PAD = (CC_KERNEL - 1) // 2
D_FF = 4 * D_MODEL
N_BRANCH = 3
OFF_RG_X = 0
OFF_RG_G = OFF_RG_X + D_RG
OFF_SG = OFF_RG_G + D_RG
OFF_CC = OFF_SG + 2 * D_SG
OFF_GATE = OFF_CC + 2 * D_CC
D_IN = OFF_GATE + N_BRANCH * D_MODEL
DEEPNORM_ALPHA = (2 * DEPTH) ** 0.25
DEEPNORM_BETA = (8 * DEPTH) ** -0.25
LN_EPS = 1e-5

kernel_name = "hybrid_rglru_sgu_conformer_encoder"


def _layer_norm(x, g, b):
    xf = x.astype(jnp.float32)
    mu = jnp.mean(xf, axis=-1, keepdims=True)
    var = jnp.mean(jnp.square(xf - mu), axis=-1, keepdims=True)
    y = (xf - mu) * lax.rsqrt(var + LN_EPS)
    return (y * g.astype(jnp.float32) + b.astype(jnp.float32)).astype(x.dtype)


def _depthwise_conv(x, w, b, pad_l, pad_r):
    c = x.shape[-1]
    y = lax.conv_general_dilated(x, w[:, None, :], window_strides=(1,), padding=[(pad_l, pad_r)],
                                 dimension_numbers=("NWC", "WIO", "NWC"), feature_group_count=c)
    return y + b


def _linear_scan(a, u):
    def combine(c1, c2):
        a1, b1 = c1
        a2, b2 = c2
        return a1 * a2, a2 * b1 + b2
    _, h = lax.associative_scan(combine, (a, u), axis=1)
    return h


def _rg_lru_direction(x, w_a, b_a, w_x, b_x, lam, reverse):
    if reverse:
        x = jnp.flip(x, axis=1)
    bn, s, _ = x.shape
    xh = x.reshape(bn, s, RG_HEADS, RG_HEAD_DIM)
    r = jax.nn.sigmoid(jnp.einsum("bshi,hio->bsho", xh, w_a) + b_a).reshape(bn, s, D_RG).astype(jnp.float32)
    i = jax.nn.sigmoid(jnp.einsum("bshi,hio->bsho", xh, w_x) + b_x).reshape(bn, s, D_RG)
    log_a = -RG_C * r * jax.nn.softplus(-lam.astype(jnp.float32))
    a = jnp.exp(log_a)
    u = jnp.sqrt(-jnp.expm1(2.0 * log_a)) * (i * x).astype(jnp.float32)
    h = _linear_scan(a, u)
    if reverse:
        h = jnp.flip(h, axis=1)
    return h.astype(x.dtype)


def _trunk(x, ln_in_g, ln_in_b, w_in, b_in, conv_a_w, conv_a_b, rg_wa, rg_ba, rg_wx, rg_bx, rg_lambda,
           sg_ln_g, sg_ln_b, sg_w, sg_b, conv_c_w, conv_c_b, cc_ln_g, cc_ln_b, w_ba, w_bb, w_bc,
           w_o, b_o, ln1_g, ln1_b, w_ff1, b_ff1, w_ff2, b_ff2, ln2_g, ln2_b):
    bn, s, _ = x.shape
    x = _layer_norm(x, ln_in_g, ln_in_b)
    for l in range(DEPTH):
        proj = x @ w_in[l] + b_in[l]
        xa = _depthwise_conv(proj[..., OFF_RG_X:OFF_RG_X + D_RG], conv_a_w[l], conv_a_b[l], RG_PAD_L, RG_PAD_R)
        h = (_rg_lru_direction(xa, rg_wa[l, 0], rg_ba[l, 0], rg_wx[l, 0], rg_bx[l, 0], rg_lambda[l, 0], False)
             + _rg_lru_direction(xa, rg_wa[l, 1], rg_ba[l, 1], rg_wx[l, 1], rg_bx[l, 1], rg_lambda[l, 1], True))
        y_a = (h * jax.nn.gelu(proj[..., OFF_RG_G:OFF_RG_G + D_RG])) @ w_ba[l]
        uv = jax.nn.gelu(proj[..., OFF_SG:OFF_SG + 2 * D_SG])
        u, v = uv[..., :D_SG], uv[..., D_SG:]
        v = _layer_norm(v, sg_ln_g[l], sg_ln_b[l])
        vc = v.reshape(bn, s // SG_CHUNK, SG_CHUNK, SG_GROUPS, SG_GROUP_DIM)
        sp = jnp.einsum("gpq,bcqgd->bcpgd", sg_w[l], vc) + sg_b[l].T[:, :, None]
        y_b = (u * sp.reshape(bn, s, D_SG)) @ w_bb[l]
        c = proj[..., OFF_CC:OFF_CC + D_CC] * jax.nn.sigmoid(proj[..., OFF_CC + D_CC:OFF_GATE])
        c = _depthwise_conv(c, conv_c_w[l], conv_c_b[l], CC_PAD, CC_PAD)
        c = jax.nn.silu(_layer_norm(c, cc_ln_g[l], cc_ln_b[l]))
        y_c = c @ w_bc[l]
        g = jax.nn.sigmoid(proj[..., OFF_GATE:])
        merged = (g[..., :D_MODEL] * y_a + g[..., D_MODEL:2 * D_MODEL] * y_b + g[..., 2 * D_MODEL:] * y_c)
        x = _layer_norm(DEEPNORM_ALPHA * x + merged @ w_o[l] + b_o[l], ln1_g[l], ln1_b[l])
        hid = jnp.square(jax.nn.relu(x @ w_ff1[l] + b_ff1[l]))
        x = _layer_norm(DEEPNORM_ALPHA * x + hid @ w_ff2[l] + b_ff2[l], ln2_g[l], ln2_b[l])
    return x


def setup_inputs(seed: int = 0) -> dict:
    key = jax.random.key(seed)
    ks = jax.random.split(key, 40)
    f32 = jnp.float32

    def nrm(k, shape, scale):
        return jax.random.normal(k, shape, f32) * scale

    a_c = jax.random.uniform(ks[12], (DEPTH, 2, D_RG), f32, 0.9, 0.999)
    a = a_c ** (1.0 / RG_C)
    return {
        "x_prompt": nrm(ks[0], (BATCH, SEQ, D_MODEL), 1.0),
        "x_sample": nrm(ks[1], (DEC_BATCH, DEC_SEQ, D_MODEL), 1.0),
        "ln_in_g": 1.0 + nrm(ks[2], (D_MODEL,), 0.02),
        "ln_in_b": nrm(ks[3], (D_MODEL,), 0.02),
        "w_in": nrm(ks[4], (DEPTH, D_MODEL, D_IN), D_MODEL ** -0.5),
        "b_in": nrm(ks[5], (DEPTH, D_IN), 0.02),
        "conv_a_w": nrm(ks[6], (DEPTH, RG_CONV, D_RG), RG_CONV ** -0.5),
        "conv_a_b": nrm(ks[7], (DEPTH, D_RG), 0.02),
        "rg_wa": nrm(ks[8], (DEPTH, 2, RG_HEADS, RG_HEAD_DIM, RG_HEAD_DIM), RG_HEAD_DIM ** -0.5),
        "rg_ba": nrm(ks[9], (DEPTH, 2, RG_HEADS, RG_HEAD_DIM), 0.02),
        "rg_wx": nrm(ks[10], (DEPTH, 2, RG_HEADS, RG_HEAD_DIM, RG_HEAD_DIM), RG_HEAD_DIM ** -0.5),
        "rg_bx": nrm(ks[11], (DEPTH, 2, RG_HEADS, RG_HEAD_DIM), 0.02),
        "rg_lambda": jnp.log(a) - jnp.log1p(-a),
        "sg_ln_g": 1.0 + nrm(ks[13], (DEPTH, D_SG), 0.02),
        "sg_ln_b": nrm(ks[14], (DEPTH, D_SG), 0.02),
        "sg_w": nrm(ks[15], (DEPTH, SG_GROUPS, SG_CHUNK, SG_CHUNK), SG_CHUNK ** -0.5),
        "sg_b": 1.0 + nrm(ks[16], (DEPTH, SG_GROUPS, SG_CHUNK), 0.02),
        "conv_c_w": nrm(ks[17], (DEPTH, CC_KERNEL, D_CC), CC_KERNEL ** -0.5),
        "conv_c_b": nrm(ks[18], (DEPTH, D_CC), 0.02),
        "cc_ln_g": 1.0 + nrm(ks[19], (DEPTH, D_CC), 0.02),
        "cc_ln_b": nrm(ks[20], (DEPTH, D_CC), 0.02),
        "w_ba": nrm(ks[21], (DEPTH, D_RG, D_MODEL), DEEPNORM_BETA * D_RG ** -0.5),
        "w_bb": nrm(ks[22], (DEPTH, D_SG, D_MODEL), DEEPNORM_BETA * D_SG ** -0.5),
        "w_bc": nrm(ks[23], (DEPTH, D_CC, D_MODEL), DEEPNORM_BETA * D_CC ** -0.5),
        "w_o": nrm(ks[24], (DEPTH, D_MODEL, D_MODEL), DEEPNORM_BETA * D_MODEL ** -0.5),
        "b_o": nrm(ks[25], (DEPTH, D_MODEL), 0.02),
        "ln1_g": 1.0 + nrm(ks[26], (DEPTH, D_MODEL), 0.02),
        "ln1_b": nrm(ks[27], (DEPTH, D_MODEL), 0.02),
        "w_ff1": nrm(ks[28], (DEPTH, D_MODEL, D_FF), DEEPNORM_BETA * D_MODEL ** -0.5),
        "b_ff1": nrm(ks[29], (DEPTH, D_FF), 0.02),
        "w_ff2": nrm(ks[30], (DEPTH, D_FF, D_MODEL), DEEPNORM_BETA * D_FF ** -0.5),
        "b_ff2": nrm(ks[31], (DEPTH, D_MODEL), 0.02),
        "ln2_g": 1.0 + nrm(ks[32], (DEPTH, D_MODEL), 0.02),
        "ln2_b": nrm(ks[33], (DEPTH, D_MODEL), 0.02),
    }


def reference(x_prompt, x_sample, ln_in_g, ln_in_b, w_in, b_in, conv_a_w, conv_a_b, rg_wa, rg_ba, rg_wx, rg_bx,
              rg_lambda, sg_ln_g, sg_ln_b, sg_w, sg_b, conv_c_w, conv_c_b, cc_ln_g, cc_ln_b, w_ba, w_bb, w_bc,
              w_o, b_o, ln1_g, ln1_b, w_ff1, b_ff1, w_ff2, b_ff2, ln2_g, ln2_b):
    params = (ln_in_g, ln_in_b, w_in, b_in, conv_a_w, conv_a_b, rg_wa, rg_ba, rg_wx, rg_bx, rg_lambda,
              sg_ln_g, sg_ln_b, sg_w, sg_b, conv_c_w, conv_c_b, cc_ln_g, cc_ln_b, w_ba, w_bb, w_bc,
              w_o, b_o, ln1_g, ln1_b, w_ff1, b_ff1, w_ff2, b_ff2, ln2_g, ln2_b)
    y_prompt = _trunk(x_prompt, *params)
    y_sample = _trunk(x_sample, *params)
    return (y_prompt, y_sample)
```

```python
import numpy as np
import concourse.bass as bass
import concourse.mybir as mybir
from concourse.bass_utils import run_bass_kernel_spmd

F32 = mybir.dt.float32
BF16 = mybir.dt.bfloat16
AF = mybir.ActivationFunctionType
ALU = mybir.AluOpType

D = 1024
T = 2048
HALO = 16
NSEG = 8
L = 2
NCORES = 8
ALPHA = float((2 * L) ** 0.25)
EPS = 1e-5
OFF_RG_X, OFF_RG_G, OFF_SG, OFF_CC, OFF_GATE = 0, 1024, 2048, 3072, 4096
NSLOT = 8
NWK = 13

_c = {}
_o = 0
def _add(name, n):
    global _o
    _c[name] = _o
    _o += n
_add("bin", 56); _add("caw", 32); _add("cab", 8); _add("rba", 16); _add("rbx", 16); _add("lam", 16)
_add("sglg", 4); _add("ccw", 124); _add("ccb", 4); _add("cclg", 4); _add("cclb", 4)
_add("bo", 8); _add("l1g", 8); _add("l1b", 8); _add("bf1", 32); _add("bf2", 8); _add("l2g", 8); _add("l2b", 8)
NCST = _o
NROW = 1536

def _units():
    u = {}
    n = 0
    for h in range(8):
        u[("x", h)] = n; u[("g", h)] = n + 1; u[("gt", h)] = n + 2; n += 3
    for ch in range(4):
        u[("u", ch)] = n; n += 1
    for j in range(4):
        u[("v", j)] = n; n += 1
    for ch in range(4):
        u[("c", ch)] = n; u[("cg", ch)] = n + 1; n += 2
    for oc in range(8):
        for k, nm in enumerate(("ga", "gb", "gc", "ba", "bb", "bc")):
            u[(nm, oc)] = n + k
        n += 6
    for oc in range(8):
        u[("o", oc)] = n; n += 1
    for J in range(4):
        for jj in range(8):
            u[("f1", J * 8 + jj)] = n; n += 1
        for jj in range(8):
            u[("f2", J * 8 + jj)] = n; n += 1
    return u, n
UNITS, NU = _units()


class Prog:
    ENGS = ("pe", "act", "dve", "pool", "sp")

    def __init__(self, nc, es):
        self.nc = nc
        self.es = es
        self.streams = {e: [] for e in self.ENGS}
        self.cnt = {e: 0 for e in self.ENGS}
        self.known = {e: {} for e in self.ENGS}
        self.buf = {}
        self.sems = {}
        self.dcnt = {}
        for e in self.ENGS:
            self.sems[e] = es.enter_context(nc.semaphore("s_" + e))
        self.out_deps = []

    def dsem(self, key):
        if key not in self.sems:
            self.sems[key] = self.es.enter_context(self.nc.semaphore("d_%d" % len(self.sems)))
            self.dcnt[key] = 0
        return key

    def _deps(self, engine, reads, writes):
        deps = {}
        def add(d):
            if d is None:
                return
            s, v = d
            if deps.get(s, 0) < v:
                deps[s] = v
        for k in reads:
            st = self.buf.get(k)
            if st:
                add(st["w"])
        for k in writes:
            st = self.buf.get(k)
            if st:
                add(st["w"])
                for s, v in st["r"].items():
                    add((s, v))
        waits = []
        kn = self.known[engine]
        for s, v in deps.items():
            if engine == "pe" and s == "pe":
                continue
            if kn.get(s, 0) < v:
                waits.append((s, v))
                kn[s] = v
        return waits

    def _commit(self, dep, reads, writes):
        for k in writes:
            self.buf[k] = {"w": dep, "r": {}}
        for k in reads:
            st = self.buf.setdefault(k, {"w": None, "r": {}})
            if st["r"].get(dep[0], 0) < dep[1]:
                st["r"][dep[0]] = dep[1]

    def op(self, engine, fn, reads=(), writes=(), signal=True):
        waits = self._deps(engine, reads, writes)
        sems = self.sems
        if signal:
            self.cnt[engine] += 1
            dep = (engine, self.cnt[engine])
        else:
            dep = (engine, self.cnt[engine] + 1)
        semh = sems[engine]
        def thunk(eng, waits=waits, fn=fn, signal=signal):
            for s, v in waits:
                eng.wait_ge(sems[s], v)
            ins = fn(eng)
            if signal:
                ins.then_inc(semh, 1)
        self.streams[engine].append(thunk)
        self._commit(dep, reads, writes)

    def dma(self, engine, out, in_, reads, writes, semkey):
        self.dsem(semkey)
        waits = self._deps(engine, reads, writes)
        self.dcnt[semkey] += 16
        dep = (semkey, self.dcnt[semkey])
        sems = self.sems
        def thunk(eng, waits=waits):
            for s, v in waits:
                eng.wait_ge(sems[s], v)
            eng.dma_start(out=out, in_=in_).then_inc(sems[semkey], 16)
        self.streams[engine].append(thunk)
        self._commit(dep, reads, writes)
        return dep

    def final_wait(self, engine, deps):
        sems = self.sems
        def thunk(eng):
            for s, v in deps:
                eng.wait_ge(sems[s], v)
        self.streams[engine].append(thunk)

    def emit(self, block):
        P = self
        @block.tensor
        def _(e):
            for t in P.streams["pe"]:
                t(e)
        @block.scalar
        def _(e):
            for t in P.streams["act"]:
                t(e)
        @block.vector
        def _(e):
            for t in P.streams["dve"]:
                t(e)
        @block.gpsimd
        def _(e):
            for t in P.streams["pool"]:
                t(e)
        @block.sync
        def _(e):
            for t in P.streams["sp"]:
                t(e)


def build(seg_kinds, debug=False):
    import contextlib
    nc = bass.Bass("TRN2", target_bir_lowering=False)
    xin = nc.dram_tensor("xin", [NSEG, T, D], F32, kind="ExternalInput").ap()
    wu = nc.dram_tensor("wu", [L, NU, 128, 1024], F32, kind="ExternalInput").ap()
    cst_d = nc.dram_tensor("cst", [L, 128, NCST], F32, kind="ExternalInput").ap()
    row_d = nc.dram_tensor("rows", [L, 1, NROW], F32, kind="ExternalInput").ap()
    lnin_d = nc.dram_tensor("lnin", [128, 16], F32, kind="ExternalInput").ap()
    sgwt_d = nc.dram_tensor("sgwt", [L, 4, 128, 128], F32, kind="ExternalInput").ap()
    ident_d = nc.dram_tensor("ident", [128, 128], F32, kind="ExternalInput").ap()
    yout = nc.dram_tensor("yout", [NSEG, T, D], F32, kind="ExternalOutput").ap()
    link_d = nc.dram_tensor("link", [128, 16], F32, kind="ExternalInput").ap()
    xd = nc.dram_tensor("xd", [NSEG * 8, 128, T], F32, kind="Internal").ap()
    xhd = nc.dram_tensor("xhd", [2 * NSEG * 8, 128, T], BF16, kind="Internal").ap()
    hbd = nc.dram_tensor("hbd", [NSEG * 8, 128, T], F32, kind="Internal").ap()

    es = contextlib.ExitStack()
    with es:
        P = Prog(nc, es)
        sb = lambda name, shape, dt: es.enter_context(nc.sbuf_tensor(name, shape, dt))
        XH = sb("XH", [128, 8 * (T + 2 * HALO)], BF16)
        BIG = sb("BIG", [128, 32768], BF16)
        WK = sb("WK", [128, NWK * 2048], BF16)
        PAD = sb("PAD", [128, 2176], BF16)
        DC = sb("DC", [128, 31 * 128], BF16)
        DA = sb("DA", [128, 4 * 128], BF16)
        RING = sb("RING", [128, NSLOT * 1024], BF16)
        CST = sb("CST", [128, L * NCST], F32)
        CX = sb("CX", [128, L * 64], F32)
        ROWP = sb("ROWP", [33, 512], F32)
        LNIN = sb("LNIN", [128, 16], F32)
        SGWT = sb("SGWT", [128, L * 512], BF16)
        BIASM = sb("BIASM", [128, L * 512], F32)
        GMAT = sb("GMAT", [128, L * 512], F32)
        IDB = sb("IDB", [128, 128], BF16)
        IDF = sb("IDF", [128, 128], F32)
        ONESD = sb("ONESD", [128, 256], F32)
        ONE = sb("ONE", [128, 128], F32)
        ONESB = sb("ONESB", [128, 256], BF16)
        KF = sb("KF", [128, 8], F32)
        SM = sb("SM", [128, 64], F32)
        LINK = sb("LINK", [128, 16], F32)
        CF = sb("CF", [128, 8], F32)
        CB = sb("CB", [128, 8], F32)
        HT = sb("HT", [128, 64], F32)
        PS = es.enter_context(nc.psum_tensor("PS", [128, 4096], F32))

        XW = T + 2 * HALO
        def xh(c, lo, n):
            return XH[:, c * XW + lo: c * XW + lo + n]
        def big_bf(g, lo=0, n=2048):
            return BIG[:, g * 2048 + lo: g * 2048 + lo + n]
        def big_f32(c, lo=0, n=2048):
            return BIG[:, c * 4096: (c + 1) * 4096].bitcast(F32)[:, lo: lo + n]
        def wk_bf(g, lo=0, n=2048):
            return WK[:, g * 2048 + lo: g * 2048 + lo + n]
        def wk_f32(g, lo=0, n=1024):
            return WK[:, g * 2048: g * 2048 + 2 * (lo + n)].bitcast(F32)[:, lo: lo + n]
        def wkk(g, n=1):
            return [("W", g + i) for i in range(n)]
        def bgk(g, n=1):
            return [("B", g + i) for i in range(n)]
        def ps(g, lo=0, n=1024):
            return PS[:, g * 1024 + lo: g * 1024 + lo + n]
        psk = lambda g: ("PS", g)
        def cst(l, name, j=0):
            o = l * NCST + _c[name] + j
            return CST[:, o: o + 1]
        def cx(l, o):
            return CX[:, l * 64 + o: l * 64 + o + 1]
        kf = lambda j: KF[:, j: j + 1]

        state = {"psg": 0, "slot": 0}
        def next_ps():
            g = state["psg"]
            state["psg"] = (g + 1) % 4
            return g

        def load_unit(l, key, n=1024):
            s = state["slot"]
            state["slot"] = (s + 1) % NSLOT
            u = UNITS[key]
            P.dma("pool", RING[:, s * 1024: s * 1024 + n], wu[l, u, :, 0:n], reads=[], writes=[("R", s)], semkey=("R", s))
            return s
        def wt(s, k, n=128, width=128):
            return RING[:, s * 1024 + k * width: s * 1024 + k * width + n]

        def act(out, in_, func, bias=None, scale=1.0, reads=(), writes=()):
            def fn(e):
                kw = {}
                if bias is not None:
                    kw["bias"] = bias
                return e.activation(out=out, in_=in_, func=func, scale=scale, **kw)
            P.op("act", fn, reads, writes)
        def tt(eng, out, in0, in1, op, reads=(), writes=()):
            P.op(eng, lambda e: e.tensor_tensor(out=out, in0=in0, in1=in1, op=op), reads, writes)
        def ts(eng, out, in0, s1, s2, op0, op1=None, reads=(), writes=()):
            if op1 is None:
                P.op(eng, lambda e: e.tensor_scalar(out=out, in0=in0, scalar1=s1, scalar2=None, op0=op0), reads, writes)
            else:
                P.op(eng, lambda e: e.tensor_scalar(out=out, in0=in0, scalar1=s1, scalar2=s2, op0=op0, op1=op1), reads, writes)
        def stt(eng, out, in0, scalar, in1, op0, op1, reads=(), writes=()):
            P.op(eng, lambda e: e.scalar_tensor_tensor(out=out, in0=in0, scalar=scalar, in1=in1, op0=op0, op1=op1), reads, writes)
        def cp(eng, out, in_, reads=(), writes=()):
            P.op(eng, lambda e: e.tensor_copy(out=out, in_=in_), reads, writes)
        def mm(out, lhsT, rhs, start, stop, reads=(), writes=(), signal=False):
            P.op("pe", lambda e: e.matmul(out, lhsT, rhs, start=start, stop=stop), reads, writes, signal=signal)
        def tr(out, in_, ident, reads=(), writes=(), signal=False):
            P.op("pe", lambda e: e.transpose(out, in_, ident), reads, writes, signal=signal)
        def memset(eng, ap, val, writes=()):
            P.op(eng, lambda e: e.memset(ap, val), (), writes)

        memset("dve", KF[:, 0:1], 1.0, [("KF",)])
        memset("dve", KF[:, 1:2], EPS, [("KF",)])
        memset("dve", KF[:, 2:3], 0.0, [("KF",)])
        memset("dve", KF[:, 3:4], -0.5, [("KF",)])
        memset("dve", KF[:, 4:5], 0.5, [("KF",)])
        memset("dve", ONE[:], 1.0, [("ONE",)])
        memset("dve", ONESD[:, 0:128], 1.0 / 1024.0, [("ONESD",)])
        memset("dve", ONESD[:, 128:256], 1.0 / 512.0, [("ONESD",)])
        memset("dve", ONESB[:, 0:128], 1.0 / 1024.0, [("ONESD",)])
        memset("dve", ONESB[:, 128:256], 1.0 / 512.0, [("ONESD",)])
        P.dma("sp", IDF[:], ident_d, [], [("IDF",)], ("IDF",))
        cp("dve", IDB[:], IDF[:], [("IDF",)], [("IDB",)])
        P.dma("sp", CST[:].rearrange("p (l n) -> p l n", l=L), cst_d.rearrange("l p n -> p l n"), [], [("CST",)], ("CST",))
        for l in range(L):
            P.dma("sp", ROWP[32 * l: 32 * l + 1, :], row_d[l, :, 1024:1536], [], [("ROWP",)], ("ROWP",))
        P.dma("sp", LNIN[:], lnin_d, [], [("LNIN",)], ("LNIN",))
        for l in range(L):
            lamv = CST[:, l * NCST + _c["lam"]: l * NCST + _c["lam"] + 16]
            c1 = CX[:, l * 64: l * 64 + 16]
            act(c1, lamv, AF.Exp, scale=-1.0, reads=[("CST",)], writes=[("CX",)])
            act(c1, c1, AF.Ln, bias=kf(0), reads=[("CX",), ("KF",)], writes=[("CX",)])
            ts("dve", c1, c1, -4.0, None, ALU.mult, reads=[("CX",)], writes=[("CX",)])
            ts("dve", CX[:, l * 64 + 48: l * 64 + 64], c1, 2.0, None, ALU.mult, reads=[("CX",)], writes=[("CX",)])
            for nm, o in (("rba", 16), ("rbx", 32)):
                src = CST[:, l * NCST + _c[nm]: l * NCST + _c[nm] + 16]
                ts("dve", CX[:, l * 64 + o: l * 64 + o + 16], src, 0.5, None, ALU.mult, reads=[("CST",)], writes=[("CX",)])
            P.dma("pool", SGWT[:, l * 512:(l + 1) * 512].rearrange("q (g p) -> q g p", g=4),
                  sgwt_d[l].rearrange("g q p -> q g p"), [], [("SGWT", l)], ("SGWT", l))
            SGWF = wk_f32(2, 0, 512)
            RSR = wk_f32(1, 0, 128)[0:1, :]
            ROWT = wk_f32(0, 0, 1024)[0:1, :]
            P.dma("sp", SGWF.rearrange("q (g p) -> q g p", g=4), sgwt_d[l].rearrange("g q p -> q g p"),
                  [], wkk(2), ("SGWF",))
            P.dma("sp", ROWT, row_d[l, :, 0:1024], [], wkk(0), ("ROWT",))
            for g in range(4):
                pg = next_ps()
                mm(ps(pg, 0, 128)[0:1, :], ONE[:, 0:1], SGWF[:, g * 128:(g + 1) * 128], True, True,
                   reads=[("ONE",)] + wkk(2), writes=[psk(pg)], signal=True)
                cp("dve", RSR, ps(pg, 0, 128)[0:1, :], [psk(pg)], wkk(1))
                pg2 = next_ps()
                mm(ps(pg2, 0, 128), ROWT[:, g * 128:(g + 1) * 128], RSR, True, False,
                   reads=wkk(0) + wkk(1), writes=[psk(pg2)])
                mm(ps(pg2, 0, 128), ONE[0:1, :], ROWT[:, 512 + g * 128: 512 + (g + 1) * 128], False, True,
                   reads=wkk(0) + [("ONE",)], writes=[psk(pg2)], signal=True)
                cp("dve", BIASM[:, l * 512 + g * 128: l * 512 + (g + 1) * 128], ps(pg2, 0, 128), [psk(pg2)], [("BIASM", l)])
                ts("dve", GMAT[:, l * 512 + g * 128: l * 512 + (g + 1) * 128], ONE[:], cst(l, "sglg", g), None, ALU.mult,
                   reads=[("ONE",), ("CST",)], writes=[("GMAT", l)])

        def layer_norm(nch, onescol, zc, zkeys, cb, tg):
            gMR, gRS, t0, t1 = tg[0], tg[1], tg[2], tg[3]
            for half in range(2):
                lo = half * 1024
                pm, pq = next_ps(), next_ps()
                for c in range(nch):
                    tq = (t0, t1)[c % 2]
                    act(wk_bf(tq, 0, 1024), zc(c, lo, 1024), AF.Square, reads=zkeys(c, half), writes=wkk(tq))
                    for t2 in range(2):
                        mm(ps(pm, t2 * 512, 512), ONESD[:, onescol:onescol + 128], zc(c, lo + t2 * 512, 512), c == 0, c == nch - 1,
                           reads=zkeys(c, half) + [("ONESD",)], writes=[psk(pm)], signal=(c == nch - 1 and t2 == 1))
                    for t2 in range(2):
                        mm(ps(pq, t2 * 512, 512), ONESB[:, onescol:onescol + 128], wk_bf(tq, t2 * 512, 512), c == 0, c == nch - 1,
                           reads=wkk(tq) + [("ONESD",)], writes=[psk(pq)], signal=(t2 == 1))
                act(wk_f32(gMR), ps(pm), AF.Identity, reads=[psk(pm)], writes=wkk(gMR))
                tt("dve", wk_f32(t0), wk_f32(gMR), wk_f32(gMR), ALU.mult, reads=wkk(gMR), writes=wkk(t0))
                tt("dve", wk_f32(t0), ps(pq), wk_f32(t0), ALU.subtract, reads=[psk(pq)] + wkk(t0), writes=wkk(t0))
                act(wk_f32(t0), wk_f32(t0), AF.Ln, bias=kf(1), reads=wkk(t0) + [("KF",)], writes=wkk(t0))
                act(wk_f32(gRS), wk_f32(t0), AF.Exp, scale=-0.5, reads=wkk(t0), writes=wkk(gRS))
                tt("dve", wk_f32(gMR), wk_f32(gMR), wk_f32(gRS), ALU.mult, reads=wkk(gMR) + wkk(gRS), writes=wkk(gMR))
                for c in range(nch):
                    tq = (t0, t1)[c % 2]
                    tt("dve", wk_f32(tq), zc(c, lo, 1024), wk_f32(gRS), ALU.mult, reads=zkeys(c, half) + wkk(gRS), writes=wkk(tq))
                    tt("dve", wk_f32(tq), wk_f32(tq), wk_f32(gMR), ALU.subtract, reads=wkk(tq) + wkk(gMR), writes=wkk(tq))
                    cb(c, half, wk_f32(tq), wkk(tq))

        def store_x(s, c, half, y, ykeys, par=None):
            lo = half * 1024
            act(xh(c, lo, 1024), y, AF.Identity, reads=ykeys, writes=[("XH", c, half)])
            P.dma("sp", xd[s * 8 + c, :, lo: lo + 1024], y, ykeys, [("XD", s, c, half)], ("XDW", c % 2, half))
            if par is not None:
                P.dma("sp", xhd[(par * NSEG + s) * 8 + c, :, lo: lo + 1024], xh(c, lo, 1024), [("XH", c, half)],
                      [("XHD", par, s, c, half)], ("XHDW", c % 2, half))

        XH3 = XH[:].rearrange("p (c w) -> p c w", c=8)
        def load_xh(par, s):
            for c in range(8):
                P.dma("sp", xh(c, 0, T), xhd[(par * NSEG + s) * 8 + c, :, :], [("XHD", par, s, c, 0), ("XHD", par, s, c, 1)],
                      [("XH", c, 0), ("XH", c, 1)], ("XHL", c % 4))
            sl, sr = max(s - 1, 0), min(s + 1, NSEG - 1)
            bl, br = (par * NSEG + sl) * 8, (par * NSEG + sr) * 8
            P.dma("sp", XH3[:, :, T: T + HALO], xhd[bl: bl + 8, :, T - HALO: T].rearrange("c p t -> p c t"),
                  [("XHD", par, sl, c, 1) for c in range(8)], [("XHH",)], ("XHH", 0))
            P.dma("sp", XH3[:, :, T + HALO: T + 2 * HALO], xhd[br: br + 8, :, 0: HALO].rearrange("c p t -> p c t"),
                  [("XHD", par, sr, c, 0) for c in range(8)], [("XHH",)], ("XHH", 1))
        lkL = lambda s: LINK[:, s: s + 1]
        lkR = lambda s: LINK[:, 8 + s: 9 + s]
        P.dma("sp", LINK[:], link_d, [], [("LINK",)], ("LINK",))

        gXA, gXAB, gR, gI, gA, gH1, gH2 = 0, 2, 3, 5, 7, 9, 11

        def rg_head(l, s, h, mode):
            sx = load_unit(l, ("x", h))
            sg_ = load_unit(l, ("g", h)) if mode == "B" else None
            sgt = load_unit(l, ("gt", h), 512)
            if mode == "B":
                P.dma("sp", wk_f32(gH2, 0, 2048), hbd[s * 8 + h, :, :], [("HBD", s, h)], wkk(gH2, 2), ("HBR",))
            for k in range(4):
                ts("dve", DA[:, k * 128:(k + 1) * 128], IDB[:], cst(l, "caw", k * 8 + h), None, ALU.mult,
                   reads=[("IDB",), ("CST",)], writes=[("DA",)])
            bx = cst(l, "bin", OFF_RG_X // 128 + h)
            for half in range(2):
                pg = next_ps()
                for t2 in range(2):
                    for k in range(8):
                        mm(ps(pg, t2 * 512, 512), wt(sx, k), xh(k, half * 1024 + t2 * 512, 512), k == 0, k == 7,
                           reads=[("R", sx), ("XH", k, half)], writes=[psk(pg)], signal=(k == 7 and t2 == 1))
                act(PAD[:, 2 + half * 1024: 2 + half * 1024 + 1024], ps(pg), AF.Identity, bias=bx,
                    reads=[psk(pg), ("CST",)], writes=[("PAD",)])
            pg = next_ps()
            for k in range(8):
                mm(ps(pg, 0, 32), wt(sx, k), xh(k, T, 32), k == 0, k == 7, reads=[("R", sx), ("XHH",)], writes=[psk(pg)], signal=(k == 7))
            act(HT[:, 0:32], ps(pg, 0, 32), AF.Identity, bias=bx, reads=[psk(pg), ("CST",)], writes=[("HT",)])
            ts("dve", PAD[:, 0:2], HT[:, 14:16], lkL(s), None, ALU.mult, reads=[("HT",), ("LINK",)], writes=[("PAD",)])
            ts("dve", PAD[:, 2 + T: 3 + T], HT[:, 16:17], lkR(s), None, ALU.mult, reads=[("HT",), ("LINK",)], writes=[("PAD",)])
            for half in range(2):
                pg = next_ps()
                for t2 in range(2):
                    for k in range(4):
                        o = k + half * 1024 + t2 * 512
                        mm(ps(pg, t2 * 512, 512), DA[:, k * 128:(k + 1) * 128], PAD[:, o: o + 512],
                           k == 0, k == 3, reads=[("DA",), ("PAD",)], writes=[psk(pg)], signal=(k == 3 and t2 == 1))
                act(wk_f32(gXA + half), ps(pg), AF.Identity, bias=cst(l, "cab", h), reads=[psk(pg), ("CST",)], writes=wkk(gXA + half))
                act(wk_bf(gXAB, half * 1024, 1024), wk_f32(gXA + half), AF.Identity, reads=wkk(gXA + half), writes=wkk(gXAB))
            d = 1 if mode == "A" else 0
            for gi, (gdst, hb_o) in enumerate(((gR, 16), (gI, 32))):
                for half in range(2):
                    pg = next_ps()
                    for t2 in range(2):
                        mm(ps(pg, t2 * 512, 512), wt(sgt, d * 2 + gi), wk_bf(gXAB, half * 1024 + t2 * 512, 512), True, True,
                           reads=[("R", sgt)] + wkk(gXAB), writes=[psk(pg)], signal=(t2 == 1))
                    act(wk_f32(gdst + half), ps(pg), AF.Tanh, bias=cx(l, hb_o + d * 8 + h), scale=0.5,
                        reads=[psk(pg), ("CX",)], writes=wkk(gdst + half))
            c1ap = cx(l, d * 8 + h)
            c2ap = cx(l, 48 + d * 8 + h)
            order = (1, 0) if mode == "A" else (0, 1)
            for half in order:
                act(wk_f32(gA + half), wk_f32(gR + half), AF.Exp, bias=c1ap, scale=c1ap, reads=wkk(gR + half) + [("CX",)], writes=wkk(gA + half))
                act(wk_f32(gR + half), wk_f32(gR + half), AF.Exp, bias=c2ap, scale=c2ap, reads=wkk(gR + half) + [("CX",)], writes=wkk(gR + half))
            for half in order:
                stt("dve", wk_f32(gI + half), wk_f32(gI + half), 1.0, wk_f32(gXA + half), ALU.add, ALU.mult,
                    reads=wkk(gI + half) + wkk(gXA + half), writes=wkk(gI + half))
            for half in order:
                ts("dve", wk_f32(gR + half), wk_f32(gR + half), 0.99999994, None, ALU.min, reads=wkk(gR + half), writes=wkk(gR + half))
                act(wk_f32(gR + half), wk_f32(gR + half), AF.Sqrt, bias=kf(0), scale=-1.0, reads=wkk(gR + half) + [("KF",)], writes=wkk(gR + half))
                stt("dve", wk_f32(gI + half), wk_f32(gI + half), 0.5, wk_f32(gR + half), ALU.mult, ALU.mult,
                    reads=wkk(gI + half) + wkk(gR + half), writes=wkk(gI + half))
            if mode == "A":
                gH = gH1 if h % 2 == 0 else gH2
                H_ = wk_f32(gH, 0, 2048)
                A_, I_ = wk_f32(gA, 0, 2048), wk_f32(gI, 0, 2048)
                ini = CB[:, h: h + 1]
                P.op("dve", lambda e: e.tensor_tensor_scan(out=H_[:, 2047:1023:-1], data0=A_[:, 2047:1023:-1], data1=I_[:, 2047:1023:-1], initial=ini, op0=ALU.mult, op1=ALU.add),
                     wkk(gA + 1) + wkk(gI + 1) + [("CB",)], wkk(gH + 1))
                ini2 = H_[:, 1024:1025]
                P.op("dve", lambda e: e.tensor_tensor_scan(out=H_[:, 1023::-1], data0=A_[:, 1023::-1], data1=I_[:, 1023::-1], initial=ini2, op0=ALU.mult, op1=ALU.add),
                     wkk(gA) + wkk(gI) + wkk(gH + 1), wkk(gH))
                P.dma("sp", hbd[s * 8 + h, :, :], H_, wkk(gH, 2), [("HBD", s, h)], ("HBW", h % 2))
                ts("dve", CB[:, h: h + 1], H_[:, 0:1], lkL(s), None, ALU.mult, reads=wkk(gH) + [("LINK",)], writes=[("CB",)])
                return
            H_ = wk_f32(gH1, 0, 2048)
            A_, I_ = wk_f32(gA, 0, 2048), wk_f32(gI, 0, 2048)
            ini = CF[:, h: h + 1]
            P.op("dve", lambda e: e.tensor_tensor_scan(out=H_[:, 0:1024], data0=A_[:, 0:1024], data1=I_[:, 0:1024], initial=ini, op0=ALU.mult, op1=ALU.add),
                 wkk(gA) + wkk(gI) + [("CF",)], wkk(gH1))
            ini2 = H_[:, 1023:1024]
            P.op("dve", lambda e: e.tensor_tensor_scan(out=H_[:, 1024:2048], data0=A_[:, 1024:2048], data1=I_[:, 1024:2048], initial=ini2, op0=ALU.mult, op1=ALU.add),
                 wkk(gA + 1) + wkk(gI + 1) + wkk(gH1), wkk(gH1 + 1))
            ts("dve", CF[:, h: h + 1], H_[:, T - 1: T], lkR(s), None, ALU.mult, reads=wkk(gH1 + 1) + [("LINK",)], writes=[("CF",)])
            for half in range(2):
                tt("dve", wk_f32(gH1 + half), wk_f32(gH1 + half), wk_f32(gH2 + half), ALU.add, reads=wkk(gH1 + half) + wkk(gH2 + half), writes=wkk(gH1 + half))
            for half in range(2):
                pg = next_ps()
                for t2 in range(2):
                    for k in range(8):
                        mm(ps(pg, t2 * 512, 512), wt(sg_, k), xh(k, half * 1024 + t2 * 512, 512), k == 0, k == 7,
                           reads=[("R", sg_), ("XH", k, half)], writes=[psk(pg)], signal=(k == 7 and t2 == 1))
                act(wk_f32(gR + half), ps(pg), AF.Gelu_apprx_tanh, bias=cst(l, "bin", OFF_RG_G // 128 + h),
                    reads=[psk(pg), ("CST",)], writes=wkk(gR + half))
                tt("dve", big_bf(h, half * 1024, 1024), wk_f32(gH1 + half), wk_f32(gR + half), ALU.mult,
                   reads=wkk(gH1 + half) + wkk(gR + half), writes=bgk(h))

        out_deps = []
        for si in range(NSEG):
            for half in range(2):
                for tb in range(8):
                    g = tb
                    tok0 = half * 1024 + tb * 128
                    xt = wk_f32(g)
                    P.dma("sp", xt, xin[si, tok0: tok0 + 128, :], [], wkk(g), ("XT", tb))
                    P.op("dve", lambda e, xt=xt: e.bn_stats(out=SM[:, 0:6], in_=xt[:, 0:512]), wkk(g), [("SM",)])
                    P.op("dve", lambda e, xt=xt: e.bn_stats(out=SM[:, 6:12], in_=xt[:, 512:1024]), wkk(g), [("SM",)])
                    P.op("dve", lambda e: e.bn_aggr(out=SM[:, 12:14], in_=SM[:, 0:12]), [("SM",)], [("SM",)])
                    act(SM[:, 14:15], SM[:, 13:14], AF.Ln, bias=kf(1), reads=[("SM",), ("KF",)], writes=[("SM",)])
                    act(SM[:, 14:15], SM[:, 14:15], AF.Exp, scale=-0.5, reads=[("SM",)], writes=[("SM",)])
                    ts("dve", xt, xt, SM[:, 12:13], SM[:, 14:15], ALU.subtract, ALU.mult, reads=wkk(g) + [("SM",)], writes=wkk(g))
                for c in range(8):
                    pg = next_ps()
                    for tb in range(8):
                        tr(ps(pg, tb * 128, 128), wk_f32(tb, c * 128, 128), IDF[:], reads=wkk(tb) + [("IDF",)], writes=[psk(pg)],
                           signal=(tb == 7))
                    yg = 8 + (c % 2)
                    act(wk_f32(yg), ps(pg), AF.Identity, bias=LNIN[:, 8 + c: 9 + c], scale=LNIN[:, c: c + 1],
                        reads=[psk(pg), ("LNIN",)], writes=wkk(yg))
                    store_x(si, c, half, wk_f32(yg), wkk(yg), par=0)

        for l in range(L):
            last = (l == L - 1)
            par = l % 2
            memset("dve", CB[:], 0.0, [("CB",)])
            for si in reversed(range(NSEG)):
                load_xh(par, si)
                for h in range(8):
                    rg_head(l, si, h, "A")
            memset("dve", CF[:], 0.0, [("CF",)])
            for si in range(NSEG):
                load_xh(par, si)
                for h in range(8):
                    rg_head(l, si, h, "B")

                for ch in range(4):
                    su = load_unit(l, ("u", ch))
                    for half in range(2):
                        pg = next_ps()
                        for t2 in range(2):
                            for k in range(8):
                                mm(ps(pg, t2 * 512, 512), wt(su, k), xh(k, half * 1024 + t2 * 512, 512), k == 0, k == 7,
                                   reads=[("R", su), ("XH", k, half)], writes=[psk(pg)], signal=(k == 7 and t2 == 1))
                        act(big_bf(8 + ch, half * 1024, 1024), ps(pg), AF.Gelu_apprx_tanh, bias=cst(l, "bin", OFF_SG // 128 + ch),
                            reads=[psk(pg), ("CST",)], writes=bgk(8 + ch))
                sv = [load_unit(l, ("v", j)) for j in range(4)]
                SGO3 = BIG[:, 8 * 2048: 12 * 2048].rearrange("p (g t) -> p g t", g=4)
                for tb in range(16):
                    half = tb // 8
                    pg = next_ps()
                    for j in range(4):
                        for k in range(8):
                            mm(ps(pg, j * 128, 128), xh(k, tb * 128, 128), wt(sv[j], k), k == 0, False,
                               reads=[("R", sv[j]), ("XH", k, half)], writes=[psk(pg)])
                        mm(ps(pg, j * 128, 128), ONE[32 * l: 32 * l + 1, :], ROWP[32 * l: 32 * l + 1, j * 128:(j + 1) * 128], False, True,
                           reads=[("ONE",), ("ROWP",)], writes=[psk(pg)], signal=(j == 3))
                    gv = tb % 2
                    V = wk_f32(gv, 0, 512)
                    VN = wk_bf(2 + tb % 2, 0, 512)
                    act(V, ps(pg, 0, 512), AF.Gelu_apprx_tanh, reads=[psk(pg)], writes=wkk(gv))
                    P.op("dve", lambda e, V=V: e.bn_stats(out=SM[:, 16:22], in_=V), wkk(gv), [("SM2",)])
                    P.op("dve", lambda e: e.bn_aggr(out=SM[:, 22:24], in_=SM[:, 16:22]), [("SM2",)], [("SM2",)])
                    act(SM[:, 24:25], SM[:, 23:24], AF.Ln, bias=kf(1), reads=[("SM2",), ("KF",)], writes=[("SM2",)])
                    act(SM[:, 24:25], SM[:, 24:25], AF.Exp, scale=-0.5, reads=[("SM2",)], writes=[("SM2",)])
                    ts("dve", VN, V, SM[:, 22:23], SM[:, 24:25], ALU.subtract, ALU.mult, reads=wkk(gv) + [("SM2",)], writes=wkk(2 + tb % 2))
                    pg2 = next_ps()
                    for g in range(4):
                        mm(ps(pg2, g * 128, 128), VN[:, g * 128:(g + 1) * 128], SGWT[:, l * 512 + g * 128: l * 512 + (g + 1) * 128], True, True,
                           reads=wkk(2 + tb % 2) + [("SGWT", l)], writes=[psk(pg2)], signal=(g == 3))
                    gt_ = 4 + tb % 2
                    TT = wk_f32(gt_, 0, 512)
                    tt("dve", TT, ps(pg2, 0, 512), GMAT[:, l * 512:(l + 1) * 512], ALU.mult, reads=[psk(pg2), ("GMAT", l)], writes=wkk(gt_))
                    tt("dve", TT, TT, BIASM[:, l * 512:(l + 1) * 512], ALU.add, reads=wkk(gt_) + [("BIASM", l)], writes=wkk(gt_))
                    uview = SGO3[:, :, tb * 128:(tb + 1) * 128]
                    tt("dve", uview, TT.rearrange("p (g t) -> p g t", g=4), uview, ALU.mult, reads=wkk(gt_) + bgk(8, 4), writes=bgk(8, 4))

                for ch in range(4):
                    sc = load_unit(l, ("c", ch))
                    scg = load_unit(l, ("cg", ch))
                    for k in range(31):
                        ts("dve", DC[:, k * 128:(k + 1) * 128], IDB[:], cst(l, "ccw", k * 4 + ch), None, ALU.mult,
                           reads=[("IDB",), ("CST",)], writes=[("DC",)])
                    bc_, bcg_ = cst(l, "bin", OFF_CC // 128 + ch), cst(l, "bin", OFF_CC // 128 + 4 + ch)
                    for half in range(2):
                        pc, pgg = next_ps(), next_ps()
                        for (pp, ss) in ((pc, sc), (pgg, scg)):
                            for t2 in range(2):
                                for k in range(8):
                                    mm(ps(pp, t2 * 512, 512), wt(ss, k), xh(k, half * 1024 + t2 * 512, 512), k == 0, k == 7,
                                       reads=[("R", ss), ("XH", k, half)], writes=[psk(pp)], signal=(k == 7 and t2 == 1))
                        gs = 8 + half
                        act(wk_f32(gs), ps(pgg), AF.Sigmoid, bias=bcg_, reads=[psk(pgg), ("CST",)], writes=wkk(gs))
                        stt("dve", PAD[:, 15 + half * 1024: 15 + half * 1024 + 1024], ps(pc), bc_, wk_f32(gs),
                            ALU.add, ALU.mult, reads=[psk(pc), ("CST",)] + wkk(gs), writes=[("PAD",)])
                    pc, pgg = next_ps(), next_ps()
                    for (pp, ss) in ((pc, sc), (pgg, scg)):
                        for k in range(8):
                            mm(ps(pp, 0, 32), wt(ss, k), xh(k, T, 32), k == 0, k == 7, reads=[("R", ss), ("XHH",)], writes=[psk(pp)], signal=(k == 7))
                    act(HT[:, 32:64], ps(pgg, 0, 32), AF.Sigmoid, bias=bcg_, reads=[psk(pgg), ("CST",)], writes=[("HT",)])
                    stt("dve", HT[:, 0:32], ps(pc, 0, 32), bc_, HT[:, 32:64], ALU.add, ALU.mult, reads=[psk(pc), ("CST",), ("HT",)], writes=[("HT",)])
                    ts("dve", PAD[:, 0:15], HT[:, 1:16], lkL(si), None, ALU.mult, reads=[("HT",), ("LINK",)], writes=[("PAD",)])
                    ts("dve", PAD[:, 15 + T: 30 + T], HT[:, 16:31], lkR(si), None, ALU.mult, reads=[("HT",), ("LINK",)], writes=[("PAD",)])
                    for half in range(2):
                        pg = next_ps()
                        for t2 in range(2):
                            for k in range(31):
                                o = k + half * 1024 + t2 * 512
                                mm(ps(pg, t2 * 512, 512), DC[:, k * 128:(k + 1) * 128], PAD[:, o: o + 512], k == 0, k == 30,
                                   reads=[("DC",), ("PAD",)], writes=[psk(pg)], signal=(k == 30 and t2 == 1))
                        act(wk_f32(2 * ch + half), ps(pg), AF.Identity, bias=cst(l, "ccb", ch), reads=[psk(pg), ("CST",)], writes=wkk(2 * ch + half))
                def cc_cb(c, half, Tap, Tk):
                    act(big_bf(12 + c, half * 1024, 1024), Tap, AF.Silu, bias=cst(l, "cclb", c), scale=cst(l, "cclg", c),
                        reads=Tk + [("CST",)], writes=bgk(12 + c))
                layer_norm(4, 128, lambda c, lo, n: wk_f32(2 * c, lo, n), lambda c, half: wkk(2 * c + half), cc_cb, [8, 9, 10, 11])

                for oc in range(8):
                    sl = {nm: load_unit(l, (nm, oc), 1024 if nm in ("ga", "gb", "gc", "ba") else 512) for nm in ("ga", "gb", "gc", "ba", "bb", "bc")}
                    for half in range(2):
                        for bi, (gn, yn, nk, src0) in enumerate((("ga", "ba", 8, 0), ("gb", "bb", 4, 8), ("gc", "bc", 4, 12))):
                            pgt, py = next_ps(), next_ps()
                            for t2 in range(2):
                                for k in range(8):
                                    mm(ps(pgt, t2 * 512, 512), wt(sl[gn], k), xh(k, half * 1024 + t2 * 512, 512), k == 0, k == 7,
                                       reads=[("R", sl[gn]), ("XH", k, half)], writes=[psk(pgt)], signal=(k == 7 and t2 == 1))
                            for t2 in range(2):
                                for k in range(nk):
                                    mm(ps(py, t2 * 512, 512), wt(sl[yn], k), big_bf(src0 + k, half * 1024 + t2 * 512, 512), k == 0, k == nk - 1,
                                       reads=[("R", sl[yn])] + bgk(src0 + k), writes=[psk(py)], signal=(k == nk - 1 and t2 == 1))
                            gg = 8 + bi
                            act(wk_f32(gg), ps(pgt), AF.Sigmoid, bias=cst(l, "bin", OFF_GATE // 128 + bi * 8 + oc), reads=[psk(pgt), ("CST",)], writes=wkk(gg))
                            tt("dve", wk_f32(gg), ps(py), wk_f32(gg), ALU.mult, reads=[psk(py)] + wkk(gg), writes=wkk(gg))
                        tt("dve", wk_f32(8), wk_f32(8), wk_f32(9), ALU.add, reads=wkk(8) + wkk(9), writes=wkk(8))
                        tt("dve", wk_bf(oc, half * 1024, 1024), wk_f32(8), wk_f32(10), ALU.add, reads=wkk(8) + wkk(10), writes=wkk(oc))

                def resid(oc, half, pg, bname, zdst, zkeys_w):
                    gx = 9 + (2 * oc + half) % 2
                    ge = 11 + (2 * oc + half) % 2
                    P.dma("sp", wk_f32(gx), xd[si * 8 + oc, :, half * 1024: half * 1024 + 1024], [("XD", si, oc, half)], wkk(gx), ("XDR", gx))
                    act(wk_f32(ge), ps(pg), AF.Identity, bias=cst(l, bname, oc), reads=[psk(pg), ("CST",)], writes=wkk(ge))
                    stt("dve", zdst, wk_f32(gx), ALPHA, wk_f32(ge), ALU.mult, ALU.add, reads=wkk(gx) + wkk(ge), writes=zkeys_w)
                for oc in range(8):
                    so = load_unit(l, ("o", oc))
                    for half in range(2):
                        pg = next_ps()
                        for t2 in range(2):
                            for k in range(8):
                                mm(ps(pg, t2 * 512, 512), wt(so, k), wk_bf(k, half * 1024 + t2 * 512, 512), k == 0, k == 7,
                                   reads=[("R", so)] + wkk(k), writes=[psk(pg)], signal=(k == 7 and t2 == 1))
                        resid(oc, half, pg, "bo", big_f32(oc, half * 1024, 1024), bgk(2 * oc + half))

                def ln_x_cb(gname, bname, par_out):
                    def cb(c, half, Tap, Tk):
                        yg = 6 + (2 * c + half) % 2
                        act(wk_f32(yg), Tap, AF.Identity, bias=cst(l, bname, c), scale=cst(l, gname, c), reads=Tk + [("CST",)], writes=wkk(yg))
                        store_x(si, c, half, wk_f32(yg), wkk(yg), par=par_out)
                    return cb
                layer_norm(8, 0, lambda c, lo, n: big_f32(c, lo, n), lambda c, half: bgk(2 * c + half), ln_x_cb("l1g", "l1b", None), [0, 1, 2, 3])

                for J in range(4):
                    for jj in range(8):
                        j = J * 8 + jj
                        s1 = load_unit(l, ("f1", j))
                        for half in range(2):
                            pg = next_ps()
                            for t2 in range(2):
                                for k in range(8):
                                    mm(ps(pg, t2 * 512, 512), wt(s1, k), xh(k, half * 1024 + t2 * 512, 512), k == 0, k == 7,
                                       reads=[("R", s1), ("XH", k, half)], writes=[psk(pg)], signal=(k == 7 and t2 == 1))
                            gr = 8 + (2 * jj + half) % 2
                            act(wk_f32(gr), ps(pg), AF.Relu, bias=cst(l, "bf1", j), reads=[psk(pg), ("CST",)], writes=wkk(gr))
                            tt("dve", wk_bf(jj, half * 1024, 1024), wk_f32(gr), wk_f32(gr), ALU.mult, reads=wkk(gr), writes=wkk(jj))
                    s2 = [load_unit(l, ("f2", J * 8 + jj)) for jj in range(8)]
                    for oc in range(8):
                        for half in range(2):
                            pg = next_ps()
                            for t2 in range(2):
                                for jj in range(8):
                                    mm(ps(pg, t2 * 512, 512), wt(s2[jj], oc), wk_bf(jj, half * 1024 + t2 * 512, 512), jj == 0, jj == 7,
                                       reads=[("R", s2[jj])] + wkk(jj), writes=[psk(pg)], signal=(jj == 7 and t2 == 1))
                            acc = big_f32(oc, half * 1024, 1024)
                            if J == 0:
                                cp("dve", acc, ps(pg), [psk(pg)], bgk(2 * oc + half))
                            else:
                                tt("dve", acc, ps(pg), acc, ALU.add, reads=[psk(pg)] + bgk(2 * oc + half), writes=bgk(2 * oc + half))
                for oc in range(8):
                    for half in range(2):
                        gx = 9 + (2 * oc + half) % 2
                        acc = big_f32(oc, half * 1024, 1024)
                        P.dma("sp", wk_f32(gx), xd[si * 8 + oc, :, half * 1024: half * 1024 + 1024], [("XD", si, oc, half)], wkk(gx), ("XDR", gx))
                        act(acc, acc, AF.Identity, bias=cst(l, "bf2", oc), reads=bgk(2 * oc + half) + [("CST",)], writes=bgk(2 * oc + half))
                        stt("dve", acc, wk_f32(gx), ALPHA, acc, ALU.mult, ALU.add, reads=wkk(gx) + bgk(2 * oc + half), writes=bgk(2 * oc + half))
                if not last:
                    layer_norm(8, 0, lambda c, lo, n: big_f32(c, lo, n), lambda c, half: bgk(2 * c + half), ln_x_cb("l2g", "l2b", 1 - par), [0, 1, 2, 3])
                else:
                    def fin_cb(c, half, Tap, Tk):
                        act(wk_f32(4 + c), Tap, AF.Identity, bias=cst(l, "l2b", c), scale=cst(l, "l2g", c), reads=Tk + [("CST",)], writes=wkk(4 + c))
                        if c == 7:
                            for tb in range(8):
                                pg = next_ps()
                                for cc in range(8):
                                    tr(ps(pg, cc * 128, 128), wk_f32(4 + cc, tb * 128, 128), IDF[:], reads=wkk(4 + cc) + [("IDF",)], writes=[psk(pg)],
                                       signal=(cc == 7))
                                go = 12 if tb % 2 == 0 else 3
                                if tb % 2 == 0:
                                    cp("dve", wk_f32(go), ps(pg), [psk(pg)], wkk(go))
                                else:
                                    act(wk_f32(go), ps(pg), AF.Identity, reads=[psk(pg)], writes=wkk(go))
                                tok0 = half * 1024 + tb * 128
                                dep = P.dma("sp", yout[si, tok0: tok0 + 128, :], wk_f32(go), wkk(go), [("YO", si, half, tb)], ("YO", tb % 2))
                                out_deps.append(dep)
                    layer_norm(8, 0, lambda c, lo, n: big_f32(c, lo, n), lambda c, half: bgk(2 * c + half), fin_cb, [0, 1, 2, 3])

        fin = {}
        for s, v in out_deps:
            fin[s] = max(fin.get(s, 0), v)
        P.final_wait("sp", list(fin.items()))
        with nc.Block() as block:
            P.emit(block)
    return nc


def _colchunk(W, j, nk):
    blk = W[:, j * 128:(j + 1) * 128].reshape(nk, 128, 128)
    out = np.zeros((128, 1024), np.float32)
    out[:, : nk * 128] = blk.transpose(1, 0, 2).reshape(128, nk * 128)
    return out


def _prep_weights(inp):
    wu = np.zeros((L, NU, 128, 1024), np.float32)
    cst = np.zeros((L, 128, NCST), np.float32)
    rows = np.zeros((L, 1, NROW), np.float32)
    for l in range(L):
        w_in = inp["w_in"][l]
        for h in range(8):
            wu[l, UNITS[("x", h)]] = _colchunk(w_in, OFF_RG_X // 128 + h, 8)
            wu[l, UNITS[("g", h)]] = _colchunk(w_in, OFF_RG_G // 128 + h, 8)
            gt = np.zeros((128, 1024), np.float32)
            gt[:, 0:128] = inp["rg_wa"][l, 0, h]
            gt[:, 128:256] = inp["rg_wx"][l, 0, h]
            gt[:, 256:384] = inp["rg_wa"][l, 1, h]
            gt[:, 384:512] = inp["rg_wx"][l, 1, h]
            wu[l, UNITS[("gt", h)]] = gt
        for ch in range(4):
            wu[l, UNITS[("u", ch)]] = _colchunk(w_in, OFF_SG // 128 + ch, 8)
            wu[l, UNITS[("v", ch)]] = _colchunk(w_in, OFF_SG // 128 + 4 + ch, 8)
            wu[l, UNITS[("c", ch)]] = _colchunk(w_in, OFF_CC // 128 + ch, 8)
            wu[l, UNITS[("cg", ch)]] = _colchunk(w_in, OFF_CC // 128 + 4 + ch, 8)
        for oc in range(8):
            wu[l, UNITS[("ga", oc)]] = _colchunk(w_in, OFF_GATE // 128 + oc, 8)
            wu[l, UNITS[("gb", oc)]] = _colchunk(w_in, OFF_GATE // 128 + 8 + oc, 8)
            wu[l, UNITS[("gc", oc)]] = _colchunk(w_in, OFF_GATE // 128 + 16 + oc, 8)
            wu[l, UNITS[("ba", oc)]] = _colchunk(inp["w_ba"][l], oc, 8)
            wu[l, UNITS[("bb", oc)]] = _colchunk(inp["w_bb"][l], oc, 4)
            wu[l, UNITS[("bc", oc)]] = _colchunk(inp["w_bc"][l], oc, 4)
            wu[l, UNITS[("o", oc)]] = _colchunk(inp["w_o"][l], oc, 8)
        for j in range(32):
            wu[l, UNITS[("f1", j)]] = _colchunk(inp["w_ff1"][l], j, 8)
            wu[l, UNITS[("f2", j)]] = inp["w_ff2"][l][j * 128:(j + 1) * 128, :]
        def put(name, vec, n):
            cst[l, :, _c[name]: _c[name] + n] = np.asarray(vec, np.float32).reshape(n, 128).T
        put("bin", inp["b_in"][l], 56)
        put("caw", inp["conv_a_w"][l].reshape(-1), 32)
        put("cab", inp["conv_a_b"][l], 8)
        put("rba", inp["rg_ba"][l].reshape(-1), 16)
        put("rbx", inp["rg_bx"][l].reshape(-1), 16)
        put("lam", inp["rg_lambda"][l].reshape(-1), 16)
        put("sglg", inp["sg_ln_g"][l], 4)
        put("ccw", inp["conv_c_w"][l].reshape(-1), 124)
        put("ccb", inp["conv_c_b"][l], 4)
        put("cclg", inp["cc_ln_g"][l], 4)
        put("cclb", inp["cc_ln_b"][l], 4)
        put("bo", inp["b_o"][l], 8)
        put("l1g", inp["ln1_g"][l], 8)
        put("l1b", inp["ln1_b"][l], 8)
        put("bf1", inp["b_ff1"][l], 32)
        put("bf2", inp["b_ff2"][l], 8)
        put("l2g", inp["ln2_g"][l], 8)
        put("l2b", inp["ln2_b"][l], 8)
        rows[l, 0, 0:512] = inp["sg_ln_b"][l]
        rows[l, 0, 512:1024] = inp["sg_b"][l].reshape(-1)
        rows[l, 0, 1024:1536] = inp["b_in"][l][OFF_SG + 512: OFF_SG + 1024]
    lnin = np.zeros((128, 16), np.float32)
    lnin[:, 0:8] = np.asarray(inp["ln_in_g"], np.float32).reshape(8, 128).T
    lnin[:, 8:16] = np.asarray(inp["ln_in_b"], np.float32).reshape(8, 128).T
    sgwt = np.ascontiguousarray(np.asarray(inp["sg_w"], np.float32).transpose(0, 1, 3, 2))
    return wu, cst, rows, lnin, sgwt


_NC_CACHE = {}
_SAMPLE_SLOTS = [(c, k) for c, n in zip(range(1, 8), (3, 3, 2, 2, 2, 2, 2)) for k in range(n)]


def _core_inputs(inp):
    xp = inp["x_prompt"].astype(np.float32, copy=False)
    xs = inp["x_sample"].astype(np.float32, copy=False)
    xins = [np.zeros((NSEG, T, D), np.float32) for _ in range(NCORES)]
    links = [np.zeros((128, 16), np.float32) for _ in range(NCORES)]
    xins[0][:] = xp[0].reshape(NSEG, T, D)
    links[0][:, 1:8] = 1.0
    links[0][:, 8:15] = 1.0
    for i, (c, k) in enumerate(_SAMPLE_SLOTS):
        xins[c][k] = xs[i]
    return xins, links


def kernel(**inputs):
    inp = {k: np.asarray(v) for k, v in inputs.items()}
    wu, cst, rows, lnin, sgwt = _prep_weights(inp)
    xins, links = _core_inputs(inp)
    if "nc" not in _NC_CACHE:
        _NC_CACHE["nc"] = build(None)
    nc = _NC_CACHE["nc"]
    ident = np.eye(128, dtype=np.float32)
    in_maps = [{"xin": xins[c], "link": links[c], "wu": wu, "cst": cst, "rows": rows, "lnin": lnin, "sgwt": sgwt, "ident": ident}
               for c in range(NCORES)]
    res = run_bass_kernel_spmd(nc, in_maps, core_ids=list(range(NCORES)))
    y_prompt = np.ascontiguousarray(res.results[0]["yout"]).reshape(1, NSEG * T, D).astype(np.float32, copy=False)
    y_sample = np.zeros((16, T, D), np.float32)
    for i, (c, k) in enumerate(_SAMPLE_SLOTS):
        y_sample[i] = res.results[c]["yout"][k]
    return (y_prompt, y_sample)
```

```python
import numpy as np
import os
KOPT = {"b", "d", "e"}
import concourse.bass as bass
import concourse.mybir as mybir
from concourse.bass_utils import run_bass_kernel_spmd

F32 = mybir.dt.float32
BF16 = mybir.dt.bfloat16
AF = mybir.ActivationFunctionType
ALU = mybir.AluOpType

D = 1024
T = 2048
HALO = 16
NSEG = 8
L = 2
NCORES = 8
ALPHA = float((2 * L) ** 0.25)
EPS = 1e-5
OFF_RG_X, OFF_RG_G, OFF_SG, OFF_CC, OFF_GATE = 0, 1024, 2048, 3072, 4096
NSLOT = 8
NWK = 13

_c = {}
_o = 0
def _add(name, n):
    global _o
    _c[name] = _o
    _o += n
_add("bin", 56); _add("caw", 32); _add("cab", 8); _add("rba", 16); _add("rbx", 16); _add("lam", 16)
_add("sglg", 4); _add("ccw", 124); _add("ccb", 4); _add("cclg", 4); _add("cclb", 4)
_add("bo", 8); _add("l1g", 8); _add("l1b", 8); _add("bf1", 32); _add("bf2", 8); _add("l2g", 8); _add("l2b", 8)
NCST = _o
NROW = 1536

def _units():
    u = {}
    n = 0
    for h in range(8):
        u[("x", h)] = n; u[("g", h)] = n + 1; u[("gt", h)] = n + 2; n += 3
    for ch in range(4):
        u[("u", ch)] = n; n += 1
    for j in range(4):
        u[("v", j)] = n; n += 1
    for ch in range(4):
        u[("c", ch)] = n; u[("cg", ch)] = n + 1; n += 2
    for oc in range(8):
        for k, nm in enumerate(("ga", "gb", "gc", "ba", "bb", "bc")):
            u[(nm, oc)] = n + k
        n += 6
    for oc in range(8):
        u[("o", oc)] = n; n += 1
    for J in range(4):
        for jj in range(8):
            u[("f1", J * 8 + jj)] = n; n += 1
        for jj in range(8):
            u[("f2", J * 8 + jj)] = n; n += 1
    return u, n
UNITS, NU = _units()


class Prog:
    ENGS = ("pe", "act", "dve", "pool", "sp")

    def __init__(self, nc, es):
        self.nc = nc
        self.es = es
        self.streams = {e: [] for e in self.ENGS}
        self.cnt = {e: 0 for e in self.ENGS}
        self.known = {e: {} for e in self.ENGS}
        self.buf = {}
        self.sems = {}
        self.dcnt = {}
        for e in self.ENGS:
            self.sems[e] = es.enter_context(nc.semaphore("s_" + e))
        self.out_deps = []

    def dsem(self, key):
        if key not in self.sems:
            self.sems[key] = self.es.enter_context(self.nc.semaphore("d_%d" % len(self.sems)))
            self.dcnt[key] = 0
        return key

    def _deps(self, engine, reads, writes):
        deps = {}
        def add(d):
            if d is None:
                return
            s, v = d
            if deps.get(s, 0) < v:
                deps[s] = v
        for k in reads:
            st = self.buf.get(k)
            if st:
                add(st["w"])
        for k in writes:
            st = self.buf.get(k)
            if st:
                add(st["w"])
                for s, v in st["r"].items():
                    add((s, v))
        waits = []
        kn = self.known[engine]
        for s, v in deps.items():
            if engine == "pe" and s == "pe":
                continue
            if kn.get(s, 0) < v:
                waits.append((s, v))
                kn[s] = v
        return waits

    def _commit(self, dep, reads, writes):
        for k in writes:
            self.buf[k] = {"w": dep, "r": {}}
        for k in reads:
            st = self.buf.setdefault(k, {"w": None, "r": {}})
            if st["r"].get(dep[0], 0) < dep[1]:
                st["r"][dep[0]] = dep[1]

    def op(self, engine, fn, reads=(), writes=(), signal=True):
        waits = self._deps(engine, reads, writes)
        sems = self.sems
        if signal:
            self.cnt[engine] += 1
            dep = (engine, self.cnt[engine])
        else:
            dep = (engine, self.cnt[engine] + 1)
        semh = sems[engine]
        def thunk(eng, waits=waits, fn=fn, signal=signal):
            for s, v in waits:
                eng.wait_ge(sems[s], v)
            ins = fn(eng)
            if signal:
                ins.then_inc(semh, 1)
        self.streams[engine].append(thunk)
        self._commit(dep, reads, writes)

    def dma(self, engine, out, in_, reads, writes, semkey):
        self.dsem(semkey)
        waits = self._deps(engine, reads, writes)
        self.dcnt[semkey] += 16
        dep = (semkey, self.dcnt[semkey])
        sems = self.sems
        def thunk(eng, waits=waits):
            for s, v in waits:
                eng.wait_ge(sems[s], v)
            eng.dma_start(out=out, in_=in_).then_inc(sems[semkey], 16)
        self.streams[engine].append(thunk)
        self._commit(dep, reads, writes)
        return dep

    def final_wait(self, engine, deps):
        sems = self.sems
        def thunk(eng):
            for s, v in deps:
                eng.wait_ge(sems[s], v)
        self.streams[engine].append(thunk)

    def emit(self, block):
        P = self
        @block.tensor
        def _(e):
            for t in P.streams["pe"]:
                t(e)
        @block.scalar
        def _(e):
            for t in P.streams["act"]:
                t(e)
        @block.vector
        def _(e):
            for t in P.streams["dve"]:
                t(e)
        @block.gpsimd
        def _(e):
            for t in P.streams["pool"]:
                t(e)
        @block.sync
        def _(e):
            for t in P.streams["sp"]:
                t(e)


def build(seg_kinds, debug=False):
    import contextlib
    nc = bass.Bass("TRN2", target_bir_lowering=False)
    xin = nc.dram_tensor("xin", [NSEG, T, D], F32, kind="ExternalInput").ap()
    wu = nc.dram_tensor("wu", [L, NU, 128, 1024], F32, kind="ExternalInput").ap()
    cst_d = nc.dram_tensor("cst", [L, 128, NCST], F32, kind="ExternalInput").ap()
    row_d = nc.dram_tensor("rows", [L, 1, NROW], F32, kind="ExternalInput").ap()
    lnin_d = nc.dram_tensor("lnin", [128, 16], F32, kind="ExternalInput").ap()
    sgwt_d = nc.dram_tensor("sgwt", [L, 4, 128, 128], F32, kind="ExternalInput").ap()
    ident_d = nc.dram_tensor("ident", [128, 128], F32, kind="ExternalInput").ap()
    yout = nc.dram_tensor("yout", [NSEG, T, D], F32, kind="ExternalOutput").ap()
    link_d = nc.dram_tensor("link", [128, 16], F32, kind="ExternalInput").ap()
    xd = nc.dram_tensor("xd", [NSEG * 8, 128, T], F32, kind="Internal").ap()
    xhd = nc.dram_tensor("xhd", [2 * NSEG * 8, 128, T], BF16, kind="Internal").ap()
    hbd = nc.dram_tensor("hbd", [NSEG * 8, 128, T], F32, kind="Internal").ap()

    es = contextlib.ExitStack()
    with es:
        P = Prog(nc, es)
        sb = lambda name, shape, dt: es.enter_context(nc.sbuf_tensor(name, shape, dt))
        XH = sb("XH", [128, 8 * (T + 2 * HALO)], BF16)
        BIG = sb("BIG", [128, 32768], BF16)
        WK = sb("WK", [128, NWK * 2048], BF16)
        PAD = sb("PAD", [128, 2176], BF16)
        DC = sb("DC", [128, 31 * 128], BF16)
        DA = sb("DA", [128, 2 * 4 * 128], BF16)
        RING = sb("RING", [128, NSLOT * 1024], BF16)
        CST = sb("CST", [128, L * NCST], F32)
        CX = sb("CX", [128, L * 64], F32)
        ROWP = sb("ROWP", [33, 512], F32)
        LNIN = sb("LNIN", [128, 16], F32)
        SGWT = sb("SGWT", [128, L * 512], BF16)
        BIASM = sb("BIASM", [128, L * 512], F32)
        GMAT = sb("GMAT", [128, L * 512], F32)
        IDB = sb("IDB", [128, 128], BF16)
        IDF = sb("IDF", [128, 128], F32)
        ONESD = sb("ONESD", [128, 256], F32)
        ONE = sb("ONE", [128, 128], F32)
        ONESB = sb("ONESB", [128, 256], BF16)
        KF = sb("KF", [128, 8], F32)
        SM = sb("SM", [128, 64], F32)
        LINK = sb("LINK", [128, 16], F32)
        CF = sb("CF", [128, 8], F32)
        CB = sb("CB", [128, 8], F32)
        HT = sb("HT", [128, 64], F32)
        PS = es.enter_context(nc.psum_tensor("PS", [128, 4096], F32))

        XW = T + 2 * HALO
        def xh(c, lo, n):
            return XH[:, c * XW + lo: c * XW + lo + n]
        def big_bf(g, lo=0, n=2048):
            return BIG[:, g * 2048 + lo: g * 2048 + lo + n]
        def big_f32(c, lo=0, n=2048):
            return BIG[:, c * 4096: (c + 1) * 4096].bitcast(F32)[:, lo: lo + n]
        def wk_bf(g, lo=0, n=2048):
            return WK[:, g * 2048 + lo: g * 2048 + lo + n]
        def wk_f32(g, lo=0, n=1024):
            return WK[:, g * 2048: g * 2048 + 2 * (lo + n)].bitcast(F32)[:, lo: lo + n]
        def wkk(g, n=1):
            return [("W", g + i) for i in range(n)]
        def bgk(g, n=1):
            return [("B", g + i) for i in range(n)]
        def ps(g, lo=0, n=1024):
            return PS[:, g * 1024 + lo: g * 1024 + lo + n]
        psk = lambda g: ("PS", g)
        def cst(l, name, j=0):
            o = l * NCST + _c[name] + j
            return CST[:, o: o + 1]
        def cx(l, o):
            return CX[:, l * 64 + o: l * 64 + o + 1]
        kf = lambda j: KF[:, j: j + 1]

        state = {"psg": 0, "slot": 0, "da": [None, None]}
        def next_ps():
            g = state["psg"]
            state["psg"] = (g + 1) % 4
            return g

        def load_unit(l, key, n=1024):
            s = state["slot"]
            state["slot"] = (s + 1) % NSLOT
            u = UNITS[key]
            P.dma("pool", RING[:, s * 1024: s * 1024 + n], wu[l, u, :, 0:n], reads=[], writes=[("R", s)], semkey=("R", s))
            return s
        def wt(s, k, n=128, width=128):
            return RING[:, s * 1024 + k * width: s * 1024 + k * width + n]

        def act(out, in_, func, bias=None, scale=1.0, reads=(), writes=()):
            def fn(e):
                kw = {}
                if bias is not None:
                    kw["bias"] = bias
                return e.activation(out=out, in_=in_, func=func, scale=scale, **kw)
            P.op("act", fn, reads, writes)
        def tt(eng, out, in0, in1, op, reads=(), writes=()):
            P.op(eng, lambda e: e.tensor_tensor(out=out, in0=in0, in1=in1, op=op), reads, writes)
        def ts(eng, out, in0, s1, s2, op0, op1=None, reads=(), writes=()):
            if op1 is None:
                P.op(eng, lambda e: e.tensor_scalar(out=out, in0=in0, scalar1=s1, scalar2=None, op0=op0), reads, writes)
            else:
                P.op(eng, lambda e: e.tensor_scalar(out=out, in0=in0, scalar1=s1, scalar2=s2, op0=op0, op1=op1), reads, writes)
        def stt(eng, out, in0, scalar, in1, op0, op1, reads=(), writes=()):
            P.op(eng, lambda e: e.scalar_tensor_tensor(out=out, in0=in0, scalar=scalar, in1=in1, op0=op0, op1=op1), reads, writes)
        def cp(eng, out, in_, reads=(), writes=()):
            P.op(eng, lambda e: e.tensor_copy(out=out, in_=in_), reads, writes)
        def mm(out, lhsT, rhs, start, stop, reads=(), writes=(), signal=False):
            P.op("pe", lambda e: e.matmul(out, lhsT, rhs, start=start, stop=stop), reads, writes, signal=signal)
        def tr(out, in_, ident, reads=(), writes=(), signal=False):
            P.op("pe", lambda e: e.transpose(out, in_, ident), reads, writes, signal=signal)
        def memset(eng, ap, val, writes=()):
            P.op(eng, lambda e: e.memset(ap, val), (), writes)

        memset("dve", KF[:, 0:1], 1.0, [("KF",)])
        memset("dve", KF[:, 1:2], EPS, [("KF",)])
        memset("dve", KF[:, 2:3], 0.0, [("KF",)])
        memset("dve", KF[:, 3:4], -0.5, [("KF",)])
        memset("dve", KF[:, 4:5], 0.5, [("KF",)])
        memset("dve", ONE[:], 1.0, [("ONE",)])
        memset("dve", ONESD[:, 0:128], 1.0 / 1024.0, [("ONESD",)])
        memset("dve", ONESD[:, 128:256], 1.0 / 512.0, [("ONESD",)])
        memset("dve", ONESB[:, 0:128], 1.0 / 1024.0, [("ONESD",)])
        memset("dve", ONESB[:, 128:256], 1.0 / 512.0, [("ONESD",)])
        P.dma("sp", IDF[:], ident_d, [], [("IDF",)], ("IDF",))
        cp("dve", IDB[:], IDF[:], [("IDF",)], [("IDB",)])
        P.dma("sp", CST[:].rearrange("p (l n) -> p l n", l=L), cst_d.rearrange("l p n -> p l n"), [], [("CST",)], ("CST",))
        for l in range(L):
            P.dma("sp", ROWP[32 * l: 32 * l + 1, :], row_d[l, :, 1024:1536], [], [("ROWP",)], ("ROWP",))
        P.dma("sp", LNIN[:], lnin_d, [], [("LNIN",)], ("LNIN",))
        for l in range(L):
            lamv = CST[:, l * NCST + _c["lam"]: l * NCST + _c["lam"] + 16]
            c1 = CX[:, l * 64: l * 64 + 16]
            act(c1, lamv, AF.Exp, scale=-1.0, reads=[("CST",)], writes=[("CX",)])
            act(c1, c1, AF.Ln, bias=kf(0), reads=[("CX",), ("KF",)], writes=[("CX",)])
            ts("dve", c1, c1, -4.0, None, ALU.mult, reads=[("CX",)], writes=[("CX",)])
            ts("dve", CX[:, l * 64 + 48: l * 64 + 64], c1, 2.0, None, ALU.mult, reads=[("CX",)], writes=[("CX",)])
            for nm, o in (("rba", 16), ("rbx", 32)):
                src = CST[:, l * NCST + _c[nm]: l * NCST + _c[nm] + 16]
                ts("dve", CX[:, l * 64 + o: l * 64 + o + 16], src, 0.5, None, ALU.mult, reads=[("CST",)], writes=[("CX",)])
            P.dma("pool", SGWT[:, l * 512:(l + 1) * 512].rearrange("q (g p) -> q g p", g=4),
                  sgwt_d[l].rearrange("g q p -> q g p"), [], [("SGWT", l)], ("SGWT", l))
            SGWF = wk_f32(2, 0, 512)
            RSR = wk_f32(1, 0, 128)[0:1, :]
            ROWT = wk_f32(0, 0, 1024)[0:1, :]
            P.dma("sp", SGWF.rearrange("q (g p) -> q g p", g=4), sgwt_d[l].rearrange("g q p -> q g p"),
                  [], wkk(2), ("SGWF",))
            P.dma("sp", ROWT, row_d[l, :, 0:1024], [], wkk(0), ("ROWT",))
            for g in range(4):
                pg = next_ps()
                mm(ps(pg, 0, 128)[0:1, :], ONE[:, 0:1], SGWF[:, g * 128:(g + 1) * 128], True, True,
                   reads=[("ONE",)] + wkk(2), writes=[psk(pg)], signal=True)
                cp("dve", RSR, ps(pg, 0, 128)[0:1, :], [psk(pg)], wkk(1))
                pg2 = next_ps()
                mm(ps(pg2, 0, 128), ROWT[:, g * 128:(g + 1) * 128], RSR, True, False,
                   reads=wkk(0) + wkk(1), writes=[psk(pg2)])
                mm(ps(pg2, 0, 128), ONE[0:1, :], ROWT[:, 512 + g * 128: 512 + (g + 1) * 128], False, True,
                   reads=wkk(0) + [("ONE",)], writes=[psk(pg2)], signal=True)
                cp("dve", BIASM[:, l * 512 + g * 128: l * 512 + (g + 1) * 128], ps(pg2, 0, 128), [psk(pg2)], [("BIASM", l)])
                ts("dve", GMAT[:, l * 512 + g * 128: l * 512 + (g + 1) * 128], ONE[:], cst(l, "sglg", g), None, ALU.mult,
                   reads=[("ONE",), ("CST",)], writes=[("GMAT", l)])

        def layer_norm(nch, onescol, zc, zkeys, cb, tg):
            gMR, gRS, t0, t1 = tg[0], tg[1], tg[2], tg[3]
            zb = tg[4:6] if (len(tg) >= 6 and "d" in KOPT) else None
            for half in range(2):
                lo = half * 1024
                pm, pq = next_ps(), next_ps()
                for c in range(nch):
                    tq = (t0, t1)[c % 2]
                    act(wk_bf(tq, 0, 1024), zc(c, lo, 1024), AF.Square, reads=zkeys(c, half), writes=wkk(tq))
                    if zb is not None:
                        tz = zb[c % 2]
                        cp("dve", wk_bf(tz, 0, 1024), zc(c, lo, 1024), zkeys(c, half), wkk(tz))
                        for t2 in range(2):
                            mm(ps(pm, t2 * 512, 512), ONESB[:, onescol:onescol + 128], wk_bf(tz, t2 * 512, 512), c == 0, c == nch - 1,
                               reads=wkk(tz) + [("ONESD",)], writes=[psk(pm)], signal=(c == nch - 1 and t2 == 1))
                    else:
                        for t2 in range(2):
                            mm(ps(pm, t2 * 512, 512), ONESD[:, onescol:onescol + 128], zc(c, lo + t2 * 512, 512), c == 0, c == nch - 1,
                               reads=zkeys(c, half) + [("ONESD",)], writes=[psk(pm)], signal=(c == nch - 1 and t2 == 1))
                    for t2 in range(2):
                        mm(ps(pq, t2 * 512, 512), ONESB[:, onescol:onescol + 128], wk_bf(tq, t2 * 512, 512), c == 0, c == nch - 1,
                           reads=wkk(tq) + [("ONESD",)], writes=[psk(pq)], signal=(t2 == 1))
                act(wk_f32(gMR), ps(pm), AF.Identity, reads=[psk(pm)], writes=wkk(gMR))
                tt("dve", wk_f32(t0), wk_f32(gMR), wk_f32(gMR), ALU.mult, reads=wkk(gMR), writes=wkk(t0))
                tt("dve", wk_f32(t0), ps(pq), wk_f32(t0), ALU.subtract, reads=[psk(pq)] + wkk(t0), writes=wkk(t0))
                act(wk_f32(t0), wk_f32(t0), AF.Ln, bias=kf(1), reads=wkk(t0) + [("KF",)], writes=wkk(t0))
                act(wk_f32(gRS), wk_f32(t0), AF.Exp, scale=-0.5, reads=wkk(t0), writes=wkk(gRS))
                tt("dve", wk_f32(gMR), wk_f32(gMR), wk_f32(gRS), ALU.mult, reads=wkk(gMR) + wkk(gRS), writes=wkk(gMR))
                for c in range(nch):
                    tq = (t0, t1)[c % 2]
                    tt("dve", wk_f32(tq), zc(c, lo, 1024), wk_f32(gRS), ALU.mult, reads=zkeys(c, half) + wkk(gRS), writes=wkk(tq))
                    tt("dve", wk_f32(tq), wk_f32(tq), wk_f32(gMR), ALU.subtract, reads=wkk(tq) + wkk(gMR), writes=wkk(tq))
                    cb(c, half, wk_f32(tq), wkk(tq))

        def store_x(s, c, half, y, ykeys, par=None):
            lo = half * 1024
            act(xh(c, lo, 1024), y, AF.Identity, reads=ykeys, writes=[("XH", c, half)])
            P.dma("sp", xd[s * 8 + c, :, lo: lo + 1024], y, ykeys, [("XD", s, c, half)], ("XDW", c % 4))
            if par is not None:
                P.dma("sp", xhd[(par * NSEG + s) * 8 + c, :, lo: lo + 1024], xh(c, lo, 1024), [("XH", c, half)],
                      [("XHD", par, s, c, half)], ("XHDW", c % 2, half))

        XH3 = XH[:].rearrange("p (c w) -> p c w", c=8)
        def load_xh(par, s):
            for c in range(8):
                P.dma("sp", xh(c, 0, T), xhd[(par * NSEG + s) * 8 + c, :, :], [("XHD", par, s, c, 0), ("XHD", par, s, c, 1)],
                      [("XH", c, 0), ("XH", c, 1)], ("XHL", c % 4))
            sl, sr = max(s - 1, 0), min(s + 1, NSEG - 1)
            bl, br = (par * NSEG + sl) * 8, (par * NSEG + sr) * 8
            P.dma("sp", XH3[:, :, T: T + HALO], xhd[bl: bl + 8, :, T - HALO: T].rearrange("c p t -> p c t"),
                  [("XHD", par, sl, c, 1) for c in range(8)], [("XHH",)], ("XHH", 0))
            P.dma("sp", XH3[:, :, T + HALO: T + 2 * HALO], xhd[br: br + 8, :, 0: HALO].rearrange("c p t -> p c t"),
                  [("XHD", par, sr, c, 0) for c in range(8)], [("XHH",)], ("XHH", 1))
        lkL = lambda s: LINK[:, s: s + 1]
        lkR = lambda s: LINK[:, 8 + s: 9 + s]
        P.dma("sp", LINK[:], link_d, [], [("LINK",)], ("LINK",))

        gXA, gXAB, gR, gI, gA, gH1, gH2 = 0, 2, 3, 5, 7, 9, 11

        def rg_head(l, s, h, mode):
            sx = load_unit(l, ("x", h))
            sg_ = load_unit(l, ("g", h)) if mode == "B" else None
            sgt = load_unit(l, ("gt", h), 512)
            if mode == "B":
                P.dma("sp", wk_f32(gH2, 0, 2048), hbd[s * 8 + h, :, :], [("HBD", s, h)], wkk(gH2, 2), ("HBR",))
            def build_da(l_, h_, buf):
                for k in range(4):
                    ts("dve", DA[:, buf * 512 + k * 128: buf * 512 + (k + 1) * 128], IDB[:], cst(l_, "caw", k * 8 + h_), None, ALU.mult,
                       reads=[("IDB",), ("CST",)], writes=[("DA", buf)])
                state["da"][buf] = (l_, h_)
            if (l, h) not in state["da"]:
                build_da(l, h, 0 if state["da"][0] != (l, (h - 1) % 8) else 1)
            dab = state["da"].index((l, h))
            build_da(l, (h + 1) % 8, 1 - dab)
            bx = cst(l, "bin", OFF_RG_X // 128 + h)
            for half in range(2):
                pg = next_ps()
                for t2 in range(2):
                    for k in range(8):
                        mm(ps(pg, t2 * 512, 512), wt(sx, k), xh(k, half * 1024 + t2 * 512, 512), k == 0, k == 7,
                           reads=[("R", sx), ("XH", k, half)], writes=[psk(pg)], signal=(k == 7 and t2 == 1))
                act(PAD[:, 2 + half * 1024: 2 + half * 1024 + 1024], ps(pg), AF.Identity, bias=bx,
                    reads=[psk(pg), ("CST",)], writes=[("PAD",)])
            pg = next_ps()
            for k in range(8):
                mm(ps(pg, 0, 32), wt(sx, k), xh(k, T, 32), k == 0, k == 7, reads=[("R", sx), ("XHH",)], writes=[psk(pg)], signal=(k == 7))
            act(HT[:, 0:32], ps(pg, 0, 32), AF.Identity, bias=bx, reads=[psk(pg), ("CST",)], writes=[("HT",)])
            if "b" in KOPT:
                act(PAD[:, 0:2], HT[:, 14:16], AF.Identity, scale=lkL(s), reads=[("HT",), ("LINK",)], writes=[("PAD",)])
                act(PAD[:, 2 + T: 3 + T], HT[:, 16:17], AF.Identity, scale=lkR(s), reads=[("HT",), ("LINK",)], writes=[("PAD",)])
            else:
                ts("dve", PAD[:, 0:2], HT[:, 14:16], lkL(s), None, ALU.mult, reads=[("HT",), ("LINK",)], writes=[("PAD",)])
                ts("dve", PAD[:, 2 + T: 3 + T], HT[:, 16:17], lkR(s), None, ALU.mult, reads=[("HT",), ("LINK",)], writes=[("PAD",)])
            for half in range(2):
                pg = next_ps()
                for t2 in range(2):
                    for k in range(4):
                        o = k + half * 1024 + t2 * 512
                        mm(ps(pg, t2 * 512, 512), DA[:, dab * 512 + k * 128: dab * 512 + (k + 1) * 128], PAD[:, o: o + 512],
                           k == 0, k == 3, reads=[("DA", dab), ("PAD",)], writes=[psk(pg)], signal=(k == 3 and t2 == 1))
                if "c" in KOPT:
                    act(wk_bf(gXAB, half * 1024, 1024), ps(pg), AF.Identity, bias=cst(l, "cab", h), reads=[psk(pg), ("CST",)], writes=wkk(gXAB))
                    ts("dve", wk_f32(gXA + half), ps(pg), cst(l, "cab", h), None, ALU.add, reads=[psk(pg), ("CST",)], writes=wkk(gXA + half))
                else:
                    act(wk_f32(gXA + half), ps(pg), AF.Identity, bias=cst(l, "cab", h), reads=[psk(pg), ("CST",)], writes=wkk(gXA + half))
                    act(wk_bf(gXAB, half * 1024, 1024), wk_f32(gXA + half), AF.Identity, reads=wkk(gXA + half), writes=wkk(gXAB))
            d = 1 if mode == "A" else 0
            for gi, (gdst, hb_o) in enumerate(((gR, 16), (gI, 32))):
                for half in range(2):
                    pg = next_ps()
                    for t2 in range(2):
                        mm(ps(pg, t2 * 512, 512), wt(sgt, d * 2 + gi), wk_bf(gXAB, half * 1024 + t2 * 512, 512), True, True,
                           reads=[("R", sgt)] + wkk(gXAB), writes=[psk(pg)], signal=(t2 == 1))
                    act(wk_f32(gdst + half), ps(pg), AF.Tanh, bias=cx(l, hb_o + d * 8 + h), scale=0.5,
                        reads=[psk(pg), ("CX",)], writes=wkk(gdst + half))
            c1ap = cx(l, d * 8 + h)
            c2ap = cx(l, 48 + d * 8 + h)
            order = (1, 0) if mode == "A" else (0, 1)
            for half in order:
                act(wk_f32(gA + half), wk_f32(gR + half), AF.Exp, bias=c1ap, scale=c1ap, reads=wkk(gR + half) + [("CX",)], writes=wkk(gA + half))
                act(wk_f32(gR + half), wk_f32(gR + half), AF.Exp, bias=c2ap, scale=c2ap, reads=wkk(gR + half) + [("CX",)], writes=wkk(gR + half))
            for half in order:
                stt("dve", wk_f32(gI + half), wk_f32(gI + half), 1.0, wk_f32(gXA + half), ALU.add, ALU.mult,
                    reads=wkk(gI + half) + wkk(gXA + half), writes=wkk(gI + half))
            for half in order:
                ts("dve", wk_f32(gR + half), wk_f32(gR + half), 0.99999994, None, ALU.min, reads=wkk(gR + half), writes=wkk(gR + half))
                act(wk_f32(gR + half), wk_f32(gR + half), AF.Sqrt, bias=kf(0), scale=-1.0, reads=wkk(gR + half) + [("KF",)], writes=wkk(gR + half))
                stt("dve", wk_f32(gI + half), wk_f32(gI + half), 0.5, wk_f32(gR + half), ALU.mult, ALU.mult,
                    reads=wkk(gI + half) + wkk(gR + half), writes=wkk(gI + half))
            if mode == "A":
                gH = gH1 if h % 2 == 0 else gH2
                H_ = wk_f32(gH, 0, 2048)
                A_, I_ = wk_f32(gA, 0, 2048), wk_f32(gI, 0, 2048)
                ini = CB[:, h: h + 1]
                P.op("dve", lambda e: e.tensor_tensor_scan(out=H_[:, 2047:1023:-1], data0=A_[:, 2047:1023:-1], data1=I_[:, 2047:1023:-1], initial=ini, op0=ALU.mult, op1=ALU.add),
                     wkk(gA + 1) + wkk(gI + 1) + [("CB",)], wkk(gH + 1))
                ini2 = H_[:, 1024:1025]
                P.op("dve", lambda e: e.tensor_tensor_scan(out=H_[:, 1023::-1], data0=A_[:, 1023::-1], data1=I_[:, 1023::-1], initial=ini2, op0=ALU.mult, op1=ALU.add),
                     wkk(gA) + wkk(gI) + wkk(gH + 1), wkk(gH))
                P.dma("sp", hbd[s * 8 + h, :, :], H_, wkk(gH, 2), [("HBD", s, h)], ("HBW", h % 2))
                ts("dve", CB[:, h: h + 1], H_[:, 0:1], lkL(s), None, ALU.mult, reads=wkk(gH) + [("LINK",)], writes=[("CB",)])
                return
            H_ = wk_f32(gH1, 0, 2048)
            A_, I_ = wk_f32(gA, 0, 2048), wk_f32(gI, 0, 2048)
            ini = CF[:, h: h + 1]
            P.op("dve", lambda e: e.tensor_tensor_scan(out=H_[:, 0:1024], data0=A_[:, 0:1024], data1=I_[:, 0:1024], initial=ini, op0=ALU.mult, op1=ALU.add),
                 wkk(gA) + wkk(gI) + [("CF",)], wkk(gH1))
            ini2 = H_[:, 1023:1024]
            P.op("dve", lambda e: e.tensor_tensor_scan(out=H_[:, 1024:2048], data0=A_[:, 1024:2048], data1=I_[:, 1024:2048], initial=ini2, op0=ALU.mult, op1=ALU.add),
                 wkk(gA + 1) + wkk(gI + 1) + wkk(gH1), wkk(gH1 + 1))
            ts("dve", CF[:, h: h + 1], H_[:, T - 1: T], lkR(s), None, ALU.mult, reads=wkk(gH1 + 1) + [("LINK",)], writes=[("CF",)])
            for half in range(2):
                tt("dve", wk_f32(gH1 + half), wk_f32(gH1 + half), wk_f32(gH2 + half), ALU.add, reads=wkk(gH1 + half) + wkk(gH2 + half), writes=wkk(gH1 + half))
            for half in range(2):
                pg = next_ps()
                for t2 in range(2):
                    for k in range(8):
                        mm(ps(pg, t2 * 512, 512), wt(sg_, k), xh(k, half * 1024 + t2 * 512, 512), k == 0, k == 7,
                           reads=[("R", sg_), ("XH", k, half)], writes=[psk(pg)], signal=(k == 7 and t2 == 1))
                act(wk_f32(gR + half), ps(pg), AF.Gelu_apprx_tanh, bias=cst(l, "bin", OFF_RG_G // 128 + h),
                    reads=[psk(pg), ("CST",)], writes=wkk(gR + half))
                tt("dve", big_bf(h, half * 1024, 1024), wk_f32(gH1 + half), wk_f32(gR + half), ALU.mult,
                   reads=wkk(gH1 + half) + wkk(gR + half), writes=bgk(h))

        out_deps = []
        for si in range(NSEG):
            for half in range(2):
                for tb in range(8):
                    g = tb
                    tok0 = half * 1024 + tb * 128
                    xt = wk_f32(g)
                    P.dma("sp", xt, xin[si, tok0: tok0 + 128, :], [], wkk(g), ("XT", tb))
                    P.op("dve", lambda e, xt=xt: e.bn_stats(out=SM[:, 0:6], in_=xt[:, 0:512]), wkk(g), [("SM",)])
                    P.op("dve", lambda e, xt=xt: e.bn_stats(out=SM[:, 6:12], in_=xt[:, 512:1024]), wkk(g), [("SM",)])
                    P.op("dve", lambda e: e.bn_aggr(out=SM[:, 12:14], in_=SM[:, 0:12]), [("SM",)], [("SM",)])
                    act(SM[:, 14:15], SM[:, 13:14], AF.Ln, bias=kf(1), reads=[("SM",), ("KF",)], writes=[("SM",)])
                    act(SM[:, 14:15], SM[:, 14:15], AF.Exp, scale=-0.5, reads=[("SM",)], writes=[("SM",)])
                    ts("dve", xt, xt, SM[:, 12:13], SM[:, 14:15], ALU.subtract, ALU.mult, reads=wkk(g) + [("SM",)], writes=wkk(g))
                for c in range(8):
                    pg = next_ps()
                    for tb in range(8):
                        tr(ps(pg, tb * 128, 128), wk_f32(tb, c * 128, 128), IDF[:], reads=wkk(tb) + [("IDF",)], writes=[psk(pg)],
                           signal=(tb == 7))
                    yg = 8 + (c % 4)
                    act(wk_f32(yg), ps(pg), AF.Identity, bias=LNIN[:, 8 + c: 9 + c], scale=LNIN[:, c: c + 1],
                        reads=[psk(pg), ("LNIN",)], writes=wkk(yg))
                    store_x(si, c, half, wk_f32(yg), wkk(yg), par=0)

        for l in range(L):
            last = (l == L - 1)
            par = l % 2
            memset("dve", CB[:], 0.0, [("CB",)])
            for si in reversed(range(NSEG)):
                load_xh(par, si)
                for h in range(8):
                    rg_head(l, si, h, "A")
            memset("dve", CF[:], 0.0, [("CF",)])
            for si in range(NSEG):
                load_xh(par, si)
                for h in range(8):
                    rg_head(l, si, h, "B")

                for ch in range(4):
                    su = load_unit(l, ("u", ch))
                    for half in range(2):
                        pg = next_ps()
                        for t2 in range(2):
                            for k in range(8):
                                mm(ps(pg, t2 * 512, 512), wt(su, k), xh(k, half * 1024 + t2 * 512, 512), k == 0, k == 7,
                                   reads=[("R", su), ("XH", k, half)], writes=[psk(pg)], signal=(k == 7 and t2 == 1))
                        act(big_bf(8 + ch, half * 1024, 1024), ps(pg), AF.Gelu_apprx_tanh, bias=cst(l, "bin", OFF_SG // 128 + ch),
                            reads=[psk(pg), ("CST",)], writes=bgk(8 + ch))
                sv = [load_unit(l, ("v", j)) for j in range(4)]
                SGO3 = BIG[:, 8 * 2048: 12 * 2048].rearrange("p (g t) -> p g t", g=4)
                for tb in range(16):
                    half = tb // 8
                    pg = next_ps()
                    for j in range(4):
                        for k in range(8):
                            mm(ps(pg, j * 128, 128), xh(k, tb * 128, 128), wt(sv[j], k), k == 0, False,
                               reads=[("R", sv[j]), ("XH", k, half)], writes=[psk(pg)])
                        mm(ps(pg, j * 128, 128), ONE[32 * l: 32 * l + 1, :], ROWP[32 * l: 32 * l + 1, j * 128:(j + 1) * 128], False, True,
                           reads=[("ONE",), ("ROWP",)], writes=[psk(pg)], signal=(j == 3))
                    gv = tb % 2
                    V = wk_f32(gv, 0, 512)
                    VN = wk_bf(2 + tb % 2, 0, 512)
                    act(V, ps(pg, 0, 512), AF.Gelu_apprx_tanh, reads=[psk(pg)], writes=wkk(gv))
                    P.op("dve", lambda e, V=V: e.bn_stats(out=SM[:, 16:22], in_=V), wkk(gv), [("SM2",)])
                    P.op("dve", lambda e: e.bn_aggr(out=SM[:, 22:24], in_=SM[:, 16:22]), [("SM2",)], [("SM2",)])
                    act(SM[:, 24:25], SM[:, 23:24], AF.Ln, bias=kf(1), reads=[("SM2",), ("KF",)], writes=[("SM2",)])
                    act(SM[:, 24:25], SM[:, 24:25], AF.Exp, scale=-0.5, reads=[("SM2",)], writes=[("SM2",)])
                    ts("dve", VN, V, SM[:, 22:23], SM[:, 24:25], ALU.subtract, ALU.mult, reads=wkk(gv) + [("SM2",)], writes=wkk(2 + tb % 2))
                    pg2 = next_ps()
                    for g in range(4):
                        mm(ps(pg2, g * 128, 128), VN[:, g * 128:(g + 1) * 128], SGWT[:, l * 512 + g * 128: l * 512 + (g + 1) * 128], True, True,
                           reads=wkk(2 + tb % 2) + [("SGWT", l)], writes=[psk(pg2)], signal=(g == 3))
                    gt_ = 4 + tb % 2
                    TT = wk_f32(gt_, 0, 512)
                    tt("dve", TT, ps(pg2, 0, 512), GMAT[:, l * 512:(l + 1) * 512], ALU.mult, reads=[psk(pg2), ("GMAT", l)], writes=wkk(gt_))
                    tt("dve", TT, TT, BIASM[:, l * 512:(l + 1) * 512], ALU.add, reads=wkk(gt_) + [("BIASM", l)], writes=wkk(gt_))
                    uview = SGO3[:, :, tb * 128:(tb + 1) * 128]
                    tt("dve", uview, TT.rearrange("p (g t) -> p g t", g=4), uview, ALU.mult, reads=wkk(gt_) + bgk(8, 4), writes=bgk(8, 4))

                for ch in range(4):
                    sc = load_unit(l, ("c", ch))
                    scg = load_unit(l, ("cg", ch))
                    for k in range(31):
                        ts("dve", DC[:, k * 128:(k + 1) * 128], IDB[:], cst(l, "ccw", k * 4 + ch), None, ALU.mult,
                           reads=[("IDB",), ("CST",)], writes=[("DC",)])
                    bc_, bcg_ = cst(l, "bin", OFF_CC // 128 + ch), cst(l, "bin", OFF_CC // 128 + 4 + ch)
                    for half in range(2):
                        pc, pgg = next_ps(), next_ps()
                        for (pp, ss) in ((pc, sc), (pgg, scg)):
                            for t2 in range(2):
                                for k in range(8):
                                    mm(ps(pp, t2 * 512, 512), wt(ss, k), xh(k, half * 1024 + t2 * 512, 512), k == 0, k == 7,
                                       reads=[("R", ss), ("XH", k, half)], writes=[psk(pp)], signal=(k == 7 and t2 == 1))
                        gs = 8 + half
                        act(wk_f32(gs), ps(pgg), AF.Sigmoid, bias=bcg_, reads=[psk(pgg), ("CST",)], writes=wkk(gs))
                        stt("dve", PAD[:, 15 + half * 1024: 15 + half * 1024 + 1024], ps(pc), bc_, wk_f32(gs),
                            ALU.add, ALU.mult, reads=[psk(pc), ("CST",)] + wkk(gs), writes=[("PAD",)])
                    pc, pgg = next_ps(), next_ps()
                    for (pp, ss) in ((pc, sc), (pgg, scg)):
                        for k in range(8):
                            mm(ps(pp, 0, 32), wt(ss, k), xh(k, T, 32), k == 0, k == 7, reads=[("R", ss), ("XHH",)], writes=[psk(pp)], signal=(k == 7))
                    act(HT[:, 32:64], ps(pgg, 0, 32), AF.Sigmoid, bias=bcg_, reads=[psk(pgg), ("CST",)], writes=[("HT",)])
                    stt("dve", HT[:, 0:32], ps(pc, 0, 32), bc_, HT[:, 32:64], ALU.add, ALU.mult, reads=[psk(pc), ("CST",), ("HT",)], writes=[("HT",)])
                    ts("dve", PAD[:, 0:15], HT[:, 1:16], lkL(si), None, ALU.mult, reads=[("HT",), ("LINK",)], writes=[("PAD",)])
                    ts("dve", PAD[:, 15 + T: 30 + T], HT[:, 16:31], lkR(si), None, ALU.mult, reads=[("HT",), ("LINK",)], writes=[("PAD",)])
                    for half in range(2):
                        pg = next_ps()
                        for t2 in range(2):
                            for k in range(31):
                                o = k + half * 1024 + t2 * 512
                                mm(ps(pg, t2 * 512, 512), DC[:, k * 128:(k + 1) * 128], PAD[:, o: o + 512], k == 0, k == 30,
                                   reads=[("DC",), ("PAD",)], writes=[psk(pg)], signal=(k == 30 and t2 == 1))
                        act(wk_f32(2 * ch + half), ps(pg), AF.Identity, bias=cst(l, "ccb", ch), reads=[psk(pg), ("CST",)], writes=wkk(2 * ch + half))
                def cc_cb(c, half, Tap, Tk):
                    act(big_bf(12 + c, half * 1024, 1024), Tap, AF.Silu, bias=cst(l, "cclb", c), scale=cst(l, "cclg", c),
                        reads=Tk + [("CST",)], writes=bgk(12 + c))
                layer_norm(4, 128, lambda c, lo, n: wk_f32(2 * c, lo, n), lambda c, half: wkk(2 * c + half), cc_cb, [8, 9, 10, 11])

                for oc in range(8):
                    sl = {nm: load_unit(l, (nm, oc), 1024 if nm in ("ga", "gb", "gc", "ba") else 512) for nm in ("ga", "gb", "gc", "ba", "bb", "bc")}
                    for half in range(2):
                        for bi, (gn, yn, nk, src0) in enumerate((("ga", "ba", 8, 0), ("gb", "bb", 4, 8), ("gc", "bc", 4, 12))):
                            pgt, py = next_ps(), next_ps()
                            for t2 in range(2):
                                for k in range(8):
                                    mm(ps(pgt, t2 * 512, 512), wt(sl[gn], k), xh(k, half * 1024 + t2 * 512, 512), k == 0, k == 7,
                                       reads=[("R", sl[gn]), ("XH", k, half)], writes=[psk(pgt)], signal=(k == 7 and t2 == 1))
                            for t2 in range(2):
                                for k in range(nk):
                                    mm(ps(py, t2 * 512, 512), wt(sl[yn], k), big_bf(src0 + k, half * 1024 + t2 * 512, 512), k == 0, k == nk - 1,
                                       reads=[("R", sl[yn])] + bgk(src0 + k), writes=[psk(py)], signal=(k == nk - 1 and t2 == 1))
                            gg = 8 + bi
                            act(wk_f32(gg), ps(pgt), AF.Sigmoid, bias=cst(l, "bin", OFF_GATE // 128 + bi * 8 + oc), reads=[psk(pgt), ("CST",)], writes=wkk(gg))
                            tt("dve", wk_f32(gg), ps(py), wk_f32(gg), ALU.mult, reads=[psk(py)] + wkk(gg), writes=wkk(gg))
                        tt("dve", wk_f32(8), wk_f32(8), wk_f32(9), ALU.add, reads=wkk(8) + wkk(9), writes=wkk(8))
                        tt("dve", wk_bf(oc, half * 1024, 1024), wk_f32(8), wk_f32(10), ALU.add, reads=wkk(8) + wkk(10), writes=wkk(oc))

                def resid(oc, half, pg, bname, zdst, zkeys_w):
                    gx = 9 + (2 * oc + half) % 2
                    ge = 11 + (2 * oc + half) % 2
                    P.dma("sp", wk_f32(gx), xd[si * 8 + oc, :, half * 1024: half * 1024 + 1024], [("XD", si, oc, half)], wkk(gx), ("XDR", gx))
                    act(wk_f32(ge), ps(pg), AF.Identity, bias=cst(l, bname, oc), reads=[psk(pg), ("CST",)], writes=wkk(ge))
                    stt("dve", zdst, wk_f32(gx), ALPHA, wk_f32(ge), ALU.mult, ALU.add, reads=wkk(gx) + wkk(ge), writes=zkeys_w)
                for oc in range(8):
                    so = load_unit(l, ("o", oc))
                    for half in range(2):
                        pg = next_ps()
                        for t2 in range(2):
                            for k in range(8):
                                mm(ps(pg, t2 * 512, 512), wt(so, k), wk_bf(k, half * 1024 + t2 * 512, 512), k == 0, k == 7,
                                   reads=[("R", so)] + wkk(k), writes=[psk(pg)], signal=(k == 7 and t2 == 1))
                        resid(oc, half, pg, "bo", big_f32(oc, half * 1024, 1024), bgk(2 * oc + half))

                def ln_x_cb(gname, bname, par_out):
                    def cb(c, half, Tap, Tk):
                        yg = (4 + (2 * c + half) % 4) if "e" in KOPT else (6 + (2 * c + half) % 2)
                        act(wk_f32(yg), Tap, AF.Identity, bias=cst(l, bname, c), scale=cst(l, gname, c), reads=Tk + [("CST",)], writes=wkk(yg))
                        store_x(si, c, half, wk_f32(yg), wkk(yg), par=par_out)
                    return cb
                layer_norm(8, 0, lambda c, lo, n: big_f32(c, lo, n), lambda c, half: bgk(2 * c + half), ln_x_cb("l1g", "l1b", None), [0, 1, 2, 3, 8, 9])

                for J in range(4):
                    for jj in range(8):
                        j = J * 8 + jj
                        s1 = load_unit(l, ("f1", j))
                        for half in range(2):
                            pg = next_ps()
                            for t2 in range(2):
                                for k in range(8):
                                    mm(ps(pg, t2 * 512, 512), wt(s1, k), xh(k, half * 1024 + t2 * 512, 512), k == 0, k == 7,
                                       reads=[("R", s1), ("XH", k, half)], writes=[psk(pg)], signal=(k == 7 and t2 == 1))
                            gr = 8 + (2 * jj + half) % 2
                            act(wk_f32(gr), ps(pg), AF.Relu, bias=cst(l, "bf1", j), reads=[psk(pg), ("CST",)], writes=wkk(gr))
                            tt("dve", wk_bf(jj, half * 1024, 1024), wk_f32(gr), wk_f32(gr), ALU.mult, reads=wkk(gr), writes=wkk(jj))
                    s2 = [load_unit(l, ("f2", J * 8 + jj)) for jj in range(8)]
                    for oc in range(8):
                        for half in range(2):
                            pg = next_ps()
                            for t2 in range(2):
                                for jj in range(8):
                                    mm(ps(pg, t2 * 512, 512), wt(s2[jj], oc), wk_bf(jj, half * 1024 + t2 * 512, 512), jj == 0, jj == 7,
                                       reads=[("R", s2[jj])] + wkk(jj), writes=[psk(pg)], signal=(jj == 7 and t2 == 1))
                            acc = big_f32(oc, half * 1024, 1024)
                            if J == 0:
                                cp("dve", acc, ps(pg), [psk(pg)], bgk(2 * oc + half))
                            else:
                                tt("dve", acc, ps(pg), acc, ALU.add, reads=[psk(pg)] + bgk(2 * oc + half), writes=bgk(2 * oc + half))
                for oc in range(8):
                    for half in range(2):
                        gx = 9 + (2 * oc + half) % 2
                        acc = big_f32(oc, half * 1024, 1024)
                        P.dma("sp", wk_f32(gx), xd[si * 8 + oc, :, half * 1024: half * 1024 + 1024], [("XD", si, oc, half)], wkk(gx), ("XDR", gx))
                        act(acc, acc, AF.Identity, bias=cst(l, "bf2", oc), reads=bgk(2 * oc + half) + [("CST",)], writes=bgk(2 * oc + half))
                        stt("dve", acc, wk_f32(gx), ALPHA, acc, ALU.mult, ALU.add, reads=wkk(gx) + bgk(2 * oc + half), writes=bgk(2 * oc + half))
                if not last:
                    layer_norm(8, 0, lambda c, lo, n: big_f32(c, lo, n), lambda c, half: bgk(2 * c + half), ln_x_cb("l2g", "l2b", 1 - par), [0, 1, 2, 3, 8, 9])
                else:
                    def fin_cb(c, half, Tap, Tk):
                        act(wk_f32(4 + c), Tap, AF.Identity, bias=cst(l, "l2b", c), scale=cst(l, "l2g", c), reads=Tk + [("CST",)], writes=wkk(4 + c))
                        if c == 7:
                            for tb in range(8):
                                pg = next_ps()
                                for cc in range(8):
                                    tr(ps(pg, cc * 128, 128), wk_f32(4 + cc, tb * 128, 128), IDF[:], reads=wkk(4 + cc) + [("IDF",)], writes=[psk(pg)],
                                       signal=(cc == 7))
                                go = 12 if tb % 2 == 0 else 3
                                if tb % 2 == 0:
                                    cp("dve", wk_f32(go), ps(pg), [psk(pg)], wkk(go))
                                else:
                                    act(wk_f32(go), ps(pg), AF.Identity, reads=[psk(pg)], writes=wkk(go))
                                tok0 = half * 1024 + tb * 128
                                dep = P.dma("sp", yout[si, tok0: tok0 + 128, :], wk_f32(go), wkk(go), [("YO", si, half, tb)], ("YO", tb % 2))
                                out_deps.append(dep)
                    layer_norm(8, 0, lambda c, lo, n: big_f32(c, lo, n), lambda c, half: bgk(2 * c + half), fin_cb, [0, 1, 2, 3])

        fin = {}
        for s, v in out_deps:
            fin[s] = max(fin.get(s, 0), v)
        P.final_wait("sp", list(fin.items()))
        with nc.Block() as block:
            P.emit(block)
    return nc


def _colchunk(W, j, nk):
    blk = W[:, j * 128:(j + 1) * 128].reshape(nk, 128, 128)
    out = np.zeros((128, 1024), np.float32)
    out[:, : nk * 128] = blk.transpose(1, 0, 2).reshape(128, nk * 128)
    return out


def _prep_weights(inp):
    wu = np.zeros((L, NU, 128, 1024), np.float32)
    cst = np.zeros((L, 128, NCST), np.float32)
    rows = np.zeros((L, 1, NROW), np.float32)
    for l in range(L):
        w_in = inp["w_in"][l]
        for h in range(8):
            wu[l, UNITS[("x", h)]] = _colchunk(w_in, OFF_RG_X // 128 + h, 8)
            wu[l, UNITS[("g", h)]] = _colchunk(w_in, OFF_RG_G // 128 + h, 8)
            gt = np.zeros((128, 1024), np.float32)
            gt[:, 0:128] = inp["rg_wa"][l, 0, h]
            gt[:, 128:256] = inp["rg_wx"][l, 0, h]
            gt[:, 256:384] = inp["rg_wa"][l, 1, h]
            gt[:, 384:512] = inp["rg_wx"][l, 1, h]
            wu[l, UNITS[("gt", h)]] = gt
        for ch in range(4):
            wu[l, UNITS[("u", ch)]] = _colchunk(w_in, OFF_SG // 128 + ch, 8)
            wu[l, UNITS[("v", ch)]] = _colchunk(w_in, OFF_SG // 128 + 4 + ch, 8)
            wu[l, UNITS[("c", ch)]] = _colchunk(w_in, OFF_CC // 128 + ch, 8)
            wu[l, UNITS[("cg", ch)]] = _colchunk(w_in, OFF_CC // 128 + 4 + ch, 8)
        for oc in range(8):
            wu[l, UNITS[("ga", oc)]] = _colchunk(w_in, OFF_GATE // 128 + oc, 8)
            wu[l, UNITS[("gb", oc)]] = _colchunk(w_in, OFF_GATE // 128 + 8 + oc, 8)
            wu[l, UNITS[("gc", oc)]] = _colchunk(w_in, OFF_GATE // 128 + 16 + oc, 8)
            wu[l, UNITS[("ba", oc)]] = _colchunk(inp["w_ba"][l], oc, 8)
            wu[l, UNITS[("bb", oc)]] = _colchunk(inp["w_bb"][l], oc, 4)
            wu[l, UNITS[("bc", oc)]] = _colchunk(inp["w_bc"][l], oc, 4)
            wu[l, UNITS[("o", oc)]] = _colchunk(inp["w_o"][l], oc, 8)
        for j in range(32):
            wu[l, UNITS[("f1", j)]] = _colchunk(inp["w_ff1"][l], j, 8)
            wu[l, UNITS[("f2", j)]] = inp["w_ff2"][l][j * 128:(j + 1) * 128, :]
        def put(name, vec, n):
            cst[l, :, _c[name]: _c[name] + n] = np.asarray(vec, np.float32).reshape(n, 128).T
        put("bin", inp["b_in"][l], 56)
        put("caw", inp["conv_a_w"][l].reshape(-1), 32)
        put("cab", inp["conv_a_b"][l], 8)
        put("rba", inp["rg_ba"][l].reshape(-1), 16)
        put("rbx", inp["rg_bx"][l].reshape(-1), 16)
        put("lam", inp["rg_lambda"][l].reshape(-1), 16)
        put("sglg", inp["sg_ln_g"][l], 4)
        put("ccw", inp["conv_c_w"][l].reshape(-1), 124)
        put("ccb", inp["conv_c_b"][l], 4)
        put("cclg", inp["cc_ln_g"][l], 4)
        put("cclb", inp["cc_ln_b"][l], 4)
        put("bo", inp["b_o"][l], 8)
        put("l1g", inp["ln1_g"][l], 8)
        put("l1b", inp["ln1_b"][l], 8)
        put("bf1", inp["b_ff1"][l], 32)
        put("bf2", inp["b_ff2"][l], 8)
        put("l2g", inp["ln2_g"][l], 8)
        put("l2b", inp["ln2_b"][l], 8)
        rows[l, 0, 0:512] = inp["sg_ln_b"][l]
        rows[l, 0, 512:1024] = inp["sg_b"][l].reshape(-1)
        rows[l, 0, 1024:1536] = inp["b_in"][l][OFF_SG + 512: OFF_SG + 1024]
    lnin = np.zeros((128, 16), np.float32)
    lnin[:, 0:8] = np.asarray(inp["ln_in_g"], np.float32).reshape(8, 128).T
    lnin[:, 8:16] = np.asarray(inp["ln_in_b"], np.float32).reshape(8, 128).T
    sgwt = np.ascontiguousarray(np.asarray(inp["sg_w"], np.float32).transpose(0, 1, 3, 2))
    return wu, cst, rows, lnin, sgwt


_NC_CACHE = {}
_SAMPLE_SLOTS = [(c, k) for c, n in zip(range(1, 8), (3, 3, 2, 2, 2, 2, 2)) for k in range(n)]


def _core_inputs(inp):
    xp = inp["x_prompt"].astype(np.float32, copy=False)
    xs = inp["x_sample"].astype(np.float32, copy=False)
    xins = [np.zeros((NSEG, T, D), np.float32) for _ in range(NCORES)]
    links = [np.zeros((128, 16), np.float32) for _ in range(NCORES)]
    xins[0][:] = xp[0].reshape(NSEG, T, D)
    links[0][:, 1:8] = 1.0
    links[0][:, 8:15] = 1.0
    for i, (c, k) in enumerate(_SAMPLE_SLOTS):
        xins[c][k] = xs[i]
    return xins, links


def kernel(**inputs):
    inp = {k: np.asarray(v) for k, v in inputs.items()}
    wu, cst, rows, lnin, sgwt = _prep_weights(inp)
    xins, links = _core_inputs(inp)
    if "nc" not in _NC_CACHE:
        _NC_CACHE["nc"] = build(None)
    nc = _NC_CACHE["nc"]
    ident = np.eye(128, dtype=np.float32)
    in_maps = [{"xin": xins[c], "link": links[c], "wu": wu, "cst": cst, "rows": rows, "lnin": lnin, "sgwt": sgwt, "ident": ident}
               for c in range(NCORES)]
    res = run_bass_kernel_spmd(nc, in_maps, core_ids=list(range(NCORES)))
    y_prompt = np.ascontiguousarray(res.results[0]["yout"]).reshape(1, NSEG * T, D).astype(np.float32, copy=False)
    y_sample = np.zeros((16, T, D), np.float32)
    for i, (c, k) in enumerate(_SAMPLE_SLOTS):
        y_sample[i] = res.results[c]["yout"][k]
    return (y_prompt, y_sample)
```

```python
import numpy as np
import os
KOPT = {"b", "d", "e"}
import concourse.bass as bass
import concourse.mybir as mybir
from concourse.bass_utils import run_bass_kernel_spmd

F32 = mybir.dt.float32
BF16 = mybir.dt.bfloat16
AF = mybir.ActivationFunctionType
ALU = mybir.AluOpType

D = 1024
T = 2048
HALO = 16
NSEG = 8
L = 2
NCORES = 8
ALPHA = float((2 * L) ** 0.25)
EPS = 1e-5
OFF_RG_X, OFF_RG_G, OFF_SG, OFF_CC, OFF_GATE = 0, 1024, 2048, 3072, 4096
NSLOT = 8
NWK = 13

_c = {}
_o = 0
def _add(name, n):
    global _o
    _c[name] = _o
    _o += n
_add("bin", 56); _add("caw", 32); _add("cab", 8); _add("rba", 16); _add("rbx", 16); _add("lam", 16)
_add("sglg", 4); _add("ccw", 124); _add("ccb", 4); _add("cclg", 4); _add("cclb", 4)
_add("bo", 8); _add("l1g", 8); _add("l1b", 8); _add("bf1", 32); _add("bf2", 8); _add("l2g", 8); _add("l2b", 8)
NCST = _o
NROW = 1536

def _units():
    u = {}
    n = 0
    for h in range(8):
        u[("x", h)] = n; u[("g", h)] = n + 1; u[("gt", h)] = n + 2; n += 3
    for ch in range(4):
        u[("u", ch)] = n; n += 1
    for j in range(4):
        u[("v", j)] = n; n += 1
    for ch in range(4):
        u[("c", ch)] = n; u[("cg", ch)] = n + 1; n += 2
    for oc in range(8):
        for k, nm in enumerate(("ga", "gb", "gc", "ba", "bb", "bc")):
            u[(nm, oc)] = n + k
        n += 6
    for oc in range(8):
        u[("o", oc)] = n; n += 1
    for J in range(4):
        for jj in range(8):
            u[("f1", J * 8 + jj)] = n; n += 1
        for jj in range(8):
            u[("f2", J * 8 + jj)] = n; n += 1
    return u, n
UNITS, NU = _units()


class Prog:
    ENGS = ("pe", "act", "dve", "pool", "sp")

    def __init__(self, nc, es):
        self.nc = nc
        self.es = es
        self.streams = {e: [] for e in self.ENGS}
        self.cnt = {e: 0 for e in self.ENGS}
        self.known = {e: {} for e in self.ENGS}
        self.buf = {}
        self.sems = {}
        self.dcnt = {}
        for e in self.ENGS:
            self.sems[e] = es.enter_context(nc.semaphore("s_" + e))
        self.out_deps = []

    def dsem(self, key):
        if key not in self.sems:
            self.sems[key] = self.es.enter_context(self.nc.semaphore("d_%d" % len(self.sems)))
            self.dcnt[key] = 0
        return key

    def _deps(self, engine, reads, writes):
        deps = {}
        def add(d):
            if d is None:
                return
            s, v = d
            if deps.get(s, 0) < v:
                deps[s] = v
        for k in reads:
            st = self.buf.get(k)
            if st:
                add(st["w"])
        for k in writes:
            st = self.buf.get(k)
            if st:
                add(st["w"])
                for s, v in st["r"].items():
                    add((s, v))
        waits = []
        kn = self.known[engine]
        for s, v in deps.items():
            if engine == "pe" and s == "pe":
                continue
            if kn.get(s, 0) < v:
                waits.append((s, v))
                kn[s] = v
        return waits

    def _commit(self, dep, reads, writes):
        for k in writes:
            self.buf[k] = {"w": dep, "r": {}}
        for k in reads:
            st = self.buf.setdefault(k, {"w": None, "r": {}})
            if st["r"].get(dep[0], 0) < dep[1]:
                st["r"][dep[0]] = dep[1]

    def op(self, engine, fn, reads=(), writes=(), signal=True):
        waits = self._deps(engine, reads, writes)
        sems = self.sems
        if signal:
            self.cnt[engine] += 1
            dep = (engine, self.cnt[engine])
        else:
            dep = (engine, self.cnt[engine] + 1)
        semh = sems[engine]
        def thunk(eng, waits=waits, fn=fn, signal=signal):
            for s, v in waits:
                eng.wait_ge(sems[s], v)
            ins = fn(eng)
            if signal:
                ins.then_inc(semh, 1)
        self.streams[engine].append(thunk)
        self._commit(dep, reads, writes)

    def dma(self, engine, out, in_, reads, writes, semkey):
        self.dsem(semkey)
        waits = self._deps(engine, reads, writes)
        self.dcnt[semkey] += 16
        dep = (semkey, self.dcnt[semkey])
        sems = self.sems
        def thunk(eng, waits=waits):
            for s, v in waits:
                eng.wait_ge(sems[s], v)
            eng.dma_start(out=out, in_=in_).then_inc(sems[semkey], 16)
        self.streams[engine].append(thunk)
        self._commit(dep, reads, writes)
        return dep

    def final_wait(self, engine, deps):
        sems = self.sems
        def thunk(eng):
            for s, v in deps:
                eng.wait_ge(sems[s], v)
        self.streams[engine].append(thunk)

    def emit(self, block):
        P = self
        @block.tensor
        def _(e):
            for t in P.streams["pe"]:
                t(e)
        @block.scalar
        def _(e):
            for t in P.streams["act"]:
                t(e)
        @block.vector
        def _(e):
            for t in P.streams["dve"]:
                t(e)
        @block.gpsimd
        def _(e):
            for t in P.streams["pool"]:
                t(e)
        @block.sync
        def _(e):
            for t in P.streams["sp"]:
                t(e)


def build(seg_kinds, debug=False):
    import contextlib
    nc = bass.Bass("TRN2", target_bir_lowering=False)
    xin = nc.dram_tensor("xin", [NSEG, T, D], F32, kind="ExternalInput").ap()
    wu = nc.dram_tensor("wu", [L, NU, 128, 1024], F32, kind="ExternalInput").ap()
    cst_d = nc.dram_tensor("cst", [L, 128, NCST], F32, kind="ExternalInput").ap()
    row_d = nc.dram_tensor("rows", [L, 1, NROW], F32, kind="ExternalInput").ap()
    lnin_d = nc.dram_tensor("lnin", [128, 16], F32, kind="ExternalInput").ap()
    sgwt_d = nc.dram_tensor("sgwt", [L, 4, 128, 128], F32, kind="ExternalInput").ap()
    ident_d = nc.dram_tensor("ident", [128, 128], F32, kind="ExternalInput").ap()
    yout = nc.dram_tensor("yout", [NSEG, T, D], F32, kind="ExternalOutput").ap()
    link_d = nc.dram_tensor("link", [128, 16], F32, kind="ExternalInput").ap()
    xd = nc.dram_tensor("xd", [NSEG * 8, 128, T], F32, kind="Internal").ap()
    xhd = nc.dram_tensor("xhd", [2 * NSEG * 8, 128, T], BF16, kind="Internal").ap()
    hbd = nc.dram_tensor("hbd", [NSEG * 8, 128, T], F32, kind="Internal").ap()

    es = contextlib.ExitStack()
    with es:
        P = Prog(nc, es)
        sb = lambda name, shape, dt: es.enter_context(nc.sbuf_tensor(name, shape, dt))
        XH = sb("XH", [128, 8 * (T + 2 * HALO)], BF16)
        BIG = sb("BIG", [128, 32768], BF16)
        WK = sb("WK", [128, NWK * 2048], BF16)
        PAD = sb("PAD", [128, 2176], BF16)
        DC = sb("DC", [128, 31 * 128], BF16)
        DA = sb("DA", [128, 2 * 4 * 128], BF16)
        RING = sb("RING", [128, NSLOT * 1024], BF16)
        CST = sb("CST", [128, L * NCST], F32)
        CX = sb("CX", [128, L * 64], F32)
        ROWP = sb("ROWP", [33, 512], F32)
        LNIN = sb("LNIN", [128, 16], F32)
        SGWT = sb("SGWT", [128, L * 512], BF16)
        BIASM = sb("BIASM", [128, L * 512], F32)
        GMAT = sb("GMAT", [128, L * 512], F32)
        IDB = sb("IDB", [128, 128], BF16)
        IDF = sb("IDF", [128, 128], F32)
        ONESD = sb("ONESD", [128, 256], F32)
        ONE = sb("ONE", [128, 128], F32)
        ONESB = sb("ONESB", [128, 256], BF16)
        KF = sb("KF", [128, 8], F32)
        SM = sb("SM", [128, 64], F32)
        LINK = sb("LINK", [128, 16], F32)
        CF = sb("CF", [128, 8], F32)
        CB = sb("CB", [128, 8], F32)
        HT = sb("HT", [128, 64], F32)
        PS = es.enter_context(nc.psum_tensor("PS", [128, 4096], F32))

        XW = T + 2 * HALO
        def xh(c, lo, n):
            return XH[:, c * XW + lo: c * XW + lo + n]
        def big_bf(g, lo=0, n=2048):
            return BIG[:, g * 2048 + lo: g * 2048 + lo + n]
        def big_f32(c, lo=0, n=2048):
            return BIG[:, c * 4096: (c + 1) * 4096].bitcast(F32)[:, lo: lo + n]
        def wk_bf(g, lo=0, n=2048):
            return WK[:, g * 2048 + lo: g * 2048 + lo + n]
        def wk_f32(g, lo=0, n=1024):
            return WK[:, g * 2048: g * 2048 + 2 * (lo + n)].bitcast(F32)[:, lo: lo + n]
        def wkk(g, n=1):
            return [("W", g + i) for i in range(n)]
        def bgk(g, n=1):
            return [("B", g + i) for i in range(n)]
        def ps(g, lo=0, n=1024):
            return PS[:, g * 1024 + lo: g * 1024 + lo + n]
        psk = lambda g: ("PS", g)
        def cst(l, name, j=0):
            o = l * NCST + _c[name] + j
            return CST[:, o: o + 1]
        def cx(l, o):
            return CX[:, l * 64 + o: l * 64 + o + 1]
        kf = lambda j: KF[:, j: j + 1]

        state = {"psg": 0, "slot": 0, "da": [None, None]}
        def next_ps():
            g = state["psg"]
            state["psg"] = (g + 1) % 4
            return g

        def load_unit(l, key, n=1024):
            s = state["slot"]
            state["slot"] = (s + 1) % NSLOT
            u = UNITS[key]
            P.dma("pool", RING[:, s * 1024: s * 1024 + n], wu[l, u, :, 0:n], reads=[], writes=[("R", s)], semkey=("R", s))
            return s
        def wt(s, k, n=128, width=128):
            return RING[:, s * 1024 + k * width: s * 1024 + k * width + n]

        def act(out, in_, func, bias=None, scale=1.0, reads=(), writes=()):
            def fn(e):
                kw = {}
                if bias is not None:
                    kw["bias"] = bias
                return e.activation(out=out, in_=in_, func=func, scale=scale, **kw)
            P.op("act", fn, reads, writes)
        def tt(eng, out, in0, in1, op, reads=(), writes=()):
            P.op(eng, lambda e: e.tensor_tensor(out=out, in0=in0, in1=in1, op=op), reads, writes)
        def ts(eng, out, in0, s1, s2, op0, op1=None, reads=(), writes=()):
            if op1 is None:
                P.op(eng, lambda e: e.tensor_scalar(out=out, in0=in0, scalar1=s1, scalar2=None, op0=op0), reads, writes)
            else:
                P.op(eng, lambda e: e.tensor_scalar(out=out, in0=in0, scalar1=s1, scalar2=s2, op0=op0, op1=op1), reads, writes)
        def stt(eng, out, in0, scalar, in1, op0, op1, reads=(), writes=()):
            P.op(eng, lambda e: e.scalar_tensor_tensor(out=out, in0=in0, scalar=scalar, in1=in1, op0=op0, op1=op1), reads, writes)
        def cp(eng, out, in_, reads=(), writes=()):
            P.op(eng, lambda e: e.tensor_copy(out=out, in_=in_), reads, writes)
        def mm(out, lhsT, rhs, start, stop, reads=(), writes=(), signal=False):
            P.op("pe", lambda e: e.matmul(out, lhsT, rhs, start=start, stop=stop), reads, writes, signal=signal)
        def tr(out, in_, ident, reads=(), writes=(), signal=False):
            P.op("pe", lambda e: e.transpose(out, in_, ident), reads, writes, signal=signal)
        def memset(eng, ap, val, writes=()):
            P.op(eng, lambda e: e.memset(ap, val), (), writes)

        memset("dve", KF[:, 0:1], 1.0, [("KF",)])
        memset("dve", KF[:, 1:2], EPS, [("KF",)])
        memset("dve", KF[:, 2:3], 0.0, [("KF",)])
        memset("dve", KF[:, 3:4], -0.5, [("KF",)])
        memset("dve", KF[:, 4:5], 0.5, [("KF",)])
        memset("dve", ONE[:], 1.0, [("ONE",)])
        memset("dve", ONESD[:, 0:128], 1.0 / 1024.0, [("ONESD",)])
        memset("dve", ONESD[:, 128:256], 1.0 / 512.0, [("ONESD",)])
        memset("dve", ONESB[:, 0:128], 1.0 / 1024.0, [("ONESD",)])
        memset("dve", ONESB[:, 128:256], 1.0 / 512.0, [("ONESD",)])
        P.dma("sp", IDF[:], ident_d, [], [("IDF",)], ("IDF",))
        cp("dve", IDB[:], IDF[:], [("IDF",)], [("IDB",)])
        P.dma("sp", CST[:].rearrange("p (l n) -> p l n", l=L), cst_d.rearrange("l p n -> p l n"), [], [("CST",)], ("CST",))
        for l in range(L):
            P.dma("sp", ROWP[32 * l: 32 * l + 1, :], row_d[l, :, 1024:1536], [], [("ROWP",)], ("ROWP",))
        P.dma("sp", LNIN[:], lnin_d, [], [("LNIN",)], ("LNIN",))
        for l in range(L):
            lamv = CST[:, l * NCST + _c["lam"]: l * NCST + _c["lam"] + 16]
            c1 = CX[:, l * 64: l * 64 + 16]
            act(c1, lamv, AF.Exp, scale=-1.0, reads=[("CST",)], writes=[("CX",)])
            act(c1, c1, AF.Ln, bias=kf(0), reads=[("CX",), ("KF",)], writes=[("CX",)])
            ts("dve", c1, c1, -4.0, None, ALU.mult, reads=[("CX",)], writes=[("CX",)])
            ts("dve", CX[:, l * 64 + 48: l * 64 + 64], c1, 2.0, None, ALU.mult, reads=[("CX",)], writes=[("CX",)])
            for nm, o in (("rba", 16), ("rbx", 32)):
                src = CST[:, l * NCST + _c[nm]: l * NCST + _c[nm] + 16]
                ts("dve", CX[:, l * 64 + o: l * 64 + o + 16], src, 0.5, None, ALU.mult, reads=[("CST",)], writes=[("CX",)])
            P.dma("pool", SGWT[:, l * 512:(l + 1) * 512].rearrange("q (g p) -> q g p", g=4),
                  sgwt_d[l].rearrange("g q p -> q g p"), [], [("SGWT", l)], ("SGWT", l))
            SGWF = wk_f32(2, 0, 512)
            RSR = wk_f32(1, 0, 128)[0:1, :]
            ROWT = wk_f32(0, 0, 1024)[0:1, :]
            P.dma("sp", SGWF.rearrange("q (g p) -> q g p", g=4), sgwt_d[l].rearrange("g q p -> q g p"),
                  [], wkk(2), ("SGWF",))
            P.dma("sp", ROWT, row_d[l, :, 0:1024], [], wkk(0), ("ROWT",))
            for g in range(4):
                pg = next_ps()
                mm(ps(pg, 0, 128)[0:1, :], ONE[:, 0:1], SGWF[:, g * 128:(g + 1) * 128], True, True,
                   reads=[("ONE",)] + wkk(2), writes=[psk(pg)], signal=True)
                cp("dve", RSR, ps(pg, 0, 128)[0:1, :], [psk(pg)], wkk(1))
                pg2 = next_ps()
                mm(ps(pg2, 0, 128), ROWT[:, g * 128:(g + 1) * 128], RSR, True, False,
                   reads=wkk(0) + wkk(1), writes=[psk(pg2)])
                mm(ps(pg2, 0, 128), ONE[0:1, :], ROWT[:, 512 + g * 128: 512 + (g + 1) * 128], False, True,
                   reads=wkk(0) + [("ONE",)], writes=[psk(pg2)], signal=True)
                cp("dve", BIASM[:, l * 512 + g * 128: l * 512 + (g + 1) * 128], ps(pg2, 0, 128), [psk(pg2)], [("BIASM", l)])
                ts("dve", GMAT[:, l * 512 + g * 128: l * 512 + (g + 1) * 128], ONE[:], cst(l, "sglg", g), None, ALU.mult,
                   reads=[("ONE",), ("CST",)], writes=[("GMAT", l)])

        def layer_norm(nch, onescol, zc, zkeys, cb, tg):
            gMR, gRS, t0, t1 = tg[0], tg[1], tg[2], tg[3]
            zb = tg[4:6] if (len(tg) >= 6 and "d" in KOPT) else None
            for half in range(2):
                lo = half * 1024
                pm, pq = next_ps(), next_ps()
                for c in range(nch):
                    tq = (t0, t1)[c % 2]
                    act(wk_bf(tq, 0, 1024), zc(c, lo, 1024), AF.Square, reads=zkeys(c, half), writes=wkk(tq))
                    if zb is not None:
                        tz = zb[c % 2]
                        cp("dve", wk_bf(tz, 0, 1024), zc(c, lo, 1024), zkeys(c, half), wkk(tz))
                        for t2 in range(2):
                            mm(ps(pm, t2 * 512, 512), ONESB[:, onescol:onescol + 128], wk_bf(tz, t2 * 512, 512), c == 0, c == nch - 1,
                               reads=wkk(tz) + [("ONESD",)], writes=[psk(pm)], signal=(c == nch - 1 and t2 == 1))
                    else:
                        for t2 in range(2):
                            mm(ps(pm, t2 * 512, 512), ONESD[:, onescol:onescol + 128], zc(c, lo + t2 * 512, 512), c == 0, c == nch - 1,
                               reads=zkeys(c, half) + [("ONESD",)], writes=[psk(pm)], signal=(c == nch - 1 and t2 == 1))
                    for t2 in range(2):
                        mm(ps(pq, t2 * 512, 512), ONESB[:, onescol:onescol + 128], wk_bf(tq, t2 * 512, 512), c == 0, c == nch - 1,
                           reads=wkk(tq) + [("ONESD",)], writes=[psk(pq)], signal=(t2 == 1))
                act(wk_f32(gMR), ps(pm), AF.Identity, reads=[psk(pm)], writes=wkk(gMR))
                tt("dve", wk_f32(t0), wk_f32(gMR), wk_f32(gMR), ALU.mult, reads=wkk(gMR), writes=wkk(t0))
                tt("dve", wk_f32(t0), ps(pq), wk_f32(t0), ALU.subtract, reads=[psk(pq)] + wkk(t0), writes=wkk(t0))
                act(wk_f32(t0), wk_f32(t0), AF.Ln, bias=kf(1), reads=wkk(t0) + [("KF",)], writes=wkk(t0))
                act(wk_f32(gRS), wk_f32(t0), AF.Exp, scale=-0.5, reads=wkk(t0), writes=wkk(gRS))
                tt("dve", wk_f32(gMR), wk_f32(gMR), wk_f32(gRS), ALU.mult, reads=wkk(gMR) + wkk(gRS), writes=wkk(gMR))
                for c in range(nch):
                    tq = (t0, t1)[c % 2]
                    tt("dve", wk_f32(tq), zc(c, lo, 1024), wk_f32(gRS), ALU.mult, reads=zkeys(c, half) + wkk(gRS), writes=wkk(tq))
                    tt("dve", wk_f32(tq), wk_f32(tq), wk_f32(gMR), ALU.subtract, reads=wkk(tq) + wkk(gMR), writes=wkk(tq))
                    cb(c, half, wk_f32(tq), wkk(tq))

        def store_x(s, c, half, y, ykeys, par=None):
            lo = half * 1024
            act(xh(c, lo, 1024), y, AF.Identity, reads=ykeys, writes=[("XH", c, half)])
            P.dma("sp", xd[s * 8 + c, :, lo: lo + 1024], y, ykeys, [("XD", s, c, half)], ("XDW", c % 4))
            if par is not None:
                P.dma("sp", xhd[(par * NSEG + s) * 8 + c, :, lo: lo + 1024], xh(c, lo, 1024), [("XH", c, half)],
                      [("XHD", par, s, c, half)], ("XHDW", c % 2, half))

        XH3 = XH[:].rearrange("p (c w) -> p c w", c=8)
        def load_xh(par, s):
            for c in range(8):
                P.dma("sp", xh(c, 0, T), xhd[(par * NSEG + s) * 8 + c, :, :], [("XHD", par, s, c, 0), ("XHD", par, s, c, 1)],
                      [("XH", c, 0), ("XH", c, 1)], ("XHL", c % 4))
            sl, sr = max(s - 1, 0), min(s + 1, NSEG - 1)
            bl, br = (par * NSEG + sl) * 8, (par * NSEG + sr) * 8
            P.dma("sp", XH3[:, :, T: T + HALO], xhd[bl: bl + 8, :, T - HALO: T].rearrange("c p t -> p c t"),
                  [("XHD", par, sl, c, 1) for c in range(8)], [("XHH",)], ("XHH", 0))
            P.dma("sp", XH3[:, :, T + HALO: T + 2 * HALO], xhd[br: br + 8, :, 0: HALO].rearrange("c p t -> p c t"),
                  [("XHD", par, sr, c, 0) for c in range(8)], [("XHH",)], ("XHH", 1))
        lkL = lambda s: LINK[:, s: s + 1]
        lkR = lambda s: LINK[:, 8 + s: 9 + s]
        P.dma("sp", LINK[:], link_d, [], [("LINK",)], ("LINK",))

        gXA, gXAB, gR, gI, gA, gH1, gH2 = 0, 2, 3, 5, 7, 9, 11

        def rg_head(l, s, h, mode):
            sx = load_unit(l, ("x", h))
            sg_ = load_unit(l, ("g", h)) if mode == "B" else None
            sgt = load_unit(l, ("gt", h), 512)
            if mode == "B":
                P.dma("sp", wk_f32(gH2, 0, 2048), hbd[s * 8 + h, :, :], [("HBD", s, h)], wkk(gH2, 2), ("HBR",))
            def build_da(l_, h_, buf):
                for k in range(4):
                    ts("dve", DA[:, buf * 512 + k * 128: buf * 512 + (k + 1) * 128], IDB[:], cst(l_, "caw", k * 8 + h_), None, ALU.mult,
                       reads=[("IDB",), ("CST",)], writes=[("DA", buf)])
                state["da"][buf] = (l_, h_)
            if (l, h) not in state["da"]:
                build_da(l, h, 0 if state["da"][0] != (l, (h - 1) % 8) else 1)
            dab = state["da"].index((l, h))
            build_da(l, (h + 1) % 8, 1 - dab)
            bx = cst(l, "bin", OFF_RG_X // 128 + h)
            for half in range(2):
                pg = next_ps()
                for t2 in range(2):
                    for k in range(8):
                        mm(ps(pg, t2 * 512, 512), wt(sx, k), xh(k, half * 1024 + t2 * 512, 512), k == 0, k == 7,
                           reads=[("R", sx), ("XH", k, half)], writes=[psk(pg)], signal=(k == 7 and t2 == 1))
                act(PAD[:, 2 + half * 1024: 2 + half * 1024 + 1024], ps(pg), AF.Identity, bias=bx,
                    reads=[psk(pg), ("CST",)], writes=[("PAD",)])
            pg = next_ps()
            for k in range(8):
                mm(ps(pg, 0, 32), wt(sx, k), xh(k, T, 32), k == 0, k == 7, reads=[("R", sx), ("XHH",)], writes=[psk(pg)], signal=(k == 7))
            act(HT[:, 0:32], ps(pg, 0, 32), AF.Identity, bias=bx, reads=[psk(pg), ("CST",)], writes=[("HT",)])
            if "b" in KOPT:
                act(PAD[:, 0:2], HT[:, 14:16], AF.Identity, scale=lkL(s), reads=[("HT",), ("LINK",)], writes=[("PAD",)])
                act(PAD[:, 2 + T: 3 + T], HT[:, 16:17], AF.Identity, scale=lkR(s), reads=[("HT",), ("LINK",)], writes=[("PAD",)])
            else:
                ts("dve", PAD[:, 0:2], HT[:, 14:16], lkL(s), None, ALU.mult, reads=[("HT",), ("LINK",)], writes=[("PAD",)])
                ts("dve", PAD[:, 2 + T: 3 + T], HT[:, 16:17], lkR(s), None, ALU.mult, reads=[("HT",), ("LINK",)], writes=[("PAD",)])
            for half in range(2):
                pg = next_ps()
                for t2 in range(2):
                    for k in range(4):
                        o = k + half * 1024 + t2 * 512
                        mm(ps(pg, t2 * 512, 512), DA[:, dab * 512 + k * 128: dab * 512 + (k + 1) * 128], PAD[:, o: o + 512],
                           k == 0, k == 3, reads=[("DA", dab), ("PAD",)], writes=[psk(pg)], signal=(k == 3 and t2 == 1))
                if "c" in KOPT:
                    act(wk_bf(gXAB, half * 1024, 1024), ps(pg), AF.Identity, bias=cst(l, "cab", h), reads=[psk(pg), ("CST",)], writes=wkk(gXAB))
                    ts("dve", wk_f32(gXA + half), ps(pg), cst(l, "cab", h), None, ALU.add, reads=[psk(pg), ("CST",)], writes=wkk(gXA + half))
                else:
                    act(wk_f32(gXA + half), ps(pg), AF.Identity, bias=cst(l, "cab", h), reads=[psk(pg), ("CST",)], writes=wkk(gXA + half))
                    act(wk_bf(gXAB, half * 1024, 1024), wk_f32(gXA + half), AF.Identity, reads=wkk(gXA + half), writes=wkk(gXAB))
            d = 1 if mode == "A" else 0
            for gi, (gdst, hb_o) in enumerate(((gR, 16), (gI, 32))):
                for half in range(2):
                    pg = next_ps()
                    for t2 in range(2):
                        mm(ps(pg, t2 * 512, 512), wt(sgt, d * 2 + gi), wk_bf(gXAB, half * 1024 + t2 * 512, 512), True, True,
                           reads=[("R", sgt)] + wkk(gXAB), writes=[psk(pg)], signal=(t2 == 1))
                    act(wk_f32(gdst + half), ps(pg), AF.Tanh, bias=cx(l, hb_o + d * 8 + h), scale=0.5,
                        reads=[psk(pg), ("CX",)], writes=wkk(gdst + half))
            c1ap = cx(l, d * 8 + h)
            c2ap = cx(l, 48 + d * 8 + h)
            order = (1, 0) if mode == "A" else (0, 1)
            for half in order:
                act(wk_f32(gA + half), wk_f32(gR + half), AF.Exp, bias=c1ap, scale=c1ap, reads=wkk(gR + half) + [("CX",)], writes=wkk(gA + half))
                act(wk_f32(gR + half), wk_f32(gR + half), AF.Exp, bias=c2ap, scale=c2ap, reads=wkk(gR + half) + [("CX",)], writes=wkk(gR + half))
            for half in order:
                stt("dve", wk_f32(gI + half), wk_f32(gI + half), 1.0, wk_f32(gXA + half), ALU.add, ALU.mult,
                    reads=wkk(gI + half) + wkk(gXA + half), writes=wkk(gI + half))
            for half in order:
                ts("dve", wk_f32(gR + half), wk_f32(gR + half), 0.99999994, None, ALU.min, reads=wkk(gR + half), writes=wkk(gR + half))
                act(wk_f32(gR + half), wk_f32(gR + half), AF.Sqrt, bias=kf(0), scale=-1.0, reads=wkk(gR + half) + [("KF",)], writes=wkk(gR + half))
                stt("dve", wk_f32(gI + half), wk_f32(gI + half), 0.5, wk_f32(gR + half), ALU.mult, ALU.mult,
                    reads=wkk(gI + half) + wkk(gR + half), writes=wkk(gI + half))
            if mode == "A":
                gH = gH1 if h % 2 == 0 else gH2
                H_ = wk_f32(gH, 0, 2048)
                A_, I_ = wk_f32(gA, 0, 2048), wk_f32(gI, 0, 2048)
                ini = CB[:, h: h + 1]
                P.op("dve", lambda e: e.tensor_tensor_scan(out=H_[:, 2047:1023:-1], data0=A_[:, 2047:1023:-1], data1=I_[:, 2047:1023:-1], initial=ini, op0=ALU.mult, op1=ALU.add),
                     wkk(gA + 1) + wkk(gI + 1) + [("CB",)], wkk(gH + 1))
                ini2 = H_[:, 1024:1025]
                P.op("dve", lambda e: e.tensor_tensor_scan(out=H_[:, 1023::-1], data0=A_[:, 1023::-1], data1=I_[:, 1023::-1], initial=ini2, op0=ALU.mult, op1=ALU.add),
                     wkk(gA) + wkk(gI) + wkk(gH + 1), wkk(gH))
                P.dma("sp", hbd[s * 8 + h, :, :], H_, wkk(gH, 2), [("HBD", s, h)], ("HBW", h % 2))
                ts("dve", CB[:, h: h + 1], H_[:, 0:1], lkL(s), None, ALU.mult, reads=wkk(gH) + [("LINK",)], writes=[("CB",)])
                return
            H_ = wk_f32(gH1, 0, 2048)
            A_, I_ = wk_f32(gA, 0, 2048), wk_f32(gI, 0, 2048)
            ini = CF[:, h: h + 1]
            P.op("dve", lambda e: e.tensor_tensor_scan(out=H_[:, 0:1024], data0=A_[:, 0:1024], data1=I_[:, 0:1024], initial=ini, op0=ALU.mult, op1=ALU.add),
                 wkk(gA) + wkk(gI) + [("CF",)], wkk(gH1))
            ini2 = H_[:, 1023:1024]
            P.op("dve", lambda e: e.tensor_tensor_scan(out=H_[:, 1024:2048], data0=A_[:, 1024:2048], data1=I_[:, 1024:2048], initial=ini2, op0=ALU.mult, op1=ALU.add),
                 wkk(gA + 1) + wkk(gI + 1) + wkk(gH1), wkk(gH1 + 1))
            ts("dve", CF[:, h: h + 1], H_[:, T - 1: T], lkR(s), None, ALU.mult, reads=wkk(gH1 + 1) + [("LINK",)], writes=[("CF",)])
            for half in range(2):
                tt("dve", wk_f32(gH1 + half), wk_f32(gH1 + half), wk_f32(gH2 + half), ALU.add, reads=wkk(gH1 + half) + wkk(gH2 + half), writes=wkk(gH1 + half))
            for half in range(2):
                pg = next_ps()
                for t2 in range(2):
                    for k in range(8):
                        mm(ps(pg, t2 * 512, 512), wt(sg_, k), xh(k, half * 1024 + t2 * 512, 512), k == 0, k == 7,
                           reads=[("R", sg_), ("XH", k, half)], writes=[psk(pg)], signal=(k == 7 and t2 == 1))
                act(wk_f32(gR + half), ps(pg), AF.Gelu_apprx_tanh, bias=cst(l, "bin", OFF_RG_G // 128 + h),
                    reads=[psk(pg), ("CST",)], writes=wkk(gR + half))
                tt("dve", big_bf(h, half * 1024, 1024), wk_f32(gH1 + half), wk_f32(gR + half), ALU.mult,
                   reads=wkk(gH1 + half) + wkk(gR + half), writes=bgk(h))

        out_deps = []
        for si in range(NSEG):
            for half in range(2):
                for tb in range(8):
                    g = tb
                    tok0 = half * 1024 + tb * 128
                    xt = wk_f32(g)
                    P.dma("sp", xt, xin[si, tok0: tok0 + 128, :], [], wkk(g), ("XT", tb))
                    P.op("dve", lambda e, xt=xt: e.bn_stats(out=SM[:, 0:6], in_=xt[:, 0:512]), wkk(g), [("SM",)])
                    P.op("dve", lambda e, xt=xt: e.bn_stats(out=SM[:, 6:12], in_=xt[:, 512:1024]), wkk(g), [("SM",)])
                    P.op("dve", lambda e: e.bn_aggr(out=SM[:, 12:14], in_=SM[:, 0:12]), [("SM",)], [("SM",)])
                    act(SM[:, 14:15], SM[:, 13:14], AF.Ln, bias=kf(1), reads=[("SM",), ("KF",)], writes=[("SM",)])
                    act(SM[:, 14:15], SM[:, 14:15], AF.Exp, scale=-0.5, reads=[("SM",)], writes=[("SM",)])
                    ts("dve", xt, xt, SM[:, 12:13], SM[:, 14:15], ALU.subtract, ALU.mult, reads=wkk(g) + [("SM",)], writes=wkk(g))
                for c in range(8):
                    pg = next_ps()
                    for tb in range(8):
                        tr(ps(pg, tb * 128, 128), wk_f32(tb, c * 128, 128), IDF[:], reads=wkk(tb) + [("IDF",)], writes=[psk(pg)],
                           signal=(tb == 7))
                    yg = 8 + (c % 4)
                    act(wk_f32(yg), ps(pg), AF.Identity, bias=LNIN[:, 8 + c: 9 + c], scale=LNIN[:, c: c + 1],
                        reads=[psk(pg), ("LNIN",)], writes=wkk(yg))
                    store_x(si, c, half, wk_f32(yg), wkk(yg), par=0)

        for l in range(L):
            last = (l == L - 1)
            par = l % 2
            memset("dve", CB[:], 0.0, [("CB",)])
            for si in reversed(range(NSEG)):
                load_xh(par, si)
                for h in range(8):
                    rg_head(l, si, h, "A")
            memset("dve", CF[:], 0.0, [("CF",)])
            for si in range(NSEG):
                load_xh(par, si)
                for h in range(8):
                    rg_head(l, si, h, "B")

                for ch in range(4):
                    su = load_unit(l, ("u", ch))
                    for half in range(2):
                        pg = next_ps()
                        for t2 in range(2):
                            for k in range(8):
                                mm(ps(pg, t2 * 512, 512), wt(su, k), xh(k, half * 1024 + t2 * 512, 512), k == 0, k == 7,
                                   reads=[("R", su), ("XH", k, half)], writes=[psk(pg)], signal=(k == 7 and t2 == 1))
                        act(big_bf(8 + ch, half * 1024, 1024), ps(pg), AF.Gelu_apprx_tanh, bias=cst(l, "bin", OFF_SG // 128 + ch),
                            reads=[psk(pg), ("CST",)], writes=bgk(8 + ch))
                sv = [load_unit(l, ("v", j)) for j in range(4)]
                SGO3 = BIG[:, 8 * 2048: 12 * 2048].rearrange("p (g t) -> p g t", g=4)
                for tb in range(16):
                    half = tb // 8
                    pg = next_ps()
                    for j in range(4):
                        for k in range(8):
                            mm(ps(pg, j * 128, 128), xh(k, tb * 128, 128), wt(sv[j], k), k == 0, False,
                               reads=[("R", sv[j]), ("XH", k, half)], writes=[psk(pg)])
                        mm(ps(pg, j * 128, 128), ONE[32 * l: 32 * l + 1, :], ROWP[32 * l: 32 * l + 1, j * 128:(j + 1) * 128], False, True,
                           reads=[("ONE",), ("ROWP",)], writes=[psk(pg)], signal=(j == 3))
                    gv = tb % 2
                    V = wk_f32(gv, 0, 512)
                    VN = wk_bf(2 + tb % 2, 0, 512)
                    act(V, ps(pg, 0, 512), AF.Gelu_apprx_tanh, reads=[psk(pg)], writes=wkk(gv))
                    P.op("dve", lambda e, V=V: e.bn_stats(out=SM[:, 16:22], in_=V), wkk(gv), [("SM2",)])
                    P.op("dve", lambda e: e.bn_aggr(out=SM[:, 22:24], in_=SM[:, 16:22]), [("SM2",)], [("SM2",)])
                    act(SM[:, 24:25], SM[:, 23:24], AF.Ln, bias=kf(1), reads=[("SM2",), ("KF",)], writes=[("SM2",)])
                    act(SM[:, 24:25], SM[:, 24:25], AF.Exp, scale=-0.5, reads=[("SM2",)], writes=[("SM2",)])
                    ts("dve", VN, V, SM[:, 22:23], SM[:, 24:25], ALU.subtract, ALU.mult, reads=wkk(gv) + [("SM2",)], writes=wkk(2 + tb % 2))
                    pg2 = next_ps()
                    for g in range(4):
                        mm(ps(pg2, g * 128, 128), VN[:, g * 128:(g + 1) * 128], SGWT[:, l * 512 + g * 128: l * 512 + (g + 1) * 128], True, True,
                           reads=wkk(2 + tb % 2) + [("SGWT", l)], writes=[psk(pg2)], signal=(g == 3))
                    gt_ = 4 + tb % 2
                    TT = wk_f32(gt_, 0, 512)
                    tt("dve", TT, ps(pg2, 0, 512), GMAT[:, l * 512:(l + 1) * 512], ALU.mult, reads=[psk(pg2), ("GMAT", l)], writes=wkk(gt_))
                    tt("dve", TT, TT, BIASM[:, l * 512:(l + 1) * 512], ALU.add, reads=wkk(gt_) + [("BIASM", l)], writes=wkk(gt_))
                    uview = SGO3[:, :, tb * 128:(tb + 1) * 128]
                    tt("dve", uview, TT.rearrange("p (g t) -> p g t", g=4), uview, ALU.mult, reads=wkk(gt_) + bgk(8, 4), writes=bgk(8, 4))

                for ch in range(4):
                    sc = load_unit(l, ("c", ch))
                    scg = load_unit(l, ("cg", ch))
                    for k in range(31):
                        ts("dve", DC[:, k * 128:(k + 1) * 128], IDB[:], cst(l, "ccw", k * 4 + ch), None, ALU.mult,
                           reads=[("IDB",), ("CST",)], writes=[("DC",)])
                    bc_, bcg_ = cst(l, "bin", OFF_CC // 128 + ch), cst(l, "bin", OFF_CC // 128 + 4 + ch)
                    for half in range(2):
                        pc, pgg = next_ps(), next_ps()
                        for (pp, ss) in ((pc, sc), (pgg, scg)):
                            for t2 in range(2):
                                for k in range(8):
                                    mm(ps(pp, t2 * 512, 512), wt(ss, k), xh(k, half * 1024 + t2 * 512, 512), k == 0, k == 7,
                                       reads=[("R", ss), ("XH", k, half)], writes=[psk(pp)], signal=(k == 7 and t2 == 1))
                        gs = 8 + half
                        act(wk_f32(gs), ps(pgg), AF.Sigmoid, bias=bcg_, reads=[psk(pgg), ("CST",)], writes=wkk(gs))
                        stt("dve", PAD[:, 15 + half * 1024: 15 + half * 1024 + 1024], ps(pc), bc_, wk_f32(gs),
                            ALU.add, ALU.mult, reads=[psk(pc), ("CST",)] + wkk(gs), writes=[("PAD",)])
                    pc, pgg = next_ps(), next_ps()
                    for (pp, ss) in ((pc, sc), (pgg, scg)):
                        for k in range(8):
                            mm(ps(pp, 0, 32), wt(ss, k), xh(k, T, 32), k == 0, k == 7, reads=[("R", ss), ("XHH",)], writes=[psk(pp)], signal=(k == 7))
                    act(HT[:, 32:64], ps(pgg, 0, 32), AF.Sigmoid, bias=bcg_, reads=[psk(pgg), ("CST",)], writes=[("HT",)])
                    stt("dve", HT[:, 0:32], ps(pc, 0, 32), bc_, HT[:, 32:64], ALU.add, ALU.mult, reads=[psk(pc), ("CST",), ("HT",)], writes=[("HT",)])
                    ts("dve", PAD[:, 0:15], HT[:, 1:16], lkL(si), None, ALU.mult, reads=[("HT",), ("LINK",)], writes=[("PAD",)])
                    ts("dve", PAD[:, 15 + T: 30 + T], HT[:, 16:31], lkR(si), None, ALU.mult, reads=[("HT",), ("LINK",)], writes=[("PAD",)])
                    for half in range(2):
                        pg = next_ps()
                        for t2 in range(2):
                            for k in range(31):
                                o = k + half * 1024 + t2 * 512
                                mm(ps(pg, t2 * 512, 512), DC[:, k * 128:(k + 1) * 128], PAD[:, o: o + 512], k == 0, k == 30,
                                   reads=[("DC",), ("PAD",)], writes=[psk(pg)], signal=(k == 30 and t2 == 1))
                        act(wk_f32(2 * ch + half), ps(pg), AF.Identity, bias=cst(l, "ccb", ch), reads=[psk(pg), ("CST",)], writes=wkk(2 * ch + half))
                def cc_cb(c, half, Tap, Tk):
                    act(big_bf(12 + c, half * 1024, 1024), Tap, AF.Silu, bias=cst(l, "cclb", c), scale=cst(l, "cclg", c),
                        reads=Tk + [("CST",)], writes=bgk(12 + c))
                layer_norm(4, 128, lambda c, lo, n: wk_f32(2 * c, lo, n), lambda c, half: wkk(2 * c + half), cc_cb, [8, 9, 10, 11])

                for oc in range(8):
                    sl = {nm: load_unit(l, (nm, oc), 1024 if nm in ("ga", "gb", "gc", "ba") else 512) for nm in ("ga", "gb", "gc", "ba", "bb", "bc")}
                    for half in range(2):
                        for bi, (gn, yn, nk, src0) in enumerate((("ga", "ba", 8, 0), ("gb", "bb", 4, 8), ("gc", "bc", 4, 12))):
                            pgt, py = next_ps(), next_ps()
                            for t2 in range(2):
                                for k in range(8):
                                    mm(ps(pgt, t2 * 512, 512), wt(sl[gn], k), xh(k, half * 1024 + t2 * 512, 512), k == 0, k == 7,
                                       reads=[("R", sl[gn]), ("XH", k, half)], writes=[psk(pgt)], signal=(k == 7 and t2 == 1))
                            for t2 in range(2):
                                for k in range(nk):
                                    mm(ps(py, t2 * 512, 512), wt(sl[yn], k), big_bf(src0 + k, half * 1024 + t2 * 512, 512), k == 0, k == nk - 1,
                                       reads=[("R", sl[yn])] + bgk(src0 + k), writes=[psk(py)], signal=(k == nk - 1 and t2 == 1))
                            gg = 8 + bi
                            act(wk_f32(gg), ps(pgt), AF.Sigmoid, bias=cst(l, "bin", OFF_GATE // 128 + bi * 8 + oc), reads=[psk(pgt), ("CST",)], writes=wkk(gg))
                            tt("dve", wk_f32(gg), ps(py), wk_f32(gg), ALU.mult, reads=[psk(py)] + wkk(gg), writes=wkk(gg))
                        tt("dve", wk_f32(8), wk_f32(8), wk_f32(9), ALU.add, reads=wkk(8) + wkk(9), writes=wkk(8))
                        tt("dve", wk_bf(oc, half * 1024, 1024), wk_f32(8), wk_f32(10), ALU.add, reads=wkk(8) + wkk(10), writes=wkk(oc))

                def resid(oc, half, pg, bname, zdst, zkeys_w):
                    gx = 9 + (2 * oc + half) % 2
                    ge = 11 + (2 * oc + half) % 2
                    P.dma("sp", wk_f32(gx), xd[si * 8 + oc, :, half * 1024: half * 1024 + 1024], [("XD", si, oc, half)], wkk(gx), ("XDR", gx))
                    act(wk_f32(ge), ps(pg), AF.Identity, bias=cst(l, bname, oc), reads=[psk(pg), ("CST",)], writes=wkk(ge))
                    stt("dve", zdst, wk_f32(gx), ALPHA, wk_f32(ge), ALU.mult, ALU.add, reads=wkk(gx) + wkk(ge), writes=zkeys_w)
                for oc in range(8):
                    so = load_unit(l, ("o", oc))
                    for half in range(2):
                        pg = next_ps()
                        for t2 in range(2):
                            for k in range(8):
                                mm(ps(pg, t2 * 512, 512), wt(so, k), wk_bf(k, half * 1024 + t2 * 512, 512), k == 0, k == 7,
                                   reads=[("R", so)] + wkk(k), writes=[psk(pg)], signal=(k == 7 and t2 == 1))
                        resid(oc, half, pg, "bo", big_f32(oc, half * 1024, 1024), bgk(2 * oc + half))

                def ln_x_cb(gname, bname, par_out):
                    def cb(c, half, Tap, Tk):
                        yg = (4 + (2 * c + half) % 4) if "e" in KOPT else (6 + (2 * c + half) % 2)
                        act(wk_f32(yg), Tap, AF.Identity, bias=cst(l, bname, c), scale=cst(l, gname, c), reads=Tk + [("CST",)], writes=wkk(yg))
                        store_x(si, c, half, wk_f32(yg), wkk(yg), par=par_out)
                    return cb
                layer_norm(8, 0, lambda c, lo, n: big_f32(c, lo, n), lambda c, half: bgk(2 * c + half), ln_x_cb("l1g", "l1b", None), [0, 1, 2, 3, 8, 9])

                for J in range(4):
                    for jj in range(8):
                        j = J * 8 + jj
                        s1 = load_unit(l, ("f1", j))
                        for half in range(2):
                            pg = next_ps()
                            for t2 in range(2):
                                for k in range(8):
                                    mm(ps(pg, t2 * 512, 512), wt(s1, k), xh(k, half * 1024 + t2 * 512, 512), k == 0, k == 7,
                                       reads=[("R", s1), ("XH", k, half)], writes=[psk(pg)], signal=(k == 7 and t2 == 1))
                            gr = 8 + (2 * jj + half) % 2
                            act(wk_f32(gr), ps(pg), AF.Relu, bias=cst(l, "bf1", j), reads=[psk(pg), ("CST",)], writes=wkk(gr))
                            tt("dve", wk_bf(jj, half * 1024, 1024), wk_f32(gr), wk_f32(gr), ALU.mult, reads=wkk(gr), writes=wkk(jj))
                    s2 = [load_unit(l, ("f2", J * 8 + jj)) for jj in range(8)]
                    for oc in range(8):
                        for half in range(2):
                            pg = next_ps()
                            for t2 in range(2):
                                for jj in range(8):
                                    mm(ps(pg, t2 * 512, 512), wt(s2[jj], oc), wk_bf(jj, half * 1024 + t2 * 512, 512), jj == 0, jj == 7,
                                       reads=[("R", s2[jj])] + wkk(jj), writes=[psk(pg)], signal=(jj == 7 and t2 == 1))
                            acc = big_f32(oc, half * 1024, 1024)
                            if J == 0:
                                cp("dve", acc, ps(pg), [psk(pg)], bgk(2 * oc + half))
                            else:
                                tt("dve", acc, ps(pg), acc, ALU.add, reads=[psk(pg)] + bgk(2 * oc + half), writes=bgk(2 * oc + half))
                for oc in range(8):
                    for half in range(2):
                        gx = 9 + (2 * oc + half) % 2
                        acc = big_f32(oc, half * 1024, 1024)
                        P.dma("sp", wk_f32(gx), xd[si * 8 + oc, :, half * 1024: half * 1024 + 1024], [("XD", si, oc, half)], wkk(gx), ("XDR", gx))
                        act(acc, acc, AF.Identity, bias=cst(l, "bf2", oc), reads=bgk(2 * oc + half) + [("CST",)], writes=bgk(2 * oc + half))
                        stt("dve", acc, wk_f32(gx), ALPHA, acc, ALU.mult, ALU.add, reads=wkk(gx) + bgk(2 * oc + half), writes=bgk(2 * oc + half))
                if not last:
                    layer_norm(8, 0, lambda c, lo, n: big_f32(c, lo, n), lambda c, half: bgk(2 * c + half), ln_x_cb("l2g", "l2b", 1 - par), [0, 1, 2, 3, 8, 9])
                else:
                    def fin_cb(c, half, Tap, Tk):
                        act(wk_f32(4 + c), Tap, AF.Identity, bias=cst(l, "l2b", c), scale=cst(l, "l2g", c), reads=Tk + [("CST",)], writes=wkk(4 + c))
                        if c == 7:
                            for tb in range(8):
                                pg = next_ps()
                                for cc in range(8):
                                    tr(ps(pg, cc * 128, 128), wk_f32(4 + cc, tb * 128, 128), IDF[:], reads=wkk(4 + cc) + [("IDF",)], writes=[psk(pg)],
                                       signal=(cc == 7))
                                go = 12 if tb % 2 == 0 else 3
                                if tb % 2 == 0:
                                    cp("dve", wk_f32(go), ps(pg), [psk(pg)], wkk(go))
                                else:
                                    act(wk_f32(go), ps(pg), AF.Identity, reads=[psk(pg)], writes=wkk(go))
                                tok0 = half * 1024 + tb * 128
                                dep = P.dma("sp", yout[si, tok0: tok0 + 128, :], wk_f32(go), wkk(go), [("YO", si, half, tb)], ("YO", tb % 2))
                                out_deps.append(dep)
                    layer_norm(8, 0, lambda c, lo, n: big_f32(c, lo, n), lambda c, half: bgk(2 * c + half), fin_cb, [0, 1, 2, 3])

        fin = {}
        for s, v in out_deps:
            fin[s] = max(fin.get(s, 0), v)
        P.final_wait("sp", list(fin.items()))
        with nc.Block() as block:
            P.emit(block)
    return nc


def _colchunk(W, j, nk):
    blk = W[:, j * 128:(j + 1) * 128].reshape(nk, 128, 128)
    out = np.zeros((128, 1024), np.float32)
    out[:, : nk * 128] = blk.transpose(1, 0, 2).reshape(128, nk * 128)
    return out


def _prep_weights(inp):
    wu = np.zeros((L, NU, 128, 1024), np.float32)
    cst = np.zeros((L, 128, NCST), np.float32)
    rows = np.zeros((L, 1, NROW), np.float32)
    for l in range(L):
        w_in = inp["w_in"][l]
        for h in range(8):
            wu[l, UNITS[("x", h)]] = _colchunk(w_in, OFF_RG_X // 128 + h, 8)
            wu[l, UNITS[("g", h)]] = _colchunk(w_in, OFF_RG_G // 128 + h, 8)
            gt = np.zeros((128, 1024), np.float32)
            gt[:, 0:128] = inp["rg_wa"][l, 0, h]
            gt[:, 128:256] = inp["rg_wx"][l, 0, h]
            gt[:, 256:384] = inp["rg_wa"][l, 1, h]
            gt[:, 384:512] = inp["rg_wx"][l, 1, h]
            wu[l, UNITS[("gt", h)]] = gt
        for ch in range(4):
            wu[l, UNITS[("u", ch)]] = _colchunk(w_in, OFF_SG // 128 + ch, 8)
            wu[l, UNITS[("v", ch)]] = _colchunk(w_in, OFF_SG // 128 + 4 + ch, 8)
            wu[l, UNITS[("c", ch)]] = _colchunk(w_in, OFF_CC // 128 + ch, 8)
            wu[l, UNITS[("cg", ch)]] = _colchunk(w_in, OFF_CC // 128 + 4 + ch, 8)
        for oc in range(8):
            wu[l, UNITS[("ga", oc)]] = _colchunk(w_in, OFF_GATE // 128 + oc, 8)
            wu[l, UNITS[("gb", oc)]] = _colchunk(w_in, OFF_GATE // 128 + 8 + oc, 8)
            wu[l, UNITS[("gc", oc)]] = _colchunk(w_in, OFF_GATE // 128 + 16 + oc, 8)
            wu[l, UNITS[("ba", oc)]] = _colchunk(inp["w_ba"][l], oc, 8)
            wu[l, UNITS[("bb", oc)]] = _colchunk(inp["w_bb"][l], oc, 4)
            wu[l, UNITS[("bc", oc)]] = _colchunk(inp["w_bc"][l], oc, 4)
            wu[l, UNITS[("o", oc)]] = _colchunk(inp["w_o"][l], oc, 8)
        for j in range(32):
            wu[l, UNITS[("f1", j)]] = _colchunk(inp["w_ff1"][l], j, 8)
            wu[l, UNITS[("f2", j)]] = inp["w_ff2"][l][j * 128:(j + 1) * 128, :]
        def put(name, vec, n):
            cst[l, :, _c[name]: _c[name] + n] = np.asarray(vec, np.float32).reshape(n, 128).T
        put("bin", inp["b_in"][l], 56)
        put("caw", inp["conv_a_w"][l].reshape(-1), 32)
        put("cab", inp["conv_a_b"][l], 8)
        put("rba", inp["rg_ba"][l].reshape(-1), 16)
        put("rbx", inp["rg_bx"][l].reshape(-1), 16)
        put("lam", inp["rg_lambda"][l].reshape(-1), 16)
        put("sglg", inp["sg_ln_g"][l], 4)
        put("ccw", inp["conv_c_w"][l].reshape(-1), 124)
        put("ccb", inp["conv_c_b"][l], 4)
        put("cclg", inp["cc_ln_g"][l], 4)
        put("cclb", inp["cc_ln_b"][l], 4)
        put("bo", inp["b_o"][l], 8)
        put("l1g", inp["ln1_g"][l], 8)
        put("l1b", inp["ln1_b"][l], 8)
        put("bf1", inp["b_ff1"][l], 32)
        put("bf2", inp["b_ff2"][l], 8)
        put("l2g", inp["ln2_g"][l], 8)
        put("l2b", inp["ln2_b"][l], 8)
        rows[l, 0, 0:512] = inp["sg_ln_b"][l]
        rows[l, 0, 512:1024] = inp["sg_b"][l].reshape(-1)
        rows[l, 0, 1024:1536] = inp["b_in"][l][OFF_SG + 512: OFF_SG + 1024]
    lnin = np.zeros((128, 16), np.float32)
    lnin[:, 0:8] = np.asarray(inp["ln_in_g"], np.float32).reshape(8, 128).T
    lnin[:, 8:16] = np.asarray(inp["ln_in_b"], np.float32).reshape(8, 128).T
    sgwt = np.ascontiguousarray(np.asarray(inp["sg_w"], np.float32).transpose(0, 1, 3, 2))
    return wu, cst, rows, lnin, sgwt


_NC_CACHE = {}
_SAMPLE_SLOTS = [(c, k) for c in range(4, 8) for k in range(4)]


def _core_inputs(inp):
    xp = inp["x_prompt"].astype(np.float32, copy=False)
    xs = inp["x_sample"].astype(np.float32, copy=False)
    xins = [np.zeros((NSEG, T, D), np.float32) for _ in range(NCORES)]
    links = [np.zeros((128, 16), np.float32) for _ in range(NCORES)]
    xins[0][:] = xp[0].reshape(NSEG, T, D)
    links[0][:, 1:8] = 1.0
    links[0][:, 8:15] = 1.0
    for i, (c, k) in enumerate(_SAMPLE_SLOTS):
        xins[c][k] = xs[i]
    return xins, links


def kernel(**inputs):
    inp = {k: np.asarray(v) for k, v in inputs.items()}
    wu, cst, rows, lnin, sgwt = _prep_weights(inp)
    xins, links = _core_inputs(inp)
    if "nc" not in _NC_CACHE:
        _NC_CACHE["nc"] = build(None)
    nc = _NC_CACHE["nc"]
    ident = np.eye(128, dtype=np.float32)
    in_maps = [{"xin": xins[c], "link": links[c], "wu": wu, "cst": cst, "rows": rows, "lnin": lnin, "sgwt": sgwt, "ident": ident}
               for c in range(NCORES)]
    res = run_bass_kernel_spmd(nc, in_maps, core_ids=list(range(NCORES)))
    y_prompt = np.ascontiguousarray(res.results[0]["yout"]).reshape(1, NSEG * T, D).astype(np.float32, copy=False)
    y_sample = np.zeros((16, T, D), np.float32)
    for i, (c, k) in enumerate(_SAMPLE_SLOTS):
        y_sample[i] = res.results[c]["yout"][k]
    return (y_prompt, y_sample)
```

```python
import numpy as np
import os
KOPT = {"b"}
import concourse.bass as bass
import concourse.mybir as mybir
from concourse.bass_utils import run_bass_kernel_spmd

F32 = mybir.dt.float32
BF16 = mybir.dt.bfloat16
AF = mybir.ActivationFunctionType
ALU = mybir.AluOpType

D = 1024
T = 2048
HALO = 16
NSEG = 8
L = 2
NCORES = 8
ALPHA = float((2 * L) ** 0.25)
EPS = 1e-5
OFF_RG_X, OFF_RG_G, OFF_SG, OFF_CC, OFF_GATE = 0, 1024, 2048, 3072, 4096
NSLOT = 8
NWK = 13

_c = {}
_o = 0
def _add(name, n):
    global _o
    _c[name] = _o
    _o += n
_add("bin", 56); _add("caw", 32); _add("cab", 8); _add("rba", 16); _add("rbx", 16); _add("lam", 16)
_add("sglg", 4); _add("ccw", 124); _add("ccb", 4); _add("cclg", 4); _add("cclb", 4)
_add("bo", 8); _add("l1g", 8); _add("l1b", 8); _add("bf1", 32); _add("bf2", 8); _add("l2g", 8); _add("l2b", 8)
NCST = _o
NROW = 1536

def _units():
    u = {}
    n = 0
    for h in range(8):
        u[("x", h)] = n; u[("g", h)] = n + 1; u[("gt", h)] = n + 2; n += 3
    for ch in range(4):
        u[("u", ch)] = n; n += 1
    for j in range(4):
        u[("v", j)] = n; n += 1
    for ch in range(4):
        u[("c", ch)] = n; u[("cg", ch)] = n + 1; n += 2
    for oc in range(8):
        for k, nm in enumerate(("ga", "gb", "gc", "ba", "bb", "bc")):
            u[(nm, oc)] = n + k
        n += 6
    for oc in range(8):
        u[("o", oc)] = n; n += 1
    for J in range(4):
        for jj in range(8):
            u[("f1", J * 8 + jj)] = n; n += 1
        for jj in range(8):
            u[("f2", J * 8 + jj)] = n; n += 1
    return u, n
UNITS, NU = _units()


class Prog:
    ENGS = ("pe", "act", "dve", "pool", "sp")

    def __init__(self, nc, es):
        self.nc = nc
        self.es = es
        self.streams = {e: [] for e in self.ENGS}
        self.cnt = {e: 0 for e in self.ENGS}
        self.known = {e: {} for e in self.ENGS}
        self.buf = {}
        self.sems = {}
        self.dcnt = {}
        for e in self.ENGS:
            self.sems[e] = es.enter_context(nc.semaphore("s_" + e))
        self.out_deps = []

    def dsem(self, key):
        if key not in self.sems:
            self.sems[key] = self.es.enter_context(self.nc.semaphore("d_%d" % len(self.sems)))
            self.dcnt[key] = 0
        return key

    def _deps(self, engine, reads, writes):
        deps = {}
        def add(d):
            if d is None:
                return
            s, v = d
            if deps.get(s, 0) < v:
                deps[s] = v
        for k in reads:
            st = self.buf.get(k)
            if st:
                add(st["w"])
        for k in writes:
            st = self.buf.get(k)
            if st:
                add(st["w"])
                for s, v in st["r"].items():
                    add((s, v))
        waits = []
        kn = self.known[engine]
        for s, v in deps.items():
            if engine == "pe" and s == "pe":
                continue
            if kn.get(s, 0) < v:
                waits.append((s, v))
                kn[s] = v
        return waits

    def _commit(self, dep, reads, writes):
        for k in writes:
            self.buf[k] = {"w": dep, "r": {}}
        for k in reads:
            st = self.buf.setdefault(k, {"w": None, "r": {}})
            if st["r"].get(dep[0], 0) < dep[1]:
                st["r"][dep[0]] = dep[1]

    def op(self, engine, fn, reads=(), writes=(), signal=True):
        waits = self._deps(engine, reads, writes)
        sems = self.sems
        if signal:
            self.cnt[engine] += 1
            dep = (engine, self.cnt[engine])
        else:
            dep = (engine, self.cnt[engine] + 1)
        semh = sems[engine]
        def thunk(eng, waits=waits, fn=fn, signal=signal):
            for s, v in waits:
                eng.wait_ge(sems[s], v)
            ins = fn(eng)
            if signal:
                ins.then_inc(semh, 1)
        self.streams[engine].append(thunk)
        self._commit(dep, reads, writes)

    def dma(self, engine, out, in_, reads, writes, semkey):
        self.dsem(semkey)
        waits = self._deps(engine, reads, writes)
        self.dcnt[semkey] += 16
        dep = (semkey, self.dcnt[semkey])
        sems = self.sems
        def thunk(eng, waits=waits):
            for s, v in waits:
                eng.wait_ge(sems[s], v)
            eng.dma_start(out=out, in_=in_).then_inc(sems[semkey], 16)
        self.streams[engine].append(thunk)
        self._commit(dep, reads, writes)
        return dep

    def final_wait(self, engine, deps):
        sems = self.sems
        def thunk(eng):
            for s, v in deps:
                eng.wait_ge(sems[s], v)
        self.streams[engine].append(thunk)

    def emit(self, block):
        P = self
        @block.tensor
        def _(e):
            for t in P.streams["pe"]:
                t(e)
        @block.scalar
        def _(e):
            for t in P.streams["act"]:
                t(e)
        @block.vector
        def _(e):
            for t in P.streams["dve"]:
                t(e)
        @block.gpsimd
        def _(e):
            for t in P.streams["pool"]:
                t(e)
        @block.sync
        def _(e):
            for t in P.streams["sp"]:
                t(e)


def build(seg_kinds, debug=False):
    import contextlib
    nc = bass.Bass("TRN2", target_bir_lowering=False)
    xin = nc.dram_tensor("xin", [NSEG, T, D], F32, kind="ExternalInput").ap()
    wu = nc.dram_tensor("wu", [L, NU, 128, 1024], F32, kind="ExternalInput").ap()
    cst_d = nc.dram_tensor("cst", [L, 128, NCST], F32, kind="ExternalInput").ap()
    row_d = nc.dram_tensor("rows", [L, 1, NROW], F32, kind="ExternalInput").ap()
    lnin_d = nc.dram_tensor("lnin", [128, 16], F32, kind="ExternalInput").ap()
    sgwt_d = nc.dram_tensor("sgwt", [L, 4, 128, 128], F32, kind="ExternalInput").ap()
    ident_d = nc.dram_tensor("ident", [128, 128], F32, kind="ExternalInput").ap()
    yout = nc.dram_tensor("yout", [NSEG, T, D], F32, kind="ExternalOutput").ap()
    link_d = nc.dram_tensor("link", [128, 16], F32, kind="ExternalInput").ap()
    xd = nc.dram_tensor("xd", [NSEG * 8, 128, T], F32, kind="Internal").ap()
    xhd = nc.dram_tensor("xhd", [2 * NSEG * 8, 128, T], BF16, kind="Internal").ap()
    hbd = nc.dram_tensor("hbd", [NSEG * 8, 128, T], F32, kind="Internal").ap()

    es = contextlib.ExitStack()
    with es:
        P = Prog(nc, es)
        sb = lambda name, shape, dt: es.enter_context(nc.sbuf_tensor(name, shape, dt))
        XH = sb("XH", [128, 8 * (T + 2 * HALO)], BF16)
        BIG = sb("BIG", [128, 32768], BF16)
        WK = sb("WK", [128, NWK * 2048], BF16)
        PAD = sb("PAD", [128, 2176], BF16)
        DC = sb("DC", [128, 31 * 128], BF16)
        DA = sb("DA", [128, 2 * 4 * 128], BF16)
        RING = sb("RING", [128, NSLOT * 1024], BF16)
        CST = sb("CST", [128, L * NCST], F32)
        CX = sb("CX", [128, L * 64], F32)
        ROWP = sb("ROWP", [33, 512], F32)
        LNIN = sb("LNIN", [128, 16], F32)
        SGWT = sb("SGWT", [128, L * 512], BF16)
        BIASM = sb("BIASM", [128, L * 512], F32)
        GMAT = sb("GMAT", [128, L * 512], F32)
        IDB = sb("IDB", [128, 128], BF16)
        IDF = sb("IDF", [128, 128], F32)
        ONESD = sb("ONESD", [128, 256], F32)
        ONE = sb("ONE", [128, 128], F32)
        ONESB = sb("ONESB", [128, 256], BF16)
        ONEB = sb("ONEB", [33, 128], BF16)
        ROWPB = sb("ROWPB", [33, 1024], BF16)
        KF = sb("KF", [128, 8], F32)
        SM = sb("SM", [128, 64], F32)
        LINK = sb("LINK", [128, 16], F32)
        CF = sb("CF", [128, 8], F32)
        CB = sb("CB", [128, 8], F32)
        HT = sb("HT", [128, 64], F32)
        PS = es.enter_context(nc.psum_tensor("PS", [128, 4096], F32))

        XW = T + 2 * HALO
        def xh(c, lo, n):
            return XH[:, c * XW + lo: c * XW + lo + n]
        def big_bf(g, lo=0, n=2048):
            return BIG[:, g * 2048 + lo: g * 2048 + lo + n]
        def big_f32(c, lo=0, n=2048):
            return BIG[:, c * 4096: (c + 1) * 4096].bitcast(F32)[:, lo: lo + n]
        def wk_bf(g, lo=0, n=2048):
            return WK[:, g * 2048 + lo: g * 2048 + lo + n]
        def wk_f32(g, lo=0, n=1024):
            return WK[:, g * 2048: g * 2048 + 2 * (lo + n)].bitcast(F32)[:, lo: lo + n]
        def wkk(g, n=1):
            return [("W", g + i) for i in range(n)]
        def bgk(g, n=1):
            return [("B", g + i) for i in range(n)]
        def ps(g, lo=0, n=1024):
            return PS[:, g * 1024 + lo: g * 1024 + lo + n]
        psk = lambda g: ("PS", g)
        def cst(l, name, j=0):
            o = l * NCST + _c[name] + j
            return CST[:, o: o + 1]
        def cx(l, o):
            return CX[:, l * 64 + o: l * 64 + o + 1]
        kf = lambda j: KF[:, j: j + 1]

        state = {"psg": 0, "slot": 0, "da": [None, None]}
        def next_ps():
            g = state["psg"]
            state["psg"] = (g + 1) % 4
            return g

        def load_unit(l, key, n=1024):
            s = state["slot"]
            state["slot"] = (s + 1) % NSLOT
            u = UNITS[key]
            P.dma("pool", RING[:, s * 1024: s * 1024 + n], wu[l, u, :, 0:n], reads=[], writes=[("R", s)], semkey=("R", s))
            return s
        def wt(s, k, n=128, width=128):
            return RING[:, s * 1024 + k * width: s * 1024 + k * width + n]

        def act(out, in_, func, bias=None, scale=1.0, reads=(), writes=()):
            def fn(e):
                kw = {}
                if bias is not None:
                    kw["bias"] = bias
                return e.activation(out=out, in_=in_, func=func, scale=scale, **kw)
            P.op("act", fn, reads, writes)
        def tt(eng, out, in0, in1, op, reads=(), writes=()):
            P.op(eng, lambda e: e.tensor_tensor(out=out, in0=in0, in1=in1, op=op), reads, writes)
        def ts(eng, out, in0, s1, s2, op0, op1=None, reads=(), writes=()):
            if op1 is None:
                P.op(eng, lambda e: e.tensor_scalar(out=out, in0=in0, scalar1=s1, scalar2=None, op0=op0), reads, writes)
            else:
                P.op(eng, lambda e: e.tensor_scalar(out=out, in0=in0, scalar1=s1, scalar2=s2, op0=op0, op1=op1), reads, writes)
        def stt(eng, out, in0, scalar, in1, op0, op1, reads=(), writes=()):
            P.op(eng, lambda e: e.scalar_tensor_tensor(out=out, in0=in0, scalar=scalar, in1=in1, op0=op0, op1=op1), reads, writes)
        def cp(eng, out, in_, reads=(), writes=()):
            P.op(eng, lambda e: e.tensor_copy(out=out, in_=in_), reads, writes)
        def mm(out, lhsT, rhs, start, stop, reads=(), writes=(), signal=False):
            P.op("pe", lambda e: e.matmul(out, lhsT, rhs, start=start, stop=stop), reads, writes, signal=signal)
        def tr(out, in_, ident, reads=(), writes=(), signal=False):
            P.op("pe", lambda e: e.transpose(out, in_, ident), reads, writes, signal=signal)
        def memset(eng, ap, val, writes=()):
            P.op(eng, lambda e: e.memset(ap, val), (), writes)

        memset("dve", KF[:, 0:1], 1.0, [("KF",)])
        memset("dve", KF[:, 1:2], EPS, [("KF",)])
        memset("dve", KF[:, 2:3], 0.0, [("KF",)])
        memset("dve", KF[:, 3:4], -0.5, [("KF",)])
        memset("dve", KF[:, 4:5], 0.5, [("KF",)])
        memset("dve", ONE[:], 1.0, [("ONE",)])
        memset("dve", ONESD[:, 0:128], 1.0 / 1024.0, [("ONESD",)])
        memset("dve", ONESD[:, 128:256], 1.0 / 512.0, [("ONESD",)])
        memset("dve", ONESB[:, 0:128], 1.0 / 1024.0, [("ONESD",)])
        memset("dve", ONESB[:, 128:256], 1.0 / 512.0, [("ONESD",)])
        P.dma("sp", IDF[:], ident_d, [], [("IDF",)], ("IDF",))
        cp("dve", IDB[:], IDF[:], [("IDF",)], [("IDB",)])
        P.dma("sp", CST[:].rearrange("p (l n) -> p l n", l=L), cst_d.rearrange("l p n -> p l n"), [], [("CST",)], ("CST",))
        for l in range(L):
            P.dma("sp", ROWP[32 * l: 32 * l + 1, :], row_d[l, :, 1024:1536], [], [("ROWP",)], ("ROWP",))
        P.dma("sp", LNIN[:], lnin_d, [], [("LNIN",)], ("LNIN",))
        memset("dve", ONEB[:], 1.0, [("ONEB",)])
        for l in range(L):
            pr = slice(32 * l, 32 * l + 1)
            cp("dve", ROWPB[pr, 0:512], ROWP[pr, :], [("ROWP",)], [("ROWPB",)])
            tt("dve", ROWP[pr, :], ROWP[pr, :], ROWPB[pr, 0:512], ALU.subtract, reads=[("ROWP",), ("ROWPB",)], writes=[("ROWP",)])
            cp("dve", ROWPB[pr, 512:1024], ROWP[pr, :], [("ROWP",)], [("ROWPB",)])
        for l in range(L):
            lamv = CST[:, l * NCST + _c["lam"]: l * NCST + _c["lam"] + 16]
            c1 = CX[:, l * 64: l * 64 + 16]
            act(c1, lamv, AF.Exp, scale=-1.0, reads=[("CST",)], writes=[("CX",)])
            act(c1, c1, AF.Ln, bias=kf(0), reads=[("CX",), ("KF",)], writes=[("CX",)])
            ts("dve", c1, c1, -4.0, None, ALU.mult, reads=[("CX",)], writes=[("CX",)])
            ts("dve", CX[:, l * 64 + 48: l * 64 + 64], c1, 2.0, None, ALU.mult, reads=[("CX",)], writes=[("CX",)])
            for nm, o in (("rba", 16), ("rbx", 32)):
                src = CST[:, l * NCST + _c[nm]: l * NCST + _c[nm] + 16]
                ts("dve", CX[:, l * 64 + o: l * 64 + o + 16], src, 0.5, None, ALU.mult, reads=[("CST",)], writes=[("CX",)])
            P.dma("pool", SGWT[:, l * 512:(l + 1) * 512].rearrange("q (g p) -> q g p", g=4),
                  sgwt_d[l].rearrange("g q p -> q g p"), [], [("SGWT", l)], ("SGWT", l))
            SGWF = wk_f32(2, 0, 512)
            RSR = wk_f32(1, 0, 128)[0:1, :]
            ROWT = wk_f32(0, 0, 1024)[0:1, :]
            P.dma("sp", SGWF.rearrange("q (g p) -> q g p", g=4), sgwt_d[l].rearrange("g q p -> q g p"),
                  [], wkk(2), ("SGWF",))
            P.dma("sp", ROWT, row_d[l, :, 0:1024], [], wkk(0), ("ROWT",))
            for g in range(4):
                pg = next_ps()
                mm(ps(pg, 0, 128)[0:1, :], ONE[:, 0:1], SGWF[:, g * 128:(g + 1) * 128], True, True,
                   reads=[("ONE",)] + wkk(2), writes=[psk(pg)], signal=True)
                cp("dve", RSR, ps(pg, 0, 128)[0:1, :], [psk(pg)], wkk(1))
                pg2 = next_ps()
                mm(ps(pg2, 0, 128), ROWT[:, g * 128:(g + 1) * 128], RSR, True, False,
                   reads=wkk(0) + wkk(1), writes=[psk(pg2)])
                mm(ps(pg2, 0, 128), ONE[0:1, :], ROWT[:, 512 + g * 128: 512 + (g + 1) * 128], False, True,
                   reads=wkk(0) + [("ONE",)], writes=[psk(pg2)], signal=True)
                cp("dve", BIASM[:, l * 512 + g * 128: l * 512 + (g + 1) * 128], ps(pg2, 0, 128), [psk(pg2)], [("BIASM", l)])
                ts("dve", GMAT[:, l * 512 + g * 128: l * 512 + (g + 1) * 128], ONE[:], cst(l, "sglg", g), None, ALU.mult,
                   reads=[("ONE",), ("CST",)], writes=[("GMAT", l)])

        def ln_stats(c, half, nch, onescol, zc, zkeys, tg, pm, pq):
            t0, t1 = tg[2], tg[3]
            zb = tg[4:6] if len(tg) >= 6 else None
            lo = half * 1024
            tq = (t0, t1)[c % 2]
            act(wk_bf(tq, 0, 1024), zc(c, lo, 1024), AF.Square, reads=zkeys(c, half), writes=wkk(tq))
            if zb is not None:
                tz = zb[c % 2]
                cp("dve", wk_bf(tz, 0, 1024), zc(c, lo, 1024), zkeys(c, half), wkk(tz))
                for t2 in range(2):
                    mm(ps(pm, t2 * 512, 512), ONESB[:, onescol:onescol + 128], wk_bf(tz, t2 * 512, 512), c == 0, c == nch - 1,
                       reads=wkk(tz) + [("ONESD",)], writes=[psk(pm)], signal=(c == nch - 1 and t2 == 1))
            else:
                for t2 in range(2):
                    mm(ps(pm, t2 * 512, 512), ONESD[:, onescol:onescol + 128], zc(c, lo + t2 * 512, 512), c == 0, c == nch - 1,
                       reads=zkeys(c, half) + [("ONESD",)], writes=[psk(pm)], signal=(c == nch - 1 and t2 == 1))
            for t2 in range(2):
                mm(ps(pq, t2 * 512, 512), ONESB[:, onescol:onescol + 128], wk_bf(tq, t2 * 512, 512), c == 0, c == nch - 1,
                   reads=wkk(tq) + [("ONESD",)], writes=[psk(pq)], signal=(t2 == 1))

        def ln_finish(half, nch, zc, zkeys, cb, tg, pm, pq):
            gMR, gRS, t0, t1 = tg[0], tg[1], tg[2], tg[3]
            lo = half * 1024
            act(wk_f32(gMR), ps(pm), AF.Identity, reads=[psk(pm)], writes=wkk(gMR))
            tt("dve", wk_f32(t0), wk_f32(gMR), wk_f32(gMR), ALU.mult, reads=wkk(gMR), writes=wkk(t0))
            tt("dve", wk_f32(t0), ps(pq), wk_f32(t0), ALU.subtract, reads=[psk(pq)] + wkk(t0), writes=wkk(t0))
            act(wk_f32(t0), wk_f32(t0), AF.Ln, bias=kf(1), reads=wkk(t0) + [("KF",)], writes=wkk(t0))
            act(wk_f32(gRS), wk_f32(t0), AF.Exp, scale=-0.5, reads=wkk(t0), writes=wkk(gRS))
            tt("dve", wk_f32(gMR), wk_f32(gMR), wk_f32(gRS), ALU.mult, reads=wkk(gMR) + wkk(gRS), writes=wkk(gMR))
            for c in range(nch):
                tq = (t0, t1)[c % 2]
                tt("dve", wk_f32(tq), zc(c, lo, 1024), wk_f32(gRS), ALU.mult, reads=zkeys(c, half) + wkk(gRS), writes=wkk(tq))
                tt("dve", wk_f32(tq), wk_f32(tq), wk_f32(gMR), ALU.subtract, reads=wkk(tq) + wkk(gMR), writes=wkk(tq))
                cb(c, half, wk_f32(tq), wkk(tq))

        def layer_norm(nch, onescol, zc, zkeys, cb, tg):
            for half in range(2):
                pm, pq = next_ps(), next_ps()
                for c in range(nch):
                    ln_stats(c, half, nch, onescol, zc, zkeys, tg, pm, pq)
                ln_finish(half, nch, zc, zkeys, cb, tg, pm, pq)

        def store_x(s, c, half, y, ykeys, par=None):
            lo = half * 1024
            act(xh(c, lo, 1024), y, AF.Identity, reads=ykeys, writes=[("XH", c, half)])
            P.dma("sp", xd[s * 8 + c, :, lo: lo + 1024], y, ykeys, [("XD", s, c, half)], ("XDW", c % 4))
            if par is not None:
                P.dma("sp", xhd[(par * NSEG + s) * 8 + c, :, lo: lo + 1024], xh(c, lo, 1024), [("XH", c, half)],
                      [("XHD", par, s, c, half)], ("XHDW", c % 2, half))

        XH3 = XH[:].rearrange("p (c w) -> p c w", c=8)
        def load_xh(par, s):
            for c in range(8):
                P.dma("sp", xh(c, 0, T), xhd[(par * NSEG + s) * 8 + c, :, :], [("XHD", par, s, c, 0), ("XHD", par, s, c, 1)],
                      [("XH", c, 0), ("XH", c, 1)], ("XHL", c % 4))
            sl, sr = max(s - 1, 0), min(s + 1, NSEG - 1)
            bl, br = (par * NSEG + sl) * 8, (par * NSEG + sr) * 8
            P.dma("sp", XH3[:, :, T: T + HALO], xhd[bl: bl + 8, :, T - HALO: T].rearrange("c p t -> p c t"),
                  [("XHD", par, sl, c, 1) for c in range(8)], [("XHH",)], ("XHH", 0))
            P.dma("sp", XH3[:, :, T + HALO: T + 2 * HALO], xhd[br: br + 8, :, 0: HALO].rearrange("c p t -> p c t"),
                  [("XHD", par, sr, c, 0) for c in range(8)], [("XHH",)], ("XHH", 1))
        lkL = lambda s: LINK[:, s: s + 1]
        lkR = lambda s: LINK[:, 8 + s: 9 + s]
        P.dma("sp", LINK[:], link_d, [], [("LINK",)], ("LINK",))

        gXA, gXAB, gR, gI, gA, gH1, gH2 = 0, 2, 3, 5, 7, 9, 11

        def rg_head(l, s, h, mode):
            sx = load_unit(l, ("x", h))
            sg_ = load_unit(l, ("g", h)) if mode == "B" else None
            sgt = load_unit(l, ("gt", h), 512)
            if mode == "B":
                P.dma("sp", wk_f32(gH2, 0, 2048), hbd[s * 8 + h, :, :], [("HBD", s, h)], wkk(gH2, 2), ("HBR",))
            def build_da(l_, h_, buf):
                for k in range(4):
                    ts("dve", DA[:, buf * 512 + k * 128: buf * 512 + (k + 1) * 128], IDB[:], cst(l_, "caw", k * 8 + h_), None, ALU.mult,
                       reads=[("IDB",), ("CST",)], writes=[("DA", buf)])
                state["da"][buf] = (l_, h_)
            if (l, h) not in state["da"]:
                build_da(l, h, 0 if state["da"][0] != (l, (h - 1) % 8) else 1)
            dab = state["da"].index((l, h))
            build_da(l, (h + 1) % 8, 1 - dab)
            bx = cst(l, "bin", OFF_RG_X // 128 + h)
            for half in range(2):
                pg = next_ps()
                for t2 in range(2):
                    for k in range(8):
                        mm(ps(pg, t2 * 512, 512), wt(sx, k), xh(k, half * 1024 + t2 * 512, 512), k == 0, k == 7,
                           reads=[("R", sx), ("XH", k, half)], writes=[psk(pg)], signal=(k == 7 and t2 == 1))
                act(PAD[:, 2 + half * 1024: 2 + half * 1024 + 1024], ps(pg), AF.Identity, bias=bx,
                    reads=[psk(pg), ("CST",)], writes=[("PAD",)])
            pg = next_ps()
            for k in range(8):
                mm(ps(pg, 0, 32), wt(sx, k), xh(k, T, 32), k == 0, k == 7, reads=[("R", sx), ("XHH",)], writes=[psk(pg)], signal=(k == 7))
            act(HT[:, 0:32], ps(pg, 0, 32), AF.Identity, bias=bx, reads=[psk(pg), ("CST",)], writes=[("HT",)])
            if "b" in KOPT:
                act(PAD[:, 0:2], HT[:, 14:16], AF.Identity, scale=lkL(s), reads=[("HT",), ("LINK",)], writes=[("PAD",)])
                act(PAD[:, 2 + T: 3 + T], HT[:, 16:17], AF.Identity, scale=lkR(s), reads=[("HT",), ("LINK",)], writes=[("PAD",)])
            else:
                ts("dve", PAD[:, 0:2], HT[:, 14:16], lkL(s), None, ALU.mult, reads=[("HT",), ("LINK",)], writes=[("PAD",)])
                ts("dve", PAD[:, 2 + T: 3 + T], HT[:, 16:17], lkR(s), None, ALU.mult, reads=[("HT",), ("LINK",)], writes=[("PAD",)])
            for half in range(2):
                pg = next_ps()
                for t2 in range(2):
                    for k in range(4):
                        o = k + half * 1024 + t2 * 512
                        mm(ps(pg, t2 * 512, 512), DA[:, dab * 512 + k * 128: dab * 512 + (k + 1) * 128], PAD[:, o: o + 512],
                           k == 0, k == 3, reads=[("DA", dab), ("PAD",)], writes=[psk(pg)], signal=(k == 3 and t2 == 1))
                if "c" in KOPT:
                    act(wk_bf(gXAB, half * 1024, 1024), ps(pg), AF.Identity, bias=cst(l, "cab", h), reads=[psk(pg), ("CST",)], writes=wkk(gXAB))
                    ts("dve", wk_f32(gXA + half), ps(pg), cst(l, "cab", h), None, ALU.add, reads=[psk(pg), ("CST",)], writes=wkk(gXA + half))
                else:
                    act(wk_f32(gXA + half), ps(pg), AF.Identity, bias=cst(l, "cab", h), reads=[psk(pg), ("CST",)], writes=wkk(gXA + half))
                    act(wk_bf(gXAB, half * 1024, 1024), wk_f32(gXA + half), AF.Identity, reads=wkk(gXA + half), writes=wkk(gXAB))
            d = 1 if mode == "A" else 0
            for gi, (gdst, hb_o) in enumerate(((gR, 16), (gI, 32))):
                for half in range(2):
                    pg = next_ps()
                    for t2 in range(2):
                        mm(ps(pg, t2 * 512, 512), wt(sgt, d * 2 + gi), wk_bf(gXAB, half * 1024 + t2 * 512, 512), True, True,
                           reads=[("R", sgt)] + wkk(gXAB), writes=[psk(pg)], signal=(t2 == 1))
                    act(wk_f32(gdst + half), ps(pg), AF.Tanh, bias=cx(l, hb_o + d * 8 + h), scale=0.5,
                        reads=[psk(pg), ("CX",)], writes=wkk(gdst + half))
            c1ap = cx(l, d * 8 + h)
            c2ap = cx(l, 48 + d * 8 + h)
            order = (1, 0) if mode == "A" else (0, 1)
            for half in order:
                act(wk_f32(gA + half), wk_f32(gR + half), AF.Exp, bias=c1ap, scale=c1ap, reads=wkk(gR + half) + [("CX",)], writes=wkk(gA + half))
                stt("dve", wk_f32(gR + half), wk_f32(gA + half), 0.999998, wk_f32(gA + half), ALU.min, ALU.mult, reads=wkk(gA + half), writes=wkk(gR + half))
            for half in order:
                stt("dve", wk_f32(gI + half), wk_f32(gI + half), 1.0, wk_f32(gXA + half), ALU.add, ALU.mult,
                    reads=wkk(gI + half) + wkk(gXA + half), writes=wkk(gI + half))
            for half in order:
                act(wk_f32(gR + half), wk_f32(gR + half), AF.Sqrt, bias=kf(0), scale=-1.0, reads=wkk(gR + half) + [("KF",)], writes=wkk(gR + half))
                stt("dve", wk_f32(gI + half), wk_f32(gI + half), 0.5, wk_f32(gR + half), ALU.mult, ALU.mult,
                    reads=wkk(gI + half) + wkk(gR + half), writes=wkk(gI + half))
            if mode == "A":
                gH = gH1 if h % 2 == 0 else gH2
                H_ = wk_f32(gH, 0, 2048)
                A_, I_ = wk_f32(gA, 0, 2048), wk_f32(gI, 0, 2048)
                ini = CB[:, h: h + 1]
                P.op("dve", lambda e: e.tensor_tensor_scan(out=H_[:, 2047:1023:-1], data0=A_[:, 2047:1023:-1], data1=I_[:, 2047:1023:-1], initial=ini, op0=ALU.mult, op1=ALU.add),
                     wkk(gA + 1) + wkk(gI + 1) + [("CB",)], wkk(gH + 1))
                ini2 = H_[:, 1024:1025]
                P.op("dve", lambda e: e.tensor_tensor_scan(out=H_[:, 1023::-1], data0=A_[:, 1023::-1], data1=I_[:, 1023::-1], initial=ini2, op0=ALU.mult, op1=ALU.add),
                     wkk(gA) + wkk(gI) + wkk(gH + 1), wkk(gH))
                P.dma("sp", hbd[s * 8 + h, :, :], H_, wkk(gH, 2), [("HBD", s, h)], ("HBW", h % 2))
                ts("dve", CB[:, h: h + 1], H_[:, 0:1], lkL(s), None, ALU.mult, reads=wkk(gH) + [("LINK",)], writes=[("CB",)])
                return
            H_ = wk_f32(gH1, 0, 2048)
            A_, I_ = wk_f32(gA, 0, 2048), wk_f32(gI, 0, 2048)
            ini = CF[:, h: h + 1]
            P.op("dve", lambda e: e.tensor_tensor_scan(out=H_[:, 0:1024], data0=A_[:, 0:1024], data1=I_[:, 0:1024], initial=ini, op0=ALU.mult, op1=ALU.add),
                 wkk(gA) + wkk(gI) + [("CF",)], wkk(gH1))
            ini2 = H_[:, 1023:1024]
            P.op("dve", lambda e: e.tensor_tensor_scan(out=H_[:, 1024:2048], data0=A_[:, 1024:2048], data1=I_[:, 1024:2048], initial=ini2, op0=ALU.mult, op1=ALU.add),
                 wkk(gA + 1) + wkk(gI + 1) + wkk(gH1), wkk(gH1 + 1))
            ts("dve", CF[:, h: h + 1], H_[:, T - 1: T], lkR(s), None, ALU.mult, reads=wkk(gH1 + 1) + [("LINK",)], writes=[("CF",)])
            for half in range(2):
                tt("dve", wk_f32(gH1 + half), wk_f32(gH1 + half), wk_f32(gH2 + half), ALU.add, reads=wkk(gH1 + half) + wkk(gH2 + half), writes=wkk(gH1 + half))
            for half in range(2):
                pg = next_ps()
                for t2 in range(2):
                    for k in range(8):
                        mm(ps(pg, t2 * 512, 512), wt(sg_, k), xh(k, half * 1024 + t2 * 512, 512), k == 0, k == 7,
                           reads=[("R", sg_), ("XH", k, half)], writes=[psk(pg)], signal=(k == 7 and t2 == 1))
                act(wk_f32(gR + half), ps(pg), AF.Gelu_apprx_tanh, bias=cst(l, "bin", OFF_RG_G // 128 + h),
                    reads=[psk(pg), ("CST",)], writes=wkk(gR + half))
                tt("dve", big_bf(h, half * 1024, 1024), wk_f32(gH1 + half), wk_f32(gR + half), ALU.mult,
                   reads=wkk(gH1 + half) + wkk(gR + half), writes=bgk(h))

        out_deps = []
        for si in range(NSEG):
            for half in range(2):
                for tb in range(8):
                    g = tb
                    tok0 = half * 1024 + tb * 128
                    xt = wk_f32(g)
                    P.dma("sp", xt, xin[si, tok0: tok0 + 128, :], [], wkk(g), ("XT", tb))
                    P.op("dve", lambda e, xt=xt: e.bn_stats(out=SM[:, 0:6], in_=xt[:, 0:512]), wkk(g), [("SM",)])
                    P.op("dve", lambda e, xt=xt: e.bn_stats(out=SM[:, 6:12], in_=xt[:, 512:1024]), wkk(g), [("SM",)])
                    P.op("dve", lambda e: e.bn_aggr(out=SM[:, 12:14], in_=SM[:, 0:12]), [("SM",)], [("SM",)])
                    act(SM[:, 14:15], SM[:, 13:14], AF.Ln, bias=kf(1), reads=[("SM",), ("KF",)], writes=[("SM",)])
                    act(SM[:, 14:15], SM[:, 14:15], AF.Exp, scale=-0.5, reads=[("SM",)], writes=[("SM",)])
                    ts("dve", xt, xt, SM[:, 12:13], SM[:, 14:15], ALU.subtract, ALU.mult, reads=wkk(g) + [("SM",)], writes=wkk(g))
                for c in range(8):
                    pg = next_ps()
                    for tb in range(8):
                        tr(ps(pg, tb * 128, 128), wk_f32(tb, c * 128, 128), IDF[:], reads=wkk(tb) + [("IDF",)], writes=[psk(pg)],
                           signal=(tb == 7))
                    yg = 8 + (c % 4)
                    act(wk_f32(yg), ps(pg), AF.Identity, bias=LNIN[:, 8 + c: 9 + c], scale=LNIN[:, c: c + 1],
                        reads=[psk(pg), ("LNIN",)], writes=wkk(yg))
                    store_x(si, c, half, wk_f32(yg), wkk(yg), par=0)

        for l in range(L):
            last = (l == L - 1)
            par = l % 2
            memset("dve", CB[:], 0.0, [("CB",)])
            for si in reversed(range(NSEG)):
                load_xh(par, si)
                for h in range(8):
                    rg_head(l, si, h, "A")
            memset("dve", CF[:], 0.0, [("CF",)])
            for si in range(NSEG):
                load_xh(par, si)
                for h in range(8):
                    rg_head(l, si, h, "B")

                for ch in range(4):
                    su = load_unit(l, ("u", ch))
                    for half in range(2):
                        pg = next_ps()
                        for t2 in range(2):
                            for k in range(8):
                                mm(ps(pg, t2 * 512, 512), wt(su, k), xh(k, half * 1024 + t2 * 512, 512), k == 0, k == 7,
                                   reads=[("R", su), ("XH", k, half)], writes=[psk(pg)], signal=(k == 7 and t2 == 1))
                        act(big_bf(8 + ch, half * 1024, 1024), ps(pg), AF.Gelu_apprx_tanh, bias=cst(l, "bin", OFF_SG // 128 + ch),
                            reads=[psk(pg), ("CST",)], writes=bgk(8 + ch))
                sv = [load_unit(l, ("v", j)) for j in range(4)]
                SGO3 = BIG[:, 8 * 2048: 12 * 2048].rearrange("p (g t) -> p g t", g=4)
                for half in range(2):
                    for i in range(8):
                        tb = half * 8 + i
                        pg = next_ps()
                        for j in range(4):
                            for k in range(8):
                                mm(ps(pg, j * 128, 128), xh(k, tb * 128, 128), wt(sv[j], k), k == 0, False,
                                   reads=[("R", sv[j]), ("XH", k, half)], writes=[psk(pg)])
                            mm(ps(pg, j * 128, 128), ONEB[32 * l: 32 * l + 1, :], ROWPB[32 * l: 32 * l + 1, j * 128:(j + 1) * 128], False, False,
                               reads=[("ROWPB",)], writes=[psk(pg)])
                            mm(ps(pg, j * 128, 128), ONEB[32 * l: 32 * l + 1, :], ROWPB[32 * l: 32 * l + 1, 512 + j * 128: 512 + (j + 1) * 128], False, True,
                               reads=[("ROWPB",)], writes=[psk(pg)], signal=(j == 3))
                        V = wk_f32(i, 0, 512)
                        act(V, ps(pg, 0, 512), AF.Gelu_apprx_tanh, reads=[psk(pg)], writes=wkk(i))
                        P.op("dve", lambda e, V=V: e.bn_stats(out=SM[:, 16:22], in_=V), wkk(i), [("SM2",)])
                        P.op("dve", lambda e, i=i: e.bn_aggr(out=SM[:, 32 + 2 * i: 34 + 2 * i], in_=SM[:, 16:22]), [("SM2",)], [("SM2",), ("SM3",)])
                    var8 = SM[:, 32:48].rearrange("p (i t) -> p i t", t=2)[:, :, 1:2]
                    rs8 = SM[:, 48:56].rearrange("p (i t) -> p i t", t=1)
                    act(rs8, var8, AF.Ln, bias=kf(1), reads=[("SM3",), ("KF",)], writes=[("SM4",)])
                    act(SM[:, 48:56], SM[:, 48:56], AF.Exp, scale=-0.5, reads=[("SM4",)], writes=[("SM4",)])
                    for i in range(8):
                        tb = half * 8 + i
                        V = wk_f32(i, 0, 512)
                        gvn = 8 + i % 2
                        VN = wk_bf(gvn, 0, 512)
                        ts("dve", VN, V, SM[:, 32 + 2 * i: 33 + 2 * i], SM[:, 48 + i: 49 + i], ALU.subtract, ALU.mult,
                           reads=wkk(i) + [("SM3",), ("SM4",)], writes=wkk(gvn))
                        pg2 = next_ps()
                        for g in range(4):
                            mm(ps(pg2, g * 128, 128), VN[:, g * 128:(g + 1) * 128], SGWT[:, l * 512 + g * 128: l * 512 + (g + 1) * 128], True, True,
                               reads=wkk(gvn) + [("SGWT", l)], writes=[psk(pg2)], signal=(g == 3))
                        gt_ = 10 + i % 2
                        TT = wk_f32(gt_, 0, 512)
                        tt("dve", TT, ps(pg2, 0, 512), GMAT[:, l * 512:(l + 1) * 512], ALU.mult, reads=[psk(pg2), ("GMAT", l)], writes=wkk(gt_))
                        tt("dve", TT, TT, BIASM[:, l * 512:(l + 1) * 512], ALU.add, reads=wkk(gt_) + [("BIASM", l)], writes=wkk(gt_))
                        uview = SGO3[:, :, tb * 128:(tb + 1) * 128]
                        tt("dve", uview, TT.rearrange("p (g t) -> p g t", g=4), uview, ALU.mult, reads=wkk(gt_) + bgk(8, 4), writes=bgk(8, 4))

                for ch in range(4):
                    sc = load_unit(l, ("c", ch))
                    scg = load_unit(l, ("cg", ch))
                    for k in range(31):
                        ts("dve", DC[:, k * 128:(k + 1) * 128], IDB[:], cst(l, "ccw", k * 4 + ch), None, ALU.mult,
                           reads=[("IDB",), ("CST",)], writes=[("DC",)])
                    bc_, bcg_ = cst(l, "bin", OFF_CC // 128 + ch), cst(l, "bin", OFF_CC // 128 + 4 + ch)
                    for half in range(2):
                        pc, pgg = next_ps(), next_ps()
                        for (pp, ss) in ((pc, sc), (pgg, scg)):
                            for t2 in range(2):
                                for k in range(8):
                                    mm(ps(pp, t2 * 512, 512), wt(ss, k), xh(k, half * 1024 + t2 * 512, 512), k == 0, k == 7,
                                       reads=[("R", ss), ("XH", k, half)], writes=[psk(pp)], signal=(k == 7 and t2 == 1))
                        gs = 8 + half
                        act(wk_f32(gs), ps(pgg), AF.Sigmoid, bias=bcg_, reads=[psk(pgg), ("CST",)], writes=wkk(gs))
                        stt("dve", PAD[:, 15 + half * 1024: 15 + half * 1024 + 1024], ps(pc), bc_, wk_f32(gs),
                            ALU.add, ALU.mult, reads=[psk(pc), ("CST",)] + wkk(gs), writes=[("PAD",)])
                    pc, pgg = next_ps(), next_ps()
                    for (pp, ss) in ((pc, sc), (pgg, scg)):
                        for k in range(8):
                            mm(ps(pp, 0, 32), wt(ss, k), xh(k, T, 32), k == 0, k == 7, reads=[("R", ss), ("XHH",)], writes=[psk(pp)], signal=(k == 7))
                    act(HT[:, 32:64], ps(pgg, 0, 32), AF.Sigmoid, bias=bcg_, reads=[psk(pgg), ("CST",)], writes=[("HT",)])
                    stt("dve", HT[:, 0:32], ps(pc, 0, 32), bc_, HT[:, 32:64], ALU.add, ALU.mult, reads=[psk(pc), ("CST",), ("HT",)], writes=[("HT",)])
                    ts("dve", PAD[:, 0:15], HT[:, 1:16], lkL(si), None, ALU.mult, reads=[("HT",), ("LINK",)], writes=[("PAD",)])
                    ts("dve", PAD[:, 15 + T: 30 + T], HT[:, 16:31], lkR(si), None, ALU.mult, reads=[("HT",), ("LINK",)], writes=[("PAD",)])
                    for half in range(2):
                        pg = next_ps()
                        for t2 in range(2):
                            for k in range(31):
                                o = k + half * 1024 + t2 * 512
                                mm(ps(pg, t2 * 512, 512), DC[:, k * 128:(k + 1) * 128], PAD[:, o: o + 512], k == 0, k == 30,
                                   reads=[("DC",), ("PAD",)], writes=[psk(pg)], signal=(k == 30 and t2 == 1))
                        act(wk_f32(2 * ch + half), ps(pg), AF.Identity, bias=cst(l, "ccb", ch), reads=[psk(pg), ("CST",)], writes=wkk(2 * ch + half))
                def cc_cb(c, half, Tap, Tk):
                    act(big_bf(12 + c, half * 1024, 1024), Tap, AF.Silu, bias=cst(l, "cclb", c), scale=cst(l, "cclg", c),
                        reads=Tk + [("CST",)], writes=bgk(12 + c))
                layer_norm(4, 128, lambda c, lo, n: wk_f32(2 * c, lo, n), lambda c, half: wkk(2 * c + half), cc_cb, [8, 9, 10, 11])

                for oc in range(8):
                    sl = {nm: load_unit(l, (nm, oc), 1024 if nm in ("ga", "gb", "gc", "ba") else 512) for nm in ("ga", "gb", "gc", "ba", "bb", "bc")}
                    for half in range(2):
                        for bi, (gn, yn, nk, src0) in enumerate((("ga", "ba", 8, 0), ("gb", "bb", 4, 8), ("gc", "bc", 4, 12))):
                            pgt, py = next_ps(), next_ps()
                            for t2 in range(2):
                                for k in range(8):
                                    mm(ps(pgt, t2 * 512, 512), wt(sl[gn], k), xh(k, half * 1024 + t2 * 512, 512), k == 0, k == 7,
                                       reads=[("R", sl[gn]), ("XH", k, half)], writes=[psk(pgt)], signal=(k == 7 and t2 == 1))
                            for t2 in range(2):
                                for k in range(nk):
                                    mm(ps(py, t2 * 512, 512), wt(sl[yn], k), big_bf(src0 + k, half * 1024 + t2 * 512, 512), k == 0, k == nk - 1,
                                       reads=[("R", sl[yn])] + bgk(src0 + k), writes=[psk(py)], signal=(k == nk - 1 and t2 == 1))
                            gg = 8 + bi
                            act(wk_f32(gg), ps(pgt), AF.Sigmoid, bias=cst(l, "bin", OFF_GATE // 128 + bi * 8 + oc), reads=[psk(pgt), ("CST",)], writes=wkk(gg))
                            tt("dve", wk_f32(gg), ps(py), wk_f32(gg), ALU.mult, reads=[psk(py)] + wkk(gg), writes=wkk(gg))
                        tt("dve", wk_f32(8), wk_f32(8), wk_f32(9), ALU.add, reads=wkk(8) + wkk(9), writes=wkk(8))
                        tt("dve", wk_bf(oc, half * 1024, 1024), wk_f32(8), wk_f32(10), ALU.add, reads=wkk(8) + wkk(10), writes=wkk(oc))

                def resid(oc, half, pg, bname, zdst, zkeys_w):
                    gx = 9 + (2 * oc + half) % 2
                    ge = 11 + (2 * oc + half) % 2
                    P.dma("sp", wk_f32(gx), xd[si * 8 + oc, :, half * 1024: half * 1024 + 1024], [("XD", si, oc, half)], wkk(gx), ("XDR", gx))
                    act(wk_f32(ge), ps(pg), AF.Identity, bias=cst(l, bname, oc), reads=[psk(pg), ("CST",)], writes=wkk(ge))
                    stt("dve", zdst, wk_f32(gx), ALPHA, wk_f32(ge), ALU.mult, ALU.add, reads=wkk(gx) + wkk(ge), writes=zkeys_w)
                for oc in range(8):
                    so = load_unit(l, ("o", oc))
                    for half in range(2):
                        pg = next_ps()
                        for t2 in range(2):
                            for k in range(8):
                                mm(ps(pg, t2 * 512, 512), wt(so, k), wk_bf(k, half * 1024 + t2 * 512, 512), k == 0, k == 7,
                                   reads=[("R", so)] + wkk(k), writes=[psk(pg)], signal=(k == 7 and t2 == 1))
                        resid(oc, half, pg, "bo", big_f32(oc, half * 1024, 1024), bgk(2 * oc + half))

                def ln_x_cb(gname, bname, par_out):
                    def cb(c, half, Tap, Tk):
                        yg = 4 + (2 * c + half) % 4
                        act(wk_f32(yg), Tap, AF.Identity, bias=cst(l, bname, c), scale=cst(l, gname, c), reads=Tk + [("CST",)], writes=wkk(yg))
                        store_x(si, c, half, wk_f32(yg), wkk(yg), par=par_out)
                    return cb
                layer_norm(8, 0, lambda c, lo, n: big_f32(c, lo, n), lambda c, half: bgk(2 * c + half), ln_x_cb("l1g", "l1b", None), [0, 1, 2, 3, 8, 9])

                for J in range(4):
                    for jj in range(8):
                        j = J * 8 + jj
                        s1 = load_unit(l, ("f1", j))
                        for half in range(2):
                            pg = next_ps()
                            for t2 in range(2):
                                for k in range(8):
                                    mm(ps(pg, t2 * 512, 512), wt(s1, k), xh(k, half * 1024 + t2 * 512, 512), k == 0, k == 7,
                                       reads=[("R", s1), ("XH", k, half)], writes=[psk(pg)], signal=(k == 7 and t2 == 1))
                            gr = 8 + (2 * jj + half) % 2
                            act(wk_f32(gr), ps(pg), AF.Relu, bias=cst(l, "bf1", j), reads=[psk(pg), ("CST",)], writes=wkk(gr))
                            tt("dve", wk_bf(jj, half * 1024, 1024), wk_f32(gr), wk_f32(gr), ALU.mult, reads=wkk(gr), writes=wkk(jj))
                    s2 = [load_unit(l, ("f2", J * 8 + jj)) for jj in range(8)]
                    for oc in range(8):
                        for half in range(2):
                            pg = next_ps()
                            for t2 in range(2):
                                for jj in range(8):
                                    mm(ps(pg, t2 * 512, 512), wt(s2[jj], oc), wk_bf(jj, half * 1024 + t2 * 512, 512), jj == 0, jj == 7,
                                       reads=[("R", s2[jj])] + wkk(jj), writes=[psk(pg)], signal=(jj == 7 and t2 == 1))
                            acc = big_f32(oc, half * 1024, 1024)
                            if J == 0:
                                cp("dve", acc, ps(pg), [psk(pg)], bgk(2 * oc + half))
                            else:
                                tt("dve", acc, ps(pg), acc, ALU.add, reads=[psk(pg)] + bgk(2 * oc + half), writes=bgk(2 * oc + half))
                for oc in range(8):
                    for half in range(2):
                        gx = 9 + (2 * oc + half) % 2
                        acc = big_f32(oc, half * 1024, 1024)
                        P.dma("sp", wk_f32(gx), xd[si * 8 + oc, :, half * 1024: half * 1024 + 1024], [("XD", si, oc, half)], wkk(gx), ("XDR", gx))
                        act(acc, acc, AF.Identity, bias=cst(l, "bf2", oc), reads=bgk(2 * oc + half) + [("CST",)], writes=bgk(2 * oc + half))
                        stt("dve", acc, wk_f32(gx), ALPHA, acc, ALU.mult, ALU.add, reads=wkk(gx) + bgk(2 * oc + half), writes=bgk(2 * oc + half))
                if not last:
                    layer_norm(8, 0, lambda c, lo, n: big_f32(c, lo, n), lambda c, half: bgk(2 * c + half), ln_x_cb("l2g", "l2b", 1 - par), [0, 1, 2, 3, 8, 9])
                else:
                    def fin_cb(c, half, Tap, Tk):
                        act(wk_f32(4 + c), Tap, AF.Identity, bias=cst(l, "l2b", c), scale=cst(l, "l2g", c), reads=Tk + [("CST",)], writes=wkk(4 + c))
                        if c == 7:
                            for tb in range(8):
                                pg = next_ps()
                                for cc in range(8):
                                    tr(ps(pg, cc * 128, 128), wk_f32(4 + cc, tb * 128, 128), IDF[:], reads=wkk(4 + cc) + [("IDF",)], writes=[psk(pg)],
                                       signal=(cc == 7))
                                go = 12 if tb % 2 == 0 else 3
                                if tb % 2 == 0:
                                    cp("dve", wk_f32(go), ps(pg), [psk(pg)], wkk(go))
                                else:
                                    act(wk_f32(go), ps(pg), AF.Identity, reads=[psk(pg)], writes=wkk(go))
                                tok0 = half * 1024 + tb * 128
                                dep = P.dma("sp", yout[si, tok0: tok0 + 128, :], wk_f32(go), wkk(go), [("YO", si, half, tb)], ("YO", tb % 2))
                                out_deps.append(dep)
                    layer_norm(8, 0, lambda c, lo, n: big_f32(c, lo, n), lambda c, half: bgk(2 * c + half), fin_cb, [0, 1, 2, 3])

        fin = {}
        for s, v in out_deps:
            fin[s] = max(fin.get(s, 0), v)
        P.final_wait("sp", list(fin.items()))
        with nc.Block() as block:
            P.emit(block)
    return nc


def _colchunk(W, j, nk):
    blk = W[:, j * 128:(j + 1) * 128].reshape(nk, 128, 128)
    out = np.zeros((128, 1024), np.float32)
    out[:, : nk * 128] = blk.transpose(1, 0, 2).reshape(128, nk * 128)
    return out


def _prep_weights(inp):
    wu = np.zeros((L, NU, 128, 1024), np.float32)
    cst = np.zeros((L, 128, NCST), np.float32)
    rows = np.zeros((L, 1, NROW), np.float32)
    for l in range(L):
        w_in = inp["w_in"][l]
        for h in range(8):
            wu[l, UNITS[("x", h)]] = _colchunk(w_in, OFF_RG_X // 128 + h, 8)
            wu[l, UNITS[("g", h)]] = _colchunk(w_in, OFF_RG_G // 128 + h, 8)
            gt = np.zeros((128, 1024), np.float32)
            gt[:, 0:128] = inp["rg_wa"][l, 0, h]
            gt[:, 128:256] = inp["rg_wx"][l, 0, h]
            gt[:, 256:384] = inp["rg_wa"][l, 1, h]
            gt[:, 384:512] = inp["rg_wx"][l, 1, h]
            wu[l, UNITS[("gt", h)]] = gt
        for ch in range(4):
            wu[l, UNITS[("u", ch)]] = _colchunk(w_in, OFF_SG // 128 + ch, 8)
            wu[l, UNITS[("v", ch)]] = _colchunk(w_in, OFF_SG // 128 + 4 + ch, 8)
            wu[l, UNITS[("c", ch)]] = _colchunk(w_in, OFF_CC // 128 + ch, 8)
            wu[l, UNITS[("cg", ch)]] = _colchunk(w_in, OFF_CC // 128 + 4 + ch, 8)
        for oc in range(8):
            wu[l, UNITS[("ga", oc)]] = _colchunk(w_in, OFF_GATE // 128 + oc, 8)
            wu[l, UNITS[("gb", oc)]] = _colchunk(w_in, OFF_GATE // 128 + 8 + oc, 8)
            wu[l, UNITS[("gc", oc)]] = _colchunk(w_in, OFF_GATE // 128 + 16 + oc, 8)
            wu[l, UNITS[("ba", oc)]] = _colchunk(inp["w_ba"][l], oc, 8)
            wu[l, UNITS[("bb", oc)]] = _colchunk(inp["w_bb"][l], oc, 4)
            wu[l, UNITS[("bc", oc)]] = _colchunk(inp["w_bc"][l], oc, 4)
            wu[l, UNITS[("o", oc)]] = _colchunk(inp["w_o"][l], oc, 8)
        for j in range(32):
            wu[l, UNITS[("f1", j)]] = _colchunk(inp["w_ff1"][l], j, 8)
            wu[l, UNITS[("f2", j)]] = inp["w_ff2"][l][j * 128:(j + 1) * 128, :]
        def put(name, vec, n):
            cst[l, :, _c[name]: _c[name] + n] = np.asarray(vec, np.float32).reshape(n, 128).T
        put("bin", inp["b_in"][l], 56)
        put("caw", inp["conv_a_w"][l].reshape(-1), 32)
        put("cab", inp["conv_a_b"][l], 8)
        put("rba", inp["rg_ba"][l].reshape(-1), 16)
        put("rbx", inp["rg_bx"][l].reshape(-1), 16)
        put("lam", inp["rg_lambda"][l].reshape(-1), 16)
        put("sglg", inp["sg_ln_g"][l], 4)
        put("ccw", inp["conv_c_w"][l].reshape(-1), 124)
        put("ccb", inp["conv_c_b"][l], 4)
        put("cclg", inp["cc_ln_g"][l], 4)
        put("cclb", inp["cc_ln_b"][l], 4)
        put("bo", inp["b_o"][l], 8)
        put("l1g", inp["ln1_g"][l], 8)
        put("l1b", inp["ln1_b"][l], 8)
        put("bf1", inp["b_ff1"][l], 32)
        put("bf2", inp["b_ff2"][l], 8)
        put("l2g", inp["ln2_g"][l], 8)
        put("l2b", inp["ln2_b"][l], 8)
        rows[l, 0, 0:512] = inp["sg_ln_b"][l]
        rows[l, 0, 512:1024] = inp["sg_b"][l].reshape(-1)
        rows[l, 0, 1024:1536] = inp["b_in"][l][OFF_SG + 512: OFF_SG + 1024]
    lnin = np.zeros((128, 16), np.float32)
    lnin[:, 0:8] = np.asarray(inp["ln_in_g"], np.float32).reshape(8, 128).T
    lnin[:, 8:16] = np.asarray(inp["ln_in_b"], np.float32).reshape(8, 128).T
    sgwt = np.ascontiguousarray(np.asarray(inp["sg_w"], np.float32).transpose(0, 1, 3, 2))
    return wu, cst, rows, lnin, sgwt


_NC_CACHE = {}
_SAMPLE_SLOTS = [(c, k) for c in range(4, 8) for k in range(4)]


def _core_inputs(inp):
    xp = inp["x_prompt"].astype(np.float32, copy=False)
    xs = inp["x_sample"].astype(np.float32, copy=False)
    xins = [np.zeros((NSEG, T, D), np.float32) for _ in range(NCORES)]
    links = [np.zeros((128, 16), np.float32) for _ in range(NCORES)]
    xins[0][:] = xp[0].reshape(NSEG, T, D)
    links[0][:, 1:8] = 1.0
    links[0][:, 8:15] = 1.0
    for i, (c, k) in enumerate(_SAMPLE_SLOTS):
        xins[c][k] = xs[i]
    return xins, links


def kernel(**inputs):
    inp = {k: np.asarray(v) for k, v in inputs.items()}
    wu, cst, rows, lnin, sgwt = _prep_weights(inp)
    xins, links = _core_inputs(inp)
    if "nc" not in _NC_CACHE:
        _NC_CACHE["nc"] = build(None)
    nc = _NC_CACHE["nc"]
    ident = np.eye(128, dtype=np.float32)
    in_maps = [{"xin": xins[c], "link": links[c], "wu": wu, "cst": cst, "rows": rows, "lnin": lnin, "sgwt": sgwt, "ident": ident}
               for c in range(NCORES)]
    res = run_bass_kernel_spmd(nc, in_maps, core_ids=list(range(NCORES)))
    y_prompt = np.ascontiguousarray(res.results[0]["yout"]).reshape(1, NSEG * T, D).astype(np.float32, copy=False)
    y_sample = np.zeros((16, T, D), np.float32)
    for i, (c, k) in enumerate(_SAMPLE_SLOTS):
        y_sample[i] = res.results[c]["yout"][k]
    return (y_prompt, y_sample)
```

```python
import numpy as np
import os
KOPT = {"b"}
import concourse.bass as bass
import concourse.mybir as mybir
from concourse.bass_utils import run_bass_kernel_spmd

F32 = mybir.dt.float32
BF16 = mybir.dt.bfloat16
AF = mybir.ActivationFunctionType
ALU = mybir.AluOpType

D = 1024
T = 2048
HALO = 16
NSEG = 8
L = 2
NCORES = 8
ALPHA = float((2 * L) ** 0.25)
EPS = 1e-5
OFF_RG_X, OFF_RG_G, OFF_SG, OFF_CC, OFF_GATE = 0, 1024, 2048, 3072, 4096
NSLOT = 8
NWK = 13

_c = {}
_o = 0
def _add(name, n):
    global _o
    _c[name] = _o
    _o += n
_add("bin", 56); _add("caw", 32); _add("cab", 8); _add("rba", 16); _add("rbx", 16); _add("lam", 16)
_add("sglg", 4); _add("ccw", 124); _add("ccb", 4); _add("cclg", 4); _add("cclb", 4)
_add("bo", 8); _add("l1g", 8); _add("l1b", 8); _add("bf1", 32); _add("bf2", 8); _add("l2g", 8); _add("l2b", 8)
NCST = _o
NROW = 1536

def _units():
    u = {}
    n = 0
    for h in range(8):
        u[("x", h)] = n; u[("g", h)] = n + 1; u[("gt", h)] = n + 2; n += 3
    for ch in range(4):
        u[("u", ch)] = n; n += 1
    for j in range(4):
        u[("v", j)] = n; n += 1
    for ch in range(4):
        u[("c", ch)] = n; u[("cg", ch)] = n + 1; n += 2
    for oc in range(8):
        for k, nm in enumerate(("ga", "gb", "gc", "ba", "bb", "bc")):
            u[(nm, oc)] = n + k
        n += 6
    for oc in range(8):
        u[("o", oc)] = n; n += 1
    for J in range(4):
        for jj in range(8):
            u[("f1", J * 8 + jj)] = n; n += 1
        for jj in range(8):
            u[("f2", J * 8 + jj)] = n; n += 1
    return u, n
UNITS, NU = _units()


class Prog:
    ENGS = ("pe", "act", "dve", "pool", "sp")

    def __init__(self, nc, es):
        self.nc = nc
        self.es = es
        self.streams = {e: [] for e in self.ENGS}
        self.cnt = {e: 0 for e in self.ENGS}
        self.known = {e: {} for e in self.ENGS}
        self.buf = {}
        self.sems = {}
        self.dcnt = {}
        for e in self.ENGS:
            self.sems[e] = es.enter_context(nc.semaphore("s_" + e))
        self.out_deps = []

    def dsem(self, key):
        if key not in self.sems:
            self.sems[key] = self.es.enter_context(self.nc.semaphore("d_%d" % len(self.sems)))
            self.dcnt[key] = 0
        return key

    def _deps(self, engine, reads, writes):
        deps = {}
        def add(d):
            if d is None:
                return
            s, v = d
            if deps.get(s, 0) < v:
                deps[s] = v
        for k in reads:
            st = self.buf.get(k)
            if st:
                add(st["w"])
        for k in writes:
            st = self.buf.get(k)
            if st:
                add(st["w"])
                for s, v in st["r"].items():
                    add((s, v))
        waits = []
        kn = self.known[engine]
        for s, v in deps.items():
            if engine == "pe" and s == "pe":
                continue
            if kn.get(s, 0) < v:
                waits.append((s, v))
                kn[s] = v
        return waits

    def _commit(self, dep, reads, writes):
        for k in writes:
            self.buf[k] = {"w": dep, "r": {}}
        for k in reads:
            st = self.buf.setdefault(k, {"w": None, "r": {}})
            if st["r"].get(dep[0], 0) < dep[1]:
                st["r"][dep[0]] = dep[1]

    def op(self, engine, fn, reads=(), writes=(), signal=True):
        waits = self._deps(engine, reads, writes)
        sems = self.sems
        if signal:
            self.cnt[engine] += 1
            dep = (engine, self.cnt[engine])
        else:
            dep = (engine, self.cnt[engine] + 1)
        semh = sems[engine]
        def thunk(eng, waits=waits, fn=fn, signal=signal):
            for s, v in waits:
                eng.wait_ge(sems[s], v)
            ins = fn(eng)
            if signal:
                ins.then_inc(semh, 1)
        self.streams[engine].append(thunk)
        self._commit(dep, reads, writes)

    def dma(self, engine, out, in_, reads, writes, semkey):
        self.dsem(semkey)
        waits = self._deps(engine, reads, writes)
        self.dcnt[semkey] += 16
        dep = (semkey, self.dcnt[semkey])
        sems = self.sems
        def thunk(eng, waits=waits):
            for s, v in waits:
                eng.wait_ge(sems[s], v)
            eng.dma_start(out=out, in_=in_).then_inc(sems[semkey], 16)
        self.streams[engine].append(thunk)
        self._commit(dep, reads, writes)
        return dep

    def final_wait(self, engine, deps):
        sems = self.sems
        def thunk(eng):
            for s, v in deps:
                eng.wait_ge(sems[s], v)
        self.streams[engine].append(thunk)

    def emit(self, block):
        P = self
        @block.tensor
        def _(e):
            for t in P.streams["pe"]:
                t(e)
        @block.scalar
        def _(e):
            for t in P.streams["act"]:
                t(e)
        @block.vector
        def _(e):
            for t in P.streams["dve"]:
                t(e)
        @block.gpsimd
        def _(e):
            for t in P.streams["pool"]:
                t(e)
        @block.sync
        def _(e):
            for t in P.streams["sp"]:
                t(e)


def build(seg_kinds, debug=False):
    import contextlib
    nc = bass.Bass("TRN2", target_bir_lowering=False)
    xin = nc.dram_tensor("xin", [NSEG, T, D], F32, kind="ExternalInput").ap()
    wu = nc.dram_tensor("wu", [L, NU, 128, 1024], F32, kind="ExternalInput").ap()
    cst_d = nc.dram_tensor("cst", [L, 128, NCST], F32, kind="ExternalInput").ap()
    row_d = nc.dram_tensor("rows", [L, 1, NROW], F32, kind="ExternalInput").ap()
    lnin_d = nc.dram_tensor("lnin", [128, 16], F32, kind="ExternalInput").ap()
    sgwt_d = nc.dram_tensor("sgwt", [L, 4, 128, 128], F32, kind="ExternalInput").ap()
    ident_d = nc.dram_tensor("ident", [128, 128], F32, kind="ExternalInput").ap()
    yout = nc.dram_tensor("yout", [NSEG, T, D], F32, kind="ExternalOutput").ap()
    link_d = nc.dram_tensor("link", [128, 16], F32, kind="ExternalInput").ap()
    xd = nc.dram_tensor("xd", [NSEG * 8, 128, T], F32, kind="Internal").ap()
    xhd = nc.dram_tensor("xhd", [2 * NSEG * 8, 128, T], BF16, kind="Internal").ap()
    hbd = nc.dram_tensor("hbd", [NSEG * 8, 128, T], F32, kind="Internal").ap()

    es = contextlib.ExitStack()
    with es:
        P = Prog(nc, es)
        sb = lambda name, shape, dt: es.enter_context(nc.sbuf_tensor(name, shape, dt))
        XH = sb("XH", [128, 8 * (T + 2 * HALO)], BF16)
        BIG = sb("BIG", [128, 32768], BF16)
        WK = sb("WK", [128, NWK * 2048], BF16)
        PAD = sb("PAD", [128, 2176], BF16)
        DC = sb("DC", [128, 31 * 128], BF16)
        DA = sb("DA", [128, 2 * 4 * 128], BF16)
        RING = sb("RING", [128, NSLOT * 1024], BF16)
        CST = sb("CST", [128, L * NCST], F32)
        CX = sb("CX", [128, L * 64], F32)
        ROWP = sb("ROWP", [33, 512], F32)
        LNIN = sb("LNIN", [128, 16], F32)
        SGWT = sb("SGWT", [128, L * 512], BF16)
        BIASM = sb("BIASM", [128, L * 512], F32)
        GMAT = sb("GMAT", [128, L * 512], F32)
        IDB = sb("IDB", [128, 128], BF16)
        IDF = sb("IDF", [128, 128], F32)
        ONESD = sb("ONESD", [128, 256], F32)
        ONE = sb("ONE", [128, 128], F32)
        ONESB = sb("ONESB", [128, 256], BF16)
        ONEB = sb("ONEB", [33, 128], BF16)
        ROWPB = sb("ROWPB", [33, 1024], BF16)
        KF = sb("KF", [128, 8], F32)
        SM = sb("SM", [128, 64], F32)
        LINK = sb("LINK", [128, 16], F32)
        CF = sb("CF", [128, 8], F32)
        CB = sb("CB", [128, 8], F32)
        HT = sb("HT", [128, 64], F32)
        PS = es.enter_context(nc.psum_tensor("PS", [128, 4096], F32))

        XW = T + 2 * HALO
        def xh(c, lo, n):
            return XH[:, c * XW + lo: c * XW + lo + n]
        def big_bf(g, lo=0, n=2048):
            return BIG[:, g * 2048 + lo: g * 2048 + lo + n]
        def big_f32(c, lo=0, n=2048):
            return BIG[:, c * 4096: (c + 1) * 4096].bitcast(F32)[:, lo: lo + n]
        def wk_bf(g, lo=0, n=2048):
            return WK[:, g * 2048 + lo: g * 2048 + lo + n]
        def wk_f32(g, lo=0, n=1024):
            return WK[:, g * 2048: g * 2048 + 2 * (lo + n)].bitcast(F32)[:, lo: lo + n]
        def wkk(g, n=1):
            return [("W", g + i) for i in range(n)]
        def bgk(g, n=1):
            return [("B", g + i) for i in range(n)]
        def ps(g, lo=0, n=1024):
            return PS[:, g * 1024 + lo: g * 1024 + lo + n]
        psk = lambda g: ("PS", g)
        def cst(l, name, j=0):
            o = l * NCST + _c[name] + j
            return CST[:, o: o + 1]
        def cx(l, o):
            return CX[:, l * 64 + o: l * 64 + o + 1]
        kf = lambda j: KF[:, j: j + 1]

        state = {"psg": 0, "slot": 0, "da": [None, None]}
        def next_ps():
            g = state["psg"]
            state["psg"] = (g + 1) % 4
            return g

        def load_unit(l, key, n=1024):
            s = state["slot"]
            state["slot"] = (s + 1) % NSLOT
            u = UNITS[key]
            P.dma("pool", RING[:, s * 1024: s * 1024 + n], wu[l, u, :, 0:n], reads=[], writes=[("R", s)], semkey=("R", s))
            return s
        def wt(s, k, n=128, width=128):
            return RING[:, s * 1024 + k * width: s * 1024 + k * width + n]

        def act(out, in_, func, bias=None, scale=1.0, reads=(), writes=()):
            def fn(e):
                kw = {}
                if bias is not None:
                    kw["bias"] = bias
                return e.activation(out=out, in_=in_, func=func, scale=scale, **kw)
            P.op("act", fn, reads, writes)
        def tt(eng, out, in0, in1, op, reads=(), writes=()):
            P.op(eng, lambda e: e.tensor_tensor(out=out, in0=in0, in1=in1, op=op), reads, writes)
        def ts(eng, out, in0, s1, s2, op0, op1=None, reads=(), writes=()):
            if op1 is None:
                P.op(eng, lambda e: e.tensor_scalar(out=out, in0=in0, scalar1=s1, scalar2=None, op0=op0), reads, writes)
            else:
                P.op(eng, lambda e: e.tensor_scalar(out=out, in0=in0, scalar1=s1, scalar2=s2, op0=op0, op1=op1), reads, writes)
        def stt(eng, out, in0, scalar, in1, op0, op1, reads=(), writes=()):
            P.op(eng, lambda e: e.scalar_tensor_tensor(out=out, in0=in0, scalar=scalar, in1=in1, op0=op0, op1=op1), reads, writes)
        def cp(eng, out, in_, reads=(), writes=()):
            P.op(eng, lambda e: e.tensor_copy(out=out, in_=in_), reads, writes)
        def mm(out, lhsT, rhs, start, stop, reads=(), writes=(), signal=False):
            P.op("pe", lambda e: e.matmul(out, lhsT, rhs, start=start, stop=stop), reads, writes, signal=signal)
        def tr(out, in_, ident, reads=(), writes=(), signal=False):
            P.op("pe", lambda e: e.transpose(out, in_, ident), reads, writes, signal=signal)
        def memset(eng, ap, val, writes=()):
            P.op(eng, lambda e: e.memset(ap, val), (), writes)

        memset("dve", KF[:, 0:1], 1.0, [("KF",)])
        memset("dve", KF[:, 1:2], EPS, [("KF",)])
        memset("dve", KF[:, 2:3], 0.0, [("KF",)])
        memset("dve", KF[:, 3:4], -0.5, [("KF",)])
        memset("dve", KF[:, 4:5], 0.5, [("KF",)])
        memset("dve", ONE[:], 1.0, [("ONE",)])
        memset("dve", ONESD[:, 0:128], 1.0 / 1024.0, [("ONESD",)])
        memset("dve", ONESD[:, 128:256], 1.0 / 512.0, [("ONESD",)])
        memset("dve", ONESB[:, 0:128], 1.0 / 1024.0, [("ONESD",)])
        memset("dve", ONESB[:, 128:256], 1.0 / 512.0, [("ONESD",)])
        P.dma("sp", IDF[:], ident_d, [], [("IDF",)], ("IDF",))
        cp("dve", IDB[:], IDF[:], [("IDF",)], [("IDB",)])
        P.dma("sp", CST[:].rearrange("p (l n) -> p l n", l=L), cst_d.rearrange("l p n -> p l n"), [], [("CST",)], ("CST",))
        for l in range(L):
            P.dma("sp", ROWP[32 * l: 32 * l + 1, :], row_d[l, :, 1024:1536], [], [("ROWP",)], ("ROWP",))
        P.dma("sp", LNIN[:], lnin_d, [], [("LNIN",)], ("LNIN",))
        memset("dve", ONEB[:], 1.0, [("ONEB",)])
        for l in range(L):
            pr = slice(32 * l, 32 * l + 1)
            cp("dve", ROWPB[pr, 0:512], ROWP[pr, :], [("ROWP",)], [("ROWPB",)])
            tt("dve", ROWP[pr, :], ROWP[pr, :], ROWPB[pr, 0:512], ALU.subtract, reads=[("ROWP",), ("ROWPB",)], writes=[("ROWP",)])
            cp("dve", ROWPB[pr, 512:1024], ROWP[pr, :], [("ROWP",)], [("ROWPB",)])
        for l in range(L):
            lamv = CST[:, l * NCST + _c["lam"]: l * NCST + _c["lam"] + 16]
            c1 = CX[:, l * 64: l * 64 + 16]
            act(c1, lamv, AF.Exp, scale=-1.0, reads=[("CST",)], writes=[("CX",)])
            act(c1, c1, AF.Ln, bias=kf(0), reads=[("CX",), ("KF",)], writes=[("CX",)])
            ts("dve", c1, c1, -4.0, None, ALU.mult, reads=[("CX",)], writes=[("CX",)])
            ts("dve", CX[:, l * 64 + 48: l * 64 + 64], c1, 2.0, None, ALU.mult, reads=[("CX",)], writes=[("CX",)])
            for nm, o in (("rba", 16), ("rbx", 32)):
                src = CST[:, l * NCST + _c[nm]: l * NCST + _c[nm] + 16]
                ts("dve", CX[:, l * 64 + o: l * 64 + o + 16], src, 0.5, None, ALU.mult, reads=[("CST",)], writes=[("CX",)])
            P.dma("pool", SGWT[:, l * 512:(l + 1) * 512].rearrange("q (g p) -> q g p", g=4),
                  sgwt_d[l].rearrange("g q p -> q g p"), [], [("SGWT", l)], ("SGWT", l))
            SGWF = wk_f32(2, 0, 512)
            RSR = wk_f32(1, 0, 128)[0:1, :]
            ROWT = wk_f32(0, 0, 1024)[0:1, :]
            P.dma("sp", SGWF.rearrange("q (g p) -> q g p", g=4), sgwt_d[l].rearrange("g q p -> q g p"),
                  [], wkk(2), ("SGWF",))
            P.dma("sp", ROWT, row_d[l, :, 0:1024], [], wkk(0), ("ROWT",))
            for g in range(4):
                pg = next_ps()
                mm(ps(pg, 0, 128)[0:1, :], ONE[:, 0:1], SGWF[:, g * 128:(g + 1) * 128], True, True,
                   reads=[("ONE",)] + wkk(2), writes=[psk(pg)], signal=True)
                cp("dve", RSR, ps(pg, 0, 128)[0:1, :], [psk(pg)], wkk(1))
                pg2 = next_ps()
                mm(ps(pg2, 0, 128), ROWT[:, g * 128:(g + 1) * 128], RSR, True, False,
                   reads=wkk(0) + wkk(1), writes=[psk(pg2)])
                mm(ps(pg2, 0, 128), ONE[0:1, :], ROWT[:, 512 + g * 128: 512 + (g + 1) * 128], False, True,
                   reads=wkk(0) + [("ONE",)], writes=[psk(pg2)], signal=True)
                cp("dve", BIASM[:, l * 512 + g * 128: l * 512 + (g + 1) * 128], ps(pg2, 0, 128), [psk(pg2)], [("BIASM", l)])
                ts("dve", GMAT[:, l * 512 + g * 128: l * 512 + (g + 1) * 128], ONE[:], cst(l, "sglg", g), None, ALU.mult,
                   reads=[("ONE",), ("CST",)], writes=[("GMAT", l)])

        def ln_stats(c, half, nch, onescol, zc, zkeys, tg, pm, pq):
            t0, t1 = tg[2], tg[3]
            zb = tg[4:6] if len(tg) >= 6 else None
            lo = half * 1024
            tq = (t0, t1)[c % 2]
            act(wk_bf(tq, 0, 1024), zc(c, lo, 1024), AF.Square, reads=zkeys(c, half), writes=wkk(tq))
            if zb is not None:
                tz = zb[c % 2]
                cp("dve", wk_bf(tz, 0, 1024), zc(c, lo, 1024), zkeys(c, half), wkk(tz))
                for t2 in range(2):
                    mm(ps(pm, t2 * 512, 512), ONESB[:, onescol:onescol + 128], wk_bf(tz, t2 * 512, 512), c == 0, c == nch - 1,
                       reads=wkk(tz) + [("ONESD",)], writes=[psk(pm)], signal=(c == nch - 1 and t2 == 1))
            else:
                for t2 in range(2):
                    mm(ps(pm, t2 * 512, 512), ONESD[:, onescol:onescol + 128], zc(c, lo + t2 * 512, 512), c == 0, c == nch - 1,
                       reads=zkeys(c, half) + [("ONESD",)], writes=[psk(pm)], signal=(c == nch - 1 and t2 == 1))
            for t2 in range(2):
                mm(ps(pq, t2 * 512, 512), ONESB[:, onescol:onescol + 128], wk_bf(tq, t2 * 512, 512), c == 0, c == nch - 1,
                   reads=wkk(tq) + [("ONESD",)], writes=[psk(pq)], signal=(t2 == 1))

        def ln_finish(half, nch, zc, zkeys, cb, tg, pm, pq):
            gMR, gRS, t0, t1 = tg[0], tg[1], tg[2], tg[3]
            lo = half * 1024
            act(wk_f32(gMR), ps(pm), AF.Identity, reads=[psk(pm)], writes=wkk(gMR))
            tt("dve", wk_f32(t0), wk_f32(gMR), wk_f32(gMR), ALU.mult, reads=wkk(gMR), writes=wkk(t0))
            tt("dve", wk_f32(t0), ps(pq), wk_f32(t0), ALU.subtract, reads=[psk(pq)] + wkk(t0), writes=wkk(t0))
            act(wk_f32(t0), wk_f32(t0), AF.Ln, bias=kf(1), reads=wkk(t0) + [("KF",)], writes=wkk(t0))
            act(wk_f32(gRS), wk_f32(t0), AF.Exp, scale=-0.5, reads=wkk(t0), writes=wkk(gRS))
            tt("dve", wk_f32(gMR), wk_f32(gMR), wk_f32(gRS), ALU.mult, reads=wkk(gMR) + wkk(gRS), writes=wkk(gMR))
            for c in range(nch):
                tq = (t0, t1)[c % 2]
                tt("dve", wk_f32(tq), zc(c, lo, 1024), wk_f32(gRS), ALU.mult, reads=zkeys(c, half) + wkk(gRS), writes=wkk(tq))
                tt("dve", wk_f32(tq), wk_f32(tq), wk_f32(gMR), ALU.subtract, reads=wkk(tq) + wkk(gMR), writes=wkk(tq))
                cb(c, half, wk_f32(tq), wkk(tq))

        def layer_norm(nch, onescol, zc, zkeys, cb, tg):
            for half in range(2):
                pm, pq = next_ps(), next_ps()
                for c in range(nch):
                    ln_stats(c, half, nch, onescol, zc, zkeys, tg, pm, pq)
                ln_finish(half, nch, zc, zkeys, cb, tg, pm, pq)

        def store_x(s, c, half, y, ykeys, par=None, xh_write=True):
            lo = half * 1024
            P.dma("sp", xd[s * 8 + c, :, lo: lo + 1024], y, ykeys, [("XD", s, c, half)], ("XDW", c % 4))
            if xh_write:
                act(xh(c, lo, 1024), y, AF.Identity, reads=ykeys, writes=[("XH", c, half)])
                src, skeys = xh(c, lo, 1024), [("XH", c, half)]
            else:
                gb = 10 + (2 * c + half) % 2
                act(wk_bf(gb, 0, 1024), y, AF.Identity, reads=ykeys, writes=wkk(gb))
                src, skeys = wk_bf(gb, 0, 1024), wkk(gb)
            if par is not None:
                P.dma("sp", xhd[(par * NSEG + s) * 8 + c, :, lo: lo + 1024], src, skeys,
                      [("XHD", par, s, c, half)], ("XHDW", c % 2, half))

        XH3 = XH[:].rearrange("p (c w) -> p c w", c=8)
        def load_xh(par, s):
            if state.get("xh") == (par, s):
                return
            state["xh"] = (par, s)
            for c in range(8):
                P.dma("sp", xh(c, 0, T), xhd[(par * NSEG + s) * 8 + c, :, :], [("XHD", par, s, c, 0), ("XHD", par, s, c, 1)],
                      [("XH", c, 0), ("XH", c, 1)], ("XHL", c % 4))
            sl, sr = max(s - 1, 0), min(s + 1, NSEG - 1)
            bl, br = (par * NSEG + sl) * 8, (par * NSEG + sr) * 8
            P.dma("sp", XH3[:, :, T: T + HALO], xhd[bl: bl + 8, :, T - HALO: T].rearrange("c p t -> p c t"),
                  [("XHD", par, sl, c, 1) for c in range(8)], [("XHH",)], ("XHH", 0))
            P.dma("sp", XH3[:, :, T + HALO: T + 2 * HALO], xhd[br: br + 8, :, 0: HALO].rearrange("c p t -> p c t"),
                  [("XHD", par, sr, c, 0) for c in range(8)], [("XHH",)], ("XHH", 1))
        lkL = lambda s: LINK[:, s: s + 1]
        lkR = lambda s: LINK[:, 8 + s: 9 + s]
        P.dma("sp", LINK[:], link_d, [], [("LINK",)], ("LINK",))

        gXA, gXAB, gR, gI, gA, gH1, gH2 = 0, 2, 3, 5, 7, 9, 11

        def rg_head(l, s, h, mode):
            sx = load_unit(l, ("x", h))
            sg_ = load_unit(l, ("g", h)) if mode == "B" else None
            sgt = load_unit(l, ("gt", h), 512)
            if mode == "B":
                P.dma("sp", wk_f32(gH2, 0, 2048), hbd[s * 8 + h, :, :], [("HBD", s, h)], wkk(gH2, 2), ("HBR",))
            def build_da(l_, h_, buf):
                for k in range(4):
                    ts("dve", DA[:, buf * 512 + k * 128: buf * 512 + (k + 1) * 128], IDB[:], cst(l_, "caw", k * 8 + h_), None, ALU.mult,
                       reads=[("IDB",), ("CST",)], writes=[("DA", buf)])
                state["da"][buf] = (l_, h_)
            if (l, h) not in state["da"]:
                build_da(l, h, 0 if state["da"][0] != (l, (h - 1) % 8) else 1)
            dab = state["da"].index((l, h))
            build_da(l, (h + 1) % 8, 1 - dab)
            bx = cst(l, "bin", OFF_RG_X // 128 + h)
            for half in range(2):
                pg = next_ps()
                for t2 in range(2):
                    for k in range(8):
                        mm(ps(pg, t2 * 512, 512), wt(sx, k), xh(k, half * 1024 + t2 * 512, 512), k == 0, k == 7,
                           reads=[("R", sx), ("XH", k, half)], writes=[psk(pg)], signal=(k == 7 and t2 == 1))
                act(PAD[:, 2 + half * 1024: 2 + half * 1024 + 1024], ps(pg), AF.Identity, bias=bx,
                    reads=[psk(pg), ("CST",)], writes=[("PAD",)])
            pg = next_ps()
            for k in range(8):
                mm(ps(pg, 0, 32), wt(sx, k), xh(k, T, 32), k == 0, k == 7, reads=[("R", sx), ("XHH",)], writes=[psk(pg)], signal=(k == 7))
            act(HT[:, 0:32], ps(pg, 0, 32), AF.Identity, bias=bx, reads=[psk(pg), ("CST",)], writes=[("HT",)])
            if mode == "A" and h == 7 and s > 0:
                load_xh(state["par"], s - 1)
            if "b" in KOPT:
                act(PAD[:, 0:2], HT[:, 14:16], AF.Identity, scale=lkL(s), reads=[("HT",), ("LINK",)], writes=[("PAD",)])
                act(PAD[:, 2 + T: 3 + T], HT[:, 16:17], AF.Identity, scale=lkR(s), reads=[("HT",), ("LINK",)], writes=[("PAD",)])
            else:
                ts("dve", PAD[:, 0:2], HT[:, 14:16], lkL(s), None, ALU.mult, reads=[("HT",), ("LINK",)], writes=[("PAD",)])
                ts("dve", PAD[:, 2 + T: 3 + T], HT[:, 16:17], lkR(s), None, ALU.mult, reads=[("HT",), ("LINK",)], writes=[("PAD",)])
            for half in range(2):
                pg = next_ps()
                for t2 in range(2):
                    for k in range(4):
                        o = k + half * 1024 + t2 * 512
                        mm(ps(pg, t2 * 512, 512), DA[:, dab * 512 + k * 128: dab * 512 + (k + 1) * 128], PAD[:, o: o + 512],
                           k == 0, k == 3, reads=[("DA", dab), ("PAD",)], writes=[psk(pg)], signal=(k == 3 and t2 == 1))
                if "c" in KOPT:
                    act(wk_bf(gXAB, half * 1024, 1024), ps(pg), AF.Identity, bias=cst(l, "cab", h), reads=[psk(pg), ("CST",)], writes=wkk(gXAB))
                    ts("dve", wk_f32(gXA + half), ps(pg), cst(l, "cab", h), None, ALU.add, reads=[psk(pg), ("CST",)], writes=wkk(gXA + half))
                else:
                    act(wk_f32(gXA + half), ps(pg), AF.Identity, bias=cst(l, "cab", h), reads=[psk(pg), ("CST",)], writes=wkk(gXA + half))
                    act(wk_bf(gXAB, half * 1024, 1024), wk_f32(gXA + half), AF.Identity, reads=wkk(gXA + half), writes=wkk(gXAB))
            d = 1 if mode == "A" else 0
            for gi, (gdst, hb_o) in enumerate(((gR, 16), (gI, 32))):
                for half in range(2):
                    pg = next_ps()
                    for t2 in range(2):
                        mm(ps(pg, t2 * 512, 512), wt(sgt, d * 2 + gi), wk_bf(gXAB, half * 1024 + t2 * 512, 512), True, True,
                           reads=[("R", sgt)] + wkk(gXAB), writes=[psk(pg)], signal=(t2 == 1))
                    act(wk_f32(gdst + half), ps(pg), AF.Tanh, bias=cx(l, hb_o + d * 8 + h), scale=0.5,
                        reads=[psk(pg), ("CX",)], writes=wkk(gdst + half))
            c1ap = cx(l, d * 8 + h)
            c2ap = cx(l, 48 + d * 8 + h)
            order = (1, 0) if mode == "A" else (0, 1)
            for half in order:
                act(wk_f32(gA + half), wk_f32(gR + half), AF.Exp, bias=c1ap, scale=c1ap, reads=wkk(gR + half) + [("CX",)], writes=wkk(gA + half))
                stt("dve", wk_f32(gR + half), wk_f32(gA + half), 0.999998, wk_f32(gA + half), ALU.min, ALU.mult, reads=wkk(gA + half), writes=wkk(gR + half))
            for half in order:
                stt("dve", wk_f32(gI + half), wk_f32(gI + half), 1.0, wk_f32(gXA + half), ALU.add, ALU.mult,
                    reads=wkk(gI + half) + wkk(gXA + half), writes=wkk(gI + half))
            for half in order:
                act(wk_f32(gR + half), wk_f32(gR + half), AF.Sqrt, bias=kf(0), scale=-1.0, reads=wkk(gR + half) + [("KF",)], writes=wkk(gR + half))
                stt("dve", wk_f32(gI + half), wk_f32(gI + half), 0.5, wk_f32(gR + half), ALU.mult, ALU.mult,
                    reads=wkk(gI + half) + wkk(gR + half), writes=wkk(gI + half))
            if mode == "A":
                gH = gH1 if h % 2 == 0 else gH2
                H_ = wk_f32(gH, 0, 2048)
                A_, I_ = wk_f32(gA, 0, 2048), wk_f32(gI, 0, 2048)
                ini = CB[:, h: h + 1]
                P.op("dve", lambda e: e.tensor_tensor_scan(out=H_[:, 2047:1023:-1], data0=A_[:, 2047:1023:-1], data1=I_[:, 2047:1023:-1], initial=ini, op0=ALU.mult, op1=ALU.add),
                     wkk(gA + 1) + wkk(gI + 1) + [("CB",)], wkk(gH + 1))
                ini2 = H_[:, 1024:1025]
                P.op("dve", lambda e: e.tensor_tensor_scan(out=H_[:, 1023::-1], data0=A_[:, 1023::-1], data1=I_[:, 1023::-1], initial=ini2, op0=ALU.mult, op1=ALU.add),
                     wkk(gA) + wkk(gI) + wkk(gH + 1), wkk(gH))
                P.dma("sp", hbd[s * 8 + h, :, :], H_, wkk(gH, 2), [("HBD", s, h)], ("HBW", h % 2))
                ts("dve", CB[:, h: h + 1], H_[:, 0:1], lkL(s), None, ALU.mult, reads=wkk(gH) + [("LINK",)], writes=[("CB",)])
                return
            H_ = wk_f32(gH1, 0, 2048)
            A_, I_ = wk_f32(gA, 0, 2048), wk_f32(gI, 0, 2048)
            ini = CF[:, h: h + 1]
            P.op("dve", lambda e: e.tensor_tensor_scan(out=H_[:, 0:1024], data0=A_[:, 0:1024], data1=I_[:, 0:1024], initial=ini, op0=ALU.mult, op1=ALU.add),
                 wkk(gA) + wkk(gI) + [("CF",)], wkk(gH1))
            ini2 = H_[:, 1023:1024]
            P.op("dve", lambda e: e.tensor_tensor_scan(out=H_[:, 1024:2048], data0=A_[:, 1024:2048], data1=I_[:, 1024:2048], initial=ini2, op0=ALU.mult, op1=ALU.add),
                 wkk(gA + 1) + wkk(gI + 1) + wkk(gH1), wkk(gH1 + 1))
            ts("dve", CF[:, h: h + 1], H_[:, T - 1: T], lkR(s), None, ALU.mult, reads=wkk(gH1 + 1) + [("LINK",)], writes=[("CF",)])
            for half in range(2):
                tt("dve", wk_f32(gH1 + half), wk_f32(gH1 + half), wk_f32(gH2 + half), ALU.add, reads=wkk(gH1 + half) + wkk(gH2 + half), writes=wkk(gH1 + half))
            for half in range(2):
                pg = next_ps()
                for t2 in range(2):
                    for k in range(8):
                        mm(ps(pg, t2 * 512, 512), wt(sg_, k), xh(k, half * 1024 + t2 * 512, 512), k == 0, k == 7,
                           reads=[("R", sg_), ("XH", k, half)], writes=[psk(pg)], signal=(k == 7 and t2 == 1))
                act(wk_f32(gR + half), ps(pg), AF.Gelu_apprx_tanh, bias=cst(l, "bin", OFF_RG_G // 128 + h),
                    reads=[psk(pg), ("CST",)], writes=wkk(gR + half))
                tt("dve", big_bf(h, half * 1024, 1024), wk_f32(gH1 + half), wk_f32(gR + half), ALU.mult,
                   reads=wkk(gH1 + half) + wkk(gR + half), writes=bgk(h))

        out_deps = []
        for si in range(NSEG):
            for half in range(2):
                for tb in range(8):
                    g = tb
                    tok0 = half * 1024 + tb * 128
                    xt = wk_f32(g)
                    P.dma("sp", xt, xin[si, tok0: tok0 + 128, :], [], wkk(g), ("XT", tb))
                    P.op("dve", lambda e, xt=xt: e.bn_stats(out=SM[:, 0:6], in_=xt[:, 0:512]), wkk(g), [("SM",)])
                    P.op("dve", lambda e, xt=xt: e.bn_stats(out=SM[:, 6:12], in_=xt[:, 512:1024]), wkk(g), [("SM",)])
                    P.op("dve", lambda e, tb=tb: e.bn_aggr(out=SM[:, 32 + 2 * tb: 34 + 2 * tb], in_=SM[:, 0:12]), [("SM",)], [("SM",), ("SM3",)])
                var8 = SM[:, 32:48].rearrange("p (i t) -> p i t", t=2)[:, :, 1:2]
                rs8 = SM[:, 48:56].rearrange("p (i t) -> p i t", t=1)
                act(rs8, var8, AF.Ln, bias=kf(1), reads=[("SM3",), ("KF",)], writes=[("SM4",)])
                act(SM[:, 48:56], SM[:, 48:56], AF.Exp, scale=-0.5, reads=[("SM4",)], writes=[("SM4",)])
                for tb in range(8):
                    xt = wk_f32(tb)
                    ts("dve", xt, xt, SM[:, 32 + 2 * tb: 33 + 2 * tb], SM[:, 48 + tb: 49 + tb], ALU.subtract, ALU.mult,
                       reads=wkk(tb) + [("SM3",), ("SM4",)], writes=wkk(tb))
                for c in range(8):
                    pg = next_ps()
                    for tb in range(8):
                        tr(ps(pg, tb * 128, 128), wk_f32(tb, c * 128, 128), IDF[:], reads=wkk(tb) + [("IDF",)], writes=[psk(pg)],
                           signal=(tb == 7))
                    yg = (8, 9, 12)[c % 3]
                    act(wk_f32(yg), ps(pg), AF.Identity, bias=LNIN[:, 8 + c: 9 + c], scale=LNIN[:, c: c + 1],
                        reads=[psk(pg), ("LNIN",)], writes=wkk(yg))
                    store_x(si, c, half, wk_f32(yg), wkk(yg), par=0, xh_write=False)

        for l in range(L):
            last = (l == L - 1)
            par = l % 2
            state["par"] = par
            memset("dve", CB[:], 0.0, [("CB",)])
            for si in reversed(range(NSEG)):
                load_xh(par, si)
                for h in range(8):
                    rg_head(l, si, h, "A")
            memset("dve", CF[:], 0.0, [("CF",)])
            for si in range(NSEG):
                load_xh(par, si)
                for h in range(8):
                    rg_head(l, si, h, "B")

                for ch in range(4):
                    su = load_unit(l, ("u", ch))
                    for half in range(2):
                        pg = next_ps()
                        for t2 in range(2):
                            for k in range(8):
                                mm(ps(pg, t2 * 512, 512), wt(su, k), xh(k, half * 1024 + t2 * 512, 512), k == 0, k == 7,
                                   reads=[("R", su), ("XH", k, half)], writes=[psk(pg)], signal=(k == 7 and t2 == 1))
                        act(big_bf(8 + ch, half * 1024, 1024), ps(pg), AF.Gelu_apprx_tanh, bias=cst(l, "bin", OFF_SG // 128 + ch),
                            reads=[psk(pg), ("CST",)], writes=bgk(8 + ch))
                sv = [load_unit(l, ("v", j)) for j in range(4)]
                SGO3 = BIG[:, 8 * 2048: 12 * 2048].rearrange("p (g t) -> p g t", g=4)
                for half in range(2):
                    for i in range(8):
                        tb = half * 8 + i
                        pg = next_ps()
                        for j in range(4):
                            for k in range(8):
                                mm(ps(pg, j * 128, 128), xh(k, tb * 128, 128), wt(sv[j], k), k == 0, False,
                                   reads=[("R", sv[j]), ("XH", k, half)], writes=[psk(pg)])
                            mm(ps(pg, j * 128, 128), ONEB[32 * l: 32 * l + 1, :], ROWPB[32 * l: 32 * l + 1, j * 128:(j + 1) * 128], False, False,
                               reads=[("ROWPB",)], writes=[psk(pg)])
                            mm(ps(pg, j * 128, 128), ONEB[32 * l: 32 * l + 1, :], ROWPB[32 * l: 32 * l + 1, 512 + j * 128: 512 + (j + 1) * 128], False, True,
                               reads=[("ROWPB",)], writes=[psk(pg)], signal=(j == 3))
                        V = wk_f32(i, 0, 512)
                        act(V, ps(pg, 0, 512), AF.Gelu_apprx_tanh, reads=[psk(pg)], writes=wkk(i))
                        P.op("dve", lambda e, V=V: e.bn_stats(out=SM[:, 16:22], in_=V), wkk(i), [("SM2",)])
                        P.op("dve", lambda e, i=i: e.bn_aggr(out=SM[:, 32 + 2 * i: 34 + 2 * i], in_=SM[:, 16:22]), [("SM2",)], [("SM2",), ("SM3",)])
                    var8 = SM[:, 32:48].rearrange("p (i t) -> p i t", t=2)[:, :, 1:2]
                    rs8 = SM[:, 48:56].rearrange("p (i t) -> p i t", t=1)
                    act(rs8, var8, AF.Ln, bias=kf(1), reads=[("SM3",), ("KF",)], writes=[("SM4",)])
                    act(SM[:, 48:56], SM[:, 48:56], AF.Exp, scale=-0.5, reads=[("SM4",)], writes=[("SM4",)])
                    for i in range(8):
                        tb = half * 8 + i
                        V = wk_f32(i, 0, 512)
                        gvn = 8 + i % 2
                        VN = wk_bf(gvn, 0, 512)
                        ts("dve", VN, V, SM[:, 32 + 2 * i: 33 + 2 * i], SM[:, 48 + i: 49 + i], ALU.subtract, ALU.mult,
                           reads=wkk(i) + [("SM3",), ("SM4",)], writes=wkk(gvn))
                        pg2 = next_ps()
                        for g in range(4):
                            mm(ps(pg2, g * 128, 128), VN[:, g * 128:(g + 1) * 128], SGWT[:, l * 512 + g * 128: l * 512 + (g + 1) * 128], True, True,
                               reads=wkk(gvn) + [("SGWT", l)], writes=[psk(pg2)], signal=(g == 3))
                        gt_ = 10 + i % 2
                        TT = wk_f32(gt_, 0, 512)
                        tt("dve", TT, ps(pg2, 0, 512), GMAT[:, l * 512:(l + 1) * 512], ALU.mult, reads=[psk(pg2), ("GMAT", l)], writes=wkk(gt_))
                        tt("dve", TT, TT, BIASM[:, l * 512:(l + 1) * 512], ALU.add, reads=wkk(gt_) + [("BIASM", l)], writes=wkk(gt_))
                        uview = SGO3[:, :, tb * 128:(tb + 1) * 128]
                        tt("dve", uview, TT.rearrange("p (g t) -> p g t", g=4), uview, ALU.mult, reads=wkk(gt_) + bgk(8, 4), writes=bgk(8, 4))

                for ch in range(4):
                    sc = load_unit(l, ("c", ch))
                    scg = load_unit(l, ("cg", ch))
                    for k in range(31):
                        ts("dve", DC[:, k * 128:(k + 1) * 128], IDB[:], cst(l, "ccw", k * 4 + ch), None, ALU.mult,
                           reads=[("IDB",), ("CST",)], writes=[("DC",)])
                    bc_, bcg_ = cst(l, "bin", OFF_CC // 128 + ch), cst(l, "bin", OFF_CC // 128 + 4 + ch)
                    for half in range(2):
                        pc, pgg = next_ps(), next_ps()
                        for (pp, ss) in ((pc, sc), (pgg, scg)):
                            for t2 in range(2):
                                for k in range(8):
                                    mm(ps(pp, t2 * 512, 512), wt(ss, k), xh(k, half * 1024 + t2 * 512, 512), k == 0, k == 7,
                                       reads=[("R", ss), ("XH", k, half)], writes=[psk(pp)], signal=(k == 7 and t2 == 1))
                        gs = 8 + half
                        act(wk_f32(gs), ps(pgg), AF.Sigmoid, bias=bcg_, reads=[psk(pgg), ("CST",)], writes=wkk(gs))
                        stt("dve", PAD[:, 15 + half * 1024: 15 + half * 1024 + 1024], ps(pc), bc_, wk_f32(gs),
                            ALU.add, ALU.mult, reads=[psk(pc), ("CST",)] + wkk(gs), writes=[("PAD",)])
                    pc, pgg = next_ps(), next_ps()
                    for (pp, ss) in ((pc, sc), (pgg, scg)):
                        for k in range(8):
                            mm(ps(pp, 0, 32), wt(ss, k), xh(k, T, 32), k == 0, k == 7, reads=[("R", ss), ("XHH",)], writes=[psk(pp)], signal=(k == 7))
                    act(HT[:, 32:64], ps(pgg, 0, 32), AF.Sigmoid, bias=bcg_, reads=[psk(pgg), ("CST",)], writes=[("HT",)])
                    stt("dve", HT[:, 0:32], ps(pc, 0, 32), bc_, HT[:, 32:64], ALU.add, ALU.mult, reads=[psk(pc), ("CST",), ("HT",)], writes=[("HT",)])
                    ts("dve", PAD[:, 0:15], HT[:, 1:16], lkL(si), None, ALU.mult, reads=[("HT",), ("LINK",)], writes=[("PAD",)])
                    ts("dve", PAD[:, 15 + T: 30 + T], HT[:, 16:31], lkR(si), None, ALU.mult, reads=[("HT",), ("LINK",)], writes=[("PAD",)])
                    for half in range(2):
                        pg = next_ps()
                        for t2 in range(2):
                            for k in range(31):
                                o = k + half * 1024 + t2 * 512
                                mm(ps(pg, t2 * 512, 512), DC[:, k * 128:(k + 1) * 128], PAD[:, o: o + 512], k == 0, k == 30,
                                   reads=[("DC",), ("PAD",)], writes=[psk(pg)], signal=(k == 30 and t2 == 1))
                        act(wk_f32(2 * ch + half), ps(pg), AF.Identity, bias=cst(l, "ccb", ch), reads=[psk(pg), ("CST",)], writes=wkk(2 * ch + half))
                def cc_cb(c, half, Tap, Tk):
                    act(big_bf(12 + c, half * 1024, 1024), Tap, AF.Silu, bias=cst(l, "cclb", c), scale=cst(l, "cclg", c),
                        reads=Tk + [("CST",)], writes=bgk(12 + c))
                layer_norm(4, 128, lambda c, lo, n: wk_f32(2 * c, lo, n), lambda c, half: wkk(2 * c + half), cc_cb, [8, 9, 10, 11])

                for oc in range(8):
                    sl = {nm: load_unit(l, (nm, oc), 1024 if nm in ("ga", "gb", "gc", "ba") else 512) for nm in ("ga", "gb", "gc", "ba", "bb", "bc")}
                    for half in range(2):
                        for bi, (gn, yn, nk, src0) in enumerate((("ga", "ba", 8, 0), ("gb", "bb", 4, 8), ("gc", "bc", 4, 12))):
                            pgt, py = next_ps(), next_ps()
                            for t2 in range(2):
                                for k in range(8):
                                    mm(ps(pgt, t2 * 512, 512), wt(sl[gn], k), xh(k, half * 1024 + t2 * 512, 512), k == 0, k == 7,
                                       reads=[("R", sl[gn]), ("XH", k, half)], writes=[psk(pgt)], signal=(k == 7 and t2 == 1))
                            for t2 in range(2):
                                for k in range(nk):
                                    mm(ps(py, t2 * 512, 512), wt(sl[yn], k), big_bf(src0 + k, half * 1024 + t2 * 512, 512), k == 0, k == nk - 1,
                                       reads=[("R", sl[yn])] + bgk(src0 + k), writes=[psk(py)], signal=(k == nk - 1 and t2 == 1))
                            gg = 8 + bi
                            act(wk_f32(gg), ps(pgt), AF.Sigmoid, bias=cst(l, "bin", OFF_GATE // 128 + bi * 8 + oc), reads=[psk(pgt), ("CST",)], writes=wkk(gg))
                            tt("dve", wk_f32(gg), ps(py), wk_f32(gg), ALU.mult, reads=[psk(py)] + wkk(gg), writes=wkk(gg))
                        tt("dve", wk_f32(8), wk_f32(8), wk_f32(9), ALU.add, reads=wkk(8) + wkk(9), writes=wkk(8))
                        tt("dve", wk_bf(oc, half * 1024, 1024), wk_f32(8), wk_f32(10), ALU.add, reads=wkk(8) + wkk(10), writes=wkk(oc))

                def resid(oc, half, pg, bname, zdst, zkeys_w):
                    gx = 9 + (2 * oc + half) % 2
                    ge = 11 + (2 * oc + half) % 2
                    P.dma("sp", wk_f32(gx), xd[si * 8 + oc, :, half * 1024: half * 1024 + 1024], [("XD", si, oc, half)], wkk(gx), ("XDR", gx))
                    act(wk_f32(ge), ps(pg), AF.Identity, bias=cst(l, bname, oc), reads=[psk(pg), ("CST",)], writes=wkk(ge))
                    stt("dve", zdst, wk_f32(gx), ALPHA, wk_f32(ge), ALU.mult, ALU.add, reads=wkk(gx) + wkk(ge), writes=zkeys_w)
                for oc in range(8):
                    so = load_unit(l, ("o", oc))
                    for half in range(2):
                        pg = next_ps()
                        for t2 in range(2):
                            for k in range(8):
                                mm(ps(pg, t2 * 512, 512), wt(so, k), wk_bf(k, half * 1024 + t2 * 512, 512), k == 0, k == 7,
                                   reads=[("R", so)] + wkk(k), writes=[psk(pg)], signal=(k == 7 and t2 == 1))
                        resid(oc, half, pg, "bo", big_f32(oc, half * 1024, 1024), bgk(2 * oc + half))

                def ln_x_cb(gname, bname, par_out):
                    def cb(c, half, Tap, Tk):
                        yg = 4 + (2 * c + half) % 4
                        act(wk_f32(yg), Tap, AF.Identity, bias=cst(l, bname, c), scale=cst(l, gname, c), reads=Tk + [("CST",)], writes=wkk(yg))
                        store_x(si, c, half, wk_f32(yg), wkk(yg), par=par_out, xh_write=(par_out is None))
                    return cb
                layer_norm(8, 0, lambda c, lo, n: big_f32(c, lo, n), lambda c, half: bgk(2 * c + half), ln_x_cb("l1g", "l1b", None), [0, 1, 2, 3, 8, 9])

                for J in range(4):
                    for jj in range(8):
                        j = J * 8 + jj
                        s1 = load_unit(l, ("f1", j))
                        for half in range(2):
                            pg = next_ps()
                            for t2 in range(2):
                                for k in range(8):
                                    mm(ps(pg, t2 * 512, 512), wt(s1, k), xh(k, half * 1024 + t2 * 512, 512), k == 0, k == 7,
                                       reads=[("R", s1), ("XH", k, half)], writes=[psk(pg)], signal=(k == 7 and t2 == 1))
                            gr = 8 + (2 * jj + half) % 2
                            act(wk_f32(gr), ps(pg), AF.Relu, bias=cst(l, "bf1", j), reads=[psk(pg), ("CST",)], writes=wkk(gr))
                            tt("dve", wk_bf(jj, half * 1024, 1024), wk_f32(gr), wk_f32(gr), ALU.mult, reads=wkk(gr), writes=wkk(jj))
                    if J == 3:
                        state["xh"] = None
                        if si < NSEG - 1:
                            load_xh(par, si + 1)
                    s2 = [load_unit(l, ("f2", J * 8 + jj)) for jj in range(8)]
                    for oc in range(8):
                        for half in range(2):
                            pg = next_ps()
                            for t2 in range(2):
                                for jj in range(8):
                                    mm(ps(pg, t2 * 512, 512), wt(s2[jj], oc), wk_bf(jj, half * 1024 + t2 * 512, 512), jj == 0, jj == 7,
                                       reads=[("R", s2[jj])] + wkk(jj), writes=[psk(pg)], signal=(jj == 7 and t2 == 1))
                            acc = big_f32(oc, half * 1024, 1024)
                            if J == 0:
                                cp("dve", acc, ps(pg), [psk(pg)], bgk(2 * oc + half))
                            else:
                                tt("dve", acc, ps(pg), acc, ALU.add, reads=[psk(pg)] + bgk(2 * oc + half), writes=bgk(2 * oc + half))
                for oc in range(8):
                    for half in range(2):
                        gx = 9 + (2 * oc + half) % 2
                        acc = big_f32(oc, half * 1024, 1024)
                        P.dma("sp", wk_f32(gx), xd[si * 8 + oc, :, half * 1024: half * 1024 + 1024], [("XD", si, oc, half)], wkk(gx), ("XDR", gx))
                        act(acc, acc, AF.Identity, bias=cst(l, "bf2", oc), reads=bgk(2 * oc + half) + [("CST",)], writes=bgk(2 * oc + half))
                        stt("dve", acc, wk_f32(gx), ALPHA, acc, ALU.mult, ALU.add, reads=wkk(gx) + bgk(2 * oc + half), writes=bgk(2 * oc + half))
                if not last:
                    layer_norm(8, 0, lambda c, lo, n: big_f32(c, lo, n), lambda c, half: bgk(2 * c + half), ln_x_cb("l2g", "l2b", 1 - par), [0, 1, 2, 3, 8, 9])
                else:
                    def fin_cb(c, half, Tap, Tk):
                        act(wk_f32(4 + c), Tap, AF.Identity, bias=cst(l, "l2b", c), scale=cst(l, "l2g", c), reads=Tk + [("CST",)], writes=wkk(4 + c))
                        if c == 7:
                            for tb in range(8):
                                pg = next_ps()
                                for cc in range(8):
                                    tr(ps(pg, cc * 128, 128), wk_f32(4 + cc, tb * 128, 128), IDF[:], reads=wkk(4 + cc) + [("IDF",)], writes=[psk(pg)],
                                       signal=(cc == 7))
                                go = 12 if tb % 2 == 0 else 3
                                if tb % 2 == 0:
                                    cp("dve", wk_f32(go), ps(pg), [psk(pg)], wkk(go))
                                else:
                                    act(wk_f32(go), ps(pg), AF.Identity, reads=[psk(pg)], writes=wkk(go))
                                tok0 = half * 1024 + tb * 128
                                dep = P.dma("sp", yout[si, tok0: tok0 + 128, :], wk_f32(go), wkk(go), [("YO", si, half, tb)], ("YO", tb % 2))
                                out_deps.append(dep)
                    layer_norm(8, 0, lambda c, lo, n: big_f32(c, lo, n), lambda c, half: bgk(2 * c + half), fin_cb, [0, 1, 2, 3])

        fin = {}
        for s, v in out_deps:
            fin[s] = max(fin.get(s, 0), v)
        P.final_wait("sp", list(fin.items()))
        with nc.Block() as block:
            P.emit(block)
    return nc


def _colchunk(W, j, nk):
    blk = W[:, j * 128:(j + 1) * 128].reshape(nk, 128, 128)
    out = np.zeros((128, 1024), np.float32)
    out[:, : nk * 128] = blk.transpose(1, 0, 2).reshape(128, nk * 128)
    return out


def _prep_weights(inp):
    wu = np.zeros((L, NU, 128, 1024), np.float32)
    cst = np.zeros((L, 128, NCST), np.float32)
    rows = np.zeros((L, 1, NROW), np.float32)
    for l in range(L):
        w_in = inp["w_in"][l]
        for h in range(8):
            wu[l, UNITS[("x", h)]] = _colchunk(w_in, OFF_RG_X // 128 + h, 8)
            wu[l, UNITS[("g", h)]] = _colchunk(w_in, OFF_RG_G // 128 + h, 8)
            gt = np.zeros((128, 1024), np.float32)
            gt[:, 0:128] = inp["rg_wa"][l, 0, h]
            gt[:, 128:256] = inp["rg_wx"][l, 0, h]
            gt[:, 256:384] = inp["rg_wa"][l, 1, h]
            gt[:, 384:512] = inp["rg_wx"][l, 1, h]
            wu[l, UNITS[("gt", h)]] = gt
        for ch in range(4):
            wu[l, UNITS[("u", ch)]] = _colchunk(w_in, OFF_SG // 128 + ch, 8)
            wu[l, UNITS[("v", ch)]] = _colchunk(w_in, OFF_SG // 128 + 4 + ch, 8)
            wu[l, UNITS[("c", ch)]] = _colchunk(w_in, OFF_CC // 128 + ch, 8)
            wu[l, UNITS[("cg", ch)]] = _colchunk(w_in, OFF_CC // 128 + 4 + ch, 8)
        for oc in range(8):
            wu[l, UNITS[("ga", oc)]] = _colchunk(w_in, OFF_GATE // 128 + oc, 8)
            wu[l, UNITS[("gb", oc)]] = _colchunk(w_in, OFF_GATE // 128 + 8 + oc, 8)
            wu[l, UNITS[("gc", oc)]] = _colchunk(w_in, OFF_GATE // 128 + 16 + oc, 8)
            wu[l, UNITS[("ba", oc)]] = _colchunk(inp["w_ba"][l], oc, 8)
            wu[l, UNITS[("bb", oc)]] = _colchunk(inp["w_bb"][l], oc, 4)
            wu[l, UNITS[("bc", oc)]] = _colchunk(inp["w_bc"][l], oc, 4)
            wu[l, UNITS[("o", oc)]] = _colchunk(inp["w_o"][l], oc, 8)
        for j in range(32):
            wu[l, UNITS[("f1", j)]] = _colchunk(inp["w_ff1"][l], j, 8)
            wu[l, UNITS[("f2", j)]] = inp["w_ff2"][l][j * 128:(j + 1) * 128, :]
        def put(name, vec, n):
            cst[l, :, _c[name]: _c[name] + n] = np.asarray(vec, np.float32).reshape(n, 128).T
        put("bin", inp["b_in"][l], 56)
        put("caw", inp["conv_a_w"][l].reshape(-1), 32)
        put("cab", inp["conv_a_b"][l], 8)
        put("rba", inp["rg_ba"][l].reshape(-1), 16)
        put("rbx", inp["rg_bx"][l].reshape(-1), 16)
        put("lam", inp["rg_lambda"][l].reshape(-1), 16)
        put("sglg", inp["sg_ln_g"][l], 4)
        put("ccw", inp["conv_c_w"][l].reshape(-1), 124)
        put("ccb", inp["conv_c_b"][l], 4)
        put("cclg", inp["cc_ln_g"][l], 4)
        put("cclb", inp["cc_ln_b"][l], 4)
        put("bo", inp["b_o"][l], 8)
        put("l1g", inp["ln1_g"][l], 8)
        put("l1b", inp["ln1_b"][l], 8)
        put("bf1", inp["b_ff1"][l], 32)
        put("bf2", inp["b_ff2"][l], 8)
        put("l2g", inp["ln2_g"][l], 8)
        put("l2b", inp["ln2_b"][l], 8)
        rows[l, 0, 0:512] = inp["sg_ln_b"][l]
        rows[l, 0, 512:1024] = inp["sg_b"][l].reshape(-1)
        rows[l, 0, 1024:1536] = inp["b_in"][l][OFF_SG + 512: OFF_SG + 1024]
    lnin = np.zeros((128, 16), np.float32)
    lnin[:, 0:8] = np.asarray(inp["ln_in_g"], np.float32).reshape(8, 128).T
    lnin[:, 8:16] = np.asarray(inp["ln_in_b"], np.float32).reshape(8, 128).T
    sgwt = np.ascontiguousarray(np.asarray(inp["sg_w"], np.float32).transpose(0, 1, 3, 2))
    return wu, cst, rows, lnin, sgwt


_NC_CACHE = {}
_SAMPLE_SLOTS = [(c, k) for c in range(4, 8) for k in range(4)]


def _core_inputs(inp):
    xp = inp["x_prompt"].astype(np.float32, copy=False)
    xs = inp["x_sample"].astype(np.float32, copy=False)
    xins = [np.zeros((NSEG, T, D), np.float32) for _ in range(NCORES)]
    links = [np.zeros((128, 16), np.float32) for _ in range(NCORES)]
    xins[0][:] = xp[0].reshape(NSEG, T, D)
    links[0][:, 1:8] = 1.0
    links[0][:, 8:15] = 1.0
    for i, (c, k) in enumerate(_SAMPLE_SLOTS):
        xins[c][k] = xs[i]
    return xins, links


def kernel(**inputs):
    inp = {k: np.asarray(v) for k, v in inputs.items()}
    wu, cst, rows, lnin, sgwt = _prep_weights(inp)
    xins, links = _core_inputs(inp)
    if "nc" not in _NC_CACHE:
        _NC_CACHE["nc"] = build(None)
    nc = _NC_CACHE["nc"]
    ident = np.eye(128, dtype=np.float32)
    in_maps = [{"xin": xins[c], "link": links[c], "wu": wu, "cst": cst, "rows": rows, "lnin": lnin, "sgwt": sgwt, "ident": ident}
               for c in range(NCORES)]
    res = run_bass_kernel_spmd(nc, in_maps, core_ids=list(range(NCORES)))
    y_prompt = np.ascontiguousarray(res.results[0]["yout"]).reshape(1, NSEG * T, D).astype(np.float32, copy=False)
    y_sample = np.zeros((16, T, D), np.float32)
    for i, (c, k) in enumerate(_SAMPLE_SLOTS):
        y_sample[i] = res.results[c]["yout"][k]
    return (y_prompt, y_sample)
```

```python
import numpy as np
import os
KOPT = {"b"}
import concourse.bass as bass
import concourse.mybir as mybir
from concourse.bass_utils import run_bass_kernel_spmd

F32 = mybir.dt.float32
BF16 = mybir.dt.bfloat16
AF = mybir.ActivationFunctionType
ALU = mybir.AluOpType

D = 1024
T = 2048
HALO = 16
NSEG = 8
L = 2
NCORES = 8
ALPHA = float((2 * L) ** 0.25)
EPS = 1e-5
OFF_RG_X, OFF_RG_G, OFF_SG, OFF_CC, OFF_GATE = 0, 1024, 2048, 3072, 4096
NSLOT = 8
NWK = 13

_c = {}
_o = 0
def _add(name, n):
    global _o
    _c[name] = _o
    _o += n
_add("bin", 56); _add("caw", 32); _add("cab", 8); _add("rba", 16); _add("rbx", 16); _add("lam", 16)
_add("sglg", 4); _add("ccw", 124); _add("ccb", 4); _add("cclg", 4); _add("cclb", 4)
_add("bo", 8); _add("l1g", 8); _add("l1b", 8); _add("bf1", 32); _add("bf2", 8); _add("l2g", 8); _add("l2b", 8)
NCST = _o
NROW = 1536

def _units():
    u = {}
    n = 0
    for h in range(8):
        u[("x", h)] = n; u[("g", h)] = n + 1; u[("gt", h)] = n + 2; n += 3
    for ch in range(4):
        u[("u", ch)] = n; n += 1
    for j in range(4):
        u[("v", j)] = n; n += 1
    for ch in range(4):
        u[("c", ch)] = n; u[("cg", ch)] = n + 1; n += 2
    for oc in range(8):
        for k, nm in enumerate(("ga", "gb", "gc", "ba", "bb", "bc")):
            u[(nm, oc)] = n + k
        n += 6
    for oc in range(8):
        u[("o", oc)] = n; n += 1
    for J in range(4):
        for jj in range(8):
            u[("f1", J * 8 + jj)] = n; n += 1
        for jj in range(8):
            u[("f2", J * 8 + jj)] = n; n += 1
    return u, n
UNITS, NU = _units()


class Prog:
    ENGS = ("pe", "act", "dve", "pool", "sp")

    def __init__(self, nc, es):
        self.nc = nc
        self.es = es
        self.streams = {e: [] for e in self.ENGS}
        self.cnt = {e: 0 for e in self.ENGS}
        self.known = {e: {} for e in self.ENGS}
        self.buf = {}
        self.sems = {}
        self.dcnt = {}
        for e in self.ENGS:
            self.sems[e] = es.enter_context(nc.semaphore("s_" + e))
        self.out_deps = []

    def dsem(self, key):
        if key not in self.sems:
            self.sems[key] = self.es.enter_context(self.nc.semaphore("d_%d" % len(self.sems)))
            self.dcnt[key] = 0
        return key

    def _deps(self, engine, reads, writes):
        deps = {}
        def add(d):
            if d is None:
                return
            s, v = d
            if deps.get(s, 0) < v:
                deps[s] = v
        for k in reads:
            st = self.buf.get(k)
            if st:
                add(st["w"])
        for k in writes:
            st = self.buf.get(k)
            if st:
                add(st["w"])
                for s, v in st["r"].items():
                    add((s, v))
        waits = []
        kn = self.known[engine]
        for s, v in deps.items():
            if engine == "pe" and s == "pe":
                continue
            if kn.get(s, 0) < v:
                waits.append((s, v))
                kn[s] = v
        return waits

    def _commit(self, dep, reads, writes):
        for k in writes:
            self.buf[k] = {"w": dep, "r": {}}
        for k in reads:
            st = self.buf.setdefault(k, {"w": None, "r": {}})
            if st["r"].get(dep[0], 0) < dep[1]:
                st["r"][dep[0]] = dep[1]

    def op(self, engine, fn, reads=(), writes=(), signal=True):
        waits = self._deps(engine, reads, writes)
        sems = self.sems
        if signal:
            self.cnt[engine] += 1
            dep = (engine, self.cnt[engine])
        else:
            dep = (engine, self.cnt[engine] + 1)
        semh = sems[engine]
        def thunk(eng, waits=waits, fn=fn, signal=signal):
            for s, v in waits:
                eng.wait_ge(sems[s], v)
            ins = fn(eng)
            if signal:
                ins.then_inc(semh, 1)
        self.streams[engine].append(thunk)
        self._commit(dep, reads, writes)

    def dma(self, engine, out, in_, reads, writes, semkey):
        self.dsem(semkey)
        waits = self._deps(engine, reads, writes)
        self.dcnt[semkey] += 16
        dep = (semkey, self.dcnt[semkey])
        sems = self.sems
        def thunk(eng, waits=waits):
            for s, v in waits:
                eng.wait_ge(sems[s], v)
            eng.dma_start(out=out, in_=in_).then_inc(sems[semkey], 16)
        self.streams[engine].append(thunk)
        self._commit(dep, reads, writes)
        return dep

    def final_wait(self, engine, deps):
        sems = self.sems
        def thunk(eng):
            for s, v in deps:
                eng.wait_ge(sems[s], v)
        self.streams[engine].append(thunk)

    def emit(self, block):
        P = self
        @block.tensor
        def _(e):
            for t in P.streams["pe"]:
                t(e)
        @block.scalar
        def _(e):
            for t in P.streams["act"]:
                t(e)
        @block.vector
        def _(e):
            for t in P.streams["dve"]:
                t(e)
        @block.gpsimd
        def _(e):
            for t in P.streams["pool"]:
                t(e)
        @block.sync
        def _(e):
            for t in P.streams["sp"]:
                t(e)


def build(seg_kinds, debug=False):
    import contextlib
    nc = bass.Bass("TRN2", target_bir_lowering=False)
    xin = nc.dram_tensor("xin", [NSEG, T, D], F32, kind="ExternalInput").ap()
    wu = nc.dram_tensor("wu", [L, NU, 128, 1024], F32, kind="ExternalInput").ap()
    cst_d = nc.dram_tensor("cst", [L, 128, NCST], F32, kind="ExternalInput").ap()
    row_d = nc.dram_tensor("rows", [L, 1, NROW], F32, kind="ExternalInput").ap()
    lnin_d = nc.dram_tensor("lnin", [128, 16], F32, kind="ExternalInput").ap()
    sgwt_d = nc.dram_tensor("sgwt", [L, 4, 128, 128], F32, kind="ExternalInput").ap()
    ident_d = nc.dram_tensor("ident", [128, 128], F32, kind="ExternalInput").ap()
    yout = nc.dram_tensor("yout", [NSEG, T, D], F32, kind="ExternalOutput").ap()
    link_d = nc.dram_tensor("link", [128, 16], F32, kind="ExternalInput").ap()
    xd = nc.dram_tensor("xd", [NSEG * 8, 128, T], F32, kind="Internal").ap()
    xhd = nc.dram_tensor("xhd", [2 * NSEG * 8, 128, T], BF16, kind="Internal").ap()
    hbd = nc.dram_tensor("hbd", [NSEG * 8, 128, T], F32, kind="Internal").ap()

    es = contextlib.ExitStack()
    with es:
        P = Prog(nc, es)
        sb = lambda name, shape, dt: es.enter_context(nc.sbuf_tensor(name, shape, dt))
        XH = sb("XH", [128, 8 * (T + 2 * HALO)], BF16)
        BIG = sb("BIG", [128, 32768], BF16)
        WK = sb("WK", [128, NWK * 2048], BF16)
        PAD = sb("PAD", [128, 2176], BF16)
        DC = sb("DC", [128, 31 * 128], BF16)
        DA = sb("DA", [128, 2 * 4 * 128], BF16)
        RING = sb("RING", [128, NSLOT * 1024], BF16)
        CST = sb("CST", [128, L * NCST], F32)
        CX = sb("CX", [128, L * 64], F32)
        ROWP = sb("ROWP", [33, 512], F32)
        LNIN = sb("LNIN", [128, 16], F32)
        SGWT = sb("SGWT", [128, L * 512], BF16)
        BIASM = sb("BIASM", [128, L * 512], F32)
        GMAT = sb("GMAT", [128, L * 512], F32)
        IDB = sb("IDB", [128, 128], BF16)
        IDF = sb("IDF", [128, 128], F32)
        ONESD = sb("ONESD", [128, 256], F32)
        ONE = sb("ONE", [128, 128], F32)
        ONESB = sb("ONESB", [128, 256], BF16)
        ONEB = sb("ONEB", [33, 128], BF16)
        ROWPB = sb("ROWPB", [33, 1024], BF16)
        KF = sb("KF", [128, 8], F32)
        SM = sb("SM", [128, 64], F32)
        LINK = sb("LINK", [128, 16], F32)
        CF = sb("CF", [128, 8], F32)
        CB = sb("CB", [128, 8], F32)
        HT = sb("HT", [128, 64], F32)
        PS = es.enter_context(nc.psum_tensor("PS", [128, 4096], F32))

        XW = T + 2 * HALO
        def xh(c, lo, n):
            return XH[:, c * XW + lo: c * XW + lo + n]
        def big_bf(g, lo=0, n=2048):
            return BIG[:, g * 2048 + lo: g * 2048 + lo + n]
        def big_f32(c, lo=0, n=2048):
            return BIG[:, c * 4096: (c + 1) * 4096].bitcast(F32)[:, lo: lo + n]
        def wk_bf(g, lo=0, n=2048):
            return WK[:, g * 2048 + lo: g * 2048 + lo + n]
        def wk_f32(g, lo=0, n=1024):
            return WK[:, g * 2048: g * 2048 + 2 * (lo + n)].bitcast(F32)[:, lo: lo + n]
        def wkk(g, n=1):
            return [("W", g + i) for i in range(n)]
        def bgk(g, n=1):
            return [("B", g + i) for i in range(n)]
        def ps(g, lo=0, n=1024):
            return PS[:, g * 1024 + lo: g * 1024 + lo + n]
        psk = lambda g: ("PS", g)
        def cst(l, name, j=0):
            o = l * NCST + _c[name] + j
            return CST[:, o: o + 1]
        def cx(l, o):
            return CX[:, l * 64 + o: l * 64 + o + 1]
        kf = lambda j: KF[:, j: j + 1]

        state = {"psg": 0, "slot": 0, "da": [None, None]}
        def next_ps():
            g = state["psg"]
            state["psg"] = (g + 1) % 4
            return g

        def load_unit(l, key, n=1024):
            s = state["slot"]
            state["slot"] = (s + 1) % NSLOT
            u = UNITS[key]
            P.dma("pool", RING[:, s * 1024: s * 1024 + n], wu[l, u, :, 0:n], reads=[], writes=[("R", s)], semkey=("R", s))
            return s
        def wt(s, k, n=128, width=128):
            return RING[:, s * 1024 + k * width: s * 1024 + k * width + n]

        def act(out, in_, func, bias=None, scale=1.0, reads=(), writes=()):
            def fn(e):
                kw = {}
                if bias is not None:
                    kw["bias"] = bias
                return e.activation(out=out, in_=in_, func=func, scale=scale, **kw)
            P.op("act", fn, reads, writes)
        def tt(eng, out, in0, in1, op, reads=(), writes=()):
            P.op(eng, lambda e: e.tensor_tensor(out=out, in0=in0, in1=in1, op=op), reads, writes)
        def ts(eng, out, in0, s1, s2, op0, op1=None, reads=(), writes=()):
            if op1 is None:
                P.op(eng, lambda e: e.tensor_scalar(out=out, in0=in0, scalar1=s1, scalar2=None, op0=op0), reads, writes)
            else:
                P.op(eng, lambda e: e.tensor_scalar(out=out, in0=in0, scalar1=s1, scalar2=s2, op0=op0, op1=op1), reads, writes)
        def stt(eng, out, in0, scalar, in1, op0, op1, reads=(), writes=()):
            P.op(eng, lambda e: e.scalar_tensor_tensor(out=out, in0=in0, scalar=scalar, in1=in1, op0=op0, op1=op1), reads, writes)
        def cp(eng, out, in_, reads=(), writes=()):
            P.op(eng, lambda e: e.tensor_copy(out=out, in_=in_), reads, writes)
        def mm(out, lhsT, rhs, start, stop, reads=(), writes=(), signal=False):
            P.op("pe", lambda e: e.matmul(out, lhsT, rhs, start=start, stop=stop), reads, writes, signal=signal)
        def tr(out, in_, ident, reads=(), writes=(), signal=False):
            P.op("pe", lambda e: e.transpose(out, in_, ident), reads, writes, signal=signal)
        def memset(eng, ap, val, writes=()):
            P.op(eng, lambda e: e.memset(ap, val), (), writes)

        memset("dve", KF[:, 0:1], 1.0, [("KF",)])
        memset("dve", KF[:, 1:2], EPS, [("KF",)])
        memset("dve", KF[:, 2:3], 0.0, [("KF",)])
        memset("dve", KF[:, 3:4], -0.5, [("KF",)])
        memset("dve", KF[:, 4:5], 0.5, [("KF",)])
        memset("dve", ONE[:], 1.0, [("ONE",)])
        memset("dve", ONESD[:, 0:128], 1.0 / 1024.0, [("ONESD",)])
        memset("dve", ONESD[:, 128:256], 1.0 / 512.0, [("ONESD",)])
        memset("dve", ONESB[:, 0:128], 1.0 / 1024.0, [("ONESD",)])
        memset("dve", ONESB[:, 128:256], 1.0 / 512.0, [("ONESD",)])
        P.dma("sp", IDF[:], ident_d, [], [("IDF",)], ("IDF",))
        cp("dve", IDB[:], IDF[:], [("IDF",)], [("IDB",)])
        P.dma("sp", CST[:].rearrange("p (l n) -> p l n", l=L), cst_d.rearrange("l p n -> p l n"), [], [("CST",)], ("CST",))
        for l in range(L):
            P.dma("sp", ROWP[32 * l: 32 * l + 1, :], row_d[l, :, 1024:1536], [], [("ROWP",)], ("ROWP",))
        P.dma("sp", LNIN[:], lnin_d, [], [("LNIN",)], ("LNIN",))
        memset("dve", ONEB[:], 1.0, [("ONEB",)])
        for l in range(L):
            pr = slice(32 * l, 32 * l + 1)
            cp("dve", ROWPB[pr, 0:512], ROWP[pr, :], [("ROWP",)], [("ROWPB",)])
            tt("dve", ROWP[pr, :], ROWP[pr, :], ROWPB[pr, 0:512], ALU.subtract, reads=[("ROWP",), ("ROWPB",)], writes=[("ROWP",)])
            cp("dve", ROWPB[pr, 512:1024], ROWP[pr, :], [("ROWP",)], [("ROWPB",)])
        for l in range(L):
            lamv = CST[:, l * NCST + _c["lam"]: l * NCST + _c["lam"] + 16]
            c1 = CX[:, l * 64: l * 64 + 16]
            act(c1, lamv, AF.Exp, scale=-1.0, reads=[("CST",)], writes=[("CX",)])
            act(c1, c1, AF.Ln, bias=kf(0), reads=[("CX",), ("KF",)], writes=[("CX",)])
            ts("dve", c1, c1, -4.0, None, ALU.mult, reads=[("CX",)], writes=[("CX",)])
            ts("dve", CX[:, l * 64 + 48: l * 64 + 64], c1, 2.0, None, ALU.mult, reads=[("CX",)], writes=[("CX",)])
            for nm, o in (("rba", 16), ("rbx", 32)):
                src = CST[:, l * NCST + _c[nm]: l * NCST + _c[nm] + 16]
                ts("dve", CX[:, l * 64 + o: l * 64 + o + 16], src, 0.5, None, ALU.mult, reads=[("CST",)], writes=[("CX",)])
            P.dma("pool", SGWT[:, l * 512:(l + 1) * 512].rearrange("q (g p) -> q g p", g=4),
                  sgwt_d[l].rearrange("g q p -> q g p"), [], [("SGWT", l)], ("SGWT", l))
            SGWF = wk_f32(2, 0, 512)
            RSR = wk_f32(1, 0, 128)[0:1, :]
            ROWT = wk_f32(0, 0, 1024)[0:1, :]
            P.dma("sp", SGWF.rearrange("q (g p) -> q g p", g=4), sgwt_d[l].rearrange("g q p -> q g p"),
                  [], wkk(2), ("SGWF",))
            P.dma("sp", ROWT, row_d[l, :, 0:1024], [], wkk(0), ("ROWT",))
            for g in range(4):
                pg = next_ps()
                mm(ps(pg, 0, 128)[0:1, :], ONE[:, 0:1], SGWF[:, g * 128:(g + 1) * 128], True, True,
                   reads=[("ONE",)] + wkk(2), writes=[psk(pg)], signal=True)
                cp("dve", RSR, ps(pg, 0, 128)[0:1, :], [psk(pg)], wkk(1))
                pg2 = next_ps()
                mm(ps(pg2, 0, 128), ROWT[:, g * 128:(g + 1) * 128], RSR, True, False,
                   reads=wkk(0) + wkk(1), writes=[psk(pg2)])
                mm(ps(pg2, 0, 128), ONE[0:1, :], ROWT[:, 512 + g * 128: 512 + (g + 1) * 128], False, True,
                   reads=wkk(0) + [("ONE",)], writes=[psk(pg2)], signal=True)
                cp("dve", BIASM[:, l * 512 + g * 128: l * 512 + (g + 1) * 128], ps(pg2, 0, 128), [psk(pg2)], [("BIASM", l)])
                ts("dve", GMAT[:, l * 512 + g * 128: l * 512 + (g + 1) * 128], ONE[:], cst(l, "sglg", g), None, ALU.mult,
                   reads=[("ONE",), ("CST",)], writes=[("GMAT", l)])

        def ln_stats(c, half, nch, onescol, zc, zkeys, tg, pm, pq):
            t0, t1 = tg[2], tg[3]
            zb = tg[4:6] if len(tg) >= 6 else None
            lo = half * 1024
            tq = (t0, t1)[c % 2]
            act(wk_bf(tq, 0, 1024), zc(c, lo, 1024), AF.Square, reads=zkeys(c, half), writes=wkk(tq))
            if zb is not None:
                tz = zb[c % 2]
                cp("dve", wk_bf(tz, 0, 1024), zc(c, lo, 1024), zkeys(c, half), wkk(tz))
                for t2 in range(2):
                    mm(ps(pm, t2 * 512, 512), ONESB[:, onescol:onescol + 128], wk_bf(tz, t2 * 512, 512), c == 0, c == nch - 1,
                       reads=wkk(tz) + [("ONESD",)], writes=[psk(pm)], signal=(c == nch - 1 and t2 == 1))
            else:
                for t2 in range(2):
                    mm(ps(pm, t2 * 512, 512), ONESD[:, onescol:onescol + 128], zc(c, lo + t2 * 512, 512), c == 0, c == nch - 1,
                       reads=zkeys(c, half) + [("ONESD",)], writes=[psk(pm)], signal=(c == nch - 1 and t2 == 1))
            for t2 in range(2):
                mm(ps(pq, t2 * 512, 512), ONESB[:, onescol:onescol + 128], wk_bf(tq, t2 * 512, 512), c == 0, c == nch - 1,
                   reads=wkk(tq) + [("ONESD",)], writes=[psk(pq)], signal=(t2 == 1))

        def ln_finish(half, nch, zc, zkeys, cb, tg, pm, pq):
            gMR, gRS, t0, t1 = tg[0], tg[1], tg[2], tg[3]
            lo = half * 1024
            act(wk_f32(gMR), ps(pm), AF.Identity, reads=[psk(pm)], writes=wkk(gMR))
            tt("dve", wk_f32(t0), wk_f32(gMR), wk_f32(gMR), ALU.mult, reads=wkk(gMR), writes=wkk(t0))
            tt("dve", wk_f32(t0), ps(pq), wk_f32(t0), ALU.subtract, reads=[psk(pq)] + wkk(t0), writes=wkk(t0))
            act(wk_f32(t0), wk_f32(t0), AF.Ln, bias=kf(1), reads=wkk(t0) + [("KF",)], writes=wkk(t0))
            act(wk_f32(gRS), wk_f32(t0), AF.Exp, scale=-0.5, reads=wkk(t0), writes=wkk(gRS))
            tt("dve", wk_f32(gMR), wk_f32(gMR), wk_f32(gRS), ALU.mult, reads=wkk(gMR) + wkk(gRS), writes=wkk(gMR))
            for c in range(nch):
                tq = (t0, t1)[c % 2]
                tt("dve", wk_f32(tq), zc(c, lo, 1024), wk_f32(gRS), ALU.mult, reads=zkeys(c, half) + wkk(gRS), writes=wkk(tq))
                tt("dve", wk_f32(tq), wk_f32(tq), wk_f32(gMR), ALU.subtract, reads=wkk(tq) + wkk(gMR), writes=wkk(tq))
                cb(c, half, wk_f32(tq), wkk(tq))

        def layer_norm(nch, onescol, zc, zkeys, cb, tg):
            for half in range(2):
                pm, pq = next_ps(), next_ps()
                for c in range(nch):
                    ln_stats(c, half, nch, onescol, zc, zkeys, tg, pm, pq)
                ln_finish(half, nch, zc, zkeys, cb, tg, pm, pq)

        def store_x(s, c, half, y, ykeys, par=None, xh_write=True):
            lo = half * 1024
            P.dma("sp", xd[s * 8 + c, :, lo: lo + 1024], y, ykeys, [("XD", s, c, half)], ("XDW", c % 4))
            if xh_write:
                act(xh(c, lo, 1024), y, AF.Identity, reads=ykeys, writes=[("XH", c, half)])
                src, skeys = xh(c, lo, 1024), [("XH", c, half)]
            else:
                gb = 10 + (2 * c + half) % 2
                act(wk_bf(gb, 0, 1024), y, AF.Identity, reads=ykeys, writes=wkk(gb))
                src, skeys = wk_bf(gb, 0, 1024), wkk(gb)
            if par is not None:
                P.dma("sp", xhd[(par * NSEG + s) * 8 + c, :, lo: lo + 1024], src, skeys,
                      [("XHD", par, s, c, half)], ("XHDW", c % 2, half))

        XH3 = XH[:].rearrange("p (c w) -> p c w", c=8)
        def load_xh(par, s):
            if state.get("xh") == (par, s):
                return
            state["xh"] = (par, s)
            for c in range(8):
                P.dma("sp", xh(c, 0, T), xhd[(par * NSEG + s) * 8 + c, :, :], [("XHD", par, s, c, 0), ("XHD", par, s, c, 1)],
                      [("XH", c, 0), ("XH", c, 1)], ("XHL", c % 4))
            sl, sr = max(s - 1, 0), min(s + 1, NSEG - 1)
            bl, br = (par * NSEG + sl) * 8, (par * NSEG + sr) * 8
            P.dma("sp", XH3[:, :, T: T + HALO], xhd[bl: bl + 8, :, T - HALO: T].rearrange("c p t -> p c t"),
                  [("XHD", par, sl, c, 1) for c in range(8)], [("XHH",)], ("XHH", 0))
            P.dma("sp", XH3[:, :, T + HALO: T + 2 * HALO], xhd[br: br + 8, :, 0: HALO].rearrange("c p t -> p c t"),
                  [("XHD", par, sr, c, 0) for c in range(8)], [("XHH",)], ("XHH", 1))
        lkL = lambda s: LINK[:, s: s + 1]
        lkR = lambda s: LINK[:, 8 + s: 9 + s]
        P.dma("sp", LINK[:], link_d, [], [("LINK",)], ("LINK",))

        gXA, gXAB, gR, gI, gA, gH1, gH2 = 0, 2, 3, 5, 7, 9, 11

        def rg_head(l, s, h, mode):
            sx = load_unit(l, ("x", h))
            sg_ = load_unit(l, ("g", h)) if mode == "B" else None
            sgt = load_unit(l, ("gt", h), 512)
            if mode == "B":
                P.dma("sp", wk_f32(gH2, 0, 2048), hbd[s * 8 + h, :, :], [("HBD", s, h)], wkk(gH2, 2), ("HBR",))
            def build_da(l_, h_, buf):
                for k in range(4):
                    ts("dve", DA[:, buf * 512 + k * 128: buf * 512 + (k + 1) * 128], IDB[:], cst(l_, "caw", k * 8 + h_), None, ALU.mult,
                       reads=[("IDB",), ("CST",)], writes=[("DA", buf)])
                state["da"][buf] = (l_, h_)
            if (l, h) not in state["da"]:
                build_da(l, h, 0 if state["da"][0] != (l, (h - 1) % 8) else 1)
            dab = state["da"].index((l, h))
            build_da(l, (h + 1) % 8, 1 - dab)
            bx = cst(l, "bin", OFF_RG_X // 128 + h)
            for half in range(2):
                pg = next_ps()
                for t2 in range(2):
                    for k in range(8):
                        mm(ps(pg, t2 * 512, 512), wt(sx, k), xh(k, half * 1024 + t2 * 512, 512), k == 0, k == 7,
                           reads=[("R", sx), ("XH", k, half)], writes=[psk(pg)], signal=(k == 7 and t2 == 1))
                act(PAD[:, 2 + half * 1024: 2 + half * 1024 + 1024], ps(pg), AF.Identity, bias=bx,
                    reads=[psk(pg), ("CST",)], writes=[("PAD",)])
            pg = next_ps()
            for k in range(8):
                mm(ps(pg, 0, 32), wt(sx, k), xh(k, T, 32), k == 0, k == 7, reads=[("R", sx), ("XHH",)], writes=[psk(pg)], signal=(k == 7))
            act(HT[:, 0:32], ps(pg, 0, 32), AF.Identity, bias=bx, reads=[psk(pg), ("CST",)], writes=[("HT",)])
            if mode == "A" and h == 7 and s > 0:
                load_xh(state["par"], s - 1)
            if "b" in KOPT:
                act(PAD[:, 0:2], HT[:, 14:16], AF.Identity, scale=lkL(s), reads=[("HT",), ("LINK",)], writes=[("PAD",)])
                act(PAD[:, 2 + T: 3 + T], HT[:, 16:17], AF.Identity, scale=lkR(s), reads=[("HT",), ("LINK",)], writes=[("PAD",)])
            else:
                ts("dve", PAD[:, 0:2], HT[:, 14:16], lkL(s), None, ALU.mult, reads=[("HT",), ("LINK",)], writes=[("PAD",)])
                ts("dve", PAD[:, 2 + T: 3 + T], HT[:, 16:17], lkR(s), None, ALU.mult, reads=[("HT",), ("LINK",)], writes=[("PAD",)])
            for half in range(2):
                pg = next_ps()
                for t2 in range(2):
                    for k in range(4):
                        o = k + half * 1024 + t2 * 512
                        mm(ps(pg, t2 * 512, 512), DA[:, dab * 512 + k * 128: dab * 512 + (k + 1) * 128], PAD[:, o: o + 512],
                           k == 0, k == 3, reads=[("DA", dab), ("PAD",)], writes=[psk(pg)], signal=(k == 3 and t2 == 1))
                if "x" in KOPT:
                    act(wk_bf(gXAB, half * 1024, 1024), ps(pg), AF.Identity, bias=cst(l, "cab", h), reads=[psk(pg), ("CST",)], writes=wkk(gXAB))
                    stt("dve", wk_f32(gXA + half), ps(pg), cst(l, "cab", h), wk_f32(gXA + half), ALU.add, ALU.bypass,
                        reads=[psk(pg), ("CST",)], writes=wkk(gXA + half))
                elif "c" in KOPT:
                    act(wk_bf(gXAB, half * 1024, 1024), ps(pg), AF.Identity, bias=cst(l, "cab", h), reads=[psk(pg), ("CST",)], writes=wkk(gXAB))
                    ts("dve", wk_f32(gXA + half), ps(pg), cst(l, "cab", h), None, ALU.add, reads=[psk(pg), ("CST",)], writes=wkk(gXA + half))
                else:
                    act(wk_f32(gXA + half), ps(pg), AF.Identity, bias=cst(l, "cab", h), reads=[psk(pg), ("CST",)], writes=wkk(gXA + half))
                    act(wk_bf(gXAB, half * 1024, 1024), wk_f32(gXA + half), AF.Identity, reads=wkk(gXA + half), writes=wkk(gXAB))
            d = 1 if mode == "A" else 0
            for gi, (gdst, hb_o) in enumerate(((gR, 16), (gI, 32))):
                for half in range(2):
                    pg = next_ps()
                    for t2 in range(2):
                        mm(ps(pg, t2 * 512, 512), wt(sgt, d * 2 + gi), wk_bf(gXAB, half * 1024 + t2 * 512, 512), True, True,
                           reads=[("R", sgt)] + wkk(gXAB), writes=[psk(pg)], signal=(t2 == 1))
                    act(wk_f32(gdst + half), ps(pg), AF.Tanh, bias=cx(l, hb_o + d * 8 + h), scale=0.5,
                        reads=[psk(pg), ("CX",)], writes=wkk(gdst + half))
            c1ap = cx(l, d * 8 + h)
            c2ap = cx(l, 48 + d * 8 + h)
            order = (1, 0) if mode == "A" else (0, 1)
            for half in order:
                act(wk_f32(gA + half), wk_f32(gR + half), AF.Exp, bias=c1ap, scale=c1ap, reads=wkk(gR + half) + [("CX",)], writes=wkk(gA + half))
                stt("dve", wk_f32(gR + half), wk_f32(gA + half), 0.999998, wk_f32(gA + half), ALU.min, ALU.mult, reads=wkk(gA + half), writes=wkk(gR + half))
            for half in order:
                stt("dve", wk_f32(gI + half), wk_f32(gI + half), 1.0, wk_f32(gXA + half), ALU.add, ALU.mult,
                    reads=wkk(gI + half) + wkk(gXA + half), writes=wkk(gI + half))
            for half in order:
                act(wk_f32(gR + half), wk_f32(gR + half), AF.Sqrt, bias=kf(0), scale=-1.0, reads=wkk(gR + half) + [("KF",)], writes=wkk(gR + half))
                stt("dve", wk_f32(gI + half), wk_f32(gI + half), 0.5, wk_f32(gR + half), ALU.mult, ALU.mult,
                    reads=wkk(gI + half) + wkk(gR + half), writes=wkk(gI + half))
            if mode == "A":
                gH = gH1 if h % 2 == 0 else gH2
                H_ = wk_f32(gH, 0, 2048)
                A_, I_ = wk_f32(gA, 0, 2048), wk_f32(gI, 0, 2048)
                ini = CB[:, h: h + 1]
                P.op("dve", lambda e: e.tensor_tensor_scan(out=H_[:, 2047:1023:-1], data0=A_[:, 2047:1023:-1], data1=I_[:, 2047:1023:-1], initial=ini, op0=ALU.mult, op1=ALU.add),
                     wkk(gA + 1) + wkk(gI + 1) + [("CB",)], wkk(gH + 1))
                ini2 = H_[:, 1024:1025]
                P.op("dve", lambda e: e.tensor_tensor_scan(out=H_[:, 1023::-1], data0=A_[:, 1023::-1], data1=I_[:, 1023::-1], initial=ini2, op0=ALU.mult, op1=ALU.add),
                     wkk(gA) + wkk(gI) + wkk(gH + 1), wkk(gH))
                P.dma("sp", hbd[s * 8 + h, :, :], H_, wkk(gH, 2), [("HBD", s, h)], ("HBW", h % 2))
                ts("dve", CB[:, h: h + 1], H_[:, 0:1], lkL(s), None, ALU.mult, reads=wkk(gH) + [("LINK",)], writes=[("CB",)])
                return
            H_ = wk_f32(gH1, 0, 2048)
            A_, I_ = wk_f32(gA, 0, 2048), wk_f32(gI, 0, 2048)
            ini = CF[:, h: h + 1]
            P.op("dve", lambda e: e.tensor_tensor_scan(out=H_[:, 0:1024], data0=A_[:, 0:1024], data1=I_[:, 0:1024], initial=ini, op0=ALU.mult, op1=ALU.add),
                 wkk(gA) + wkk(gI) + [("CF",)], wkk(gH1))
            ini2 = H_[:, 1023:1024]
            P.op("dve", lambda e: e.tensor_tensor_scan(out=H_[:, 1024:2048], data0=A_[:, 1024:2048], data1=I_[:, 1024:2048], initial=ini2, op0=ALU.mult, op1=ALU.add),
                 wkk(gA + 1) + wkk(gI + 1) + wkk(gH1), wkk(gH1 + 1))
            ts("dve", CF[:, h: h + 1], H_[:, T - 1: T], lkR(s), None, ALU.mult, reads=wkk(gH1 + 1) + [("LINK",)], writes=[("CF",)])
            for half in range(2):
                tt("dve", wk_f32(gH1 + half), wk_f32(gH1 + half), wk_f32(gH2 + half), ALU.add, reads=wkk(gH1 + half) + wkk(gH2 + half), writes=wkk(gH1 + half))
            for half in range(2):
                pg = next_ps()
                for t2 in range(2):
                    for k in range(8):
                        mm(ps(pg, t2 * 512, 512), wt(sg_, k), xh(k, half * 1024 + t2 * 512, 512), k == 0, k == 7,
                           reads=[("R", sg_), ("XH", k, half)], writes=[psk(pg)], signal=(k == 7 and t2 == 1))
                act(wk_f32(gR + half), ps(pg), AF.Gelu_apprx_tanh, bias=cst(l, "bin", OFF_RG_G // 128 + h),
                    reads=[psk(pg), ("CST",)], writes=wkk(gR + half))
                tt("dve", big_bf(h, half * 1024, 1024), wk_f32(gH1 + half), wk_f32(gR + half), ALU.mult,
                   reads=wkk(gH1 + half) + wkk(gR + half), writes=bgk(h))

        out_deps = []
        for si in range(NSEG):
            for half in range(2):
                for tb in range(8):
                    g = tb
                    tok0 = half * 1024 + tb * 128
                    xt = wk_f32(g)
                    P.dma("sp", xt, xin[si, tok0: tok0 + 128, :], [], wkk(g), ("XT", tb))
                    P.op("dve", lambda e, xt=xt: e.bn_stats(out=SM[:, 0:6], in_=xt[:, 0:512]), wkk(g), [("SM",)])
                    P.op("dve", lambda e, xt=xt: e.bn_stats(out=SM[:, 6:12], in_=xt[:, 512:1024]), wkk(g), [("SM",)])
                    P.op("dve", lambda e, tb=tb: e.bn_aggr(out=SM[:, 32 + 2 * tb: 34 + 2 * tb], in_=SM[:, 0:12]), [("SM",)], [("SM",), ("SM3",)])
                var8 = SM[:, 32:48].rearrange("p (i t) -> p i t", t=2)[:, :, 1:2]
                rs8 = SM[:, 48:56].rearrange("p (i t) -> p i t", t=1)
                act(rs8, var8, AF.Ln, bias=kf(1), reads=[("SM3",), ("KF",)], writes=[("SM4",)])
                act(SM[:, 48:56], SM[:, 48:56], AF.Exp, scale=-0.5, reads=[("SM4",)], writes=[("SM4",)])
                for tb in range(8):
                    xt = wk_f32(tb)
                    ts("dve", xt, xt, SM[:, 32 + 2 * tb: 33 + 2 * tb], SM[:, 48 + tb: 49 + tb], ALU.subtract, ALU.mult,
                       reads=wkk(tb) + [("SM3",), ("SM4",)], writes=wkk(tb))
                for c in range(8):
                    pg = next_ps()
                    for tb in range(8):
                        tr(ps(pg, tb * 128, 128), wk_f32(tb, c * 128, 128), IDF[:], reads=wkk(tb) + [("IDF",)], writes=[psk(pg)],
                           signal=(tb == 7))
                    yg = (8, 9, 12)[c % 3]
                    act(wk_f32(yg), ps(pg), AF.Identity, bias=LNIN[:, 8 + c: 9 + c], scale=LNIN[:, c: c + 1],
                        reads=[psk(pg), ("LNIN",)], writes=wkk(yg))
                    store_x(si, c, half, wk_f32(yg), wkk(yg), par=0, xh_write=False)

        for l in range(L):
            last = (l == L - 1)
            par = l % 2
            state["par"] = par
            memset("dve", CB[:], 0.0, [("CB",)])
            for si in reversed(range(NSEG)):
                load_xh(par, si)
                for h in range(8):
                    rg_head(l, si, h, "A")
            memset("dve", CF[:], 0.0, [("CF",)])
            for si in range(NSEG):
                load_xh(par, si)
                for h in range(8):
                    rg_head(l, si, h, "B")

                for ch in range(4):
                    su = load_unit(l, ("u", ch))
                    for half in range(2):
                        pg = next_ps()
                        for t2 in range(2):
                            for k in range(8):
                                mm(ps(pg, t2 * 512, 512), wt(su, k), xh(k, half * 1024 + t2 * 512, 512), k == 0, k == 7,
                                   reads=[("R", su), ("XH", k, half)], writes=[psk(pg)], signal=(k == 7 and t2 == 1))
                        act(big_bf(8 + ch, half * 1024, 1024), ps(pg), AF.Gelu_apprx_tanh, bias=cst(l, "bin", OFF_SG // 128 + ch),
                            reads=[psk(pg), ("CST",)], writes=bgk(8 + ch))
                sv = [load_unit(l, ("v", j)) for j in range(4)]
                SGO3 = BIG[:, 8 * 2048: 12 * 2048].rearrange("p (g t) -> p g t", g=4)
                for half in range(2):
                    for i in range(8):
                        tb = half * 8 + i
                        pg = next_ps()
                        for j in range(4):
                            for k in range(8):
                                mm(ps(pg, j * 128, 128), xh(k, tb * 128, 128), wt(sv[j], k), k == 0, False,
                                   reads=[("R", sv[j]), ("XH", k, half)], writes=[psk(pg)])
                            mm(ps(pg, j * 128, 128), ONEB[32 * l: 32 * l + 1, :], ROWPB[32 * l: 32 * l + 1, j * 128:(j + 1) * 128], False, False,
                               reads=[("ROWPB",)], writes=[psk(pg)])
                            mm(ps(pg, j * 128, 128), ONEB[32 * l: 32 * l + 1, :], ROWPB[32 * l: 32 * l + 1, 512 + j * 128: 512 + (j + 1) * 128], False, True,
                               reads=[("ROWPB",)], writes=[psk(pg)], signal=(j == 3))
                        V = wk_f32(i, 0, 512)
                        act(V, ps(pg, 0, 512), AF.Gelu_apprx_tanh, reads=[psk(pg)], writes=wkk(i))
                        P.op("dve", lambda e, V=V: e.bn_stats(out=SM[:, 16:22], in_=V), wkk(i), [("SM2",)])
                        P.op("dve", lambda e, i=i: e.bn_aggr(out=SM[:, 32 + 2 * i: 34 + 2 * i], in_=SM[:, 16:22]), [("SM2",)], [("SM2",), ("SM3",)])
                    var8 = SM[:, 32:48].rearrange("p (i t) -> p i t", t=2)[:, :, 1:2]
                    rs8 = SM[:, 48:56].rearrange("p (i t) -> p i t", t=1)
                    act(rs8, var8, AF.Ln, bias=kf(1), reads=[("SM3",), ("KF",)], writes=[("SM4",)])
                    act(SM[:, 48:56], SM[:, 48:56], AF.Exp, scale=-0.5, reads=[("SM4",)], writes=[("SM4",)])
                    for i in range(8):
                        tb = half * 8 + i
                        V = wk_f32(i, 0, 512)
                        gvn = 8 + i % 2
                        VN = wk_bf(gvn, 0, 512)
                        ts("dve", VN, V, SM[:, 32 + 2 * i: 33 + 2 * i], SM[:, 48 + i: 49 + i], ALU.subtract, ALU.mult,
                           reads=wkk(i) + [("SM3",), ("SM4",)], writes=wkk(gvn))
                        pg2 = next_ps()
                        for g in range(4):
                            mm(ps(pg2, g * 128, 128), VN[:, g * 128:(g + 1) * 128], SGWT[:, l * 512 + g * 128: l * 512 + (g + 1) * 128], True, True,
                               reads=wkk(gvn) + [("SGWT", l)], writes=[psk(pg2)], signal=(g == 3))
                        gt_ = 10 + i % 2
                        TT = wk_f32(gt_, 0, 512)
                        tt("dve", TT, ps(pg2, 0, 512), GMAT[:, l * 512:(l + 1) * 512], ALU.mult, reads=[psk(pg2), ("GMAT", l)], writes=wkk(gt_))
                        tt("dve", TT, TT, BIASM[:, l * 512:(l + 1) * 512], ALU.add, reads=wkk(gt_) + [("BIASM", l)], writes=wkk(gt_))
                        uview = SGO3[:, :, tb * 128:(tb + 1) * 128]
                        tt("dve", uview, TT.rearrange("p (g t) -> p g t", g=4), uview, ALU.mult, reads=wkk(gt_) + bgk(8, 4), writes=bgk(8, 4))

                for ch in range(4):
                    sc = load_unit(l, ("c", ch))
                    scg = load_unit(l, ("cg", ch))
                    for k in range(31):
                        ts("dve", DC[:, k * 128:(k + 1) * 128], IDB[:], cst(l, "ccw", k * 4 + ch), None, ALU.mult,
                           reads=[("IDB",), ("CST",)], writes=[("DC",)])
                    bc_, bcg_ = cst(l, "bin", OFF_CC // 128 + ch), cst(l, "bin", OFF_CC // 128 + 4 + ch)
                    for half in range(2):
                        pc, pgg = next_ps(), next_ps()
                        for (pp, ss) in ((pc, sc), (pgg, scg)):
                            for t2 in range(2):
                                for k in range(8):
                                    mm(ps(pp, t2 * 512, 512), wt(ss, k), xh(k, half * 1024 + t2 * 512, 512), k == 0, k == 7,
                                       reads=[("R", ss), ("XH", k, half)], writes=[psk(pp)], signal=(k == 7 and t2 == 1))
                        gs = 8 + half
                        act(wk_f32(gs), ps(pgg), AF.Sigmoid, bias=bcg_, reads=[psk(pgg), ("CST",)], writes=wkk(gs))
                        stt("dve", PAD[:, 15 + half * 1024: 15 + half * 1024 + 1024], ps(pc), bc_, wk_f32(gs),
                            ALU.add, ALU.mult, reads=[psk(pc), ("CST",)] + wkk(gs), writes=[("PAD",)])
                    pc, pgg = next_ps(), next_ps()
                    for (pp, ss) in ((pc, sc), (pgg, scg)):
                        for k in range(8):
                            mm(ps(pp, 0, 32), wt(ss, k), xh(k, T, 32), k == 0, k == 7, reads=[("R", ss), ("XHH",)], writes=[psk(pp)], signal=(k == 7))
                    act(HT[:, 32:64], ps(pgg, 0, 32), AF.Sigmoid, bias=bcg_, reads=[psk(pgg), ("CST",)], writes=[("HT",)])
                    stt("dve", HT[:, 0:32], ps(pc, 0, 32), bc_, HT[:, 32:64], ALU.add, ALU.mult, reads=[psk(pc), ("CST",), ("HT",)], writes=[("HT",)])
                    ts("dve", PAD[:, 0:15], HT[:, 1:16], lkL(si), None, ALU.mult, reads=[("HT",), ("LINK",)], writes=[("PAD",)])
                    ts("dve", PAD[:, 15 + T: 30 + T], HT[:, 16:31], lkR(si), None, ALU.mult, reads=[("HT",), ("LINK",)], writes=[("PAD",)])
                    for half in range(2):
                        pg = next_ps()
                        for t2 in range(2):
                            for k in range(31):
                                o = k + half * 1024 + t2 * 512
                                mm(ps(pg, t2 * 512, 512), DC[:, k * 128:(k + 1) * 128], PAD[:, o: o + 512], k == 0, k == 30,
                                   reads=[("DC",), ("PAD",)], writes=[psk(pg)], signal=(k == 30 and t2 == 1))
                        act(wk_f32(2 * ch + half), ps(pg), AF.Identity, bias=cst(l, "ccb", ch), reads=[psk(pg), ("CST",)], writes=wkk(2 * ch + half))
                def cc_cb(c, half, Tap, Tk):
                    act(big_bf(12 + c, half * 1024, 1024), Tap, AF.Silu, bias=cst(l, "cclb", c), scale=cst(l, "cclg", c),
                        reads=Tk + [("CST",)], writes=bgk(12 + c))
                layer_norm(4, 128, lambda c, lo, n: wk_f32(2 * c, lo, n), lambda c, half: wkk(2 * c + half), cc_cb, [8, 9, 10, 11])

                for oc in range(8):
                    sl = {nm: load_unit(l, (nm, oc), 1024 if nm in ("ga", "gb", "gc", "ba") else 512) for nm in ("ga", "gb", "gc", "ba", "bb", "bc")}
                    for half in range(2):
                        for bi, (gn, yn, nk, src0) in enumerate((("ga", "ba", 8, 0), ("gb", "bb", 4, 8), ("gc", "bc", 4, 12))):
                            pgt, py = next_ps(), next_ps()
                            for t2 in range(2):
                                for k in range(8):
                                    mm(ps(pgt, t2 * 512, 512), wt(sl[gn], k), xh(k, half * 1024 + t2 * 512, 512), k == 0, k == 7,
                                       reads=[("R", sl[gn]), ("XH", k, half)], writes=[psk(pgt)], signal=(k == 7 and t2 == 1))
                            for t2 in range(2):
                                for k in range(nk):
                                    mm(ps(py, t2 * 512, 512), wt(sl[yn], k), big_bf(src0 + k, half * 1024 + t2 * 512, 512), k == 0, k == nk - 1,
                                       reads=[("R", sl[yn])] + bgk(src0 + k), writes=[psk(py)], signal=(k == nk - 1 and t2 == 1))
                            gg = 8 + bi
                            act(wk_f32(gg), ps(pgt), AF.Sigmoid, bias=cst(l, "bin", OFF_GATE // 128 + bi * 8 + oc), reads=[psk(pgt), ("CST",)], writes=wkk(gg))
                            tt("dve", wk_f32(gg), ps(py), wk_f32(gg), ALU.mult, reads=[psk(py)] + wkk(gg), writes=wkk(gg))
                        tt("dve", wk_f32(8), wk_f32(8), wk_f32(9), ALU.add, reads=wkk(8) + wkk(9), writes=wkk(8))
                        tt("dve", wk_bf(oc, half * 1024, 1024), wk_f32(8), wk_f32(10), ALU.add, reads=wkk(8) + wkk(10), writes=wkk(oc))

                def resid(oc, half, pg, bname, zdst, zkeys_w):
                    gx = 9 + (2 * oc + half) % 2
                    ge = 11 + (2 * oc + half) % 2
                    P.dma("sp", wk_f32(gx), xd[si * 8 + oc, :, half * 1024: half * 1024 + 1024], [("XD", si, oc, half)], wkk(gx), ("XDR", gx))
                    act(wk_f32(ge), ps(pg), AF.Identity, bias=cst(l, bname, oc), reads=[psk(pg), ("CST",)], writes=wkk(ge))
                    stt("dve", zdst, wk_f32(gx), ALPHA, wk_f32(ge), ALU.mult, ALU.add, reads=wkk(gx) + wkk(ge), writes=zkeys_w)
                for oc in range(8):
                    so = load_unit(l, ("o", oc))
                    for half in range(2):
                        pg = next_ps()
                        for t2 in range(2):
                            for k in range(8):
                                mm(ps(pg, t2 * 512, 512), wt(so, k), wk_bf(k, half * 1024 + t2 * 512, 512), k == 0, k == 7,
                                   reads=[("R", so)] + wkk(k), writes=[psk(pg)], signal=(k == 7 and t2 == 1))
                        resid(oc, half, pg, "bo", big_f32(oc, half * 1024, 1024), bgk(2 * oc + half))

                def ln_x_cb(gname, bname, par_out):
                    def cb(c, half, Tap, Tk):
                        yg = 4 + (2 * c + half) % 4
                        act(wk_f32(yg), Tap, AF.Identity, bias=cst(l, bname, c), scale=cst(l, gname, c), reads=Tk + [("CST",)], writes=wkk(yg))
                        store_x(si, c, half, wk_f32(yg), wkk(yg), par=par_out, xh_write=(par_out is None))
                    return cb
                layer_norm(8, 0, lambda c, lo, n: big_f32(c, lo, n), lambda c, half: bgk(2 * c + half), ln_x_cb("l1g", "l1b", None), [0, 1, 2, 3, 8, 9])

                for J in range(4):
                    for jj in range(8):
                        j = J * 8 + jj
                        s1 = load_unit(l, ("f1", j))
                        for half in range(2):
                            pg = next_ps()
                            for t2 in range(2):
                                for k in range(8):
                                    mm(ps(pg, t2 * 512, 512), wt(s1, k), xh(k, half * 1024 + t2 * 512, 512), k == 0, k == 7,
                                       reads=[("R", s1), ("XH", k, half)], writes=[psk(pg)], signal=(k == 7 and t2 == 1))
                            gr = 8 + (2 * jj + half) % 2
                            act(wk_f32(gr), ps(pg), AF.Relu, bias=cst(l, "bf1", j), reads=[psk(pg), ("CST",)], writes=wkk(gr))
                            tt("dve", wk_bf(jj, half * 1024, 1024), wk_f32(gr), wk_f32(gr), ALU.mult, reads=wkk(gr), writes=wkk(jj))
                    if J == 3:
                        state["xh"] = None
                        if si < NSEG - 1:
                            load_xh(par, si + 1)
                    s2 = [load_unit(l, ("f2", J * 8 + jj)) for jj in range(8)]
                    for oc in range(8):
                        for half in range(2):
                            pg = next_ps()
                            for t2 in range(2):
                                for jj in range(8):
                                    mm(ps(pg, t2 * 512, 512), wt(s2[jj], oc), wk_bf(jj, half * 1024 + t2 * 512, 512), jj == 0, jj == 7,
                                       reads=[("R", s2[jj])] + wkk(jj), writes=[psk(pg)], signal=(jj == 7 and t2 == 1))
                            acc = big_f32(oc, half * 1024, 1024)
                            if J == 0:
                                cp("dve", acc, ps(pg), [psk(pg)], bgk(2 * oc + half))
                            else:
                                tt("dve", acc, ps(pg), acc, ALU.add, reads=[psk(pg)] + bgk(2 * oc + half), writes=bgk(2 * oc + half))
                for oc in range(8):
                    for half in range(2):
                        gx = 9 + (2 * oc + half) % 2
                        acc = big_f32(oc, half * 1024, 1024)
                        P.dma("sp", wk_f32(gx), xd[si * 8 + oc, :, half * 1024: half * 1024 + 1024], [("XD", si, oc, half)], wkk(gx), ("XDR", gx))
                        act(acc, acc, AF.Identity, bias=cst(l, "bf2", oc), reads=bgk(2 * oc + half) + [("CST",)], writes=bgk(2 * oc + half))
                        stt("dve", acc, wk_f32(gx), ALPHA, acc, ALU.mult, ALU.add, reads=wkk(gx) + bgk(2 * oc + half), writes=bgk(2 * oc + half))
                if not last:
                    layer_norm(8, 0, lambda c, lo, n: big_f32(c, lo, n), lambda c, half: bgk(2 * c + half), ln_x_cb("l2g", "l2b", 1 - par), [0, 1, 2, 3, 8, 9])
                else:
                    def fin_cb(c, half, Tap, Tk):
                        act(wk_f32(4 + c), Tap, AF.Identity, bias=cst(l, "l2b", c), scale=cst(l, "l2g", c), reads=Tk + [("CST",)], writes=wkk(4 + c))
                        if c == 7:
                            for tb in range(8):
                                pg = next_ps()
                                for cc in range(8):
                                    tr(ps(pg, cc * 128, 128), wk_f32(4 + cc, tb * 128, 128), IDF[:], reads=wkk(4 + cc) + [("IDF",)], writes=[psk(pg)],
                                       signal=(cc == 7))
                                go = 12 if tb % 2 == 0 else 3
                                if tb % 2 == 0:
                                    cp("dve", wk_f32(go), ps(pg), [psk(pg)], wkk(go))
                                else:
                                    act(wk_f32(go), ps(pg), AF.Identity, reads=[psk(pg)], writes=wkk(go))
                                tok0 = half * 1024 + tb * 128
                                dep = P.dma("sp", yout[si, tok0: tok0 + 128, :], wk_f32(go), wkk(go), [("YO", si, half, tb)], ("YO", tb % 2))
                                out_deps.append(dep)
                    layer_norm(8, 0, lambda c, lo, n: big_f32(c, lo, n), lambda c, half: bgk(2 * c + half), fin_cb, [0, 1, 2, 3])

        fin = {}
        for s, v in out_deps:
            fin[s] = max(fin.get(s, 0), v)
        P.final_wait("sp", list(fin.items()))
        with nc.Block() as block:
            P.emit(block)
    return nc


def _colchunk(W, j, nk):
    blk = W[:, j * 128:(j + 1) * 128].reshape(nk, 128, 128)
    out = np.zeros((128, 1024), np.float32)
    out[:, : nk * 128] = blk.transpose(1, 0, 2).reshape(128, nk * 128)
    return out


def _prep_weights(inp):
    wu = np.zeros((L, NU, 128, 1024), np.float32)
    cst = np.zeros((L, 128, NCST), np.float32)
    rows = np.zeros((L, 1, NROW), np.float32)
    for l in range(L):
        w_in = inp["w_in"][l]
        for h in range(8):
            wu[l, UNITS[("x", h)]] = _colchunk(w_in, OFF_RG_X // 128 + h, 8)
            wu[l, UNITS[("g", h)]] = _colchunk(w_in, OFF_RG_G // 128 + h, 8)
            gt = np.zeros((128, 1024), np.float32)
            gt[:, 0:128] = inp["rg_wa"][l, 0, h]
            gt[:, 128:256] = inp["rg_wx"][l, 0, h]
            gt[:, 256:384] = inp["rg_wa"][l, 1, h]
            gt[:, 384:512] = inp["rg_wx"][l, 1, h]
            wu[l, UNITS[("gt", h)]] = gt
        for ch in range(4):
            wu[l, UNITS[("u", ch)]] = _colchunk(w_in, OFF_SG // 128 + ch, 8)
            wu[l, UNITS[("v", ch)]] = _colchunk(w_in, OFF_SG // 128 + 4 + ch, 8)
            wu[l, UNITS[("c", ch)]] = _colchunk(w_in, OFF_CC // 128 + ch, 8)
            wu[l, UNITS[("cg", ch)]] = _colchunk(w_in, OFF_CC // 128 + 4 + ch, 8)
        for oc in range(8):
            wu[l, UNITS[("ga", oc)]] = _colchunk(w_in, OFF_GATE // 128 + oc, 8)
            wu[l, UNITS[("gb", oc)]] = _colchunk(w_in, OFF_GATE // 128 + 8 + oc, 8)
            wu[l, UNITS[("gc", oc)]] = _colchunk(w_in, OFF_GATE // 128 + 16 + oc, 8)
            wu[l, UNITS[("ba", oc)]] = _colchunk(inp["w_ba"][l], oc, 8)
            wu[l, UNITS[("bb", oc)]] = _colchunk(inp["w_bb"][l], oc, 4)
            wu[l, UNITS[("bc", oc)]] = _colchunk(inp["w_bc"][l], oc, 4)
            wu[l, UNITS[("o", oc)]] = _colchunk(inp["w_o"][l], oc, 8)
        for j in range(32):
            wu[l, UNITS[("f1", j)]] = _colchunk(inp["w_ff1"][l], j, 8)
            wu[l, UNITS[("f2", j)]] = inp["w_ff2"][l][j * 128:(j + 1) * 128, :]
        def put(name, vec, n):
            cst[l, :, _c[name]: _c[name] + n] = np.asarray(vec, np.float32).reshape(n, 128).T
        put("bin", inp["b_in"][l], 56)
        put("caw", inp["conv_a_w"][l].reshape(-1), 32)
        put("cab", inp["conv_a_b"][l], 8)
        put("rba", inp["rg_ba"][l].reshape(-1), 16)
        put("rbx", inp["rg_bx"][l].reshape(-1), 16)
        put("lam", inp["rg_lambda"][l].reshape(-1), 16)
        put("sglg", inp["sg_ln_g"][l], 4)
        put("ccw", inp["conv_c_w"][l].reshape(-1), 124)
        put("ccb", inp["conv_c_b"][l], 4)
        put("cclg", inp["cc_ln_g"][l], 4)
        put("cclb", inp["cc_ln_b"][l], 4)
        put("bo", inp["b_o"][l], 8)
        put("l1g", inp["ln1_g"][l], 8)
        put("l1b", inp["ln1_b"][l], 8)
        put("bf1", inp["b_ff1"][l], 32)
        put("bf2", inp["b_ff2"][l], 8)
        put("l2g", inp["ln2_g"][l], 8)
        put("l2b", inp["ln2_b"][l], 8)
        rows[l, 0, 0:512] = inp["sg_ln_b"][l]
        rows[l, 0, 512:1024] = inp["sg_b"][l].reshape(-1)
        rows[l, 0, 1024:1536] = inp["b_in"][l][OFF_SG + 512: OFF_SG + 1024]
    lnin = np.zeros((128, 16), np.float32)
    lnin[:, 0:8] = np.asarray(inp["ln_in_g"], np.float32).reshape(8, 128).T
    lnin[:, 8:16] = np.asarray(inp["ln_in_b"], np.float32).reshape(8, 128).T
    sgwt = np.ascontiguousarray(np.asarray(inp["sg_w"], np.float32).transpose(0, 1, 3, 2))
    return wu, cst, rows, lnin, sgwt


_NC_CACHE = {}
_SAMPLE_SLOTS = [(c, k) for c in range(4, 8) for k in range(4)]


def _core_inputs(inp):
    xp = inp["x_prompt"].astype(np.float32, copy=False)
    xs = inp["x_sample"].astype(np.float32, copy=False)
    xins = [np.zeros((NSEG, T, D), np.float32) for _ in range(NCORES)]
    links = [np.zeros((128, 16), np.float32) for _ in range(NCORES)]
    xins[0][:] = xp[0].reshape(NSEG, T, D)
    links[0][:, 1:8] = 1.0
    links[0][:, 8:15] = 1.0
    for i, (c, k) in enumerate(_SAMPLE_SLOTS):
        xins[c][k] = xs[i]
    return xins, links


def kernel(**inputs):
    inp = {k: np.asarray(v) for k, v in inputs.items()}
    wu, cst, rows, lnin, sgwt = _prep_weights(inp)
    xins, links = _core_inputs(inp)
    if "nc" not in _NC_CACHE:
        _NC_CACHE["nc"] = build(None)
    nc = _NC_CACHE["nc"]
    ident = np.eye(128, dtype=np.float32)
    used = {0} | {c for c, _ in _SAMPLE_SLOTS}
    wz = np.zeros_like(wu)
    in_maps = [{"xin": xins[c], "link": links[c], "wu": (wu if c in used else wz), "cst": cst, "rows": rows, "lnin": lnin,
                "sgwt": sgwt, "ident": ident} for c in range(NCORES)]
    res = run_bass_kernel_spmd(nc, in_maps, core_ids=list(range(NCORES)))
    y_prompt = np.ascontiguousarray(res.results[0]["yout"]).reshape(1, NSEG * T, D).astype(np.float32, copy=False)
    y_sample = np.zeros((16, T, D), np.float32)
    for i, (c, k) in enumerate(_SAMPLE_SLOTS):
        y_sample[i] = res.results[c]["yout"][k]
    return (y_prompt, y_sample)
```

```python
import numpy as np
import os
KOPT = {"b"}
import concourse.bass as bass
import concourse.mybir as mybir
from concourse.bass_utils import run_bass_kernel_spmd

F32 = mybir.dt.float32
BF16 = mybir.dt.bfloat16
AF = mybir.ActivationFunctionType
ALU = mybir.AluOpType

D = 1024
T = 2048
HALO = 16
NSEG = 8
L = 2
NCORES = 8
ALPHA = float((2 * L) ** 0.25)
EPS = 1e-5
OFF_RG_X, OFF_RG_G, OFF_SG, OFF_CC, OFF_GATE = 0, 1024, 2048, 3072, 4096
NSLOT = 8
NWK = 13

_c = {}
_o = 0
def _add(name, n):
    global _o
    _c[name] = _o
    _o += n
_add("bin", 56); _add("caw", 32); _add("cab", 8); _add("rba", 16); _add("rbx", 16); _add("lam", 16)
_add("sglg", 4); _add("ccw", 124); _add("ccb", 4); _add("cclg", 4); _add("cclb", 4)
_add("bo", 8); _add("l1g", 8); _add("l1b", 8); _add("bf1", 32); _add("bf2", 8); _add("l2g", 8); _add("l2b", 8)
NCST = _o
NROW = 1536

def _units():
    u = {}
    n = 0
    for h in range(8):
        u[("x", h)] = n; u[("g", h)] = n + 1; u[("gt", h)] = n + 2; n += 3
    for ch in range(4):
        u[("u", ch)] = n; n += 1
    for j in range(4):
        u[("v", j)] = n; n += 1
    for ch in range(4):
        u[("c", ch)] = n; u[("cg", ch)] = n + 1; n += 2
    for oc in range(8):
        for k, nm in enumerate(("ga", "gb", "gc", "ba", "bb", "bc")):
            u[(nm, oc)] = n + k
        n += 6
    for oc in range(8):
        u[("o", oc)] = n; n += 1
    for J in range(4):
        for jj in range(8):
            u[("f1", J * 8 + jj)] = n; n += 1
        for jj in range(8):
            u[("f2", J * 8 + jj)] = n; n += 1
    return u, n
UNITS, NU = _units()


class Prog:
    ENGS = ("pe", "act", "dve", "pool", "sp")

    def __init__(self, nc, es):
        self.nc = nc
        self.es = es
        self.streams = {e: [] for e in self.ENGS}
        self.cnt = {e: 0 for e in self.ENGS}
        self.known = {e: {} for e in self.ENGS}
        self.buf = {}
        self.sems = {}
        self.dcnt = {}
        for e in self.ENGS:
            self.sems[e] = es.enter_context(nc.semaphore("s_" + e))
        self.out_deps = []

    def dsem(self, key):
        if key not in self.sems:
            self.sems[key] = self.es.enter_context(self.nc.semaphore("d_%d" % len(self.sems)))
            self.dcnt[key] = 0
        return key

    def _deps(self, engine, reads, writes):
        deps = {}
        def add(d):
            if d is None:
                return
            s, v = d
            if deps.get(s, 0) < v:
                deps[s] = v
        for k in reads:
            st = self.buf.get(k)
            if st:
                add(st["w"])
        for k in writes:
            st = self.buf.get(k)
            if st:
                add(st["w"])
                for s, v in st["r"].items():
                    add((s, v))
        waits = []
        kn = self.known[engine]
        for s, v in deps.items():
            if engine == "pe" and s == "pe":
                continue
            if kn.get(s, 0) < v:
                waits.append((s, v))
                kn[s] = v
        return waits

    def _commit(self, dep, reads, writes):
        for k in writes:
            self.buf[k] = {"w": dep, "r": {}}
        for k in reads:
            st = self.buf.setdefault(k, {"w": None, "r": {}})
            if st["r"].get(dep[0], 0) < dep[1]:
                st["r"][dep[0]] = dep[1]

    def op(self, engine, fn, reads=(), writes=(), signal=True):
        waits = self._deps(engine, reads, writes)
        sems = self.sems
        if signal:
            self.cnt[engine] += 1
            dep = (engine, self.cnt[engine])
        else:
            dep = (engine, self.cnt[engine] + 1)
        semh = sems[engine]
        def thunk(eng, waits=waits, fn=fn, signal=signal):
            for s, v in waits:
                eng.wait_ge(sems[s], v)
            ins = fn(eng)
            if signal:
                ins.then_inc(semh, 1)
        self.streams[engine].append(thunk)
        self._commit(dep, reads, writes)

    def dma(self, engine, out, in_, reads, writes, semkey):
        self.dsem(semkey)
        waits = self._deps(engine, reads, writes)
        self.dcnt[semkey] += 16
        dep = (semkey, self.dcnt[semkey])
        sems = self.sems
        def thunk(eng, waits=waits):
            for s, v in waits:
                eng.wait_ge(sems[s], v)
            eng.dma_start(out=out, in_=in_).then_inc(sems[semkey], 16)
        self.streams[engine].append(thunk)
        self._commit(dep, reads, writes)
        return dep

    def final_wait(self, engine, deps):
        sems = self.sems
        def thunk(eng):
            for s, v in deps:
                eng.wait_ge(sems[s], v)
        self.streams[engine].append(thunk)

    def emit(self, block):
        P = self
        @block.tensor
        def _(e):
            for t in P.streams["pe"]:
                t(e)
        @block.scalar
        def _(e):
            for t in P.streams["act"]:
                t(e)
        @block.vector
        def _(e):
            for t in P.streams["dve"]:
                t(e)
        @block.gpsimd
        def _(e):
            for t in P.streams["pool"]:
                t(e)
        @block.sync
        def _(e):
            for t in P.streams["sp"]:
                t(e)


def build(seg_kinds, debug=False):
    import contextlib
    nc = bass.Bass("TRN2", target_bir_lowering=False)
    xin = nc.dram_tensor("xin", [NSEG, T, D], F32, kind="ExternalInput").ap()
    wu = nc.dram_tensor("wu", [L, NU, 128, 1024], F32, kind="ExternalInput").ap()
    cst_d = nc.dram_tensor("cst", [L, 128, NCST], F32, kind="ExternalInput").ap()
    row_d = nc.dram_tensor("rows", [L, 1, NROW], F32, kind="ExternalInput").ap()
    lnin_d = nc.dram_tensor("lnin", [128, 16], F32, kind="ExternalInput").ap()
    sgwt_d = nc.dram_tensor("sgwt", [L, 4, 128, 128], F32, kind="ExternalInput").ap()
    ident_d = nc.dram_tensor("ident", [128, 128], F32, kind="ExternalInput").ap()
    yout = nc.dram_tensor("yout", [NSEG, T, D], F32, kind="ExternalOutput").ap()
    link_d = nc.dram_tensor("link", [128, 16], F32, kind="ExternalInput").ap()
    xd = nc.dram_tensor("xd", [NSEG * 8, 128, T], F32, kind="Internal").ap()
    xhd = nc.dram_tensor("xhd", [2 * NSEG * 8, 128, T], BF16, kind="Internal").ap()
    hbd = nc.dram_tensor("hbd", [NSEG * 8, 128, T], F32, kind="Internal").ap()

    es = contextlib.ExitStack()
    with es:
        P = Prog(nc, es)
        sb = lambda name, shape, dt: es.enter_context(nc.sbuf_tensor(name, shape, dt))
        XH = sb("XH", [128, 8 * (T + 2 * HALO)], BF16)
        BIG = sb("BIG", [128, 32768], BF16)
        WK = sb("WK", [128, NWK * 2048], BF16)
        PAD = sb("PAD", [128, 2176], BF16)
        DC = sb("DC", [128, 31 * 128], BF16)
        DA = sb("DA", [128, 2 * 4 * 128], BF16)
        RING = sb("RING", [128, NSLOT * 1024], BF16)
        CST = sb("CST", [128, L * NCST], F32)
        CX = sb("CX", [128, L * 64], F32)
        ROWP = sb("ROWP", [33, 512], F32)
        LNIN = sb("LNIN", [128, 16], F32)
        SGWT = sb("SGWT", [128, L * 512], BF16)
        BIASM = sb("BIASM", [128, L * 512], F32)
        GMAT = sb("GMAT", [128, L * 512], F32)
        IDB = sb("IDB", [128, 128], BF16)
        IDF = sb("IDF", [128, 128], F32)
        ONESD = sb("ONESD", [128, 256], F32)
        ONE = sb("ONE", [128, 128], F32)
        ONESB = sb("ONESB", [128, 256], BF16)
        ONEB = sb("ONEB", [33, 128], BF16)
        ROWPB = sb("ROWPB", [33, 1024], BF16)
        KF = sb("KF", [128, 8], F32)
        SM = sb("SM", [128, 64], F32)
        LINK = sb("LINK", [128, 16], F32)
        CF = sb("CF", [128, 8], F32)
        CB = sb("CB", [128, 8], F32)
        HT = sb("HT", [128, 64], F32)
        PS = es.enter_context(nc.psum_tensor("PS", [128, 4096], F32))

        XW = T + 2 * HALO
        def xh(c, lo, n):
            return XH[:, c * XW + lo: c * XW + lo + n]
        def big_bf(g, lo=0, n=2048):
            return BIG[:, g * 2048 + lo: g * 2048 + lo + n]
        def big_f32(c, lo=0, n=2048):
            return BIG[:, c * 4096: (c + 1) * 4096].bitcast(F32)[:, lo: lo + n]
        def wk_bf(g, lo=0, n=2048):
            return WK[:, g * 2048 + lo: g * 2048 + lo + n]
        def wk_f32(g, lo=0, n=1024):
            return WK[:, g * 2048: g * 2048 + 2 * (lo + n)].bitcast(F32)[:, lo: lo + n]
        def wkk(g, n=1):
            return [("W", g + i) for i in range(n)]
        def bgk(g, n=1):
            return [("B", g + i) for i in range(n)]
        def ps(g, lo=0, n=1024):
            return PS[:, g * 1024 + lo: g * 1024 + lo + n]
        psk = lambda g: ("PS", g)
        def cst(l, name, j=0):
            o = l * NCST + _c[name] + j
            return CST[:, o: o + 1]
        def cx(l, o):
            return CX[:, l * 64 + o: l * 64 + o + 1]
        kf = lambda j: KF[:, j: j + 1]

        state = {"psg": 0, "slot": 0, "da": [None, None]}
        def next_ps():
            g = state["psg"]
            state["psg"] = (g + 1) % 4
            return g

        def load_unit(l, key, n=1024):
            s = state["slot"]
            state["slot"] = (s + 1) % NSLOT
            u = UNITS[key]
            P.dma("pool", RING[:, s * 1024: s * 1024 + n], wu[l, u, :, 0:n], reads=[], writes=[("R", s)], semkey=("R", s))
            return s
        def wt(s, k, n=128, width=128):
            return RING[:, s * 1024 + k * width: s * 1024 + k * width + n]

        def act(out, in_, func, bias=None, scale=1.0, reads=(), writes=()):
            def fn(e):
                kw = {}
                if bias is not None:
                    kw["bias"] = bias
                return e.activation(out=out, in_=in_, func=func, scale=scale, **kw)
            P.op("act", fn, reads, writes)
        def tt(eng, out, in0, in1, op, reads=(), writes=()):
            P.op(eng, lambda e: e.tensor_tensor(out=out, in0=in0, in1=in1, op=op), reads, writes)
        def ts(eng, out, in0, s1, s2, op0, op1=None, reads=(), writes=()):
            if op1 is None:
                P.op(eng, lambda e: e.tensor_scalar(out=out, in0=in0, scalar1=s1, scalar2=None, op0=op0), reads, writes)
            else:
                P.op(eng, lambda e: e.tensor_scalar(out=out, in0=in0, scalar1=s1, scalar2=s2, op0=op0, op1=op1), reads, writes)
        def stt(eng, out, in0, scalar, in1, op0, op1, reads=(), writes=()):
            P.op(eng, lambda e: e.scalar_tensor_tensor(out=out, in0=in0, scalar=scalar, in1=in1, op0=op0, op1=op1), reads, writes)
        def cp(eng, out, in_, reads=(), writes=()):
            P.op(eng, lambda e: e.tensor_copy(out=out, in_=in_), reads, writes)
        def mm(out, lhsT, rhs, start, stop, reads=(), writes=(), signal=False):
            P.op("pe", lambda e: e.matmul(out, lhsT, rhs, start=start, stop=stop), reads, writes, signal=signal)
        def tr(out, in_, ident, reads=(), writes=(), signal=False):
            P.op("pe", lambda e: e.transpose(out, in_, ident), reads, writes, signal=signal)
        def memset(eng, ap, val, writes=()):
            P.op(eng, lambda e: e.memset(ap, val), (), writes)

        memset("dve", KF[:, 0:1], 1.0, [("KF",)])
        memset("dve", KF[:, 1:2], EPS, [("KF",)])
        memset("dve", KF[:, 2:3], 0.0, [("KF",)])
        memset("dve", KF[:, 3:4], -0.5, [("KF",)])
        memset("dve", KF[:, 4:5], 0.5, [("KF",)])
        memset("dve", ONE[:], 1.0, [("ONE",)])
        memset("dve", ONESD[:, 0:128], 1.0 / 1024.0, [("ONESD",)])
        memset("dve", ONESD[:, 128:256], 1.0 / 512.0, [("ONESD",)])
        memset("dve", ONESB[:, 0:128], 1.0 / 1024.0, [("ONESD",)])
        memset("dve", ONESB[:, 128:256], 1.0 / 512.0, [("ONESD",)])
        P.dma("sp", IDF[:], ident_d, [], [("IDF",)], ("IDF",))
        cp("dve", IDB[:], IDF[:], [("IDF",)], [("IDB",)])
        P.dma("sp", CST[:].rearrange("p (l n) -> p l n", l=L), cst_d.rearrange("l p n -> p l n"), [], [("CST",)], ("CST",))
        for l in range(L):
            P.dma("sp", ROWP[32 * l: 32 * l + 1, :], row_d[l, :, 1024:1536], [], [("ROWP",)], ("ROWP",))
        P.dma("sp", LNIN[:], lnin_d, [], [("LNIN",)], ("LNIN",))
        memset("dve", ONEB[:], 1.0, [("ONEB",)])
        for l in range(L):
            pr = slice(32 * l, 32 * l + 1)
            cp("dve", ROWPB[pr, 0:512], ROWP[pr, :], [("ROWP",)], [("ROWPB",)])
            tt("dve", ROWP[pr, :], ROWP[pr, :], ROWPB[pr, 0:512], ALU.subtract, reads=[("ROWP",), ("ROWPB",)], writes=[("ROWP",)])
            cp("dve", ROWPB[pr, 512:1024], ROWP[pr, :], [("ROWP",)], [("ROWPB",)])
        for l in range(L):
            lamv = CST[:, l * NCST + _c["lam"]: l * NCST + _c["lam"] + 16]
            c1 = CX[:, l * 64: l * 64 + 16]
            act(c1, lamv, AF.Exp, scale=-1.0, reads=[("CST",)], writes=[("CX",)])
            act(c1, c1, AF.Ln, bias=kf(0), reads=[("CX",), ("KF",)], writes=[("CX",)])
            ts("dve", c1, c1, -4.0, None, ALU.mult, reads=[("CX",)], writes=[("CX",)])
            ts("dve", CX[:, l * 64 + 48: l * 64 + 64], c1, 2.0, None, ALU.mult, reads=[("CX",)], writes=[("CX",)])
            for nm, o in (("rba", 16), ("rbx", 32)):
                src = CST[:, l * NCST + _c[nm]: l * NCST + _c[nm] + 16]
                ts("dve", CX[:, l * 64 + o: l * 64 + o + 16], src, 0.5, None, ALU.mult, reads=[("CST",)], writes=[("CX",)])
            P.dma("pool", SGWT[:, l * 512:(l + 1) * 512].rearrange("q (g p) -> q g p", g=4),
                  sgwt_d[l].rearrange("g q p -> q g p"), [], [("SGWT", l)], ("SGWT", l))
            SGWF = wk_f32(2, 0, 512)
            RSR = wk_f32(1, 0, 128)[0:1, :]
            ROWT = wk_f32(0, 0, 1024)[0:1, :]
            P.dma("sp", SGWF.rearrange("q (g p) -> q g p", g=4), sgwt_d[l].rearrange("g q p -> q g p"),
                  [], wkk(2), ("SGWF",))
            P.dma("sp", ROWT, row_d[l, :, 0:1024], [], wkk(0), ("ROWT",))
            for g in range(4):
                pg = next_ps()
                mm(ps(pg, 0, 128)[0:1, :], ONE[:, 0:1], SGWF[:, g * 128:(g + 1) * 128], True, True,
                   reads=[("ONE",)] + wkk(2), writes=[psk(pg)], signal=True)
                cp("dve", RSR, ps(pg, 0, 128)[0:1, :], [psk(pg)], wkk(1))
                pg2 = next_ps()
                mm(ps(pg2, 0, 128), ROWT[:, g * 128:(g + 1) * 128], RSR, True, False,
                   reads=wkk(0) + wkk(1), writes=[psk(pg2)])
                mm(ps(pg2, 0, 128), ONE[0:1, :], ROWT[:, 512 + g * 128: 512 + (g + 1) * 128], False, True,
                   reads=wkk(0) + [("ONE",)], writes=[psk(pg2)], signal=True)
                cp("dve", BIASM[:, l * 512 + g * 128: l * 512 + (g + 1) * 128], ps(pg2, 0, 128), [psk(pg2)], [("BIASM", l)])
                ts("dve", GMAT[:, l * 512 + g * 128: l * 512 + (g + 1) * 128], ONE[:], cst(l, "sglg", g), None, ALU.mult,
                   reads=[("ONE",), ("CST",)], writes=[("GMAT", l)])

        def ln_stats(c, half, nch, onescol, zc, zkeys, tg, pm, pq):
            t0, t1 = tg[2], tg[3]
            zb = tg[4:6] if len(tg) >= 6 else None
            lo = half * 1024
            tq = (t0, t1)[c % 2]
            act(wk_bf(tq, 0, 1024), zc(c, lo, 1024), AF.Square, reads=zkeys(c, half), writes=wkk(tq))
            if zb is not None:
                tz = zb[c % 2]
                cp("dve", wk_bf(tz, 0, 1024), zc(c, lo, 1024), zkeys(c, half), wkk(tz))
                for t2 in range(2):
                    mm(ps(pm, t2 * 512, 512), ONESB[:, onescol:onescol + 128], wk_bf(tz, t2 * 512, 512), c == 0, c == nch - 1,
                       reads=wkk(tz) + [("ONESD",)], writes=[psk(pm)], signal=(c == nch - 1 and t2 == 1))
            else:
                for t2 in range(2):
                    mm(ps(pm, t2 * 512, 512), ONESD[:, onescol:onescol + 128], zc(c, lo + t2 * 512, 512), c == 0, c == nch - 1,
                       reads=zkeys(c, half) + [("ONESD",)], writes=[psk(pm)], signal=(c == nch - 1 and t2 == 1))
            for t2 in range(2):
                mm(ps(pq, t2 * 512, 512), ONESB[:, onescol:onescol + 128], wk_bf(tq, t2 * 512, 512), c == 0, c == nch - 1,
                   reads=wkk(tq) + [("ONESD",)], writes=[psk(pq)], signal=(t2 == 1))

        def ln_finish(half, nch, zc, zkeys, cb, tg, pm, pq):
            gMR, gRS, t0, t1 = tg[0], tg[1], tg[2], tg[3]
            lo = half * 1024
            act(wk_f32(gMR), ps(pm), AF.Identity, reads=[psk(pm)], writes=wkk(gMR))
            tt("dve", wk_f32(t0), wk_f32(gMR), wk_f32(gMR), ALU.mult, reads=wkk(gMR), writes=wkk(t0))
            tt("dve", wk_f32(t0), ps(pq), wk_f32(t0), ALU.subtract, reads=[psk(pq)] + wkk(t0), writes=wkk(t0))
            act(wk_f32(t0), wk_f32(t0), AF.Ln, bias=kf(1), reads=wkk(t0) + [("KF",)], writes=wkk(t0))
            act(wk_f32(gRS), wk_f32(t0), AF.Exp, scale=-0.5, reads=wkk(t0), writes=wkk(gRS))
            tt("dve", wk_f32(gMR), wk_f32(gMR), wk_f32(gRS), ALU.mult, reads=wkk(gMR) + wkk(gRS), writes=wkk(gMR))
            for c in range(nch):
                tq = (t0, t1)[c % 2]
                tt("dve", wk_f32(tq), zc(c, lo, 1024), wk_f32(gRS), ALU.mult, reads=zkeys(c, half) + wkk(gRS), writes=wkk(tq))
                tt("dve", wk_f32(tq), wk_f32(tq), wk_f32(gMR), ALU.subtract, reads=wkk(tq) + wkk(gMR), writes=wkk(tq))
                cb(c, half, wk_f32(tq), wkk(tq))

        def layer_norm(nch, onescol, zc, zkeys, cb, tg):
            for half in range(2):
                pm, pq = next_ps(), next_ps()
                for c in range(nch):
                    ln_stats(c, half, nch, onescol, zc, zkeys, tg, pm, pq)
                ln_finish(half, nch, zc, zkeys, cb, tg, pm, pq)

        def store_x(s, c, half, y, ykeys, par=None, xh_write=True):
            lo = half * 1024
            P.dma("sp", xd[s * 8 + c, :, lo: lo + 1024], y, ykeys, [("XD", s, c, half)], ("XDW", c % 4))
            if xh_write:
                act(xh(c, lo, 1024), y, AF.Identity, reads=ykeys, writes=[("XH", c, half)])
                src, skeys = xh(c, lo, 1024), [("XH", c, half)]
            else:
                gb = 10 + (2 * c + half) % 2
                act(wk_bf(gb, 0, 1024), y, AF.Identity, reads=ykeys, writes=wkk(gb))
                src, skeys = wk_bf(gb, 0, 1024), wkk(gb)
            if par is not None:
                P.dma("sp", xhd[(par * NSEG + s) * 8 + c, :, lo: lo + 1024], src, skeys,
                      [("XHD", par, s, c, half)], ("XHDW", c % 2, half))

        XH3 = XH[:].rearrange("p (c w) -> p c w", c=8)
        def load_xh(par, s):
            if state.get("xh") == (par, s):
                return
            state["xh"] = (par, s)
            for c in range(8):
                P.dma("sp", xh(c, 0, T), xhd[(par * NSEG + s) * 8 + c, :, :], [("XHD", par, s, c, 0), ("XHD", par, s, c, 1)],
                      [("XH", c, 0), ("XH", c, 1)], ("XHL", c % 4))
            sl, sr = max(s - 1, 0), min(s + 1, NSEG - 1)
            bl, br = (par * NSEG + sl) * 8, (par * NSEG + sr) * 8
            P.dma("sp", XH3[:, :, T: T + HALO], xhd[bl: bl + 8, :, T - HALO: T].rearrange("c p t -> p c t"),
                  [("XHD", par, sl, c, 1) for c in range(8)], [("XHH",)], ("XHH", 0))
            P.dma("sp", XH3[:, :, T + HALO: T + 2 * HALO], xhd[br: br + 8, :, 0: HALO].rearrange("c p t -> p c t"),
                  [("XHD", par, sr, c, 0) for c in range(8)], [("XHH",)], ("XHH", 1))
        lkL = lambda s: LINK[:, s: s + 1]
        lkR = lambda s: LINK[:, 8 + s: 9 + s]
        P.dma("sp", LINK[:], link_d, [], [("LINK",)], ("LINK",))

        gXA, gXAB, gR, gI, gA, gH1, gH2 = 0, 2, 3, 5, 7, 9, 11

        def rg_head(l, s, h, mode):
            sx = load_unit(l, ("x", h))
            sg_ = load_unit(l, ("g", h)) if mode == "B" else None
            sgt = load_unit(l, ("gt", h), 512)
            if mode == "B":
                P.dma("sp", wk_f32(gH2, 0, 2048), hbd[s * 8 + h, :, :], [("HBD", s, h)], wkk(gH2, 2), ("HBR",))
            def build_da(l_, h_, buf):
                for k in range(4):
                    ts("dve", DA[:, buf * 512 + k * 128: buf * 512 + (k + 1) * 128], IDB[:], cst(l_, "caw", k * 8 + h_), None, ALU.mult,
                       reads=[("IDB",), ("CST",)], writes=[("DA", buf)])
                state["da"][buf] = (l_, h_)
            if (l, h) not in state["da"]:
                build_da(l, h, 0 if state["da"][0] != (l, (h - 1) % 8) else 1)
            dab = state["da"].index((l, h))
            build_da(l, (h + 1) % 8, 1 - dab)
            bx = cst(l, "bin", OFF_RG_X // 128 + h)
            for half in range(2):
                pg = next_ps()
                for t2 in range(2):
                    for k in range(8):
                        mm(ps(pg, t2 * 512, 512), wt(sx, k), xh(k, half * 1024 + t2 * 512, 512), k == 0, k == 7,
                           reads=[("R", sx), ("XH", k, half)], writes=[psk(pg)], signal=(k == 7 and t2 == 1))
                act(PAD[:, 2 + half * 1024: 2 + half * 1024 + 1024], ps(pg), AF.Identity, bias=bx,
                    reads=[psk(pg), ("CST",)], writes=[("PAD",)])
            pg = next_ps()
            for k in range(8):
                mm(ps(pg, 0, 32), wt(sx, k), xh(k, T, 32), k == 0, k == 7, reads=[("R", sx), ("XHH",)], writes=[psk(pg)], signal=(k == 7))
            act(HT[:, 0:32], ps(pg, 0, 32), AF.Identity, bias=bx, reads=[psk(pg), ("CST",)], writes=[("HT",)])
            if mode == "A" and h == 7 and s > 0:
                load_xh(state["par"], s - 1)
            if "b" in KOPT:
                act(PAD[:, 0:2], HT[:, 14:16], AF.Identity, scale=lkL(s), reads=[("HT",), ("LINK",)], writes=[("PAD",)])
                act(PAD[:, 2 + T: 3 + T], HT[:, 16:17], AF.Identity, scale=lkR(s), reads=[("HT",), ("LINK",)], writes=[("PAD",)])
            else:
                ts("dve", PAD[:, 0:2], HT[:, 14:16], lkL(s), None, ALU.mult, reads=[("HT",), ("LINK",)], writes=[("PAD",)])
                ts("dve", PAD[:, 2 + T: 3 + T], HT[:, 16:17], lkR(s), None, ALU.mult, reads=[("HT",), ("LINK",)], writes=[("PAD",)])
            for half in range(2):
                pg = next_ps()
                for t2 in range(2):
                    for k in range(4):
                        o = k + half * 1024 + t2 * 512
                        mm(ps(pg, t2 * 512, 512), DA[:, dab * 512 + k * 128: dab * 512 + (k + 1) * 128], PAD[:, o: o + 512],
                           k == 0, k == 3, reads=[("DA", dab), ("PAD",)], writes=[psk(pg)], signal=(k == 3 and t2 == 1))
                if "x" in KOPT:
                    act(wk_bf(gXAB, half * 1024, 1024), ps(pg), AF.Identity, bias=cst(l, "cab", h), reads=[psk(pg), ("CST",)], writes=wkk(gXAB))
                    stt("dve", wk_f32(gXA + half), ps(pg), cst(l, "cab", h), wk_f32(gXA + half), ALU.add, ALU.bypass,
                        reads=[psk(pg), ("CST",)], writes=wkk(gXA + half))
                elif "c" in KOPT:
                    act(wk_bf(gXAB, half * 1024, 1024), ps(pg), AF.Identity, bias=cst(l, "cab", h), reads=[psk(pg), ("CST",)], writes=wkk(gXAB))
                    ts("dve", wk_f32(gXA + half), ps(pg), cst(l, "cab", h), None, ALU.add, reads=[psk(pg), ("CST",)], writes=wkk(gXA + half))
                else:
                    act(wk_f32(gXA + half), ps(pg), AF.Identity, bias=cst(l, "cab", h), reads=[psk(pg), ("CST",)], writes=wkk(gXA + half))
                    cp("dve", wk_bf(gXAB, half * 1024, 1024), wk_f32(gXA + half), wkk(gXA + half), wkk(gXAB))
            d = 1 if mode == "A" else 0
            for gi, (gdst, hb_o) in enumerate(((gR, 16), (gI, 32))):
                for half in range(2):
                    pg = next_ps()
                    for t2 in range(2):
                        mm(ps(pg, t2 * 512, 512), wt(sgt, d * 2 + gi), wk_bf(gXAB, half * 1024 + t2 * 512, 512), True, True,
                           reads=[("R", sgt)] + wkk(gXAB), writes=[psk(pg)], signal=(t2 == 1))
                    act(wk_f32(gdst + half), ps(pg), AF.Tanh, bias=cx(l, hb_o + d * 8 + h), scale=0.5,
                        reads=[psk(pg), ("CX",)], writes=wkk(gdst + half))
            c1ap = cx(l, d * 8 + h)
            c2ap = cx(l, 48 + d * 8 + h)
            order = (1, 0) if mode == "A" else (0, 1)
            for half in order:
                act(wk_f32(gA + half), wk_f32(gR + half), AF.Exp, bias=c1ap, scale=c1ap, reads=wkk(gR + half) + [("CX",)], writes=wkk(gA + half))
                stt("dve", wk_f32(gR + half), wk_f32(gA + half), 0.999998, wk_f32(gA + half), ALU.min, ALU.mult, reads=wkk(gA + half), writes=wkk(gR + half))
            for half in order:
                stt("dve", wk_f32(gI + half), wk_f32(gI + half), 1.0, wk_f32(gXA + half), ALU.add, ALU.mult,
                    reads=wkk(gI + half) + wkk(gXA + half), writes=wkk(gI + half))
            for half in order:
                act(wk_f32(gR + half), wk_f32(gR + half), AF.Sqrt, bias=kf(0), scale=-1.0, reads=wkk(gR + half) + [("KF",)], writes=wkk(gR + half))
                stt("dve", wk_f32(gI + half), wk_f32(gI + half), 0.5, wk_f32(gR + half), ALU.mult, ALU.mult,
                    reads=wkk(gI + half) + wkk(gR + half), writes=wkk(gI + half))
            if mode == "A":
                gH = gH1 if h % 2 == 0 else gH2
                H_ = wk_f32(gH, 0, 2048)
                A_, I_ = wk_f32(gA, 0, 2048), wk_f32(gI, 0, 2048)
                ini = CB[:, h: h + 1]
                P.op("dve", lambda e: e.tensor_tensor_scan(out=H_[:, 2047:1023:-1], data0=A_[:, 2047:1023:-1], data1=I_[:, 2047:1023:-1], initial=ini, op0=ALU.mult, op1=ALU.add),
                     wkk(gA + 1) + wkk(gI + 1) + [("CB",)], wkk(gH + 1))
                ini2 = H_[:, 1024:1025]
                P.op("dve", lambda e: e.tensor_tensor_scan(out=H_[:, 1023::-1], data0=A_[:, 1023::-1], data1=I_[:, 1023::-1], initial=ini2, op0=ALU.mult, op1=ALU.add),
                     wkk(gA) + wkk(gI) + wkk(gH + 1), wkk(gH))
                P.dma("sp", hbd[s * 8 + h, :, :], H_, wkk(gH, 2), [("HBD", s, h)], ("HBW", h % 2))
                ts("dve", CB[:, h: h + 1], H_[:, 0:1], lkL(s), None, ALU.mult, reads=wkk(gH) + [("LINK",)], writes=[("CB",)])
                return
            H_ = wk_f32(gH1, 0, 2048)
            A_, I_ = wk_f32(gA, 0, 2048), wk_f32(gI, 0, 2048)
            ini = CF[:, h: h + 1]
            P.op("dve", lambda e: e.tensor_tensor_scan(out=H_[:, 0:1024], data0=A_[:, 0:1024], data1=I_[:, 0:1024], initial=ini, op0=ALU.mult, op1=ALU.add),
                 wkk(gA) + wkk(gI) + [("CF",)], wkk(gH1))
            ini2 = H_[:, 1023:1024]
            P.op("dve", lambda e: e.tensor_tensor_scan(out=H_[:, 1024:2048], data0=A_[:, 1024:2048], data1=I_[:, 1024:2048], initial=ini2, op0=ALU.mult, op1=ALU.add),
                 wkk(gA + 1) + wkk(gI + 1) + wkk(gH1), wkk(gH1 + 1))
            ts("dve", CF[:, h: h + 1], H_[:, T - 1: T], lkR(s), None, ALU.mult, reads=wkk(gH1 + 1) + [("LINK",)], writes=[("CF",)])
            for half in range(2):
                tt("dve", wk_f32(gH1 + half), wk_f32(gH1 + half), wk_f32(gH2 + half), ALU.add, reads=wkk(gH1 + half) + wkk(gH2 + half), writes=wkk(gH1 + half))
            for half in range(2):
                pg = next_ps()
                for t2 in range(2):
                    for k in range(8):
                        mm(ps(pg, t2 * 512, 512), wt(sg_, k), xh(k, half * 1024 + t2 * 512, 512), k == 0, k == 7,
                           reads=[("R", sg_), ("XH", k, half)], writes=[psk(pg)], signal=(k == 7 and t2 == 1))
                act(wk_f32(gR + half), ps(pg), AF.Gelu_apprx_tanh, bias=cst(l, "bin", OFF_RG_G // 128 + h),
                    reads=[psk(pg), ("CST",)], writes=wkk(gR + half))
                tt("dve", big_bf(h, half * 1024, 1024), wk_f32(gH1 + half), wk_f32(gR + half), ALU.mult,
                   reads=wkk(gH1 + half) + wkk(gR + half), writes=bgk(h))

        out_deps = []
        for si in range(NSEG):
            for half in range(2):
                for tb in range(8):
                    g = tb
                    tok0 = half * 1024 + tb * 128
                    xt = wk_f32(g)
                    P.dma("sp", xt, xin[si, tok0: tok0 + 128, :], [], wkk(g), ("XT", tb))
                    P.op("dve", lambda e, xt=xt: e.bn_stats(out=SM[:, 0:6], in_=xt[:, 0:512]), wkk(g), [("SM",)])
                    P.op("dve", lambda e, xt=xt: e.bn_stats(out=SM[:, 6:12], in_=xt[:, 512:1024]), wkk(g), [("SM",)])
                    P.op("dve", lambda e, tb=tb: e.bn_aggr(out=SM[:, 32 + 2 * tb: 34 + 2 * tb], in_=SM[:, 0:12]), [("SM",)], [("SM",), ("SM3",)])
                var8 = SM[:, 32:48].rearrange("p (i t) -> p i t", t=2)[:, :, 1:2]
                rs8 = SM[:, 48:56].rearrange("p (i t) -> p i t", t=1)
                act(rs8, var8, AF.Ln, bias=kf(1), reads=[("SM3",), ("KF",)], writes=[("SM4",)])
                act(SM[:, 48:56], SM[:, 48:56], AF.Exp, scale=-0.5, reads=[("SM4",)], writes=[("SM4",)])
                for tb in range(8):
                    xt = wk_f32(tb)
                    ts("dve", xt, xt, SM[:, 32 + 2 * tb: 33 + 2 * tb], SM[:, 48 + tb: 49 + tb], ALU.subtract, ALU.mult,
                       reads=wkk(tb) + [("SM3",), ("SM4",)], writes=wkk(tb))
                for c in range(8):
                    pg = next_ps()
                    for tb in range(8):
                        tr(ps(pg, tb * 128, 128), wk_f32(tb, c * 128, 128), IDF[:], reads=wkk(tb) + [("IDF",)], writes=[psk(pg)],
                           signal=(tb == 7))
                    yg = (8, 9, 12)[c % 3]
                    act(wk_f32(yg), ps(pg), AF.Identity, bias=LNIN[:, 8 + c: 9 + c], scale=LNIN[:, c: c + 1],
                        reads=[psk(pg), ("LNIN",)], writes=wkk(yg))
                    store_x(si, c, half, wk_f32(yg), wkk(yg), par=0, xh_write=False)

        for l in range(L):
            last = (l == L - 1)
            par = l % 2
            state["par"] = par
            memset("dve", CB[:], 0.0, [("CB",)])
            for si in reversed(range(NSEG)):
                load_xh(par, si)
                for h in range(8):
                    rg_head(l, si, h, "A")
            memset("dve", CF[:], 0.0, [("CF",)])
            for si in range(NSEG):
                load_xh(par, si)
                for h in range(8):
                    rg_head(l, si, h, "B")

                for ch in range(4):
                    su = load_unit(l, ("u", ch))
                    for half in range(2):
                        pg = next_ps()
                        for t2 in range(2):
                            for k in range(8):
                                mm(ps(pg, t2 * 512, 512), wt(su, k), xh(k, half * 1024 + t2 * 512, 512), k == 0, k == 7,
                                   reads=[("R", su), ("XH", k, half)], writes=[psk(pg)], signal=(k == 7 and t2 == 1))
                        act(big_bf(8 + ch, half * 1024, 1024), ps(pg), AF.Gelu_apprx_tanh, bias=cst(l, "bin", OFF_SG // 128 + ch),
                            reads=[psk(pg), ("CST",)], writes=bgk(8 + ch))
                sv = [load_unit(l, ("v", j)) for j in range(4)]
                SGO3 = BIG[:, 8 * 2048: 12 * 2048].rearrange("p (g t) -> p g t", g=4)
                for half in range(2):
                    for i in range(8):
                        tb = half * 8 + i
                        pg = next_ps()
                        for j in range(4):
                            for k in range(8):
                                mm(ps(pg, j * 128, 128), xh(k, tb * 128, 128), wt(sv[j], k), k == 0, False,
                                   reads=[("R", sv[j]), ("XH", k, half)], writes=[psk(pg)])
                            mm(ps(pg, j * 128, 128), ONEB[32 * l: 32 * l + 1, :], ROWPB[32 * l: 32 * l + 1, j * 128:(j + 1) * 128], False, False,
                               reads=[("ROWPB",)], writes=[psk(pg)])
                            mm(ps(pg, j * 128, 128), ONEB[32 * l: 32 * l + 1, :], ROWPB[32 * l: 32 * l + 1, 512 + j * 128: 512 + (j + 1) * 128], False, True,
                               reads=[("ROWPB",)], writes=[psk(pg)], signal=(j == 3))
                        V = wk_f32(i, 0, 512)
                        act(V, ps(pg, 0, 512), AF.Gelu_apprx_tanh, reads=[psk(pg)], writes=wkk(i))
                        P.op("dve", lambda e, V=V: e.bn_stats(out=SM[:, 16:22], in_=V), wkk(i), [("SM2",)])
                        P.op("dve", lambda e, i=i: e.bn_aggr(out=SM[:, 32 + 2 * i: 34 + 2 * i], in_=SM[:, 16:22]), [("SM2",)], [("SM2",), ("SM3",)])
                    var8 = SM[:, 32:48].rearrange("p (i t) -> p i t", t=2)[:, :, 1:2]
                    rs8 = SM[:, 48:56].rearrange("p (i t) -> p i t", t=1)
                    act(rs8, var8, AF.Ln, bias=kf(1), reads=[("SM3",), ("KF",)], writes=[("SM4",)])
                    act(SM[:, 48:56], SM[:, 48:56], AF.Exp, scale=-0.5, reads=[("SM4",)], writes=[("SM4",)])
                    for i in range(8):
                        tb = half * 8 + i
                        V = wk_f32(i, 0, 512)
                        gvn = 8 + i % 2
                        VN = wk_bf(gvn, 0, 512)
                        ts("dve", VN, V, SM[:, 32 + 2 * i: 33 + 2 * i], SM[:, 48 + i: 49 + i], ALU.subtract, ALU.mult,
                           reads=wkk(i) + [("SM3",), ("SM4",)], writes=wkk(gvn))
                        pg2 = next_ps()
                        for g in range(4):
                            mm(ps(pg2, g * 128, 128), VN[:, g * 128:(g + 1) * 128], SGWT[:, l * 512 + g * 128: l * 512 + (g + 1) * 128], True, True,
                               reads=wkk(gvn) + [("SGWT", l)], writes=[psk(pg2)], signal=(g == 3))
                        gt_ = 10 + i % 2
                        TT = wk_f32(gt_, 0, 512)
                        tt("dve", TT, ps(pg2, 0, 512), GMAT[:, l * 512:(l + 1) * 512], ALU.mult, reads=[psk(pg2), ("GMAT", l)], writes=wkk(gt_))
                        tt("dve", TT, TT, BIASM[:, l * 512:(l + 1) * 512], ALU.add, reads=wkk(gt_) + [("BIASM", l)], writes=wkk(gt_))
                        uview = SGO3[:, :, tb * 128:(tb + 1) * 128]
                        tt("dve", uview, TT.rearrange("p (g t) -> p g t", g=4), uview, ALU.mult, reads=wkk(gt_) + bgk(8, 4), writes=bgk(8, 4))

                for ch in range(4):
                    sc = load_unit(l, ("c", ch))
                    scg = load_unit(l, ("cg", ch))
                    for k in range(31):
                        ts("dve", DC[:, k * 128:(k + 1) * 128], IDB[:], cst(l, "ccw", k * 4 + ch), None, ALU.mult,
                           reads=[("IDB",), ("CST",)], writes=[("DC",)])
                    bc_, bcg_ = cst(l, "bin", OFF_CC // 128 + ch), cst(l, "bin", OFF_CC // 128 + 4 + ch)
                    for half in range(2):
                        pc, pgg = next_ps(), next_ps()
                        for (pp, ss) in ((pc, sc), (pgg, scg)):
                            for t2 in range(2):
                                for k in range(8):
                                    mm(ps(pp, t2 * 512, 512), wt(ss, k), xh(k, half * 1024 + t2 * 512, 512), k == 0, k == 7,
                                       reads=[("R", ss), ("XH", k, half)], writes=[psk(pp)], signal=(k == 7 and t2 == 1))
                        gs = 8 + half
                        act(wk_f32(gs), ps(pgg), AF.Sigmoid, bias=bcg_, reads=[psk(pgg), ("CST",)], writes=wkk(gs))
                        stt("dve", PAD[:, 15 + half * 1024: 15 + half * 1024 + 1024], ps(pc), bc_, wk_f32(gs),
                            ALU.add, ALU.mult, reads=[psk(pc), ("CST",)] + wkk(gs), writes=[("PAD",)])
                    pc, pgg = next_ps(), next_ps()
                    for (pp, ss) in ((pc, sc), (pgg, scg)):
                        for k in range(8):
                            mm(ps(pp, 0, 32), wt(ss, k), xh(k, T, 32), k == 0, k == 7, reads=[("R", ss), ("XHH",)], writes=[psk(pp)], signal=(k == 7))
                    act(HT[:, 32:64], ps(pgg, 0, 32), AF.Sigmoid, bias=bcg_, reads=[psk(pgg), ("CST",)], writes=[("HT",)])
                    stt("dve", HT[:, 0:32], ps(pc, 0, 32), bc_, HT[:, 32:64], ALU.add, ALU.mult, reads=[psk(pc), ("CST",), ("HT",)], writes=[("HT",)])
                    ts("dve", PAD[:, 0:15], HT[:, 1:16], lkL(si), None, ALU.mult, reads=[("HT",), ("LINK",)], writes=[("PAD",)])
                    ts("dve", PAD[:, 15 + T: 30 + T], HT[:, 16:31], lkR(si), None, ALU.mult, reads=[("HT",), ("LINK",)], writes=[("PAD",)])
                    for half in range(2):
                        pg = next_ps()
                        for t2 in range(2):
                            for k in range(31):
                                o = k + half * 1024 + t2 * 512
                                mm(ps(pg, t2 * 512, 512), DC[:, k * 128:(k + 1) * 128], PAD[:, o: o + 512], k == 0, k == 30,
                                   reads=[("DC",), ("PAD",)], writes=[psk(pg)], signal=(k == 30 and t2 == 1))
                        act(wk_f32(2 * ch + half), ps(pg), AF.Identity, bias=cst(l, "ccb", ch), reads=[psk(pg), ("CST",)], writes=wkk(2 * ch + half))
                def cc_cb(c, half, Tap, Tk):
                    act(big_bf(12 + c, half * 1024, 1024), Tap, AF.Silu, bias=cst(l, "cclb", c), scale=cst(l, "cclg", c),
                        reads=Tk + [("CST",)], writes=bgk(12 + c))
                layer_norm(4, 128, lambda c, lo, n: wk_f32(2 * c, lo, n), lambda c, half: wkk(2 * c + half), cc_cb, [8, 9, 10, 11])

                for oc in range(8):
                    sl = {nm: load_unit(l, (nm, oc), 1024 if nm in ("ga", "gb", "gc", "ba") else 512) for nm in ("ga", "gb", "gc", "ba", "bb", "bc")}
                    for half in range(2):
                        for bi, (gn, yn, nk, src0) in enumerate((("ga", "ba", 8, 0), ("gb", "bb", 4, 8), ("gc", "bc", 4, 12))):
                            pgt, py = next_ps(), next_ps()
                            for t2 in range(2):
                                for k in range(8):
                                    mm(ps(pgt, t2 * 512, 512), wt(sl[gn], k), xh(k, half * 1024 + t2 * 512, 512), k == 0, k == 7,
                                       reads=[("R", sl[gn]), ("XH", k, half)], writes=[psk(pgt)], signal=(k == 7 and t2 == 1))
                            for t2 in range(2):
                                for k in range(nk):
                                    mm(ps(py, t2 * 512, 512), wt(sl[yn], k), big_bf(src0 + k, half * 1024 + t2 * 512, 512), k == 0, k == nk - 1,
                                       reads=[("R", sl[yn])] + bgk(src0 + k), writes=[psk(py)], signal=(k == nk - 1 and t2 == 1))
                            gg = 8 + bi
                            act(wk_f32(gg), ps(pgt), AF.Sigmoid, bias=cst(l, "bin", OFF_GATE // 128 + bi * 8 + oc), reads=[psk(pgt), ("CST",)], writes=wkk(gg))
                            tt("dve", wk_f32(gg), ps(py), wk_f32(gg), ALU.mult, reads=[psk(py)] + wkk(gg), writes=wkk(gg))
                        tt("dve", wk_f32(8), wk_f32(8), wk_f32(9), ALU.add, reads=wkk(8) + wkk(9), writes=wkk(8))
                        tt("dve", wk_bf(oc, half * 1024, 1024), wk_f32(8), wk_f32(10), ALU.add, reads=wkk(8) + wkk(10), writes=wkk(oc))

                def resid(oc, half, pg, bname, zdst, zkeys_w):
                    gx = 9 + (2 * oc + half) % 2
                    ge = 11 + (2 * oc + half) % 2
                    P.dma("sp", wk_f32(gx), xd[si * 8 + oc, :, half * 1024: half * 1024 + 1024], [("XD", si, oc, half)], wkk(gx), ("XDR", gx))
                    act(wk_f32(ge), ps(pg), AF.Identity, bias=cst(l, bname, oc), reads=[psk(pg), ("CST",)], writes=wkk(ge))
                    stt("dve", zdst, wk_f32(gx), ALPHA, wk_f32(ge), ALU.mult, ALU.add, reads=wkk(gx) + wkk(ge), writes=zkeys_w)
                for oc in range(8):
                    so = load_unit(l, ("o", oc))
                    for half in range(2):
                        pg = next_ps()
                        for t2 in range(2):
                            for k in range(8):
                                mm(ps(pg, t2 * 512, 512), wt(so, k), wk_bf(k, half * 1024 + t2 * 512, 512), k == 0, k == 7,
                                   reads=[("R", so)] + wkk(k), writes=[psk(pg)], signal=(k == 7 and t2 == 1))
                        resid(oc, half, pg, "bo", big_f32(oc, half * 1024, 1024), bgk(2 * oc + half))

                def ln_x_cb(gname, bname, par_out):
                    def cb(c, half, Tap, Tk):
                        yg = 4 + (2 * c + half) % 4
                        act(wk_f32(yg), Tap, AF.Identity, bias=cst(l, bname, c), scale=cst(l, gname, c), reads=Tk + [("CST",)], writes=wkk(yg))
                        store_x(si, c, half, wk_f32(yg), wkk(yg), par=par_out, xh_write=(par_out is None))
                    return cb
                layer_norm(8, 0, lambda c, lo, n: big_f32(c, lo, n), lambda c, half: bgk(2 * c + half), ln_x_cb("l1g", "l1b", None), [0, 1, 2, 3, 8, 9])

                for J in range(4):
                    for jj in range(8):
                        j = J * 8 + jj
                        s1 = load_unit(l, ("f1", j))
                        for half in range(2):
                            pg = next_ps()
                            for t2 in range(2):
                                for k in range(8):
                                    mm(ps(pg, t2 * 512, 512), wt(s1, k), xh(k, half * 1024 + t2 * 512, 512), k == 0, k == 7,
                                       reads=[("R", s1), ("XH", k, half)], writes=[psk(pg)], signal=(k == 7 and t2 == 1))
                            gr = 8 + (2 * jj + half) % 2
                            act(wk_f32(gr), ps(pg), AF.Relu, bias=cst(l, "bf1", j), reads=[psk(pg), ("CST",)], writes=wkk(gr))
                            tt("dve", wk_bf(jj, half * 1024, 1024), wk_f32(gr), wk_f32(gr), ALU.mult, reads=wkk(gr), writes=wkk(jj))
                    if J == 3:
                        state["xh"] = None
                        if si < NSEG - 1:
                            load_xh(par, si + 1)
                    s2 = [load_unit(l, ("f2", J * 8 + jj)) for jj in range(8)]
                    for oc in range(8):
                        for half in range(2):
                            pg = next_ps()
                            for t2 in range(2):
                                for jj in range(8):
                                    mm(ps(pg, t2 * 512, 512), wt(s2[jj], oc), wk_bf(jj, half * 1024 + t2 * 512, 512), jj == 0, jj == 7,
                                       reads=[("R", s2[jj])] + wkk(jj), writes=[psk(pg)], signal=(jj == 7 and t2 == 1))
                            acc = big_f32(oc, half * 1024, 1024)
                            if J == 0:
                                cp("dve", acc, ps(pg), [psk(pg)], bgk(2 * oc + half))
                            else:
                                tt("dve", acc, ps(pg), acc, ALU.add, reads=[psk(pg)] + bgk(2 * oc + half), writes=bgk(2 * oc + half))
                for oc in range(8):
                    for half in range(2):
                        gx = 9 + (2 * oc + half) % 2
                        acc = big_f32(oc, half * 1024, 1024)
                        P.dma("sp", wk_f32(gx), xd[si * 8 + oc, :, half * 1024: half * 1024 + 1024], [("XD", si, oc, half)], wkk(gx), ("XDR", gx))
                        act(acc, acc, AF.Identity, bias=cst(l, "bf2", oc), reads=bgk(2 * oc + half) + [("CST",)], writes=bgk(2 * oc + half))
                        stt("dve", acc, wk_f32(gx), ALPHA, acc, ALU.mult, ALU.add, reads=wkk(gx) + bgk(2 * oc + half), writes=bgk(2 * oc + half))
                if not last:
                    layer_norm(8, 0, lambda c, lo, n: big_f32(c, lo, n), lambda c, half: bgk(2 * c + half), ln_x_cb("l2g", "l2b", 1 - par), [0, 1, 2, 3, 8, 9])
                else:
                    def fin_cb(c, half, Tap, Tk):
                        act(wk_f32(4 + c), Tap, AF.Identity, bias=cst(l, "l2b", c), scale=cst(l, "l2g", c), reads=Tk + [("CST",)], writes=wkk(4 + c))
                        if c == 7:
                            for tb in range(8):
                                pg = next_ps()
                                for cc in range(8):
                                    tr(ps(pg, cc * 128, 128), wk_f32(4 + cc, tb * 128, 128), IDF[:], reads=wkk(4 + cc) + [("IDF",)], writes=[psk(pg)],
                                       signal=(cc == 7))
                                go = 12 if tb % 2 == 0 else 3
                                if tb % 2 == 0:
                                    cp("dve", wk_f32(go), ps(pg), [psk(pg)], wkk(go))
                                else:
                                    act(wk_f32(go), ps(pg), AF.Identity, reads=[psk(pg)], writes=wkk(go))
                                tok0 = half * 1024 + tb * 128
                                dep = P.dma("sp", yout[si, tok0: tok0 + 128, :], wk_f32(go), wkk(go), [("YO", si, half, tb)], ("YO", tb % 2))
                                out_deps.append(dep)
                    layer_norm(8, 0, lambda c, lo, n: big_f32(c, lo, n), lambda c, half: bgk(2 * c + half), fin_cb, [0, 1, 2, 3])

        fin = {}
        for s, v in out_deps:
            fin[s] = max(fin.get(s, 0), v)
        P.final_wait("sp", list(fin.items()))
        with nc.Block() as block:
            P.emit(block)
    return nc


def _colchunk(W, j, nk):
    blk = W[:, j * 128:(j + 1) * 128].reshape(nk, 128, 128)
    out = np.zeros((128, 1024), np.float32)
    out[:, : nk * 128] = blk.transpose(1, 0, 2).reshape(128, nk * 128)
    return out


def _prep_weights(inp):
    wu = np.zeros((L, NU, 128, 1024), np.float32)
    cst = np.zeros((L, 128, NCST), np.float32)
    rows = np.zeros((L, 1, NROW), np.float32)
    for l in range(L):
        w_in = inp["w_in"][l]
        for h in range(8):
            wu[l, UNITS[("x", h)]] = _colchunk(w_in, OFF_RG_X // 128 + h, 8)
            wu[l, UNITS[("g", h)]] = _colchunk(w_in, OFF_RG_G // 128 + h, 8)
            gt = np.zeros((128, 1024), np.float32)
            gt[:, 0:128] = inp["rg_wa"][l, 0, h]
            gt[:, 128:256] = inp["rg_wx"][l, 0, h]
            gt[:, 256:384] = inp["rg_wa"][l, 1, h]
            gt[:, 384:512] = inp["rg_wx"][l, 1, h]
            wu[l, UNITS[("gt", h)]] = gt
        for ch in range(4):
            wu[l, UNITS[("u", ch)]] = _colchunk(w_in, OFF_SG // 128 + ch, 8)
            wu[l, UNITS[("v", ch)]] = _colchunk(w_in, OFF_SG // 128 + 4 + ch, 8)
            wu[l, UNITS[("c", ch)]] = _colchunk(w_in, OFF_CC // 128 + ch, 8)
            wu[l, UNITS[("cg", ch)]] = _colchunk(w_in, OFF_CC // 128 + 4 + ch, 8)
        for oc in range(8):
            wu[l, UNITS[("ga", oc)]] = _colchunk(w_in, OFF_GATE // 128 + oc, 8)
            wu[l, UNITS[("gb", oc)]] = _colchunk(w_in, OFF_GATE // 128 + 8 + oc, 8)
            wu[l, UNITS[("gc", oc)]] = _colchunk(w_in, OFF_GATE // 128 + 16 + oc, 8)
            wu[l, UNITS[("ba", oc)]] = _colchunk(inp["w_ba"][l], oc, 8)
            wu[l, UNITS[("bb", oc)]] = _colchunk(inp["w_bb"][l], oc, 4)
            wu[l, UNITS[("bc", oc)]] = _colchunk(inp["w_bc"][l], oc, 4)
            wu[l, UNITS[("o", oc)]] = _colchunk(inp["w_o"][l], oc, 8)
        for j in range(32):
            wu[l, UNITS[("f1", j)]] = _colchunk(inp["w_ff1"][l], j, 8)
            wu[l, UNITS[("f2", j)]] = inp["w_ff2"][l][j * 128:(j + 1) * 128, :]
        def put(name, vec, n):
            cst[l, :, _c[name]: _c[name] + n] = np.asarray(vec, np.float32).reshape(n, 128).T
        put("bin", inp["b_in"][l], 56)
        put("caw", inp["conv_a_w"][l].reshape(-1), 32)
        put("cab", inp["conv_a_b"][l], 8)
        put("rba", inp["rg_ba"][l].reshape(-1), 16)
        put("rbx", inp["rg_bx"][l].reshape(-1), 16)
        put("lam", inp["rg_lambda"][l].reshape(-1), 16)
        put("sglg", inp["sg_ln_g"][l], 4)
        put("ccw", inp["conv_c_w"][l].reshape(-1), 124)
        put("ccb", inp["conv_c_b"][l], 4)
        put("cclg", inp["cc_ln_g"][l], 4)
        put("cclb", inp["cc_ln_b"][l], 4)
        put("bo", inp["b_o"][l], 8)
        put("l1g", inp["ln1_g"][l], 8)
        put("l1b", inp["ln1_b"][l], 8)
        put("bf1", inp["b_ff1"][l], 32)
        put("bf2", inp["b_ff2"][l], 8)
        put("l2g", inp["ln2_g"][l], 8)
        put("l2b", inp["ln2_b"][l], 8)
        rows[l, 0, 0:512] = inp["sg_ln_b"][l]
        rows[l, 0, 512:1024] = inp["sg_b"][l].reshape(-1)
        rows[l, 0, 1024:1536] = inp["b_in"][l][OFF_SG + 512: OFF_SG + 1024]
    lnin = np.zeros((128, 16), np.float32)
    lnin[:, 0:8] = np.asarray(inp["ln_in_g"], np.float32).reshape(8, 128).T
    lnin[:, 8:16] = np.asarray(inp["ln_in_b"], np.float32).reshape(8, 128).T
    sgwt = np.ascontiguousarray(np.asarray(inp["sg_w"], np.float32).transpose(0, 1, 3, 2))
    return wu, cst, rows, lnin, sgwt


_NC_CACHE = {}
_SAMPLE_SLOTS = [(c, k) for c in range(4, 8) for k in range(4)]


def _core_inputs(inp):
    xp = inp["x_prompt"].astype(np.float32, copy=False)
    xs = inp["x_sample"].astype(np.float32, copy=False)
    xins = [np.zeros((NSEG, T, D), np.float32) for _ in range(NCORES)]
    links = [np.zeros((128, 16), np.float32) for _ in range(NCORES)]
    xins[0][:] = xp[0].reshape(NSEG, T, D)
    links[0][:, 1:8] = 1.0
    links[0][:, 8:15] = 1.0
    for i, (c, k) in enumerate(_SAMPLE_SLOTS):
        xins[c][k] = xs[i]
    return xins, links


def kernel(**inputs):
    inp = {k: np.asarray(v) for k, v in inputs.items()}
    wu, cst, rows, lnin, sgwt = _prep_weights(inp)
    xins, links = _core_inputs(inp)
    if "nc" not in _NC_CACHE:
        _NC_CACHE["nc"] = build(None)
    nc = _NC_CACHE["nc"]
    ident = np.eye(128, dtype=np.float32)
    used = {0} | {c for c, _ in _SAMPLE_SLOTS}
    wz = np.zeros_like(wu)
    in_maps = [{"xin": xins[c], "link": links[c], "wu": (wu if c in used else wz), "cst": cst, "rows": rows, "lnin": lnin,
                "sgwt": sgwt, "ident": ident} for c in range(NCORES)]
    res = run_bass_kernel_spmd(nc, in_maps, core_ids=list(range(NCORES)))
    y_prompt = np.ascontiguousarray(res.results[0]["yout"]).reshape(1, NSEG * T, D).astype(np.float32, copy=False)
    y_sample = np.zeros((16, T, D), np.float32)
    for i, (c, k) in enumerate(_SAMPLE_SLOTS):
        y_sample[i] = res.results[c]["yout"][k]
    return (y_prompt, y_sample)
```

```python
import numpy as np
import concourse.bass as bass
import concourse.mybir as mybir
from concourse.bass_utils import run_bass_kernel_spmd

F32 = mybir.dt.float32
BF16 = mybir.dt.bfloat16
AF = mybir.ActivationFunctionType
ALU = mybir.AluOpType

D = 1024
T = 2048
HALO = 16
NSEG = 8
L = 2
NCORES = 8
ALPHA = float((2 * L) ** 0.25)
EPS = 1e-5
OFF_RG_X, OFF_RG_G, OFF_SG, OFF_CC, OFF_GATE = 0, 1024, 2048, 3072, 4096
NSLOT = 8
NWK = 13

_c = {}
_o = 0
def _add(name, n):
    global _o
    _c[name] = _o
    _o += n
_add("bin", 56); _add("caw", 32); _add("cab", 8); _add("rba", 16); _add("rbx", 16); _add("lam", 16)
_add("sglg", 4); _add("ccw", 124); _add("ccb", 4); _add("cclg", 4); _add("cclb", 4)
_add("bo", 8); _add("l1g", 8); _add("l1b", 8); _add("bf1", 32); _add("bf2", 8); _add("l2g", 8); _add("l2b", 8)
NCST = _o
NROW = 1536

def _units():
    u = {}
    n = 0
    for h in range(8):
        u[("x", h)] = n; u[("g", h)] = n + 1; u[("gt", h)] = n + 2; n += 3
    for ch in range(4):
        u[("u", ch)] = n; n += 1
    for j in range(4):
        u[("v", j)] = n; n += 1
    for ch in range(4):
        u[("c", ch)] = n; u[("cg", ch)] = n + 1; n += 2
    for oc in range(8):
        for k, nm in enumerate(("ga", "gb", "gc", "ba", "bb", "bc")):
            u[(nm, oc)] = n + k
        n += 6
    for oc in range(8):
        u[("o", oc)] = n; n += 1
    for J in range(4):
        for jj in range(8):
            u[("f1", J * 8 + jj)] = n; n += 1
        for jj in range(8):
            u[("f2", J * 8 + jj)] = n; n += 1
    return u, n
UNITS, NU = _units()


class Prog:
    ENGS = ("pe", "act", "dve", "pool", "sp")

    def __init__(self, nc, es):
        self.nc = nc
        self.es = es
        self.streams = {e: [] for e in self.ENGS}
        self.cnt = {e: 0 for e in self.ENGS}
        self.known = {e: {} for e in self.ENGS}
        self.buf = {}
        self.sems = {}
        self.dcnt = {}
        for e in self.ENGS:
            self.sems[e] = es.enter_context(nc.semaphore("s_" + e))
        self.out_deps = []

    def dsem(self, key):
        if key not in self.sems:
            self.sems[key] = self.es.enter_context(self.nc.semaphore("d_%d" % len(self.sems)))
            self.dcnt[key] = 0
        return key

    def _deps(self, engine, reads, writes):
        deps = {}
        def add(d):
            if d is None:
                return
            s, v = d
            if deps.get(s, 0) < v:
                deps[s] = v
        for k in reads:
            st = self.buf.get(k)
            if st:
                add(st["w"])
        for k in writes:
            st = self.buf.get(k)
            if st:
                add(st["w"])
                for s, v in st["r"].items():
                    add((s, v))
        waits = []
        kn = self.known[engine]
        for s, v in deps.items():
            if engine == "pe" and s == "pe":
                continue
            if kn.get(s, 0) < v:
                waits.append((s, v))
                kn[s] = v
        return waits

    def _commit(self, dep, reads, writes):
        for k in writes:
            self.buf[k] = {"w": dep, "r": {}}
        for k in reads:
            st = self.buf.setdefault(k, {"w": None, "r": {}})
            if st["r"].get(dep[0], 0) < dep[1]:
                st["r"][dep[0]] = dep[1]

    def op(self, engine, fn, reads=(), writes=(), signal=True):
        waits = self._deps(engine, reads, writes)
        sems = self.sems
        if signal:
            self.cnt[engine] += 1
            dep = (engine, self.cnt[engine])
        else:
            dep = (engine, self.cnt[engine] + 1)
        semh = sems[engine]
        def thunk(eng, waits=waits, fn=fn, signal=signal):
            for s, v in waits:
                eng.wait_ge(sems[s], v)
            ins = fn(eng)
            if signal:
                ins.then_inc(semh, 1)
        self.streams[engine].append(thunk)
        self._commit(dep, reads, writes)

    def dma(self, engine, out, in_, reads, writes, semkey):
        self.dsem(semkey)
        waits = self._deps(engine, reads, writes)
        self.dcnt[semkey] += 16
        dep = (semkey, self.dcnt[semkey])
        sems = self.sems
        def thunk(eng, waits=waits):
            for s, v in waits:
                eng.wait_ge(sems[s], v)
            eng.dma_start(out=out, in_=in_).then_inc(sems[semkey], 16)
        self.streams[engine].append(thunk)
        self._commit(dep, reads, writes)
        return dep

    def final_wait(self, engine, deps):
        sems = self.sems
        def thunk(eng):
            for s, v in deps:
                eng.wait_ge(sems[s], v)
        self.streams[engine].append(thunk)

    def emit(self, block):
        P = self
        @block.tensor
        def _(e):
            for t in P.streams["pe"]:
                t(e)
        @block.scalar
        def _(e):
            for t in P.streams["act"]:
                t(e)
        @block.vector
        def _(e):
            for t in P.streams["dve"]:
                t(e)
        @block.gpsimd
        def _(e):
            for t in P.streams["pool"]:
                t(e)
        @block.sync
        def _(e):
            for t in P.streams["sp"]:
                t(e)


def build(seg_kinds, debug=False):
    import contextlib
    nc = bass.Bass("TRN2", target_bir_lowering=False)
    xin = nc.dram_tensor("xin", [NSEG, T, D], F32, kind="ExternalInput").ap()
    wu = nc.dram_tensor("wu", [L, NU, 128, 1024], F32, kind="ExternalInput").ap()
    cst_d = nc.dram_tensor("cst", [L, 128, NCST], F32, kind="ExternalInput").ap()
    row_d = nc.dram_tensor("rows", [L, 1, NROW], F32, kind="ExternalInput").ap()
    lnin_d = nc.dram_tensor("lnin", [128, 16], F32, kind="ExternalInput").ap()
    sgwt_d = nc.dram_tensor("sgwt", [L, 4, 128, 128], F32, kind="ExternalInput").ap()
    ident_d = nc.dram_tensor("ident", [128, 128], F32, kind="ExternalInput").ap()
    yout = nc.dram_tensor("yout", [NSEG, T, D], F32, kind="ExternalOutput").ap()
    link_d = nc.dram_tensor("link", [128, 16], F32, kind="ExternalInput").ap()
    xd = nc.dram_tensor("xd", [NSEG * 8, 128, T], F32, kind="Internal").ap()
    xhd = nc.dram_tensor("xhd", [2 * NSEG * 8, 128, T], BF16, kind="Internal").ap()
    hbd = nc.dram_tensor("hbd", [NSEG * 8, 128, T], F32, kind="Internal").ap()
    xad = nc.dram_tensor("xad", [NSEG * 8, 128, T], F32, kind="Internal").ap()

    es = contextlib.ExitStack()
    with es:
        P = Prog(nc, es)
        sb = lambda name, shape, dt: es.enter_context(nc.sbuf_tensor(name, shape, dt))
        XH = sb("XH", [128, 8 * (T + 2 * HALO)], BF16)
        BIG = sb("BIG", [128, 32768], BF16)
        WK = sb("WK", [128, NWK * 2048], BF16)
        PAD = sb("PAD", [128, 2176], BF16)
        DC = sb("DC", [128, 31 * 128], BF16)
        DA = sb("DA", [128, 2 * 4 * 128], BF16)
        RING = sb("RING", [128, NSLOT * 1024], BF16)
        CST = sb("CST", [128, L * NCST], F32)
        CX = sb("CX", [128, L * 64], F32)
        ROWP = sb("ROWP", [33, 512], F32)
        LNIN = sb("LNIN", [128, 16], F32)
        SGWT = sb("SGWT", [128, L * 512], BF16)
        BIASM = sb("BIASM", [128, L * 512], F32)
        GMAT = sb("GMAT", [128, L * 512], F32)
        IDB = sb("IDB", [128, 128], BF16)
        IDF = sb("IDF", [128, 128], F32)
        ONESD = sb("ONESD", [128, 256], F32)
        ONE = sb("ONE", [128, 128], F32)
        ONESB = sb("ONESB", [128, 256], BF16)
        ONEB = sb("ONEB", [33, 128], BF16)
        ROWPB = sb("ROWPB", [33, 1024], BF16)
        KF = sb("KF", [128, 8], F32)
        SM = sb("SM", [128, 64], F32)
        LINK = sb("LINK", [128, 16], F32)
        CF = sb("CF", [128, 8], F32)
        CB = sb("CB", [128, 8], F32)
        HT = sb("HT", [128, 64], F32)
        PS = es.enter_context(nc.psum_tensor("PS", [128, 4096], F32))

        XW = T + 2 * HALO
        def xh(c, lo, n):
            return XH[:, c * XW + lo: c * XW + lo + n]
        def big_bf(g, lo=0, n=2048):
            return BIG[:, g * 2048 + lo: g * 2048 + lo + n]
        def big_f32(c, lo=0, n=2048):
            return BIG[:, c * 4096: (c + 1) * 4096].bitcast(F32)[:, lo: lo + n]
        def wk_bf(g, lo=0, n=2048):
            return WK[:, g * 2048 + lo: g * 2048 + lo + n]
        def wk_f32(g, lo=0, n=1024):
            return WK[:, g * 2048: g * 2048 + 2 * (lo + n)].bitcast(F32)[:, lo: lo + n]
        def wkk(g, n=1):
            return [("W", g + i) for i in range(n)]
        def bgk(g, n=1):
            return [("B", g + i) for i in range(n)]
        def ps(g, lo=0, n=1024):
            return PS[:, g * 1024 + lo: g * 1024 + lo + n]
        psk = lambda g: ("PS", g)
        def cst(l, name, j=0):
            o = l * NCST + _c[name] + j
            return CST[:, o: o + 1]
        def cx(l, o):
            return CX[:, l * 64 + o: l * 64 + o + 1]
        kf = lambda j: KF[:, j: j + 1]

        state = {"psg": 0, "slot": 0, "da": [None, None]}
        def next_ps():
            g = state["psg"]
            state["psg"] = (g + 1) % 4
            return g

        def load_unit(l, key, n=1024):
            s = state["slot"]
            state["slot"] = (s + 1) % NSLOT
            u = UNITS[key]
            P.dma("pool", RING[:, s * 1024: s * 1024 + n], wu[l, u, :, 0:n], reads=[], writes=[("R", s)], semkey=("R", s))
            return s
        def wt(s, k, n=128, width=128):
            return RING[:, s * 1024 + k * width: s * 1024 + k * width + n]

        def act(out, in_, func, bias=None, scale=1.0, reads=(), writes=()):
            def fn(e):
                kw = {}
                if bias is not None:
                    kw["bias"] = bias
                return e.activation(out=out, in_=in_, func=func, scale=scale, **kw)
            P.op("act", fn, reads, writes)
        def tt(eng, out, in0, in1, op, reads=(), writes=()):
            P.op(eng, lambda e: e.tensor_tensor(out=out, in0=in0, in1=in1, op=op), reads, writes)
        def ts(eng, out, in0, s1, s2, op0, op1=None, reads=(), writes=()):
            if op1 is None:
                P.op(eng, lambda e: e.tensor_scalar(out=out, in0=in0, scalar1=s1, scalar2=None, op0=op0), reads, writes)
            else:
                P.op(eng, lambda e: e.tensor_scalar(out=out, in0=in0, scalar1=s1, scalar2=s2, op0=op0, op1=op1), reads, writes)
        def stt(eng, out, in0, scalar, in1, op0, op1, reads=(), writes=()):
            P.op(eng, lambda e: e.scalar_tensor_tensor(out=out, in0=in0, scalar=scalar, in1=in1, op0=op0, op1=op1), reads, writes)
        def cp(eng, out, in_, reads=(), writes=()):
            P.op(eng, lambda e: e.tensor_copy(out=out, in_=in_), reads, writes)
        def mm(out, lhsT, rhs, start, stop, reads=(), writes=(), signal=False):
            P.op("pe", lambda e: e.matmul(out, lhsT, rhs, start=start, stop=stop), reads, writes, signal=signal)
        def tr(out, in_, ident, reads=(), writes=(), signal=False):
            P.op("pe", lambda e: e.transpose(out, in_, ident), reads, writes, signal=signal)
        def memset(eng, ap, val, writes=()):
            P.op(eng, lambda e: e.memset(ap, val), (), writes)

        memset("dve", KF[:, 0:1], 1.0, [("KF",)])
        memset("dve", KF[:, 1:2], EPS, [("KF",)])
        memset("dve", KF[:, 2:3], 0.0, [("KF",)])
        memset("dve", KF[:, 3:4], -0.5, [("KF",)])
        memset("dve", KF[:, 4:5], 0.5, [("KF",)])
        memset("dve", ONE[:], 1.0, [("ONE",)])
        memset("dve", ONESD[:, 0:128], 1.0 / 1024.0, [("ONESD",)])
        memset("dve", ONESD[:, 128:256], 1.0 / 512.0, [("ONESD",)])
        memset("dve", ONESB[:, 0:128], 1.0 / 1024.0, [("ONESD",)])
        memset("dve", ONESB[:, 128:256], 1.0 / 512.0, [("ONESD",)])
        P.dma("sp", IDF[:], ident_d, [], [("IDF",)], ("IDF",))
        cp("dve", IDB[:], IDF[:], [("IDF",)], [("IDB",)])
        P.dma("sp", CST[:].rearrange("p (l n) -> p l n", l=L), cst_d.rearrange("l p n -> p l n"), [], [("CST",)], ("CST",))
        for l in range(L):
            P.dma("sp", ROWP[32 * l: 32 * l + 1, :], row_d[l, :, 1024:1536], [], [("ROWP",)], ("ROWP",))
        P.dma("sp", LNIN[:], lnin_d, [], [("LNIN",)], ("LNIN",))
        memset("dve", ONEB[:], 1.0, [("ONEB",)])
        for l in range(L):
            pr = slice(32 * l, 32 * l + 1)
            cp("dve", ROWPB[pr, 0:512], ROWP[pr, :], [("ROWP",)], [("ROWPB",)])
            tt("dve", ROWP[pr, :], ROWP[pr, :], ROWPB[pr, 0:512], ALU.subtract, reads=[("ROWP",), ("ROWPB",)], writes=[("ROWP",)])
            cp("dve", ROWPB[pr, 512:1024], ROWP[pr, :], [("ROWP",)], [("ROWPB",)])
        for l in range(L):
            lamv = CST[:, l * NCST + _c["lam"]: l * NCST + _c["lam"] + 16]
            c1 = CX[:, l * 64: l * 64 + 16]
            act(c1, lamv, AF.Exp, scale=-1.0, reads=[("CST",)], writes=[("CX",)])
            act(c1, c1, AF.Ln, bias=kf(0), reads=[("CX",), ("KF",)], writes=[("CX",)])
            ts("dve", c1, c1, -4.0, None, ALU.mult, reads=[("CX",)], writes=[("CX",)])
            ts("dve", CX[:, l * 64 + 48: l * 64 + 64], c1, 2.0, None, ALU.mult, reads=[("CX",)], writes=[("CX",)])
            for nm, o in (("rba", 16), ("rbx", 32)):
                src = CST[:, l * NCST + _c[nm]: l * NCST + _c[nm] + 16]
                ts("dve", CX[:, l * 64 + o: l * 64 + o + 16], src, 0.5, None, ALU.mult, reads=[("CST",)], writes=[("CX",)])
            P.dma("pool", SGWT[:, l * 512:(l + 1) * 512].rearrange("q (g p) -> q g p", g=4),
                  sgwt_d[l].rearrange("g q p -> q g p"), [], [("SGWT", l)], ("SGWT", l))
            SGWF = wk_f32(2, 0, 512)
            RSR = wk_f32(1, 0, 128)[0:1, :]
            ROWT = wk_f32(0, 0, 1024)[0:1, :]
            P.dma("sp", SGWF.rearrange("q (g p) -> q g p", g=4), sgwt_d[l].rearrange("g q p -> q g p"),
                  [], wkk(2), ("SGWF",))
            P.dma("sp", ROWT, row_d[l, :, 0:1024], [], wkk(0), ("ROWT",))
            for g in range(4):
                pg = next_ps()
                mm(ps(pg, 0, 128)[0:1, :], ONE[:, 0:1], SGWF[:, g * 128:(g + 1) * 128], True, True,
                   reads=[("ONE",)] + wkk(2), writes=[psk(pg)], signal=True)
                cp("dve", RSR, ps(pg, 0, 128)[0:1, :], [psk(pg)], wkk(1))
                pg2 = next_ps()
                mm(ps(pg2, 0, 128), ROWT[:, g * 128:(g + 1) * 128], RSR, True, False,
                   reads=wkk(0) + wkk(1), writes=[psk(pg2)])
                mm(ps(pg2, 0, 128), ONE[0:1, :], ROWT[:, 512 + g * 128: 512 + (g + 1) * 128], False, True,
                   reads=wkk(0) + [("ONE",)], writes=[psk(pg2)], signal=True)
                cp("dve", BIASM[:, l * 512 + g * 128: l * 512 + (g + 1) * 128], ps(pg2, 0, 128), [psk(pg2)], [("BIASM", l)])
                ts("dve", GMAT[:, l * 512 + g * 128: l * 512 + (g + 1) * 128], ONE[:], cst(l, "sglg", g), None, ALU.mult,
                   reads=[("ONE",), ("CST",)], writes=[("GMAT", l)])

        def ln_stats(c, half, nch, onescol, zc, zkeys, tg, pm, pq):
            t0, t1 = tg[2], tg[3]
            zb = tg[4:6] if len(tg) >= 6 else None
            lo = half * 1024
            tq = (t0, t1)[c % 2]
            act(wk_bf(tq, 0, 1024), zc(c, lo, 1024), AF.Square, reads=zkeys(c, half), writes=wkk(tq))
            if zb is not None:
                tz = zb[c % 2]
                cp("dve", wk_bf(tz, 0, 1024), zc(c, lo, 1024), zkeys(c, half), wkk(tz))
                for t2 in range(2):
                    mm(ps(pm, t2 * 512, 512), ONESB[:, onescol:onescol + 128], wk_bf(tz, t2 * 512, 512), c == 0, c == nch - 1,
                       reads=wkk(tz) + [("ONESD",)], writes=[psk(pm)], signal=(c == nch - 1 and t2 == 1))
            else:
                for t2 in range(2):
                    mm(ps(pm, t2 * 512, 512), ONESD[:, onescol:onescol + 128], zc(c, lo + t2 * 512, 512), c == 0, c == nch - 1,
                       reads=zkeys(c, half) + [("ONESD",)], writes=[psk(pm)], signal=(c == nch - 1 and t2 == 1))
            for t2 in range(2):
                mm(ps(pq, t2 * 512, 512), ONESB[:, onescol:onescol + 128], wk_bf(tq, t2 * 512, 512), c == 0, c == nch - 1,
                   reads=wkk(tq) + [("ONESD",)], writes=[psk(pq)], signal=(t2 == 1))

        def ln_finish(half, nch, zc, zkeys, cb, tg, pm, pq):
            gMR, gRS, t0, t1 = tg[0], tg[1], tg[2], tg[3]
            lo = half * 1024
            act(wk_f32(gMR), ps(pm), AF.Identity, reads=[psk(pm)], writes=wkk(gMR))
            tt("dve", wk_f32(t0), wk_f32(gMR), wk_f32(gMR), ALU.mult, reads=wkk(gMR), writes=wkk(t0))
            tt("dve", wk_f32(t0), ps(pq), wk_f32(t0), ALU.subtract, reads=[psk(pq)] + wkk(t0), writes=wkk(t0))
            act(wk_f32(t0), wk_f32(t0), AF.Ln, bias=kf(1), reads=wkk(t0) + [("KF",)], writes=wkk(t0))
            act(wk_f32(gRS), wk_f32(t0), AF.Exp, scale=-0.5, reads=wkk(t0), writes=wkk(gRS))
            tt("dve", wk_f32(gMR), wk_f32(gMR), wk_f32(gRS), ALU.mult, reads=wkk(gMR) + wkk(gRS), writes=wkk(gMR))
            for c in range(nch):
                tq = (t0, t1)[c % 2]
                tt("dve", wk_f32(tq), zc(c, lo, 1024), wk_f32(gRS), ALU.mult, reads=zkeys(c, half) + wkk(gRS), writes=wkk(tq))
                tt("dve", wk_f32(tq), wk_f32(tq), wk_f32(gMR), ALU.subtract, reads=wkk(tq) + wkk(gMR), writes=wkk(tq))
                cb(c, half, wk_f32(tq), wkk(tq))

        def layer_norm(nch, onescol, zc, zkeys, cb, tg):
            for half in range(2):
                pm, pq = next_ps(), next_ps()
                for c in range(nch):
                    ln_stats(c, half, nch, onescol, zc, zkeys, tg, pm, pq)
                ln_finish(half, nch, zc, zkeys, cb, tg, pm, pq)

        def store_x(s, c, half, y, ykeys, par=None, xh_write=True):
            lo = half * 1024
            P.dma("sp", xd[s * 8 + c, :, lo: lo + 1024], y, ykeys, [("XD", s, c, half)], ("XDW",) + tuple(ykeys[0]))
            if xh_write:
                act(xh(c, lo, 1024), y, AF.Identity, reads=ykeys, writes=[("XH", c, half)])
                src, skeys, sk = xh(c, lo, 1024), [("XH", c, half)], ("XHDW", c, half)
            else:
                gb = 10 + c % 2
                act(wk_bf(gb, 0, 1024), y, AF.Identity, reads=ykeys, writes=wkk(gb))
                src, skeys, sk = wk_bf(gb, 0, 1024), wkk(gb), ("XHDW", gb)
            if par is not None:
                P.dma("sp", xhd[(par * NSEG + s) * 8 + c, :, lo: lo + 1024], src, skeys,
                      [("XHD", par, s, c, half)], sk)

        XH3 = XH[:].rearrange("p (c w) -> p c w", c=8)
        def load_xh(par, s):
            if state.get("xh") == (par, s):
                return
            state["xh"] = (par, s)
            for c in range(8):
                P.dma("sp", xh(c, 0, T), xhd[(par * NSEG + s) * 8 + c, :, :], [("XHD", par, s, c, 0), ("XHD", par, s, c, 1)],
                      [("XH", c, 0), ("XH", c, 1)], ("XHL", c % 4))
            sl, sr = max(s - 1, 0), min(s + 1, NSEG - 1)
            bl, br = (par * NSEG + sl) * 8, (par * NSEG + sr) * 8
            P.dma("sp", XH3[:, :, T: T + HALO], xhd[bl: bl + 8, :, T - HALO: T].rearrange("c p t -> p c t"),
                  [("XHD", par, sl, c, 1) for c in range(8)], [("XHH",)], ("XHH", 0))
            P.dma("sp", XH3[:, :, T + HALO: T + 2 * HALO], xhd[br: br + 8, :, 0: HALO].rearrange("c p t -> p c t"),
                  [("XHD", par, sr, c, 0) for c in range(8)], [("XHH",)], ("XHH", 1))
        lkL = lambda s: LINK[:, s: s + 1]
        lkR = lambda s: LINK[:, 8 + s: 9 + s]
        P.dma("sp", LINK[:], link_d, [], [("LINK",)], ("LINK",))

        gXA, gXAB, gR, gI, gA, gH1, gH2 = 0, 2, 3, 5, 7, 9, 11

        def rg_head(l, s, h, mode):
            sx = load_unit(l, ("x", h)) if mode == "A" else None
            sg_ = load_unit(l, ("g", h)) if mode == "B" else None
            sgt = load_unit(l, ("gt", h), 512)
            if mode == "B":
                P.dma("sp", wk_f32(gXA, 0, 2048), xad[s * 8 + h, :, :], [("XAD", s, h)], wkk(gXA, 2), ("XAR",))
                P.dma("sp", wk_f32(gH2, 0, 2048), hbd[s * 8 + h, :, :], [("HBD", s, h)], wkk(gH2, 2), ("HBR",))
                for half in range(2):
                    cp("dve", wk_bf(gXAB, half * 1024, 1024), wk_f32(gXA + half), wkk(gXA + half), wkk(gXAB))
            else:
                def build_da(l_, h_, buf):
                    for k in range(4):
                        ts("dve", DA[:, buf * 512 + k * 128: buf * 512 + (k + 1) * 128], IDB[:], cst(l_, "caw", k * 8 + h_), None, ALU.mult,
                           reads=[("IDB",), ("CST",)], writes=[("DA", buf)])
                    state["da"][buf] = (l_, h_)
                if (l, h) not in state["da"]:
                    build_da(l, h, 0 if state["da"][0] != (l, (h - 1) % 8) else 1)
                dab = state["da"].index((l, h))
                build_da(l, (h + 1) % 8, 1 - dab)
                bx = cst(l, "bin", OFF_RG_X // 128 + h)
                for half in range(2):
                    pg = next_ps()
                    for t2 in range(2):
                        for k in range(8):
                            mm(ps(pg, t2 * 512, 512), wt(sx, k), xh(k, half * 1024 + t2 * 512, 512), k == 0, k == 7,
                               reads=[("R", sx), ("XH", k, half)], writes=[psk(pg)], signal=(k == 7 and t2 == 1))
                    act(PAD[:, 2 + half * 1024: 2 + half * 1024 + 1024], ps(pg), AF.Identity, bias=bx,
                        reads=[psk(pg), ("CST",)], writes=[("PAD",)])
                pg = next_ps()
                for k in range(8):
                    mm(ps(pg, 0, 32), wt(sx, k), xh(k, T, 32), k == 0, k == 7, reads=[("R", sx), ("XHH",)], writes=[psk(pg)], signal=(k == 7))
                act(HT[:, 0:32], ps(pg, 0, 32), AF.Identity, bias=bx, reads=[psk(pg), ("CST",)], writes=[("HT",)])
                if h == 7 and s > 0:
                    load_xh(state["par"], s - 1)
                act(PAD[:, 0:2], HT[:, 14:16], AF.Identity, scale=lkL(s), reads=[("HT",), ("LINK",)], writes=[("PAD",)])
                act(PAD[:, 2 + T: 3 + T], HT[:, 16:17], AF.Identity, scale=lkR(s), reads=[("HT",), ("LINK",)], writes=[("PAD",)])
                for half in range(2):
                    pg = next_ps()
                    for t2 in range(2):
                        for k in range(4):
                            o = k + half * 1024 + t2 * 512
                            mm(ps(pg, t2 * 512, 512), DA[:, dab * 512 + k * 128: dab * 512 + (k + 1) * 128], PAD[:, o: o + 512],
                               k == 0, k == 3, reads=[("DA", dab), ("PAD",)], writes=[psk(pg)], signal=(k == 3 and t2 == 1))
                    act(wk_f32(gXA + half), ps(pg), AF.Identity, bias=cst(l, "cab", h), reads=[psk(pg), ("CST",)], writes=wkk(gXA + half))
                    cp("dve", wk_bf(gXAB, half * 1024, 1024), wk_f32(gXA + half), wkk(gXA + half), wkk(gXAB))
                P.dma("sp", xad[s * 8 + h, :, :], wk_f32(gXA, 0, 2048), wkk(gXA, 2), [("XAD", s, h)], ("XAW", h % 2))
            d = 1 if mode == "A" else 0
            for gi, (gdst, hb_o) in enumerate(((gR, 16), (gI, 32))):
                for half in range(2):
                    pg = next_ps()
                    for t2 in range(2):
                        mm(ps(pg, t2 * 512, 512), wt(sgt, d * 2 + gi), wk_bf(gXAB, half * 1024 + t2 * 512, 512), True, True,
                           reads=[("R", sgt)] + wkk(gXAB), writes=[psk(pg)], signal=(t2 == 1))
                    act(wk_f32(gdst + half), ps(pg), AF.Tanh, bias=cx(l, hb_o + d * 8 + h), scale=0.5,
                        reads=[psk(pg), ("CX",)], writes=wkk(gdst + half))
            c1ap = cx(l, d * 8 + h)
            c2ap = cx(l, 48 + d * 8 + h)
            order = (1, 0) if mode == "A" else (0, 1)
            for half in order:
                act(wk_f32(gA + half), wk_f32(gR + half), AF.Exp, bias=c1ap, scale=c1ap, reads=wkk(gR + half) + [("CX",)], writes=wkk(gA + half))
                stt("dve", wk_f32(gR + half), wk_f32(gA + half), 0.999998, wk_f32(gA + half), ALU.min, ALU.mult, reads=wkk(gA + half), writes=wkk(gR + half))
            for half in order:
                stt("dve", wk_f32(gI + half), wk_f32(gI + half), 1.0, wk_f32(gXA + half), ALU.add, ALU.mult,
                    reads=wkk(gI + half) + wkk(gXA + half), writes=wkk(gI + half))
            for half in order:
                act(wk_f32(gR + half), wk_f32(gR + half), AF.Sqrt, bias=kf(0), scale=-1.0, reads=wkk(gR + half) + [("KF",)], writes=wkk(gR + half))
                stt("dve", wk_f32(gI + half), wk_f32(gI + half), 0.5, wk_f32(gR + half), ALU.mult, ALU.mult,
                    reads=wkk(gI + half) + wkk(gR + half), writes=wkk(gI + half))
            if mode == "A":
                gH = gH1 if h % 2 == 0 else gH2
                H_ = wk_f32(gH, 0, 2048)
                A_, I_ = wk_f32(gA, 0, 2048), wk_f32(gI, 0, 2048)
                ini = CB[:, h: h + 1]
                P.op("dve", lambda e: e.tensor_tensor_scan(out=H_[:, 2047:1023:-1], data0=A_[:, 2047:1023:-1], data1=I_[:, 2047:1023:-1], initial=ini, op0=ALU.mult, op1=ALU.add),
                     wkk(gA + 1) + wkk(gI + 1) + [("CB",)], wkk(gH + 1))
                ini2 = H_[:, 1024:1025]
                P.op("dve", lambda e: e.tensor_tensor_scan(out=H_[:, 1023::-1], data0=A_[:, 1023::-1], data1=I_[:, 1023::-1], initial=ini2, op0=ALU.mult, op1=ALU.add),
                     wkk(gA) + wkk(gI) + wkk(gH + 1), wkk(gH))
                P.dma("sp", hbd[s * 8 + h, :, :], H_, wkk(gH, 2), [("HBD", s, h)], ("HBW", h % 2))
                ts("dve", CB[:, h: h + 1], H_[:, 0:1], lkL(s), None, ALU.mult, reads=wkk(gH) + [("LINK",)], writes=[("CB",)])
                return
            H_ = wk_f32(gH1, 0, 2048)
            A_, I_ = wk_f32(gA, 0, 2048), wk_f32(gI, 0, 2048)
            ini = CF[:, h: h + 1]
            P.op("dve", lambda e: e.tensor_tensor_scan(out=H_[:, 0:1024], data0=A_[:, 0:1024], data1=I_[:, 0:1024], initial=ini, op0=ALU.mult, op1=ALU.add),
                 wkk(gA) + wkk(gI) + [("CF",)], wkk(gH1))
            ini2 = H_[:, 1023:1024]
            P.op("dve", lambda e: e.tensor_tensor_scan(out=H_[:, 1024:2048], data0=A_[:, 1024:2048], data1=I_[:, 1024:2048], initial=ini2, op0=ALU.mult, op1=ALU.add),
                 wkk(gA + 1) + wkk(gI + 1) + wkk(gH1), wkk(gH1 + 1))
            ts("dve", CF[:, h: h + 1], H_[:, T - 1: T], lkR(s), None, ALU.mult, reads=wkk(gH1 + 1) + [("LINK",)], writes=[("CF",)])
            for half in range(2):
                tt("dve", wk_f32(gH1 + half), wk_f32(gH1 + half), wk_f32(gH2 + half), ALU.add, reads=wkk(gH1 + half) + wkk(gH2 + half), writes=wkk(gH1 + half))
            for half in range(2):
                pg = next_ps()
                for t2 in range(2):
                    for k in range(8):
                        mm(ps(pg, t2 * 512, 512), wt(sg_, k), xh(k, half * 1024 + t2 * 512, 512), k == 0, k == 7,
                           reads=[("R", sg_), ("XH", k, half)], writes=[psk(pg)], signal=(k == 7 and t2 == 1))
                act(wk_f32(gR + half), ps(pg), AF.Gelu_apprx_tanh, bias=cst(l, "bin", OFF_RG_G // 128 + h),
                    reads=[psk(pg), ("CST",)], writes=wkk(gR + half))
                tt("dve", big_bf(h, half * 1024, 1024), wk_f32(gH1 + half), wk_f32(gR + half), ALU.mult,
                   reads=wkk(gH1 + half) + wkk(gR + half), writes=bgk(h))

        out_deps = []
        for si in range(NSEG):
            for half in range(2):
                for tb in range(8):
                    g = tb
                    tok0 = half * 1024 + tb * 128
                    xt = wk_f32(g)
                    P.dma("sp", xt, xin[si, tok0: tok0 + 128, :], [], wkk(g), ("XT", tb))
                    P.op("dve", lambda e, xt=xt: e.bn_stats(out=SM[:, 0:6], in_=xt[:, 0:512]), wkk(g), [("SM",)])
                    P.op("dve", lambda e, xt=xt: e.bn_stats(out=SM[:, 6:12], in_=xt[:, 512:1024]), wkk(g), [("SM",)])
                    P.op("dve", lambda e, tb=tb: e.bn_aggr(out=SM[:, 32 + 2 * tb: 34 + 2 * tb], in_=SM[:, 0:12]), [("SM",)], [("SM",), ("SM3",)])
                var8 = SM[:, 32:48].rearrange("p (i t) -> p i t", t=2)[:, :, 1:2]
                rs8 = SM[:, 48:56].rearrange("p (i t) -> p i t", t=1)
                act(rs8, var8, AF.Ln, bias=kf(1), reads=[("SM3",), ("KF",)], writes=[("SM4",)])
                act(SM[:, 48:56], SM[:, 48:56], AF.Exp, scale=-0.5, reads=[("SM4",)], writes=[("SM4",)])
                for tb in range(8):
                    xt = wk_f32(tb)
                    ts("dve", xt, xt, SM[:, 32 + 2 * tb: 33 + 2 * tb], SM[:, 48 + tb: 49 + tb], ALU.subtract, ALU.mult,
                       reads=wkk(tb) + [("SM3",), ("SM4",)], writes=wkk(tb))
                for c in range(8):
                    pg = next_ps()
                    for tb in range(8):
                        tr(ps(pg, tb * 128, 128), wk_f32(tb, c * 128, 128), IDF[:], reads=wkk(tb) + [("IDF",)], writes=[psk(pg)],
                           signal=(tb == 7))
                    yg = (8, 9, 12)[c % 3]
                    act(wk_f32(yg), ps(pg), AF.Identity, bias=LNIN[:, 8 + c: 9 + c], scale=LNIN[:, c: c + 1],
                        reads=[psk(pg), ("LNIN",)], writes=wkk(yg))
                    store_x(si, c, half, wk_f32(yg), wkk(yg), par=0, xh_write=False)

        for l in range(L):
            last = (l == L - 1)
            par = l % 2
            state["par"] = par
            memset("dve", CB[:], 0.0, [("CB",)])
            for si in reversed(range(NSEG)):
                load_xh(par, si)
                for h in range(8):
                    rg_head(l, si, h, "A")
            memset("dve", CF[:], 0.0, [("CF",)])
            for si in range(NSEG):
                load_xh(par, si)
                for h in range(8):
                    rg_head(l, si, h, "B")

                for ch in range(4):
                    su = load_unit(l, ("u", ch))
                    for half in range(2):
                        pg = next_ps()
                        for t2 in range(2):
                            for k in range(8):
                                mm(ps(pg, t2 * 512, 512), wt(su, k), xh(k, half * 1024 + t2 * 512, 512), k == 0, k == 7,
                                   reads=[("R", su), ("XH", k, half)], writes=[psk(pg)], signal=(k == 7 and t2 == 1))
                        act(big_bf(8 + ch, half * 1024, 1024), ps(pg), AF.Gelu_apprx_tanh, bias=cst(l, "bin", OFF_SG // 128 + ch),
                            reads=[psk(pg), ("CST",)], writes=bgk(8 + ch))
                sv = [load_unit(l, ("v", j)) for j in range(4)]
                SGO3 = BIG[:, 8 * 2048: 12 * 2048].rearrange("p (g t) -> p g t", g=4)
                for half in range(2):
                    for i in range(8):
                        tb = half * 8 + i
                        pg = next_ps()
                        for j in range(4):
                            for k in range(8):
                                mm(ps(pg, j * 128, 128), xh(k, tb * 128, 128), wt(sv[j], k), k == 0, False,
                                   reads=[("R", sv[j]), ("XH", k, half)], writes=[psk(pg)])
                            mm(ps(pg, j * 128, 128), ONEB[32 * l: 32 * l + 1, :], ROWPB[32 * l: 32 * l + 1, j * 128:(j + 1) * 128], False, False,
                               reads=[("ROWPB",)], writes=[psk(pg)])
                            mm(ps(pg, j * 128, 128), ONEB[32 * l: 32 * l + 1, :], ROWPB[32 * l: 32 * l + 1, 512 + j * 128: 512 + (j + 1) * 128], False, True,
                               reads=[("ROWPB",)], writes=[psk(pg)], signal=(j == 3))
                        V = wk_f32(i, 0, 512)
                        act(V, ps(pg, 0, 512), AF.Gelu_apprx_tanh, reads=[psk(pg)], writes=wkk(i))
                        P.op("dve", lambda e, V=V: e.bn_stats(out=SM[:, 16:22], in_=V), wkk(i), [("SM2",)])
                        P.op("dve", lambda e, i=i: e.bn_aggr(out=SM[:, 32 + 2 * i: 34 + 2 * i], in_=SM[:, 16:22]), [("SM2",)], [("SM2",), ("SM3",)])
                    var8 = SM[:, 32:48].rearrange("p (i t) -> p i t", t=2)[:, :, 1:2]
                    rs8 = SM[:, 48:56].rearrange("p (i t) -> p i t", t=1)
                    act(rs8, var8, AF.Ln, bias=kf(1), reads=[("SM3",), ("KF",)], writes=[("SM4",)])
                    act(SM[:, 48:56], SM[:, 48:56], AF.Exp, scale=-0.5, reads=[("SM4",)], writes=[("SM4",)])
                    for i in range(8):
                        tb = half * 8 + i
                        V = wk_f32(i, 0, 512)
                        gvn = 8 + i % 2
                        VN = wk_bf(gvn, 0, 512)
                        ts("dve", VN, V, SM[:, 32 + 2 * i: 33 + 2 * i], SM[:, 48 + i: 49 + i], ALU.subtract, ALU.mult,
                           reads=wkk(i) + [("SM3",), ("SM4",)], writes=wkk(gvn))
                        pg2 = next_ps()
                        for g in range(4):
                            mm(ps(pg2, g * 128, 128), VN[:, g * 128:(g + 1) * 128], SGWT[:, l * 512 + g * 128: l * 512 + (g + 1) * 128], True, True,
                               reads=wkk(gvn) + [("SGWT", l)], writes=[psk(pg2)], signal=(g == 3))
                        gt_ = 10 + i % 2
                        TT = wk_f32(gt_, 0, 512)
                        tt("dve", TT, ps(pg2, 0, 512), GMAT[:, l * 512:(l + 1) * 512], ALU.mult, reads=[psk(pg2), ("GMAT", l)], writes=wkk(gt_))
                        tt("dve", TT, TT, BIASM[:, l * 512:(l + 1) * 512], ALU.add, reads=wkk(gt_) + [("BIASM", l)], writes=wkk(gt_))
                        uview = SGO3[:, :, tb * 128:(tb + 1) * 128]
                        tt("dve", uview, TT.rearrange("p (g t) -> p g t", g=4), uview, ALU.mult, reads=wkk(gt_) + bgk(8, 4), writes=bgk(8, 4))

                for ch in range(4):
                    sc = load_unit(l, ("c", ch))
                    scg = load_unit(l, ("cg", ch))
                    for k in range(31):
                        ts("dve", DC[:, k * 128:(k + 1) * 128], IDB[:], cst(l, "ccw", k * 4 + ch), None, ALU.mult,
                           reads=[("IDB",), ("CST",)], writes=[("DC",)])
                    bc_, bcg_ = cst(l, "bin", OFF_CC // 128 + ch), cst(l, "bin", OFF_CC // 128 + 4 + ch)
                    for half in range(2):
                        pc, pgg = next_ps(), next_ps()
                        for (pp, ss) in ((pc, sc), (pgg, scg)):
                            for t2 in range(2):
                                for k in range(8):
                                    mm(ps(pp, t2 * 512, 512), wt(ss, k), xh(k, half * 1024 + t2 * 512, 512), k == 0, k == 7,
                                       reads=[("R", ss), ("XH", k, half)], writes=[psk(pp)], signal=(k == 7 and t2 == 1))
                        gs = 8 + half
                        act(wk_f32(gs), ps(pgg), AF.Sigmoid, bias=bcg_, reads=[psk(pgg), ("CST",)], writes=wkk(gs))
                        stt("dve", PAD[:, 15 + half * 1024: 15 + half * 1024 + 1024], ps(pc), bc_, wk_f32(gs),
                            ALU.add, ALU.mult, reads=[psk(pc), ("CST",)] + wkk(gs), writes=[("PAD",)])
                    pc, pgg = next_ps(), next_ps()
                    for (pp, ss) in ((pc, sc), (pgg, scg)):
                        for k in range(8):
                            mm(ps(pp, 0, 32), wt(ss, k), xh(k, T, 32), k == 0, k == 7, reads=[("R", ss), ("XHH",)], writes=[psk(pp)], signal=(k == 7))
                    act(HT[:, 32:64], ps(pgg, 0, 32), AF.Sigmoid, bias=bcg_, reads=[psk(pgg), ("CST",)], writes=[("HT",)])
                    stt("dve", HT[:, 0:32], ps(pc, 0, 32), bc_, HT[:, 32:64], ALU.add, ALU.mult, reads=[psk(pc), ("CST",), ("HT",)], writes=[("HT",)])
                    ts("dve", PAD[:, 0:15], HT[:, 1:16], lkL(si), None, ALU.mult, reads=[("HT",), ("LINK",)], writes=[("PAD",)])
                    ts("dve", PAD[:, 15 + T: 30 + T], HT[:, 16:31], lkR(si), None, ALU.mult, reads=[("HT",), ("LINK",)], writes=[("PAD",)])
                    for half in range(2):
                        pg = next_ps()
                        for t2 in range(2):
                            for k in range(31):
                                o = k + half * 1024 + t2 * 512
                                mm(ps(pg, t2 * 512, 512), DC[:, k * 128:(k + 1) * 128], PAD[:, o: o + 512], k == 0, k == 30,
                                   reads=[("DC",), ("PAD",)], writes=[psk(pg)], signal=(k == 30 and t2 == 1))
                        act(wk_f32(2 * ch + half), ps(pg), AF.Identity, bias=cst(l, "ccb", ch), reads=[psk(pg), ("CST",)], writes=wkk(2 * ch + half))
                def cc_cb(c, half, Tap, Tk):
                    act(big_bf(12 + c, half * 1024, 1024), Tap, AF.Silu, bias=cst(l, "cclb", c), scale=cst(l, "cclg", c),
                        reads=Tk + [("CST",)], writes=bgk(12 + c))
                layer_norm(4, 128, lambda c, lo, n: wk_f32(2 * c, lo, n), lambda c, half: wkk(2 * c + half), cc_cb, [8, 9, 10, 11])

                for oc in range(8):
                    sl = {nm: load_unit(l, (nm, oc), 1024 if nm in ("ga", "gb", "gc", "ba") else 512) for nm in ("ga", "gb", "gc", "ba", "bb", "bc")}
                    for half in range(2):
                        for bi, (gn, yn, nk, src0) in enumerate((("ga", "ba", 8, 0), ("gb", "bb", 4, 8), ("gc", "bc", 4, 12))):
                            pgt, py = next_ps(), next_ps()
                            for t2 in range(2):
                                for k in range(8):
                                    mm(ps(pgt, t2 * 512, 512), wt(sl[gn], k), xh(k, half * 1024 + t2 * 512, 512), k == 0, k == 7,
                                       reads=[("R", sl[gn]), ("XH", k, half)], writes=[psk(pgt)], signal=(k == 7 and t2 == 1))
                            for t2 in range(2):
                                for k in range(nk):
                                    mm(ps(py, t2 * 512, 512), wt(sl[yn], k), big_bf(src0 + k, half * 1024 + t2 * 512, 512), k == 0, k == nk - 1,
                                       reads=[("R", sl[yn])] + bgk(src0 + k), writes=[psk(py)], signal=(k == nk - 1 and t2 == 1))
                            gg = 8 + bi
                            act(wk_f32(gg), ps(pgt), AF.Sigmoid, bias=cst(l, "bin", OFF_GATE // 128 + bi * 8 + oc), reads=[psk(pgt), ("CST",)], writes=wkk(gg))
                            tt("dve", wk_f32(gg), ps(py), wk_f32(gg), ALU.mult, reads=[psk(py)] + wkk(gg), writes=wkk(gg))
                        tt("dve", wk_f32(8), wk_f32(8), wk_f32(9), ALU.add, reads=wkk(8) + wkk(9), writes=wkk(8))
                        tt("dve", wk_bf(oc, half * 1024, 1024), wk_f32(8), wk_f32(10), ALU.add, reads=wkk(8) + wkk(10), writes=wkk(oc))

                def resid(oc, half, pg, bname, zdst, zkeys_w):
                    gx = 9 + (2 * oc + half) % 2
                    ge = 11 + (2 * oc + half) % 2
                    P.dma("sp", wk_f32(gx), xd[si * 8 + oc, :, half * 1024: half * 1024 + 1024], [("XD", si, oc, half)], wkk(gx), ("XDR", gx))
                    act(wk_f32(ge), ps(pg), AF.Identity, bias=cst(l, bname, oc), reads=[psk(pg), ("CST",)], writes=wkk(ge))
                    stt("dve", zdst, wk_f32(gx), ALPHA, wk_f32(ge), ALU.mult, ALU.add, reads=wkk(gx) + wkk(ge), writes=zkeys_w)
                for oc in range(8):
                    so = load_unit(l, ("o", oc))
                    for half in range(2):
                        pg = next_ps()
                        for t2 in range(2):
                            for k in range(8):
                                mm(ps(pg, t2 * 512, 512), wt(so, k), wk_bf(k, half * 1024 + t2 * 512, 512), k == 0, k == 7,
                                   reads=[("R", so)] + wkk(k), writes=[psk(pg)], signal=(k == 7 and t2 == 1))
                        resid(oc, half, pg, "bo", big_f32(oc, half * 1024, 1024), bgk(2 * oc + half))

                def ln_x_cb(gname, bname, par_out):
                    def cb(c, half, Tap, Tk):
                        yg = 4 + (2 * c + half) % 4
                        act(wk_f32(yg), Tap, AF.Identity, bias=cst(l, bname, c), scale=cst(l, gname, c), reads=Tk + [("CST",)], writes=wkk(yg))
                        store_x(si, c, half, wk_f32(yg), wkk(yg), par=par_out, xh_write=(par_out is None))
                    return cb
                layer_norm(8, 0, lambda c, lo, n: big_f32(c, lo, n), lambda c, half: bgk(2 * c + half), ln_x_cb("l1g", "l1b", None), [0, 1, 2, 3, 8, 9])

                for J in range(4):
                    for jj in range(8):
                        j = J * 8 + jj
                        s1 = load_unit(l, ("f1", j))
                        for half in range(2):
                            pg = next_ps()
                            for t2 in range(2):
                                for k in range(8):
                                    mm(ps(pg, t2 * 512, 512), wt(s1, k), xh(k, half * 1024 + t2 * 512, 512), k == 0, k == 7,
                                       reads=[("R", s1), ("XH", k, half)], writes=[psk(pg)], signal=(k == 7 and t2 == 1))
                            gr = 8 + (2 * jj + half) % 2
                            act(wk_f32(gr), ps(pg), AF.Relu, bias=cst(l, "bf1", j), reads=[psk(pg), ("CST",)], writes=wkk(gr))
                            tt("dve", wk_bf(jj, half * 1024, 1024), wk_f32(gr), wk_f32(gr), ALU.mult, reads=wkk(gr), writes=wkk(jj))
                    if J == 3:
                        state["xh"] = None
                        if si < NSEG - 1:
                            load_xh(par, si + 1)
                    s2 = [load_unit(l, ("f2", J * 8 + jj)) for jj in range(8)]
                    for oc in range(8):
                        for half in range(2):
                            pg = next_ps()
                            for t2 in range(2):
                                for jj in range(8):
                                    mm(ps(pg, t2 * 512, 512), wt(s2[jj], oc), wk_bf(jj, half * 1024 + t2 * 512, 512), jj == 0, jj == 7,
                                       reads=[("R", s2[jj])] + wkk(jj), writes=[psk(pg)], signal=(jj == 7 and t2 == 1))
                            acc = big_f32(oc, half * 1024, 1024)
                            if J == 0:
                                cp("dve", acc, ps(pg), [psk(pg)], bgk(2 * oc + half))
                            else:
                                tt("dve", acc, ps(pg), acc, ALU.add, reads=[psk(pg)] + bgk(2 * oc + half), writes=bgk(2 * oc + half))
                for oc in range(8):
                    for half in range(2):
                        gx = 9 + (2 * oc + half) % 2
                        acc = big_f32(oc, half * 1024, 1024)
                        P.dma("sp", wk_f32(gx), xd[si * 8 + oc, :, half * 1024: half * 1024 + 1024], [("XD", si, oc, half)], wkk(gx), ("XDR", gx))
                        act(acc, acc, AF.Identity, bias=cst(l, "bf2", oc), reads=bgk(2 * oc + half) + [("CST",)], writes=bgk(2 * oc + half))
                        stt("dve", acc, wk_f32(gx), ALPHA, acc, ALU.mult, ALU.add, reads=wkk(gx) + bgk(2 * oc + half), writes=bgk(2 * oc + half))
                if not last:
                    layer_norm(8, 0, lambda c, lo, n: big_f32(c, lo, n), lambda c, half: bgk(2 * c + half), ln_x_cb("l2g", "l2b", 1 - par), [0, 1, 2, 3, 8, 9])
                else:
                    def fin_cb(c, half, Tap, Tk):
                        act(wk_f32(4 + c), Tap, AF.Identity, bias=cst(l, "l2b", c), scale=cst(l, "l2g", c), reads=Tk + [("CST",)], writes=wkk(4 + c))
                        if c == 7:
                            for tb in range(8):
                                pg = next_ps()
                                for cc in range(8):
                                    tr(ps(pg, cc * 128, 128), wk_f32(4 + cc, tb * 128, 128), IDF[:], reads=wkk(4 + cc) + [("IDF",)], writes=[psk(pg)],
                                       signal=(cc == 7))
                                go = 12 if tb % 2 == 0 else 3
                                if tb % 2 == 0:
                                    cp("dve", wk_f32(go), ps(pg), [psk(pg)], wkk(go))
                                else:
                                    act(wk_f32(go), ps(pg), AF.Identity, reads=[psk(pg)], writes=wkk(go))
                                tok0 = half * 1024 + tb * 128
                                dep = P.dma("sp", yout[si, tok0: tok0 + 128, :], wk_f32(go), wkk(go), [("YO", si, half, tb)], ("YO", tb % 2))
                                out_deps.append(dep)
                    layer_norm(8, 0, lambda c, lo, n: big_f32(c, lo, n), lambda c, half: bgk(2 * c + half), fin_cb, [0, 1, 2, 3])

        fin = {}
        for s, v in out_deps:
            fin[s] = max(fin.get(s, 0), v)
        P.final_wait("sp", list(fin.items()))
        with nc.Block() as block:
            P.emit(block)
    return nc


def _colchunk(W, j, nk):
    blk = W[:, j * 128:(j + 1) * 128].reshape(nk, 128, 128)
    out = np.zeros((128, 1024), np.float32)
    out[:, : nk * 128] = blk.transpose(1, 0, 2).reshape(128, nk * 128)
    return out


def _prep_weights(inp):
    wu = np.zeros((L, NU, 128, 1024), np.float32)
    cst = np.zeros((L, 128, NCST), np.float32)
    rows = np.zeros((L, 1, NROW), np.float32)
    for l in range(L):
        w_in = inp["w_in"][l]
        for h in range(8):
            wu[l, UNITS[("x", h)]] = _colchunk(w_in, OFF_RG_X // 128 + h, 8)
            wu[l, UNITS[("g", h)]] = _colchunk(w_in, OFF_RG_G // 128 + h, 8)
            gt = np.zeros((128, 1024), np.float32)
            gt[:, 0:128] = inp["rg_wa"][l, 0, h]
            gt[:, 128:256] = inp["rg_wx"][l, 0, h]
            gt[:, 256:384] = inp["rg_wa"][l, 1, h]
            gt[:, 384:512] = inp["rg_wx"][l, 1, h]
            wu[l, UNITS[("gt", h)]] = gt
        for ch in range(4):
            wu[l, UNITS[("u", ch)]] = _colchunk(w_in, OFF_SG // 128 + ch, 8)
            wu[l, UNITS[("v", ch)]] = _colchunk(w_in, OFF_SG // 128 + 4 + ch, 8)
            wu[l, UNITS[("c", ch)]] = _colchunk(w_in, OFF_CC // 128 + ch, 8)
            wu[l, UNITS[("cg", ch)]] = _colchunk(w_in, OFF_CC // 128 + 4 + ch, 8)
        for oc in range(8):
            wu[l, UNITS[("ga", oc)]] = _colchunk(w_in, OFF_GATE // 128 + oc, 8)
            wu[l, UNITS[("gb", oc)]] = _colchunk(w_in, OFF_GATE // 128 + 8 + oc, 8)
            wu[l, UNITS[("gc", oc)]] = _colchunk(w_in, OFF_GATE // 128 + 16 + oc, 8)
            wu[l, UNITS[("ba", oc)]] = _colchunk(inp["w_ba"][l], oc, 8)
            wu[l, UNITS[("bb", oc)]] = _colchunk(inp["w_bb"][l], oc, 4)
            wu[l, UNITS[("bc", oc)]] = _colchunk(inp["w_bc"][l], oc, 4)
            wu[l, UNITS[("o", oc)]] = _colchunk(inp["w_o"][l], oc, 8)
        for j in range(32):
            wu[l, UNITS[("f1", j)]] = _colchunk(inp["w_ff1"][l], j, 8)
            wu[l, UNITS[("f2", j)]] = inp["w_ff2"][l][j * 128:(j + 1) * 128, :]
        def put(name, vec, n):
            cst[l, :, _c[name]: _c[name] + n] = np.asarray(vec, np.float32).reshape(n, 128).T
        put("bin", inp["b_in"][l], 56)
        put("caw", inp["conv_a_w"][l].reshape(-1), 32)
        put("cab", inp["conv_a_b"][l], 8)
        put("rba", inp["rg_ba"][l].reshape(-1), 16)
        put("rbx", inp["rg_bx"][l].reshape(-1), 16)
        put("lam", inp["rg_lambda"][l].reshape(-1), 16)
        put("sglg", inp["sg_ln_g"][l], 4)
        put("ccw", inp["conv_c_w"][l].reshape(-1), 124)
        put("ccb", inp["conv_c_b"][l], 4)
        put("cclg", inp["cc_ln_g"][l], 4)
        put("cclb", inp["cc_ln_b"][l], 4)
        put("bo", inp["b_o"][l], 8)
        put("l1g", inp["ln1_g"][l], 8)
        put("l1b", inp["ln1_b"][l], 8)
        put("bf1", inp["b_ff1"][l], 32)
        put("bf2", inp["b_ff2"][l], 8)
        put("l2g", inp["ln2_g"][l], 8)
        put("l2b", inp["ln2_b"][l], 8)
        rows[l, 0, 0:512] = inp["sg_ln_b"][l]
        rows[l, 0, 512:1024] = inp["sg_b"][l].reshape(-1)
        rows[l, 0, 1024:1536] = inp["b_in"][l][OFF_SG + 512: OFF_SG + 1024]
    lnin = np.zeros((128, 16), np.float32)
    lnin[:, 0:8] = np.asarray(inp["ln_in_g"], np.float32).reshape(8, 128).T
    lnin[:, 8:16] = np.asarray(inp["ln_in_b"], np.float32).reshape(8, 128).T
    sgwt = np.ascontiguousarray(np.asarray(inp["sg_w"], np.float32).transpose(0, 1, 3, 2))
    return wu, cst, rows, lnin, sgwt


_NC_CACHE = {}
_SAMPLE_SLOTS = [(c, k) for c in range(4, 8) for k in range(4)]


def _core_inputs(inp):
    xp = inp["x_prompt"].astype(np.float32, copy=False)
    xs = inp["x_sample"].astype(np.float32, copy=False)
    xins = [np.zeros((NSEG, T, D), np.float32) for _ in range(NCORES)]
    links = [np.zeros((128, 16), np.float32) for _ in range(NCORES)]
    xins[0][:] = xp[0].reshape(NSEG, T, D)
    links[0][:, 1:8] = 1.0
    links[0][:, 8:15] = 1.0
    for i, (c, k) in enumerate(_SAMPLE_SLOTS):
        xins[c][k] = xs[i]
    return xins, links


def kernel(**inputs):
    inp = {k: np.asarray(v) for k, v in inputs.items()}
    wu, cst, rows, lnin, sgwt = _prep_weights(inp)
    xins, links = _core_inputs(inp)
    if "nc" not in _NC_CACHE:
        _NC_CACHE["nc"] = build(None)
    nc = _NC_CACHE["nc"]
    ident = np.eye(128, dtype=np.float32)
    used = {0} | {c for c, _ in _SAMPLE_SLOTS}
    wz = np.zeros_like(wu)
    in_maps = [{"xin": xins[c], "link": links[c], "wu": (wu if c in used else wz), "cst": cst, "rows": rows, "lnin": lnin,
                "sgwt": sgwt, "ident": ident} for c in range(NCORES)]
    res = run_bass_kernel_spmd(nc, in_maps, core_ids=list(range(NCORES)))
    y_prompt = np.ascontiguousarray(res.results[0]["yout"]).reshape(1, NSEG * T, D).astype(np.float32, copy=False)
    y_sample = np.zeros((16, T, D), np.float32)
    for i, (c, k) in enumerate(_SAMPLE_SLOTS):
        y_sample[i] = res.results[c]["yout"][k]
    return (y_prompt, y_sample)
```

```python
import numpy as np
import concourse.bass as bass
import concourse.mybir as mybir
from concourse.bass_utils import run_bass_kernel_spmd

F32 = mybir.dt.float32
BF16 = mybir.dt.bfloat16
AF = mybir.ActivationFunctionType
ALU = mybir.AluOpType

D = 1024
T = 2048
HALO = 16
NSEG = 8
L = 2
NCORES = 8
ALPHA = float((2 * L) ** 0.25)
EPS = 1e-5
OFF_RG_X, OFF_RG_G, OFF_SG, OFF_CC, OFF_GATE = 0, 1024, 2048, 3072, 4096
NSLOT = 8
NWK = 13

_c = {}
_o = 0
def _add(name, n):
    global _o
    _c[name] = _o
    _o += n
_add("bin", 56); _add("caw", 32); _add("cab", 8); _add("rba", 16); _add("rbx", 16); _add("lam", 16)
_add("sglg", 4); _add("ccw", 124); _add("ccb", 4); _add("cclg", 4); _add("cclb", 4)
_add("bo", 8); _add("l1g", 8); _add("l1b", 8); _add("bf1", 32); _add("bf2", 8); _add("l2g", 8); _add("l2b", 8)
NCST = _o
NROW = 1536

def _units():
    u = {}
    n = 0
    for h in range(8):
        u[("x", h)] = n; u[("g", h)] = n + 1; u[("gt", h)] = n + 2; n += 3
    for ch in range(4):
        u[("u", ch)] = n; n += 1
    for j in range(4):
        u[("v", j)] = n; n += 1
    for ch in range(4):
        u[("c", ch)] = n; u[("cg", ch)] = n + 1; n += 2
    for oc in range(8):
        for k, nm in enumerate(("ga", "gb", "gc", "ba", "bb", "bc")):
            u[(nm, oc)] = n + k
        n += 6
    for oc in range(8):
        u[("o", oc)] = n; n += 1
    for J in range(4):
        for jj in range(8):
            u[("f1", J * 8 + jj)] = n; n += 1
        for jj in range(8):
            u[("f2", J * 8 + jj)] = n; n += 1
    return u, n
UNITS, NU = _units()


class Prog:
    ENGS = ("pe", "act", "dve", "pool", "sp")

    def __init__(self, nc, es):
        self.nc = nc
        self.es = es
        self.streams = {e: [] for e in self.ENGS}
        self.cnt = {e: 0 for e in self.ENGS}
        self.known = {e: {} for e in self.ENGS}
        self.buf = {}
        self.sems = {}
        self.dcnt = {}
        for e in self.ENGS:
            self.sems[e] = es.enter_context(nc.semaphore("s_" + e))
        self.out_deps = []

    def dsem(self, key):
        if key not in self.sems:
            self.sems[key] = self.es.enter_context(self.nc.semaphore("d_%d" % len(self.sems)))
            self.dcnt[key] = 0
        return key

    def _deps(self, engine, reads, writes):
        deps = {}
        def add(d):
            if d is None:
                return
            s, v = d
            if deps.get(s, 0) < v:
                deps[s] = v
        for k in reads:
            st = self.buf.get(k)
            if st:
                add(st["w"])
        for k in writes:
            st = self.buf.get(k)
            if st:
                add(st["w"])
                for s, v in st["r"].items():
                    add((s, v))
        waits = []
        kn = self.known[engine]
        for s, v in deps.items():
            if engine == "pe" and s == "pe":
                continue
            if kn.get(s, 0) < v:
                waits.append((s, v))
                kn[s] = v
        return waits

    def _commit(self, dep, reads, writes):
        for k in writes:
            self.buf[k] = {"w": dep, "r": {}}
        for k in reads:
            st = self.buf.setdefault(k, {"w": None, "r": {}})
            if st["r"].get(dep[0], 0) < dep[1]:
                st["r"][dep[0]] = dep[1]

    def op(self, engine, fn, reads=(), writes=(), signal=True):
        waits = self._deps(engine, reads, writes)
        sems = self.sems
        if signal:
            self.cnt[engine] += 1
            dep = (engine, self.cnt[engine])
        else:
            dep = (engine, self.cnt[engine] + 1)
        semh = sems[engine]
        def thunk(eng, waits=waits, fn=fn, signal=signal):
            for s, v in waits:
                eng.wait_ge(sems[s], v)
            ins = fn(eng)
            if signal:
                ins.then_inc(semh, 1)
        self.streams[engine].append(thunk)
        self._commit(dep, reads, writes)

    def dma(self, engine, out, in_, reads, writes, semkey):
        self.dsem(semkey)
        waits = self._deps(engine, reads, writes)
        prev = self.dcnt[semkey]
        if prev > 0 and self.known[engine].get(semkey, 0) < prev:
            waits.append((semkey, prev))
            self.known[engine][semkey] = prev
        self.dcnt[semkey] += 16
        dep = (semkey, self.dcnt[semkey])
        sems = self.sems
        def thunk(eng, waits=waits):
            for s, v in waits:
                eng.wait_ge(sems[s], v)
            eng.dma_start(out=out, in_=in_).then_inc(sems[semkey], 16)
        self.streams[engine].append(thunk)
        self._commit(dep, reads, writes)
        return dep

    def final_wait(self, engine, deps):
        sems = self.sems
        def thunk(eng):
            for s, v in deps:
                eng.wait_ge(sems[s], v)
        self.streams[engine].append(thunk)

    def emit(self, block):
        P = self
        @block.tensor
        def _(e):
            for t in P.streams["pe"]:
                t(e)
        @block.scalar
        def _(e):
            for t in P.streams["act"]:
                t(e)
        @block.vector
        def _(e):
            for t in P.streams["dve"]:
                t(e)
        @block.gpsimd
        def _(e):
            for t in P.streams["pool"]:
                t(e)
        @block.sync
        def _(e):
            for t in P.streams["sp"]:
                t(e)


def build(seg_kinds, debug=False):
    import contextlib
    nc = bass.Bass("TRN2", target_bir_lowering=False)
    xin = nc.dram_tensor("xin", [NSEG, T, D], F32, kind="ExternalInput").ap()
    wu = nc.dram_tensor("wu", [L, NU, 128, 1024], F32, kind="ExternalInput").ap()
    cst_d = nc.dram_tensor("cst", [L, 128, NCST], F32, kind="ExternalInput").ap()
    row_d = nc.dram_tensor("rows", [L, 1, NROW], F32, kind="ExternalInput").ap()
    lnin_d = nc.dram_tensor("lnin", [128, 16], F32, kind="ExternalInput").ap()
    sgwt_d = nc.dram_tensor("sgwt", [L, 4, 128, 128], F32, kind="ExternalInput").ap()
    ident_d = nc.dram_tensor("ident", [128, 128], F32, kind="ExternalInput").ap()
    yout = nc.dram_tensor("yout", [NSEG, T, D], F32, kind="ExternalOutput").ap()
    link_d = nc.dram_tensor("link", [128, 16], F32, kind="ExternalInput").ap()
    xd = nc.dram_tensor("xd", [NSEG * 8, 128, T], F32, kind="Internal").ap()
    xhd = nc.dram_tensor("xhd", [2 * NSEG * 8, 128, T], BF16, kind="Internal").ap()
    hbd = nc.dram_tensor("hbd", [NSEG * 8, 128, T], F32, kind="Internal").ap()
    xad = nc.dram_tensor("xad", [NSEG * 8, 128, T], F32, kind="Internal").ap()

    es = contextlib.ExitStack()
    with es:
        P = Prog(nc, es)
        sb = lambda name, shape, dt: es.enter_context(nc.sbuf_tensor(name, shape, dt))
        XH = sb("XH", [128, 8 * (T + 2 * HALO)], BF16)
        BIG = sb("BIG", [128, 32768], BF16)
        WK = sb("WK", [128, NWK * 2048], BF16)
        PAD = sb("PAD", [128, 2176], BF16)
        DC = sb("DC", [128, 31 * 128], BF16)
        DA = sb("DA", [128, 2 * 4 * 128], BF16)
        RING = sb("RING", [128, NSLOT * 1024], BF16)
        CST = sb("CST", [128, L * NCST], F32)
        CX = sb("CX", [128, L * 64], F32)
        ROWP = sb("ROWP", [33, 512], F32)
        LNIN = sb("LNIN", [128, 16], F32)
        SGWT = sb("SGWT", [128, L * 512], BF16)
        BIASM = sb("BIASM", [128, L * 512], F32)
        GMAT = sb("GMAT", [128, L * 512], F32)
        IDB = sb("IDB", [128, 128], BF16)
        IDF = sb("IDF", [128, 128], F32)
        ONESD = sb("ONESD", [128, 256], F32)
        ONE = sb("ONE", [128, 128], F32)
        ONESB = sb("ONESB", [128, 256], BF16)
        ONEB = sb("ONEB", [33, 128], BF16)
        ROWPB = sb("ROWPB", [33, 1024], BF16)
        KF = sb("KF", [128, 8], F32)
        SM = sb("SM", [128, 64], F32)
        LINK = sb("LINK", [128, 16], F32)
        CF = sb("CF", [128, 8], F32)
        CB = sb("CB", [128, 8], F32)
        HT = sb("HT", [128, 64], F32)
        PS = es.enter_context(nc.psum_tensor("PS", [128, 4096], F32))

        XW = T + 2 * HALO
        def xh(c, lo, n):
            return XH[:, c * XW + lo: c * XW + lo + n]
        def big_bf(g, lo=0, n=2048):
            return BIG[:, g * 2048 + lo: g * 2048 + lo + n]
        def big_f32(c, lo=0, n=2048):
            return BIG[:, c * 4096: (c + 1) * 4096].bitcast(F32)[:, lo: lo + n]
        def wk_bf(g, lo=0, n=2048):
            return WK[:, g * 2048 + lo: g * 2048 + lo + n]
        def wk_f32(g, lo=0, n=1024):
            return WK[:, g * 2048: g * 2048 + 2 * (lo + n)].bitcast(F32)[:, lo: lo + n]
        def wkk(g, n=1):
            return [("W", g + i) for i in range(n)]
        def bgk(g, n=1):
            return [("B", g + i) for i in range(n)]
        def ps(g, lo=0, n=1024):
            return PS[:, g * 1024 + lo: g * 1024 + lo + n]
        psk = lambda g: ("PS", g)
        def cst(l, name, j=0):
            o = l * NCST + _c[name] + j
            return CST[:, o: o + 1]
        def cx(l, o):
            return CX[:, l * 64 + o: l * 64 + o + 1]
        kf = lambda j: KF[:, j: j + 1]

        state = {"psg": 0, "slot": 0, "da": [None, None]}
        def next_ps():
            g = state["psg"]
            state["psg"] = (g + 1) % 4
            return g

        def load_unit(l, key, n=1024):
            s = state["slot"]
            state["slot"] = (s + 1) % NSLOT
            u = UNITS[key]
            P.dma("pool", RING[:, s * 1024: s * 1024 + n], wu[l, u, :, 0:n], reads=[], writes=[("R", s)], semkey=("R", s))
            return s
        def wt(s, k, n=128, width=128):
            return RING[:, s * 1024 + k * width: s * 1024 + k * width + n]

        def act(out, in_, func, bias=None, scale=1.0, reads=(), writes=()):
            def fn(e):
                kw = {}
                if bias is not None:
                    kw["bias"] = bias
                return e.activation(out=out, in_=in_, func=func, scale=scale, **kw)
            P.op("act", fn, reads, writes)
        def tt(eng, out, in0, in1, op, reads=(), writes=()):
            P.op(eng, lambda e: e.tensor_tensor(out=out, in0=in0, in1=in1, op=op), reads, writes)
        def ts(eng, out, in0, s1, s2, op0, op1=None, reads=(), writes=()):
            if op1 is None:
                P.op(eng, lambda e: e.tensor_scalar(out=out, in0=in0, scalar1=s1, scalar2=None, op0=op0), reads, writes)
            else:
                P.op(eng, lambda e: e.tensor_scalar(out=out, in0=in0, scalar1=s1, scalar2=s2, op0=op0, op1=op1), reads, writes)
        def stt(eng, out, in0, scalar, in1, op0, op1, reads=(), writes=()):
            P.op(eng, lambda e: e.scalar_tensor_tensor(out=out, in0=in0, scalar=scalar, in1=in1, op0=op0, op1=op1), reads, writes)
        def cp(eng, out, in_, reads=(), writes=()):
            P.op(eng, lambda e: e.tensor_copy(out=out, in_=in_), reads, writes)
        def mm(out, lhsT, rhs, start, stop, reads=(), writes=(), signal=False):
            P.op("pe", lambda e: e.matmul(out, lhsT, rhs, start=start, stop=stop), reads, writes, signal=signal)
        def tr(out, in_, ident, reads=(), writes=(), signal=False):
            P.op("pe", lambda e: e.transpose(out, in_, ident), reads, writes, signal=signal)
        def memset(eng, ap, val, writes=()):
            P.op(eng, lambda e: e.memset(ap, val), (), writes)

        memset("dve", KF[:, 0:1], 1.0, [("KF",)])
        memset("dve", KF[:, 1:2], EPS, [("KF",)])
        memset("dve", KF[:, 2:3], 0.0, [("KF",)])
        memset("dve", KF[:, 3:4], -0.5, [("KF",)])
        memset("dve", KF[:, 4:5], 0.5, [("KF",)])
        memset("dve", ONE[:], 1.0, [("ONE",)])
        memset("dve", ONESD[:, 0:128], 1.0 / 1024.0, [("ONESD",)])
        memset("dve", ONESD[:, 128:256], 1.0 / 512.0, [("ONESD",)])
        memset("dve", ONESB[:, 0:128], 1.0 / 1024.0, [("ONESD",)])
        memset("dve", ONESB[:, 128:256], 1.0 / 512.0, [("ONESD",)])
        P.dma("sp", IDF[:], ident_d, [], [("IDF",)], ("IDF",))
        cp("dve", IDB[:], IDF[:], [("IDF",)], [("IDB",)])
        P.dma("sp", CST[:].rearrange("p (l n) -> p l n", l=L), cst_d.rearrange("l p n -> p l n"), [], [("CST",)], ("CST",))
        for l in range(L):
            P.dma("sp", ROWP[32 * l: 32 * l + 1, :], row_d[l, :, 1024:1536], [], [("ROWP",)], ("ROWP",))
        P.dma("sp", LNIN[:], lnin_d, [], [("LNIN",)], ("LNIN",))
        memset("dve", ONEB[:], 1.0, [("ONEB",)])
        for l in range(L):
            pr = slice(32 * l, 32 * l + 1)
            cp("dve", ROWPB[pr, 0:512], ROWP[pr, :], [("ROWP",)], [("ROWPB",)])
            tt("dve", ROWP[pr, :], ROWP[pr, :], ROWPB[pr, 0:512], ALU.subtract, reads=[("ROWP",), ("ROWPB",)], writes=[("ROWP",)])
            cp("dve", ROWPB[pr, 512:1024], ROWP[pr, :], [("ROWP",)], [("ROWPB",)])
        for l in range(L):
            lamv = CST[:, l * NCST + _c["lam"]: l * NCST + _c["lam"] + 16]
            c1 = CX[:, l * 64: l * 64 + 16]
            act(c1, lamv, AF.Exp, scale=-1.0, reads=[("CST",)], writes=[("CX",)])
            act(c1, c1, AF.Ln, bias=kf(0), reads=[("CX",), ("KF",)], writes=[("CX",)])
            ts("dve", c1, c1, -4.0, None, ALU.mult, reads=[("CX",)], writes=[("CX",)])
            ts("dve", CX[:, l * 64 + 48: l * 64 + 64], c1, 2.0, None, ALU.mult, reads=[("CX",)], writes=[("CX",)])
            for nm, o in (("rba", 16), ("rbx", 32)):
                src = CST[:, l * NCST + _c[nm]: l * NCST + _c[nm] + 16]
                ts("dve", CX[:, l * 64 + o: l * 64 + o + 16], src, 0.5, None, ALU.mult, reads=[("CST",)], writes=[("CX",)])
            P.dma("pool", SGWT[:, l * 512:(l + 1) * 512].rearrange("q (g p) -> q g p", g=4),
                  sgwt_d[l].rearrange("g q p -> q g p"), [], [("SGWT", l)], ("SGWT", l))
            SGWF = wk_f32(2, 0, 512)
            RSR = wk_f32(1, 0, 128)[0:1, :]
            ROWT = wk_f32(0, 0, 1024)[0:1, :]
            P.dma("sp", SGWF.rearrange("q (g p) -> q g p", g=4), sgwt_d[l].rearrange("g q p -> q g p"),
                  [], wkk(2), ("SGWF",))
            P.dma("sp", ROWT, row_d[l, :, 0:1024], [], wkk(0), ("ROWT",))
            for g in range(4):
                pg = next_ps()
                mm(ps(pg, 0, 128)[0:1, :], ONE[:, 0:1], SGWF[:, g * 128:(g + 1) * 128], True, True,
                   reads=[("ONE",)] + wkk(2), writes=[psk(pg)], signal=True)
                cp("dve", RSR, ps(pg, 0, 128)[0:1, :], [psk(pg)], wkk(1))
                pg2 = next_ps()
                mm(ps(pg2, 0, 128), ROWT[:, g * 128:(g + 1) * 128], RSR, True, False,
                   reads=wkk(0) + wkk(1), writes=[psk(pg2)])
                mm(ps(pg2, 0, 128), ONE[0:1, :], ROWT[:, 512 + g * 128: 512 + (g + 1) * 128], False, True,
                   reads=wkk(0) + [("ONE",)], writes=[psk(pg2)], signal=True)
                cp("dve", BIASM[:, l * 512 + g * 128: l * 512 + (g + 1) * 128], ps(pg2, 0, 128), [psk(pg2)], [("BIASM", l)])
                ts("dve", GMAT[:, l * 512 + g * 128: l * 512 + (g + 1) * 128], ONE[:], cst(l, "sglg", g), None, ALU.mult,
                   reads=[("ONE",), ("CST",)], writes=[("GMAT", l)])

        def ln_stats(c, half, nch, onescol, zc, zkeys, tg, pm, pq):
            t0, t1 = tg[2], tg[3]
            zb = tg[4:6] if len(tg) >= 6 else None
            lo = half * 1024
            tq = (t0, t1)[c % 2]
            act(wk_bf(tq, 0, 1024), zc(c, lo, 1024), AF.Square, reads=zkeys(c, half), writes=wkk(tq))
            if zb is not None:
                tz = zb[c % 2]
                cp("dve", wk_bf(tz, 0, 1024), zc(c, lo, 1024), zkeys(c, half), wkk(tz))
                for t2 in range(2):
                    mm(ps(pm, t2 * 512, 512), ONESB[:, onescol:onescol + 128], wk_bf(tz, t2 * 512, 512), c == 0, c == nch - 1,
                       reads=wkk(tz) + [("ONESD",)], writes=[psk(pm)], signal=(c == nch - 1 and t2 == 1))
            else:
                for t2 in range(2):
                    mm(ps(pm, t2 * 512, 512), ONESD[:, onescol:onescol + 128], zc(c, lo + t2 * 512, 512), c == 0, c == nch - 1,
                       reads=zkeys(c, half) + [("ONESD",)], writes=[psk(pm)], signal=(c == nch - 1 and t2 == 1))
            for t2 in range(2):
                mm(ps(pq, t2 * 512, 512), ONESB[:, onescol:onescol + 128], wk_bf(tq, t2 * 512, 512), c == 0, c == nch - 1,
                   reads=wkk(tq) + [("ONESD",)], writes=[psk(pq)], signal=(t2 == 1))

        def ln_finish(half, nch, zc, zkeys, cb, tg, pm, pq):
            gMR, gRS, t0, t1 = tg[0], tg[1], tg[2], tg[3]
            lo = half * 1024
            act(wk_f32(gMR), ps(pm), AF.Identity, reads=[psk(pm)], writes=wkk(gMR))
            tt("dve", wk_f32(t0), wk_f32(gMR), wk_f32(gMR), ALU.mult, reads=wkk(gMR), writes=wkk(t0))
            tt("dve", wk_f32(t0), ps(pq), wk_f32(t0), ALU.subtract, reads=[psk(pq)] + wkk(t0), writes=wkk(t0))
            act(wk_f32(t0), wk_f32(t0), AF.Ln, bias=kf(1), reads=wkk(t0) + [("KF",)], writes=wkk(t0))
            act(wk_f32(gRS), wk_f32(t0), AF.Exp, scale=-0.5, reads=wkk(t0), writes=wkk(gRS))
            tt("dve", wk_f32(gMR), wk_f32(gMR), wk_f32(gRS), ALU.mult, reads=wkk(gMR) + wkk(gRS), writes=wkk(gMR))
            for c in range(nch):
                tq = (t0, t1)[c % 2]
                tt("dve", wk_f32(tq), zc(c, lo, 1024), wk_f32(gRS), ALU.mult, reads=zkeys(c, half) + wkk(gRS), writes=wkk(tq))
                tt("dve", wk_f32(tq), wk_f32(tq), wk_f32(gMR), ALU.subtract, reads=wkk(tq) + wkk(gMR), writes=wkk(tq))
                cb(c, half, wk_f32(tq), wkk(tq))

        def layer_norm(nch, onescol, zc, zkeys, cb, tg):
            for half in range(2):
                pm, pq = next_ps(), next_ps()
                for c in range(nch):
                    ln_stats(c, half, nch, onescol, zc, zkeys, tg, pm, pq)
                ln_finish(half, nch, zc, zkeys, cb, tg, pm, pq)

        def store_x(s, c, half, y, ykeys, par=None, xh_write=True):
            lo = half * 1024
            P.dma("sp", xd[s * 8 + c, :, lo: lo + 1024], y, ykeys, [("XD", s, c, half)], ("XDW",) + tuple(ykeys[0]))
            if xh_write:
                act(xh(c, lo, 1024), y, AF.Identity, reads=ykeys, writes=[("XH", c, half)])
                src, skeys, sk = xh(c, lo, 1024), [("XH", c, half)], ("XHDW", c, half)
            else:
                gb = 10 + c % 2
                act(wk_bf(gb, 0, 1024), y, AF.Identity, reads=ykeys, writes=wkk(gb))
                src, skeys, sk = wk_bf(gb, 0, 1024), wkk(gb), ("XHDW", gb)
            if par is not None:
                P.dma("sp", xhd[(par * NSEG + s) * 8 + c, :, lo: lo + 1024], src, skeys,
                      [("XHD", par, s, c, half)], sk)

        XH3 = XH[:].rearrange("p (c w) -> p c w", c=8)
        def load_xh(par, s):
            if state.get("xh") == (par, s):
                return
            state["xh"] = (par, s)
            for c in range(8):
                P.dma("sp", xh(c, 0, T), xhd[(par * NSEG + s) * 8 + c, :, :], [("XHD", par, s, c, 0), ("XHD", par, s, c, 1)],
                      [("XH", c, 0), ("XH", c, 1)], ("XHL", c))
            sl, sr = max(s - 1, 0), min(s + 1, NSEG - 1)
            bl, br = (par * NSEG + sl) * 8, (par * NSEG + sr) * 8
            P.dma("sp", XH3[:, :, T: T + HALO], xhd[bl: bl + 8, :, T - HALO: T].rearrange("c p t -> p c t"),
                  [("XHD", par, sl, c, 1) for c in range(8)], [("XHH",)], ("XHH", 0))
            P.dma("sp", XH3[:, :, T + HALO: T + 2 * HALO], xhd[br: br + 8, :, 0: HALO].rearrange("c p t -> p c t"),
                  [("XHD", par, sr, c, 0) for c in range(8)], [("XHH",)], ("XHH", 1))
        lkL = lambda s: LINK[:, s: s + 1]
        lkR = lambda s: LINK[:, 8 + s: 9 + s]
        P.dma("sp", LINK[:], link_d, [], [("LINK",)], ("LINK",))

        gXA, gXAB, gR, gI, gA, gH1, gH2 = 0, 2, 3, 5, 7, 9, 11

        def rg_head(l, s, h, mode):
            sx = load_unit(l, ("x", h)) if mode == "A" else None
            sg_ = load_unit(l, ("g", h)) if mode == "B" else None
            sgt = load_unit(l, ("gt", h), 512)
            if mode == "B":
                P.dma("sp", wk_f32(gXA, 0, 2048), xad[s * 8 + h, :, :], [("XAD", s, h)], wkk(gXA, 2), ("XAR",))
                P.dma("sp", wk_f32(gH2, 0, 2048), hbd[s * 8 + h, :, :], [("HBD", s, h)], wkk(gH2, 2), ("HBR",))
                for half in range(2):
                    cp("dve", wk_bf(gXAB, half * 1024, 1024), wk_f32(gXA + half), wkk(gXA + half), wkk(gXAB))
                for half in range(2):
                    pg = next_ps()
                    for t2 in range(2):
                        for k in range(8):
                            mm(ps(pg, t2 * 512, 512), wt(sg_, k), xh(k, half * 1024 + t2 * 512, 512), k == 0, k == 7,
                               reads=[("R", sg_), ("XH", k, half)], writes=[psk(pg)], signal=(k == 7 and t2 == 1))
                    act(big_f32(6, half * 1024, 1024), ps(pg), AF.Gelu_apprx_tanh, bias=cst(l, "bin", OFF_RG_G // 128 + h),
                        reads=[psk(pg), ("CST",)], writes=bgk(12 + half))
            else:
                def build_da(l_, h_, buf):
                    for k in range(4):
                        ts("dve", DA[:, buf * 512 + k * 128: buf * 512 + (k + 1) * 128], IDB[:], cst(l_, "caw", k * 8 + h_), None, ALU.mult,
                           reads=[("IDB",), ("CST",)], writes=[("DA", buf)])
                    state["da"][buf] = (l_, h_)
                if (l, h) not in state["da"]:
                    build_da(l, h, 0 if state["da"][0] != (l, (h - 1) % 8) else 1)
                dab = state["da"].index((l, h))
                build_da(l, (h + 1) % 8, 1 - dab)
                bx = cst(l, "bin", OFF_RG_X // 128 + h)
                for half in range(2):
                    pg = next_ps()
                    for t2 in range(2):
                        for k in range(8):
                            mm(ps(pg, t2 * 512, 512), wt(sx, k), xh(k, half * 1024 + t2 * 512, 512), k == 0, k == 7,
                               reads=[("R", sx), ("XH", k, half)], writes=[psk(pg)], signal=(k == 7 and t2 == 1))
                    act(PAD[:, 2 + half * 1024: 2 + half * 1024 + 1024], ps(pg), AF.Identity, bias=bx,
                        reads=[psk(pg), ("CST",)], writes=[("PAD",)])
                pg = next_ps()
                for k in range(8):
                    mm(ps(pg, 0, 32), wt(sx, k), xh(k, T, 32), k == 0, k == 7, reads=[("R", sx), ("XHH",)], writes=[psk(pg)], signal=(k == 7))
                act(HT[:, 0:32], ps(pg, 0, 32), AF.Identity, bias=bx, reads=[psk(pg), ("CST",)], writes=[("HT",)])
                if h == 7 and s > 0:
                    load_xh(state["par"], s - 1)
                act(PAD[:, 0:2], HT[:, 14:16], AF.Identity, scale=lkL(s), reads=[("HT",), ("LINK",)], writes=[("PAD",)])
                act(PAD[:, 2 + T: 3 + T], HT[:, 16:17], AF.Identity, scale=lkR(s), reads=[("HT",), ("LINK",)], writes=[("PAD",)])
                for half in range(2):
                    pg = next_ps()
                    for t2 in range(2):
                        for k in range(4):
                            o = k + half * 1024 + t2 * 512
                            mm(ps(pg, t2 * 512, 512), DA[:, dab * 512 + k * 128: dab * 512 + (k + 1) * 128], PAD[:, o: o + 512],
                               k == 0, k == 3, reads=[("DA", dab), ("PAD",)], writes=[psk(pg)], signal=(k == 3 and t2 == 1))
                    act(wk_f32(gXA + half), ps(pg), AF.Identity, bias=cst(l, "cab", h), reads=[psk(pg), ("CST",)], writes=wkk(gXA + half))
                    cp("dve", wk_bf(gXAB, half * 1024, 1024), wk_f32(gXA + half), wkk(gXA + half), wkk(gXAB))
                P.dma("sp", xad[s * 8 + h, :, :], wk_f32(gXA, 0, 2048), wkk(gXA, 2), [("XAD", s, h)], ("XAW", h % 2))
            d = 1 if mode == "A" else 0
            for gi, (gdst, hb_o) in enumerate(((gR, 16), (gI, 32))):
                for half in range(2):
                    pg = next_ps()
                    for t2 in range(2):
                        mm(ps(pg, t2 * 512, 512), wt(sgt, d * 2 + gi), wk_bf(gXAB, half * 1024 + t2 * 512, 512), True, True,
                           reads=[("R", sgt)] + wkk(gXAB), writes=[psk(pg)], signal=(t2 == 1))
                    act(wk_f32(gdst + half), ps(pg), AF.Tanh, bias=cx(l, hb_o + d * 8 + h), scale=0.5,
                        reads=[psk(pg), ("CX",)], writes=wkk(gdst + half))
            c1ap = cx(l, d * 8 + h)
            c2ap = cx(l, 48 + d * 8 + h)
            order = (1, 0) if mode == "A" else (0, 1)
            for half in order:
                act(wk_f32(gA + half), wk_f32(gR + half), AF.Exp, bias=c1ap, scale=c1ap, reads=wkk(gR + half) + [("CX",)], writes=wkk(gA + half))
                stt("dve", wk_f32(gR + half), wk_f32(gA + half), 0.999998, wk_f32(gA + half), ALU.min, ALU.mult, reads=wkk(gA + half), writes=wkk(gR + half))
            for half in order:
                stt("dve", wk_f32(gI + half), wk_f32(gI + half), 1.0, wk_f32(gXA + half), ALU.add, ALU.mult,
                    reads=wkk(gI + half) + wkk(gXA + half), writes=wkk(gI + half))
            for half in order:
                act(wk_f32(gR + half), wk_f32(gR + half), AF.Sqrt, bias=kf(0), scale=-1.0, reads=wkk(gR + half) + [("KF",)], writes=wkk(gR + half))
                stt("dve", wk_f32(gI + half), wk_f32(gI + half), 0.5, wk_f32(gR + half), ALU.mult, ALU.mult,
                    reads=wkk(gI + half) + wkk(gR + half), writes=wkk(gI + half))
            if mode == "A":
                gH = gH1 if h % 2 == 0 else gH2
                H_ = wk_f32(gH, 0, 2048)
                A_, I_ = wk_f32(gA, 0, 2048), wk_f32(gI, 0, 2048)
                ini = CB[:, h: h + 1]
                P.op("dve", lambda e: e.tensor_tensor_scan(out=H_[:, 2047:1023:-1], data0=A_[:, 2047:1023:-1], data1=I_[:, 2047:1023:-1], initial=ini, op0=ALU.mult, op1=ALU.add),
                     wkk(gA + 1) + wkk(gI + 1) + [("CB",)], wkk(gH + 1))
                ini2 = H_[:, 1024:1025]
                P.op("dve", lambda e: e.tensor_tensor_scan(out=H_[:, 1023::-1], data0=A_[:, 1023::-1], data1=I_[:, 1023::-1], initial=ini2, op0=ALU.mult, op1=ALU.add),
                     wkk(gA) + wkk(gI) + wkk(gH + 1), wkk(gH))
                P.dma("sp", hbd[s * 8 + h, :, :], H_, wkk(gH, 2), [("HBD", s, h)], ("HBW", h % 2))
                ts("dve", CB[:, h: h + 1], H_[:, 0:1], lkL(s), None, ALU.mult, reads=wkk(gH) + [("LINK",)], writes=[("CB",)])
                return
            H_ = wk_f32(gH1, 0, 2048)
            A_, I_ = wk_f32(gA, 0, 2048), wk_f32(gI, 0, 2048)
            ini = CF[:, h: h + 1]
            P.op("dve", lambda e: e.tensor_tensor_scan(out=H_[:, 0:1024], data0=A_[:, 0:1024], data1=I_[:, 0:1024], initial=ini, op0=ALU.mult, op1=ALU.add),
                 wkk(gA) + wkk(gI) + [("CF",)], wkk(gH1))
            ini2 = H_[:, 1023:1024]
            P.op("dve", lambda e: e.tensor_tensor_scan(out=H_[:, 1024:2048], data0=A_[:, 1024:2048], data1=I_[:, 1024:2048], initial=ini2, op0=ALU.mult, op1=ALU.add),
                 wkk(gA + 1) + wkk(gI + 1) + wkk(gH1), wkk(gH1 + 1))
            ts("dve", CF[:, h: h + 1], H_[:, T - 1: T], lkR(s), None, ALU.mult, reads=wkk(gH1 + 1) + [("LINK",)], writes=[("CF",)])
            for half in range(2):
                tt("dve", wk_f32(gH1 + half), wk_f32(gH1 + half), wk_f32(gH2 + half), ALU.add, reads=wkk(gH1 + half) + wkk(gH2 + half), writes=wkk(gH1 + half))
            for half in range(2):
                tt("dve", big_bf(h, half * 1024, 1024), wk_f32(gH1 + half), big_f32(6, half * 1024, 1024), ALU.mult,
                   reads=wkk(gH1 + half) + bgk(12 + half), writes=bgk(h))

        out_deps = []
        for si in range(NSEG):
            for half in range(2):
                for tb in range(8):
                    g = tb
                    tok0 = half * 1024 + tb * 128
                    xt = wk_f32(g)
                    P.dma("sp", xt, xin[si, tok0: tok0 + 128, :], [], wkk(g), ("XT", tb))
                    P.op("dve", lambda e, xt=xt: e.bn_stats(out=SM[:, 0:6], in_=xt[:, 0:512]), wkk(g), [("SM",)])
                    P.op("dve", lambda e, xt=xt: e.bn_stats(out=SM[:, 6:12], in_=xt[:, 512:1024]), wkk(g), [("SM",)])
                    P.op("dve", lambda e, tb=tb: e.bn_aggr(out=SM[:, 32 + 2 * tb: 34 + 2 * tb], in_=SM[:, 0:12]), [("SM",)], [("SM",), ("SM3",)])
                var8 = SM[:, 32:48].rearrange("p (i t) -> p i t", t=2)[:, :, 1:2]
                rs8 = SM[:, 48:56].rearrange("p (i t) -> p i t", t=1)
                act(rs8, var8, AF.Ln, bias=kf(1), reads=[("SM3",), ("KF",)], writes=[("SM4",)])
                act(SM[:, 48:56], SM[:, 48:56], AF.Exp, scale=-0.5, reads=[("SM4",)], writes=[("SM4",)])
                for tb in range(8):
                    xt = wk_f32(tb)
                    ts("dve", xt, xt, SM[:, 32 + 2 * tb: 33 + 2 * tb], SM[:, 48 + tb: 49 + tb], ALU.subtract, ALU.mult,
                       reads=wkk(tb) + [("SM3",), ("SM4",)], writes=wkk(tb))
                for c in range(8):
                    pg = next_ps()
                    for tb in range(8):
                        tr(ps(pg, tb * 128, 128), wk_f32(tb, c * 128, 128), IDF[:], reads=wkk(tb) + [("IDF",)], writes=[psk(pg)],
                           signal=(tb == 7))
                    yg = (8, 9, 12)[c % 3]
                    act(wk_f32(yg), ps(pg), AF.Identity, bias=LNIN[:, 8 + c: 9 + c], scale=LNIN[:, c: c + 1],
                        reads=[psk(pg), ("LNIN",)], writes=wkk(yg))
                    store_x(si, c, half, wk_f32(yg), wkk(yg), par=0, xh_write=False)

        for l in range(L):
            last = (l == L - 1)
            par = l % 2
            state["par"] = par
            memset("dve", CB[:], 0.0, [("CB",)])
            for si in reversed(range(NSEG)):
                load_xh(par, si)
                for h in range(8):
                    rg_head(l, si, h, "A")
            memset("dve", CF[:], 0.0, [("CF",)])
            for si in range(NSEG):
                load_xh(par, si)
                for h in range(8):
                    rg_head(l, si, h, "B")

                for ch in range(4):
                    su = load_unit(l, ("u", ch))
                    for half in range(2):
                        pg = next_ps()
                        for t2 in range(2):
                            for k in range(8):
                                mm(ps(pg, t2 * 512, 512), wt(su, k), xh(k, half * 1024 + t2 * 512, 512), k == 0, k == 7,
                                   reads=[("R", su), ("XH", k, half)], writes=[psk(pg)], signal=(k == 7 and t2 == 1))
                        act(big_bf(8 + ch, half * 1024, 1024), ps(pg), AF.Gelu_apprx_tanh, bias=cst(l, "bin", OFF_SG // 128 + ch),
                            reads=[psk(pg), ("CST",)], writes=bgk(8 + ch))
                sv = [load_unit(l, ("v", j)) for j in range(4)]
                SGO3 = BIG[:, 8 * 2048: 12 * 2048].rearrange("p (g t) -> p g t", g=4)
                for half in range(2):
                    for i in range(8):
                        tb = half * 8 + i
                        pg = next_ps()
                        for j in range(4):
                            for k in range(8):
                                mm(ps(pg, j * 128, 128), xh(k, tb * 128, 128), wt(sv[j], k), k == 0, False,
                                   reads=[("R", sv[j]), ("XH", k, half)], writes=[psk(pg)])
                            mm(ps(pg, j * 128, 128), ONEB[32 * l: 32 * l + 1, :], ROWPB[32 * l: 32 * l + 1, j * 128:(j + 1) * 128], False, False,
                               reads=[("ROWPB",)], writes=[psk(pg)])
                            mm(ps(pg, j * 128, 128), ONEB[32 * l: 32 * l + 1, :], ROWPB[32 * l: 32 * l + 1, 512 + j * 128: 512 + (j + 1) * 128], False, True,
                               reads=[("ROWPB",)], writes=[psk(pg)], signal=(j == 3))
                        V = wk_f32(i, 0, 512)
                        act(V, ps(pg, 0, 512), AF.Gelu_apprx_tanh, reads=[psk(pg)], writes=wkk(i))
                        P.op("dve", lambda e, V=V: e.bn_stats(out=SM[:, 16:22], in_=V), wkk(i), [("SM2",)])
                        P.op("dve", lambda e, i=i: e.bn_aggr(out=SM[:, 32 + 2 * i: 34 + 2 * i], in_=SM[:, 16:22]), [("SM2",)], [("SM2",), ("SM3",)])
                    var8 = SM[:, 32:48].rearrange("p (i t) -> p i t", t=2)[:, :, 1:2]
                    rs8 = SM[:, 48:56].rearrange("p (i t) -> p i t", t=1)
                    act(rs8, var8, AF.Ln, bias=kf(1), reads=[("SM3",), ("KF",)], writes=[("SM4",)])
                    act(SM[:, 48:56], SM[:, 48:56], AF.Exp, scale=-0.5, reads=[("SM4",)], writes=[("SM4",)])
                    for i in range(8):
                        tb = half * 8 + i
                        V = wk_f32(i, 0, 512)
                        gvn = 8 + i % 2
                        VN = wk_bf(gvn, 0, 512)
                        ts("dve", VN, V, SM[:, 32 + 2 * i: 33 + 2 * i], SM[:, 48 + i: 49 + i], ALU.subtract, ALU.mult,
                           reads=wkk(i) + [("SM3",), ("SM4",)], writes=wkk(gvn))
                        pg2 = next_ps()
                        for g in range(4):
                            mm(ps(pg2, g * 128, 128), VN[:, g * 128:(g + 1) * 128], SGWT[:, l * 512 + g * 128: l * 512 + (g + 1) * 128], True, True,
                               reads=wkk(gvn) + [("SGWT", l)], writes=[psk(pg2)], signal=(g == 3))
                        gt_ = 10 + i % 2
                        TT = wk_f32(gt_, 0, 512)
                        tt("dve", TT, ps(pg2, 0, 512), GMAT[:, l * 512:(l + 1) * 512], ALU.mult, reads=[psk(pg2), ("GMAT", l)], writes=wkk(gt_))
                        tt("dve", TT, TT, BIASM[:, l * 512:(l + 1) * 512], ALU.add, reads=wkk(gt_) + [("BIASM", l)], writes=wkk(gt_))
                        uview = SGO3[:, :, tb * 128:(tb + 1) * 128]
                        tt("dve", uview, TT.rearrange("p (g t) -> p g t", g=4), uview, ALU.mult, reads=wkk(gt_) + bgk(8, 4), writes=bgk(8, 4))

                for ch in range(4):
                    sc = load_unit(l, ("c", ch))
                    scg = load_unit(l, ("cg", ch))
                    for k in range(31):
                        ts("dve", DC[:, k * 128:(k + 1) * 128], IDB[:], cst(l, "ccw", k * 4 + ch), None, ALU.mult,
                           reads=[("IDB",), ("CST",)], writes=[("DC",)])
                    bc_, bcg_ = cst(l, "bin", OFF_CC // 128 + ch), cst(l, "bin", OFF_CC // 128 + 4 + ch)
                    for half in range(2):
                        pc, pgg = next_ps(), next_ps()
                        for (pp, ss) in ((pc, sc), (pgg, scg)):
                            for t2 in range(2):
                                for k in range(8):
                                    mm(ps(pp, t2 * 512, 512), wt(ss, k), xh(k, half * 1024 + t2 * 512, 512), k == 0, k == 7,
                                       reads=[("R", ss), ("XH", k, half)], writes=[psk(pp)], signal=(k == 7 and t2 == 1))
                        gs = 8 + half
                        act(wk_f32(gs), ps(pgg), AF.Sigmoid, bias=bcg_, reads=[psk(pgg), ("CST",)], writes=wkk(gs))
                        stt("dve", PAD[:, 15 + half * 1024: 15 + half * 1024 + 1024], ps(pc), bc_, wk_f32(gs),
                            ALU.add, ALU.mult, reads=[psk(pc), ("CST",)] + wkk(gs), writes=[("PAD",)])
                    pc, pgg = next_ps(), next_ps()
                    for (pp, ss) in ((pc, sc), (pgg, scg)):
                        for k in range(8):
                            mm(ps(pp, 0, 32), wt(ss, k), xh(k, T, 32), k == 0, k == 7, reads=[("R", ss), ("XHH",)], writes=[psk(pp)], signal=(k == 7))
                    act(HT[:, 32:64], ps(pgg, 0, 32), AF.Sigmoid, bias=bcg_, reads=[psk(pgg), ("CST",)], writes=[("HT",)])
                    stt("dve", HT[:, 0:32], ps(pc, 0, 32), bc_, HT[:, 32:64], ALU.add, ALU.mult, reads=[psk(pc), ("CST",), ("HT",)], writes=[("HT",)])
                    ts("dve", PAD[:, 0:15], HT[:, 1:16], lkL(si), None, ALU.mult, reads=[("HT",), ("LINK",)], writes=[("PAD",)])
                    ts("dve", PAD[:, 15 + T: 30 + T], HT[:, 16:31], lkR(si), None, ALU.mult, reads=[("HT",), ("LINK",)], writes=[("PAD",)])
                    for half in range(2):
                        pg = next_ps()
                        for t2 in range(2):
                            for k in range(31):
                                o = k + half * 1024 + t2 * 512
                                mm(ps(pg, t2 * 512, 512), DC[:, k * 128:(k + 1) * 128], PAD[:, o: o + 512], k == 0, k == 30,
                                   reads=[("DC",), ("PAD",)], writes=[psk(pg)], signal=(k == 30 and t2 == 1))
                        act(wk_f32(2 * ch + half), ps(pg), AF.Identity, bias=cst(l, "ccb", ch), reads=[psk(pg), ("CST",)], writes=wkk(2 * ch + half))
                def cc_cb(c, half, Tap, Tk):
                    act(big_bf(12 + c, half * 1024, 1024), Tap, AF.Silu, bias=cst(l, "cclb", c), scale=cst(l, "cclg", c),
                        reads=Tk + [("CST",)], writes=bgk(12 + c))
                layer_norm(4, 128, lambda c, lo, n: wk_f32(2 * c, lo, n), lambda c, half: wkk(2 * c + half), cc_cb, [8, 9, 10, 11])

                for oc in range(8):
                    sl = {nm: load_unit(l, (nm, oc), 1024 if nm in ("ga", "gb", "gc", "ba") else 512) for nm in ("ga", "gb", "gc", "ba", "bb", "bc")}
                    for half in range(2):
                        for bi, (gn, yn, nk, src0) in enumerate((("ga", "ba", 8, 0), ("gb", "bb", 4, 8), ("gc", "bc", 4, 12))):
                            pgt, py = next_ps(), next_ps()
                            for t2 in range(2):
                                for k in range(8):
                                    mm(ps(pgt, t2 * 512, 512), wt(sl[gn], k), xh(k, half * 1024 + t2 * 512, 512), k == 0, k == 7,
                                       reads=[("R", sl[gn]), ("XH", k, half)], writes=[psk(pgt)], signal=(k == 7 and t2 == 1))
                            for t2 in range(2):
                                for k in range(nk):
                                    mm(ps(py, t2 * 512, 512), wt(sl[yn], k), big_bf(src0 + k, half * 1024 + t2 * 512, 512), k == 0, k == nk - 1,
                                       reads=[("R", sl[yn])] + bgk(src0 + k), writes=[psk(py)], signal=(k == nk - 1 and t2 == 1))
                            gg = 8 + bi
                            act(wk_f32(gg), ps(pgt), AF.Sigmoid, bias=cst(l, "bin", OFF_GATE // 128 + bi * 8 + oc), reads=[psk(pgt), ("CST",)], writes=wkk(gg))
                            tt("dve", wk_f32(gg), ps(py), wk_f32(gg), ALU.mult, reads=[psk(py)] + wkk(gg), writes=wkk(gg))
                        tt("dve", wk_f32(8), wk_f32(8), wk_f32(9), ALU.add, reads=wkk(8) + wkk(9), writes=wkk(8))
                        tt("dve", wk_bf(oc, half * 1024, 1024), wk_f32(8), wk_f32(10), ALU.add, reads=wkk(8) + wkk(10), writes=wkk(oc))

                def resid(oc, half, pg, bname, zdst, zkeys_w):
                    gx = 9 + (2 * oc + half) % 2
                    ge = 11 + (2 * oc + half) % 2
                    P.dma("sp", wk_f32(gx), xd[si * 8 + oc, :, half * 1024: half * 1024 + 1024], [("XD", si, oc, half)], wkk(gx), ("XDR", gx))
                    act(wk_f32(ge), ps(pg), AF.Identity, bias=cst(l, bname, oc), reads=[psk(pg), ("CST",)], writes=wkk(ge))
                    stt("dve", zdst, wk_f32(gx), ALPHA, wk_f32(ge), ALU.mult, ALU.add, reads=wkk(gx) + wkk(ge), writes=zkeys_w)
                for oc in range(8):
                    so = load_unit(l, ("o", oc))
                    for half in range(2):
                        pg = next_ps()
                        for t2 in range(2):
                            for k in range(8):
                                mm(ps(pg, t2 * 512, 512), wt(so, k), wk_bf(k, half * 1024 + t2 * 512, 512), k == 0, k == 7,
                                   reads=[("R", so)] + wkk(k), writes=[psk(pg)], signal=(k == 7 and t2 == 1))
                        resid(oc, half, pg, "bo", big_f32(oc, half * 1024, 1024), bgk(2 * oc + half))

                def ln_x_cb(gname, bname, par_out):
                    def cb(c, half, Tap, Tk):
                        yg = 4 + (2 * c + half) % 4
                        act(wk_f32(yg), Tap, AF.Identity, bias=cst(l, bname, c), scale=cst(l, gname, c), reads=Tk + [("CST",)], writes=wkk(yg))
                        store_x(si, c, half, wk_f32(yg), wkk(yg), par=par_out, xh_write=(par_out is None))
                    return cb
                layer_norm(8, 0, lambda c, lo, n: big_f32(c, lo, n), lambda c, half: bgk(2 * c + half), ln_x_cb("l1g", "l1b", None), [0, 1, 2, 3, 8, 9])

                for J in range(4):
                    for jj in range(8):
                        j = J * 8 + jj
                        s1 = load_unit(l, ("f1", j))
                        for half in range(2):
                            pg = next_ps()
                            for t2 in range(2):
                                for k in range(8):
                                    mm(ps(pg, t2 * 512, 512), wt(s1, k), xh(k, half * 1024 + t2 * 512, 512), k == 0, k == 7,
                                       reads=[("R", s1), ("XH", k, half)], writes=[psk(pg)], signal=(k == 7 and t2 == 1))
                            gr = 8 + (2 * jj + half) % 2
                            act(wk_f32(gr), ps(pg), AF.Relu, bias=cst(l, "bf1", j), reads=[psk(pg), ("CST",)], writes=wkk(gr))
                            tt("dve", wk_bf(jj, half * 1024, 1024), wk_f32(gr), wk_f32(gr), ALU.mult, reads=wkk(gr), writes=wkk(jj))
                    if J == 3:
                        state["xh"] = None
                        if si < NSEG - 1:
                            load_xh(par, si + 1)
                    s2 = [load_unit(l, ("f2", J * 8 + jj)) for jj in range(8)]
                    for oc in range(8):
                        for half in range(2):
                            pg = next_ps()
                            for t2 in range(2):
                                for jj in range(8):
                                    mm(ps(pg, t2 * 512, 512), wt(s2[jj], oc), wk_bf(jj, half * 1024 + t2 * 512, 512), jj == 0, jj == 7,
                                       reads=[("R", s2[jj])] + wkk(jj), writes=[psk(pg)], signal=(jj == 7 and t2 == 1))
                            acc = big_f32(oc, half * 1024, 1024)
                            if J == 0:
                                cp("dve", acc, ps(pg), [psk(pg)], bgk(2 * oc + half))
                            else:
                                tt("dve", acc, ps(pg), acc, ALU.add, reads=[psk(pg)] + bgk(2 * oc + half), writes=bgk(2 * oc + half))
                for oc in range(8):
                    for half in range(2):
                        gx = 9 + (2 * oc + half) % 2
                        acc = big_f32(oc, half * 1024, 1024)
                        P.dma("sp", wk_f32(gx), xd[si * 8 + oc, :, half * 1024: half * 1024 + 1024], [("XD", si, oc, half)], wkk(gx), ("XDR", gx))
                        act(acc, acc, AF.Identity, bias=cst(l, "bf2", oc), reads=bgk(2 * oc + half) + [("CST",)], writes=bgk(2 * oc + half))
                        stt("dve", acc, wk_f32(gx), ALPHA, acc, ALU.mult, ALU.add, reads=wkk(gx) + bgk(2 * oc + half), writes=bgk(2 * oc + half))
                if not last:
                    layer_norm(8, 0, lambda c, lo, n: big_f32(c, lo, n), lambda c, half: bgk(2 * c + half), ln_x_cb("l2g", "l2b", 1 - par), [0, 1, 2, 3, 8, 9])
                else:
                    def fin_cb(c, half, Tap, Tk):
                        act(wk_f32(4 + c), Tap, AF.Identity, bias=cst(l, "l2b", c), scale=cst(l, "l2g", c), reads=Tk + [("CST",)], writes=wkk(4 + c))
                        if c == 7:
                            for tb in range(8):
                                pg = next_ps()
                                for cc in range(8):
                                    tr(ps(pg, cc * 128, 128), wk_f32(4 + cc, tb * 128, 128), IDF[:], reads=wkk(4 + cc) + [("IDF",)], writes=[psk(pg)],
                                       signal=(cc == 7))
                                go = 12 if tb % 2 == 0 else 3
                                if tb % 2 == 0:
                                    cp("dve", wk_f32(go), ps(pg), [psk(pg)], wkk(go))
                                else:
                                    act(wk_f32(go), ps(pg), AF.Identity, reads=[psk(pg)], writes=wkk(go))
                                tok0 = half * 1024 + tb * 128
                                dep = P.dma("sp", yout[si, tok0: tok0 + 128, :], wk_f32(go), wkk(go), [("YO", si, half, tb)], ("YO", tb % 2))
                                out_deps.append(dep)
                    layer_norm(8, 0, lambda c, lo, n: big_f32(c, lo, n), lambda c, half: bgk(2 * c + half), fin_cb, [0, 1, 2, 3])

        fin = {}
        for s, v in out_deps:
            fin[s] = max(fin.get(s, 0), v)
        P.final_wait("sp", list(fin.items()))
        with nc.Block() as block:
            P.emit(block)
    return nc


def _colchunk(W, j, nk):
    blk = W[:, j * 128:(j + 1) * 128].reshape(nk, 128, 128)
    out = np.zeros((128, 1024), np.float32)
    out[:, : nk * 128] = blk.transpose(1, 0, 2).reshape(128, nk * 128)
    return out


def _prep_weights(inp):
    wu = np.zeros((L, NU, 128, 1024), np.float32)
    cst = np.zeros((L, 128, NCST), np.float32)
    rows = np.zeros((L, 1, NROW), np.float32)
    for l in range(L):
        w_in = inp["w_in"][l]
        for h in range(8):
            wu[l, UNITS[("x", h)]] = _colchunk(w_in, OFF_RG_X // 128 + h, 8)
            wu[l, UNITS[("g", h)]] = _colchunk(w_in, OFF_RG_G // 128 + h, 8)
            gt = np.zeros((128, 1024), np.float32)
            gt[:, 0:128] = inp["rg_wa"][l, 0, h]
            gt[:, 128:256] = inp["rg_wx"][l, 0, h]
            gt[:, 256:384] = inp["rg_wa"][l, 1, h]
            gt[:, 384:512] = inp["rg_wx"][l, 1, h]
            wu[l, UNITS[("gt", h)]] = gt
        for ch in range(4):
            wu[l, UNITS[("u", ch)]] = _colchunk(w_in, OFF_SG // 128 + ch, 8)
            wu[l, UNITS[("v", ch)]] = _colchunk(w_in, OFF_SG // 128 + 4 + ch, 8)
            wu[l, UNITS[("c", ch)]] = _colchunk(w_in, OFF_CC // 128 + ch, 8)
            wu[l, UNITS[("cg", ch)]] = _colchunk(w_in, OFF_CC // 128 + 4 + ch, 8)
        for oc in range(8):
            wu[l, UNITS[("ga", oc)]] = _colchunk(w_in, OFF_GATE // 128 + oc, 8)
            wu[l, UNITS[("gb", oc)]] = _colchunk(w_in, OFF_GATE // 128 + 8 + oc, 8)
            wu[l, UNITS[("gc", oc)]] = _colchunk(w_in, OFF_GATE // 128 + 16 + oc, 8)
            wu[l, UNITS[("ba", oc)]] = _colchunk(inp["w_ba"][l], oc, 8)
            wu[l, UNITS[("bb", oc)]] = _colchunk(inp["w_bb"][l], oc, 4)
            wu[l, UNITS[("bc", oc)]] = _colchunk(inp["w_bc"][l], oc, 4)
            wu[l, UNITS[("o", oc)]] = _colchunk(inp["w_o"][l], oc, 8)
        for j in range(32):
            wu[l, UNITS[("f1", j)]] = _colchunk(inp["w_ff1"][l], j, 8)
            wu[l, UNITS[("f2", j)]] = inp["w_ff2"][l][j * 128:(j + 1) * 128, :]
        def put(name, vec, n):
            cst[l, :, _c[name]: _c[name] + n] = np.asarray(vec, np.float32).reshape(n, 128).T
        put("bin", inp["b_in"][l], 56)
        put("caw", inp["conv_a_w"][l].reshape(-1), 32)
        put("cab", inp["conv_a_b"][l], 8)
        put("rba", inp["rg_ba"][l].reshape(-1), 16)
        put("rbx", inp["rg_bx"][l].reshape(-1), 16)
        put("lam", inp["rg_lambda"][l].reshape(-1), 16)
        put("sglg", inp["sg_ln_g"][l], 4)
        put("ccw", inp["conv_c_w"][l].reshape(-1), 124)
        put("ccb", inp["conv_c_b"][l], 4)
        put("cclg", inp["cc_ln_g"][l], 4)
        put("cclb", inp["cc_ln_b"][l], 4)
        put("bo", inp["b_o"][l], 8)
        put("l1g", inp["ln1_g"][l], 8)
        put("l1b", inp["ln1_b"][l], 8)
        put("bf1", inp["b_ff1"][l], 32)
        put("bf2", inp["b_ff2"][l], 8)
        put("l2g", inp["ln2_g"][l], 8)
        put("l2b", inp["ln2_b"][l], 8)
        rows[l, 0, 0:512] = inp["sg_ln_b"][l]
        rows[l, 0, 512:1024] = inp["sg_b"][l].reshape(-1)
        rows[l, 0, 1024:1536] = inp["b_in"][l][OFF_SG + 512: OFF_SG + 1024]
    lnin = np.zeros((128, 16), np.float32)
    lnin[:, 0:8] = np.asarray(inp["ln_in_g"], np.float32).reshape(8, 128).T
    lnin[:, 8:16] = np.asarray(inp["ln_in_b"], np.float32).reshape(8, 128).T
    sgwt = np.ascontiguousarray(np.asarray(inp["sg_w"], np.float32).transpose(0, 1, 3, 2))
    return wu, cst, rows, lnin, sgwt


_NC_CACHE = {}
_SAMPLE_SLOTS = [(c, k) for c in range(4, 8) for k in range(4)]


def _core_inputs(inp):
    xp = inp["x_prompt"].astype(np.float32, copy=False)
    xs = inp["x_sample"].astype(np.float32, copy=False)
    xins = [np.zeros((NSEG, T, D), np.float32) for _ in range(NCORES)]
    links = [np.zeros((128, 16), np.float32) for _ in range(NCORES)]
    xins[0][:] = xp[0].reshape(NSEG, T, D)
    links[0][:, 1:8] = 1.0
    links[0][:, 8:15] = 1.0
    for i, (c, k) in enumerate(_SAMPLE_SLOTS):
        xins[c][k] = xs[i]
    return xins, links


def kernel(**inputs):
    inp = {k: np.asarray(v) for k, v in inputs.items()}
    wu, cst, rows, lnin, sgwt = _prep_weights(inp)
    xins, links = _core_inputs(inp)
    if "nc" not in _NC_CACHE:
        _NC_CACHE["nc"] = build(None)
    nc = _NC_CACHE["nc"]
    ident = np.eye(128, dtype=np.float32)
    used = {0} | {c for c, _ in _SAMPLE_SLOTS}
    wz = np.zeros_like(wu)
    in_maps = [{"xin": xins[c], "link": links[c], "wu": (wu if c in used else wz), "cst": cst, "rows": rows, "lnin": lnin,
                "sgwt": sgwt, "ident": ident} for c in range(NCORES)]
    res = run_bass_kernel_spmd(nc, in_maps, core_ids=list(range(NCORES)))
    y_prompt = np.ascontiguousarray(res.results[0]["yout"]).reshape(1, NSEG * T, D).astype(np.float32, copy=False)
    y_sample = np.zeros((16, T, D), np.float32)
    for i, (c, k) in enumerate(_SAMPLE_SLOTS):
        y_sample[i] = res.results[c]["yout"][k]
    return (y_prompt, y_sample)
```

```python
import numpy as np
import concourse.bass as bass
import concourse.mybir as mybir
from concourse.bass_utils import run_bass_kernel_spmd

F32 = mybir.dt.float32
BF16 = mybir.dt.bfloat16
AF = mybir.ActivationFunctionType
ALU = mybir.AluOpType

D = 1024
T = 2048
HALO = 16
NSEG = 8
L = 2
NCORES = 8
ALPHA = float((2 * L) ** 0.25)
EPS = 1e-5
OFF_RG_X, OFF_RG_G, OFF_SG, OFF_CC, OFF_GATE = 0, 1024, 2048, 3072, 4096
NSLOT = 8
NWK = 13

_c = {}
_o = 0
def _add(name, n):
    global _o
    _c[name] = _o
    _o += n
_add("bin", 56); _add("caw", 32); _add("cab", 8); _add("rba", 16); _add("rbx", 16); _add("lam", 16)
_add("sglg", 4); _add("ccw", 124); _add("ccb", 4); _add("cclg", 4); _add("cclb", 4)
_add("bo", 8); _add("l1g", 8); _add("l1b", 8); _add("bf1", 32); _add("bf2", 8); _add("l2g", 8); _add("l2b", 8)
NCST = _o
NROW = 1536

def _units():
    u = {}
    n = 0
    for h in range(8):
        u[("x", h)] = n; u[("g", h)] = n + 1; u[("gt", h)] = n + 2; n += 3
    for ch in range(4):
        u[("u", ch)] = n; n += 1
    for j in range(4):
        u[("v", j)] = n; n += 1
    for ch in range(4):
        u[("c", ch)] = n; u[("cg", ch)] = n + 1; n += 2
    for oc in range(8):
        for k, nm in enumerate(("ga", "gb", "gc", "ba", "bb", "bc")):
            u[(nm, oc)] = n + k
        n += 6
    for oc in range(8):
        u[("o", oc)] = n; n += 1
    for J in range(4):
        for jj in range(8):
            u[("f1", J * 8 + jj)] = n; n += 1
        for jj in range(8):
            u[("f2", J * 8 + jj)] = n; n += 1
    return u, n
UNITS, NU = _units()


class Prog:
    ENGS = ("pe", "act", "dve", "pool", "sp")

    def __init__(self, nc, es):
        self.nc = nc
        self.es = es
        self.streams = {e: [] for e in self.ENGS}
        self.cnt = {e: 0 for e in self.ENGS}
        self.known = {e: {} for e in self.ENGS}
        self.buf = {}
        self.sems = {}
        self.dcnt = {}
        for e in self.ENGS:
            self.sems[e] = es.enter_context(nc.semaphore("s_" + e))
        self.out_deps = []

    def dsem(self, key):
        if key not in self.sems:
            self.sems[key] = self.es.enter_context(self.nc.semaphore("d_%d" % len(self.sems)))
            self.dcnt[key] = 0
        return key

    def _deps(self, engine, reads, writes):
        deps = {}
        def add(d):
            if d is None:
                return
            s, v = d
            if deps.get(s, 0) < v:
                deps[s] = v
        for k in reads:
            st = self.buf.get(k)
            if st:
                add(st["w"])
        for k in writes:
            st = self.buf.get(k)
            if st:
                add(st["w"])
                for s, v in st["r"].items():
                    add((s, v))
        waits = []
        kn = self.known[engine]
        for s, v in deps.items():
            if engine == "pe" and s == "pe":
                continue
            if kn.get(s, 0) < v:
                waits.append((s, v))
                kn[s] = v
        return waits

    def _commit(self, dep, reads, writes):
        for k in writes:
            self.buf[k] = {"w": dep, "r": {}}
        for k in reads:
            st = self.buf.setdefault(k, {"w": None, "r": {}})
            if st["r"].get(dep[0], 0) < dep[1]:
                st["r"][dep[0]] = dep[1]

    def op(self, engine, fn, reads=(), writes=(), signal=True):
        waits = self._deps(engine, reads, writes)
        sems = self.sems
        if signal:
            self.cnt[engine] += 1
            dep = (engine, self.cnt[engine])
        else:
            dep = (engine, self.cnt[engine] + 1)
        semh = sems[engine]
        def thunk(eng, waits=waits, fn=fn, signal=signal):
            for s, v in waits:
                eng.wait_ge(sems[s], v)
            ins = fn(eng)
            if signal:
                ins.then_inc(semh, 1)
        self.streams[engine].append(thunk)
        self._commit(dep, reads, writes)

    def dma(self, engine, out, in_, reads, writes, semkey):
        self.dsem(semkey)
        waits = self._deps(engine, reads, writes)
        prev = self.dcnt[semkey]
        if prev > 0 and self.known[engine].get(semkey, 0) < prev:
            waits.append((semkey, prev))
            self.known[engine][semkey] = prev
        self.dcnt[semkey] += 16
        dep = (semkey, self.dcnt[semkey])
        sems = self.sems
        def thunk(eng, waits=waits):
            for s, v in waits:
                eng.wait_ge(sems[s], v)
            eng.dma_start(out=out, in_=in_).then_inc(sems[semkey], 16)
        self.streams[engine].append(thunk)
        self._commit(dep, reads, writes)
        return dep

    def final_wait(self, engine, deps):
        sems = self.sems
        def thunk(eng):
            for s, v in deps:
                eng.wait_ge(sems[s], v)
        self.streams[engine].append(thunk)

    def emit(self, block):
        P = self
        @block.tensor
        def _(e):
            for t in P.streams["pe"]:
                t(e)
        @block.scalar
        def _(e):
            for t in P.streams["act"]:
                t(e)
        @block.vector
        def _(e):
            for t in P.streams["dve"]:
                t(e)
        @block.gpsimd
        def _(e):
            for t in P.streams["pool"]:
                t(e)
        @block.sync
        def _(e):
            for t in P.streams["sp"]:
                t(e)


def build(seg_kinds, debug=False):
    import contextlib
    nc = bass.Bass("TRN2", target_bir_lowering=False)
    xin = nc.dram_tensor("xin", [NSEG, T, D], F32, kind="ExternalInput").ap()
    wu = nc.dram_tensor("wu", [L, NU, 128, 1024], F32, kind="ExternalInput").ap()
    cst_d = nc.dram_tensor("cst", [L, 128, NCST], F32, kind="ExternalInput").ap()
    row_d = nc.dram_tensor("rows", [L, 1, NROW], F32, kind="ExternalInput").ap()
    lnin_d = nc.dram_tensor("lnin", [128, 16], F32, kind="ExternalInput").ap()
    sgwt_d = nc.dram_tensor("sgwt", [L, 4, 128, 128], F32, kind="ExternalInput").ap()
    ident_d = nc.dram_tensor("ident", [128, 128], F32, kind="ExternalInput").ap()
    yout = nc.dram_tensor("yout", [NSEG, T, D], F32, kind="ExternalOutput").ap()
    link_d = nc.dram_tensor("link", [128, 16], F32, kind="ExternalInput").ap()
    xd = nc.dram_tensor("xd", [NSEG * 8, 128, T], F32, kind="Internal").ap()
    xhd = nc.dram_tensor("xhd", [2 * NSEG * 8, 128, T], BF16, kind="Internal").ap()
    hbd = nc.dram_tensor("hbd", [NSEG * 8, 128, T], F32, kind="Internal").ap()
    xad = nc.dram_tensor("xad", [NSEG * 8, 128, T], F32, kind="Internal").ap()

    es = contextlib.ExitStack()
    with es:
        P = Prog(nc, es)
        sb = lambda name, shape, dt: es.enter_context(nc.sbuf_tensor(name, shape, dt))
        XH = sb("XH", [128, 8 * (T + 2 * HALO)], BF16)
        BIG = sb("BIG", [128, 32768], BF16)
        WK = sb("WK", [128, NWK * 2048], BF16)
        PAD = sb("PAD", [128, 2176], BF16)
        DC = sb("DC", [128, 31 * 128], BF16)
        DA = sb("DA", [128, 2 * 4 * 128], BF16)
        RING = sb("RING", [128, NSLOT * 1024], BF16)
        CST = sb("CST", [128, L * NCST], F32)
        CX = sb("CX", [128, L * 64], F32)
        ROWP = sb("ROWP", [33, 512], F32)
        LNIN = sb("LNIN", [128, 16], F32)
        SGWT = sb("SGWT", [128, L * 512], BF16)
        BIASM = sb("BIASM", [128, L * 512], F32)
        GMAT = sb("GMAT", [128, L * 512], F32)
        IDB = sb("IDB", [128, 128], BF16)
        IDF = sb("IDF", [128, 128], F32)
        ONESD = sb("ONESD", [128, 256], F32)
        ONE = sb("ONE", [128, 128], F32)
        ONESB = sb("ONESB", [128, 256], BF16)
        ONEB = sb("ONEB", [33, 128], BF16)
        ROWPB = sb("ROWPB", [33, 1024], BF16)
        KF = sb("KF", [128, 8], F32)
        SM = sb("SM", [128, 64], F32)
        LINK = sb("LINK", [128, 16], F32)
        CF = sb("CF", [128, 8], F32)
        CB = sb("CB", [128, 8], F32)
        HT = sb("HT", [128, 64], F32)
        PS = es.enter_context(nc.psum_tensor("PS", [128, 4096], F32))

        XW = T + 2 * HALO
        def xh(c, lo, n):
            return XH[:, c * XW + lo: c * XW + lo + n]
        def big_bf(g, lo=0, n=2048):
            return BIG[:, g * 2048 + lo: g * 2048 + lo + n]
        def big_f32(c, lo=0, n=2048):
            return BIG[:, c * 4096: (c + 1) * 4096].bitcast(F32)[:, lo: lo + n]
        def wk_bf(g, lo=0, n=2048):
            return WK[:, g * 2048 + lo: g * 2048 + lo + n]
        def wk_f32(g, lo=0, n=1024):
            return WK[:, g * 2048: g * 2048 + 2 * (lo + n)].bitcast(F32)[:, lo: lo + n]
        def wkk(g, n=1):
            return [("W", g + i) for i in range(n)]
        def bgk(g, n=1):
            return [("B", g + i) for i in range(n)]
        def ps(g, lo=0, n=1024):
            return PS[:, g * 1024 + lo: g * 1024 + lo + n]
        psk = lambda g: ("PS", g)
        def cst(l, name, j=0):
            o = l * NCST + _c[name] + j
            return CST[:, o: o + 1]
        def cx(l, o):
            return CX[:, l * 64 + o: l * 64 + o + 1]
        kf = lambda j: KF[:, j: j + 1]

        state = {"psg": 0, "slot": 0, "da": [None, None]}
        def next_ps():
            g = state["psg"]
            state["psg"] = (g + 1) % 4
            return g

        def load_unit(l, key, n=1024):
            s = state["slot"]
            state["slot"] = (s + 1) % NSLOT
            u = UNITS[key]
            P.dma("pool", RING[:, s * 1024: s * 1024 + n], wu[l, u, :, 0:n], reads=[], writes=[("R", s)], semkey=("R", s))
            return s
        def wt(s, k, n=128, width=128):
            return RING[:, s * 1024 + k * width: s * 1024 + k * width + n]

        def act(out, in_, func, bias=None, scale=1.0, reads=(), writes=()):
            def fn(e):
                kw = {}
                if bias is not None:
                    kw["bias"] = bias
                return e.activation(out=out, in_=in_, func=func, scale=scale, **kw)
            P.op("act", fn, reads, writes)
        def tt(eng, out, in0, in1, op, reads=(), writes=()):
            P.op(eng, lambda e: e.tensor_tensor(out=out, in0=in0, in1=in1, op=op), reads, writes)
        def ts(eng, out, in0, s1, s2, op0, op1=None, reads=(), writes=()):
            if op1 is None:
                P.op(eng, lambda e: e.tensor_scalar(out=out, in0=in0, scalar1=s1, scalar2=None, op0=op0), reads, writes)
            else:
                P.op(eng, lambda e: e.tensor_scalar(out=out, in0=in0, scalar1=s1, scalar2=s2, op0=op0, op1=op1), reads, writes)
        def stt(eng, out, in0, scalar, in1, op0, op1, reads=(), writes=()):
            P.op(eng, lambda e: e.scalar_tensor_tensor(out=out, in0=in0, scalar=scalar, in1=in1, op0=op0, op1=op1), reads, writes)
        def cp(eng, out, in_, reads=(), writes=()):
            P.op(eng, lambda e: e.tensor_copy(out=out, in_=in_), reads, writes)
        def mm(out, lhsT, rhs, start, stop, reads=(), writes=(), signal=False):
            P.op("pe", lambda e: e.matmul(out, lhsT, rhs, start=start, stop=stop), reads, writes, signal=signal)
        def tr(out, in_, ident, reads=(), writes=(), signal=False):
            P.op("pe", lambda e: e.transpose(out, in_, ident), reads, writes, signal=signal)
        def memset(eng, ap, val, writes=()):
            P.op(eng, lambda e: e.memset(ap, val), (), writes)

        memset("dve", KF[:, 0:1], 1.0, [("KF",)])
        memset("dve", KF[:, 1:2], EPS, [("KF",)])
        memset("dve", KF[:, 2:3], 0.0, [("KF",)])
        memset("dve", KF[:, 3:4], -0.5, [("KF",)])
        memset("dve", KF[:, 4:5], 0.5, [("KF",)])
        memset("dve", ONE[:], 1.0, [("ONE",)])
        memset("dve", ONESD[:, 0:128], 1.0 / 1024.0, [("ONESD",)])
        memset("dve", ONESD[:, 128:256], 1.0 / 512.0, [("ONESD",)])
        memset("dve", ONESB[:, 0:128], 1.0 / 1024.0, [("ONESD",)])
        memset("dve", ONESB[:, 128:256], 1.0 / 512.0, [("ONESD",)])
        P.dma("sp", IDF[:], ident_d, [], [("IDF",)], ("IDF",))
        cp("dve", IDB[:], IDF[:], [("IDF",)], [("IDB",)])
        P.dma("sp", CST[:].rearrange("p (l n) -> p l n", l=L), cst_d.rearrange("l p n -> p l n"), [], [("CST",)], ("CST",))
        for l in range(L):
            P.dma("sp", ROWP[32 * l: 32 * l + 1, :], row_d[l, :, 1024:1536], [], [("ROWP",)], ("ROWP",))
        P.dma("sp", LNIN[:], lnin_d, [], [("LNIN",)], ("LNIN",))
        memset("dve", ONEB[:], 1.0, [("ONEB",)])
        for l in range(L):
            pr = slice(32 * l, 32 * l + 1)
            cp("dve", ROWPB[pr, 0:512], ROWP[pr, :], [("ROWP",)], [("ROWPB",)])
            tt("dve", ROWP[pr, :], ROWP[pr, :], ROWPB[pr, 0:512], ALU.subtract, reads=[("ROWP",), ("ROWPB",)], writes=[("ROWP",)])
            cp("dve", ROWPB[pr, 512:1024], ROWP[pr, :], [("ROWP",)], [("ROWPB",)])
        for l in range(L):
            lamv = CST[:, l * NCST + _c["lam"]: l * NCST + _c["lam"] + 16]
            c1 = CX[:, l * 64: l * 64 + 16]
            act(c1, lamv, AF.Exp, scale=-1.0, reads=[("CST",)], writes=[("CX",)])
            act(c1, c1, AF.Ln, bias=kf(0), reads=[("CX",), ("KF",)], writes=[("CX",)])
            ts("dve", c1, c1, -4.0, None, ALU.mult, reads=[("CX",)], writes=[("CX",)])
            ts("dve", CX[:, l * 64 + 48: l * 64 + 64], c1, 2.0, None, ALU.mult, reads=[("CX",)], writes=[("CX",)])
            for nm, o in (("rba", 16), ("rbx", 32)):
                src = CST[:, l * NCST + _c[nm]: l * NCST + _c[nm] + 16]
                ts("dve", CX[:, l * 64 + o: l * 64 + o + 16], src, 0.5, None, ALU.mult, reads=[("CST",)], writes=[("CX",)])
            P.dma("pool", SGWT[:, l * 512:(l + 1) * 512].rearrange("q (g p) -> q g p", g=4),
                  sgwt_d[l].rearrange("g q p -> q g p"), [], [("SGWT", l)], ("SGWT", l))
            SGWF = wk_f32(2, 0, 512)
            RSR = wk_f32(1, 0, 128)[0:1, :]
            ROWT = wk_f32(0, 0, 1024)[0:1, :]
            P.dma("sp", SGWF.rearrange("q (g p) -> q g p", g=4), sgwt_d[l].rearrange("g q p -> q g p"),
                  [], wkk(2), ("SGWF",))
            P.dma("sp", ROWT, row_d[l, :, 0:1024], [], wkk(0), ("ROWT",))
            for g in range(4):
                pg = next_ps()
                mm(ps(pg, 0, 128)[0:1, :], ONE[:, 0:1], SGWF[:, g * 128:(g + 1) * 128], True, True,
                   reads=[("ONE",)] + wkk(2), writes=[psk(pg)], signal=True)
                cp("dve", RSR, ps(pg, 0, 128)[0:1, :], [psk(pg)], wkk(1))
                pg2 = next_ps()
                mm(ps(pg2, 0, 128), ROWT[:, g * 128:(g + 1) * 128], RSR, True, False,
                   reads=wkk(0) + wkk(1), writes=[psk(pg2)])
                mm(ps(pg2, 0, 128), ONE[0:1, :], ROWT[:, 512 + g * 128: 512 + (g + 1) * 128], False, True,
                   reads=wkk(0) + [("ONE",)], writes=[psk(pg2)], signal=True)
                cp("dve", BIASM[:, l * 512 + g * 128: l * 512 + (g + 1) * 128], ps(pg2, 0, 128), [psk(pg2)], [("BIASM", l)])
                ts("dve", GMAT[:, l * 512 + g * 128: l * 512 + (g + 1) * 128], ONE[:], cst(l, "sglg", g), None, ALU.mult,
                   reads=[("ONE",), ("CST",)], writes=[("GMAT", l)])

        def ln_stats(c, half, nch, onescol, zc, zkeys, tg, pm, pq):
            t0, t1 = tg[2], tg[3]
            zb = tg[4:6] if len(tg) >= 6 else None
            lo = half * 1024
            tq = (t0, t1)[c % 2]
            act(wk_bf(tq, 0, 1024), zc(c, lo, 1024), AF.Square, reads=zkeys(c, half), writes=wkk(tq))
            if zb is not None:
                tz = zb[c % 2]
                cp("dve", wk_bf(tz, 0, 1024), zc(c, lo, 1024), zkeys(c, half), wkk(tz))
                for t2 in range(2):
                    mm(ps(pm, t2 * 512, 512), ONESB[:, onescol:onescol + 128], wk_bf(tz, t2 * 512, 512), c == 0, c == nch - 1,
                       reads=wkk(tz) + [("ONESD",)], writes=[psk(pm)], signal=(c == nch - 1 and t2 == 1))
            else:
                for t2 in range(2):
                    mm(ps(pm, t2 * 512, 512), ONESD[:, onescol:onescol + 128], zc(c, lo + t2 * 512, 512), c == 0, c == nch - 1,
                       reads=zkeys(c, half) + [("ONESD",)], writes=[psk(pm)], signal=(c == nch - 1 and t2 == 1))
            for t2 in range(2):
                mm(ps(pq, t2 * 512, 512), ONESB[:, onescol:onescol + 128], wk_bf(tq, t2 * 512, 512), c == 0, c == nch - 1,
                   reads=wkk(tq) + [("ONESD",)], writes=[psk(pq)], signal=(t2 == 1))

        def ln_finish(half, nch, zc, zkeys, cb, tg, pm, pq):
            gMR, gRS, t0, t1 = tg[0], tg[1], tg[2], tg[3]
            lo = half * 1024
            act(wk_f32(gMR), ps(pm), AF.Identity, reads=[psk(pm)], writes=wkk(gMR))
            tt("dve", wk_f32(t0), wk_f32(gMR), wk_f32(gMR), ALU.mult, reads=wkk(gMR), writes=wkk(t0))
            tt("dve", wk_f32(t0), ps(pq), wk_f32(t0), ALU.subtract, reads=[psk(pq)] + wkk(t0), writes=wkk(t0))
            act(wk_f32(t0), wk_f32(t0), AF.Ln, bias=kf(1), reads=wkk(t0) + [("KF",)], writes=wkk(t0))
            act(wk_f32(gRS), wk_f32(t0), AF.Exp, scale=-0.5, reads=wkk(t0), writes=wkk(gRS))
            tt("dve", wk_f32(gMR), wk_f32(gMR), wk_f32(gRS), ALU.mult, reads=wkk(gMR) + wkk(gRS), writes=wkk(gMR))
            for c in range(nch):
                tq = (t0, t1)[c % 2]
                tt("dve", wk_f32(tq), zc(c, lo, 1024), wk_f32(gRS), ALU.mult, reads=zkeys(c, half) + wkk(gRS), writes=wkk(tq))
                tt("dve", wk_f32(tq), wk_f32(tq), wk_f32(gMR), ALU.subtract, reads=wkk(tq) + wkk(gMR), writes=wkk(tq))
                cb(c, half, wk_f32(tq), wkk(tq))

        def layer_norm(nch, onescol, zc, zkeys, cb, tg):
            for half in range(2):
                pm, pq = next_ps(), next_ps()
                for c in range(nch):
                    ln_stats(c, half, nch, onescol, zc, zkeys, tg, pm, pq)
                ln_finish(half, nch, zc, zkeys, cb, tg, pm, pq)

        def store_x(s, c, half, y, ykeys, par=None, xh_write=True):
            lo = half * 1024
            P.dma("sp", xd[s * 8 + c, :, lo: lo + 1024], y, ykeys, [("XD", s, c, half)], ("XDW",) + tuple(ykeys[0]))
            if xh_write:
                act(xh(c, lo, 1024), y, AF.Identity, reads=ykeys, writes=[("XH", c, half)])
                src, skeys, sk = xh(c, lo, 1024), [("XH", c, half)], ("XHDW", c, half)
            else:
                gb = 10 + c % 2
                act(wk_bf(gb, 0, 1024), y, AF.Identity, reads=ykeys, writes=wkk(gb))
                src, skeys, sk = wk_bf(gb, 0, 1024), wkk(gb), ("XHDW", gb)
            if par is not None:
                P.dma("sp", xhd[(par * NSEG + s) * 8 + c, :, lo: lo + 1024], src, skeys,
                      [("XHD", par, s, c, half)], sk)

        XH3 = XH[:].rearrange("p (c w) -> p c w", c=8)
        def load_xh(par, s):
            if state.get("xh") == (par, s):
                return
            state["xh"] = (par, s)
            for c in range(8):
                P.dma("sp", xh(c, 0, T), xhd[(par * NSEG + s) * 8 + c, :, :], [("XHD", par, s, c, 0), ("XHD", par, s, c, 1)],
                      [("XH", c, 0), ("XH", c, 1)], ("XHL", c))
            sl, sr = max(s - 1, 0), min(s + 1, NSEG - 1)
            bl, br = (par * NSEG + sl) * 8, (par * NSEG + sr) * 8
            P.dma("sp", XH3[:, :, T: T + HALO], xhd[bl: bl + 8, :, T - HALO: T].rearrange("c p t -> p c t"),
                  [("XHD", par, sl, c, 1) for c in range(8)], [("XHH",)], ("XHH", 0))
            P.dma("sp", XH3[:, :, T + HALO: T + 2 * HALO], xhd[br: br + 8, :, 0: HALO].rearrange("c p t -> p c t"),
                  [("XHD", par, sr, c, 0) for c in range(8)], [("XHH",)], ("XHH", 1))
        lkL = lambda s: LINK[:, s: s + 1]
        lkR = lambda s: LINK[:, 8 + s: 9 + s]
        P.dma("sp", LINK[:], link_d, [], [("LINK",)], ("LINK",))

        gXA, gXAB, gR, gI, gA, gH1, gH2 = 0, 2, 3, 5, 7, 9, 11

        def rg_head(l, s, h, mode):
            sx = load_unit(l, ("x", h)) if mode == "A" else None
            sg_ = load_unit(l, ("g", h)) if mode == "B" else None
            sgt = load_unit(l, ("gt", h), 512)
            if mode == "B":
                P.dma("sp", wk_f32(gXA, 0, 2048), xad[s * 8 + h, :, :], [("XAD", s, h)], wkk(gXA, 2), ("XAR",))
                P.dma("sp", wk_f32(gH2, 0, 2048), hbd[s * 8 + h, :, :], [("HBD", s, h)], wkk(gH2, 2), ("HBR",))
                for half in range(2):
                    act(wk_bf(gXAB, half * 1024, 1024), wk_f32(gXA + half), AF.Identity, reads=wkk(gXA + half), writes=wkk(gXAB))
                for half in range(2):
                    pg = next_ps()
                    for t2 in range(2):
                        for k in range(8):
                            mm(ps(pg, t2 * 512, 512), wt(sg_, k), xh(k, half * 1024 + t2 * 512, 512), k == 0, k == 7,
                               reads=[("R", sg_), ("XH", k, half)], writes=[psk(pg)], signal=(k == 7 and t2 == 1))
                    act(big_f32(6, half * 1024, 1024), ps(pg), AF.Gelu_apprx_tanh, bias=cst(l, "bin", OFF_RG_G // 128 + h),
                        reads=[psk(pg), ("CST",)], writes=bgk(12 + half))
            else:
                def build_da(l_, h_, buf):
                    for k in range(4):
                        ts("dve", DA[:, buf * 512 + k * 128: buf * 512 + (k + 1) * 128], IDB[:], cst(l_, "caw", k * 8 + h_), None, ALU.mult,
                           reads=[("IDB",), ("CST",)], writes=[("DA", buf)])
                    state["da"][buf] = (l_, h_)
                if (l, h) not in state["da"]:
                    build_da(l, h, 0 if state["da"][0] != (l, (h - 1) % 8) else 1)
                dab = state["da"].index((l, h))
                build_da(l, (h + 1) % 8, 1 - dab)
                bx = cst(l, "bin", OFF_RG_X // 128 + h)
                for half in range(2):
                    pg = next_ps()
                    for t2 in range(2):
                        for k in range(8):
                            mm(ps(pg, t2 * 512, 512), wt(sx, k), xh(k, half * 1024 + t2 * 512, 512), k == 0, k == 7,
                               reads=[("R", sx), ("XH", k, half)], writes=[psk(pg)], signal=(k == 7 and t2 == 1))
                    act(PAD[:, 2 + half * 1024: 2 + half * 1024 + 1024], ps(pg), AF.Identity, bias=bx,
                        reads=[psk(pg), ("CST",)], writes=[("PAD",)])
                pg = next_ps()
                for k in range(8):
                    mm(ps(pg, 0, 32), wt(sx, k), xh(k, T, 32), k == 0, k == 7, reads=[("R", sx), ("XHH",)], writes=[psk(pg)], signal=(k == 7))
                act(HT[:, 0:32], ps(pg, 0, 32), AF.Identity, bias=bx, reads=[psk(pg), ("CST",)], writes=[("HT",)])
                if h == 7 and s > 0:
                    load_xh(state["par"], s - 1)
                act(PAD[:, 0:2], HT[:, 14:16], AF.Identity, scale=lkL(s), reads=[("HT",), ("LINK",)], writes=[("PAD",)])
                act(PAD[:, 2 + T: 3 + T], HT[:, 16:17], AF.Identity, scale=lkR(s), reads=[("HT",), ("LINK",)], writes=[("PAD",)])
                for half in range(2):
                    pg = next_ps()
                    for t2 in range(2):
                        for k in range(4):
                            o = k + half * 1024 + t2 * 512
                            mm(ps(pg, t2 * 512, 512), DA[:, dab * 512 + k * 128: dab * 512 + (k + 1) * 128], PAD[:, o: o + 512],
                               k == 0, k == 3, reads=[("DA", dab), ("PAD",)], writes=[psk(pg)], signal=(k == 3 and t2 == 1))
                    act(wk_f32(gXA + half), ps(pg), AF.Identity, bias=cst(l, "cab", h), reads=[psk(pg), ("CST",)], writes=wkk(gXA + half))
                    cp("dve", wk_bf(gXAB, half * 1024, 1024), wk_f32(gXA + half), wkk(gXA + half), wkk(gXAB))
                P.dma("sp", xad[s * 8 + h, :, :], wk_f32(gXA, 0, 2048), wkk(gXA, 2), [("XAD", s, h)], ("XAW", h % 2))
            d = 1 if mode == "A" else 0
            for gi, (gdst, hb_o) in enumerate(((gR, 16), (gI, 32))):
                for half in range(2):
                    pg = next_ps()
                    for t2 in range(2):
                        mm(ps(pg, t2 * 512, 512), wt(sgt, d * 2 + gi), wk_bf(gXAB, half * 1024 + t2 * 512, 512), True, True,
                           reads=[("R", sgt)] + wkk(gXAB), writes=[psk(pg)], signal=(t2 == 1))
                    act(wk_f32(gdst + half), ps(pg), AF.Tanh, bias=cx(l, hb_o + d * 8 + h), scale=0.5,
                        reads=[psk(pg), ("CX",)], writes=wkk(gdst + half))
            c1ap = cx(l, d * 8 + h)
            c2ap = cx(l, 48 + d * 8 + h)
            order = (1, 0) if mode == "A" else (0, 1)
            for half in order:
                act(wk_f32(gA + half), wk_f32(gR + half), AF.Exp, bias=c1ap, scale=c1ap, reads=wkk(gR + half) + [("CX",)], writes=wkk(gA + half))
                stt("dve", wk_f32(gR + half), wk_f32(gA + half), 0.999998, wk_f32(gA + half), ALU.min, ALU.mult, reads=wkk(gA + half), writes=wkk(gR + half))
            for half in order:
                stt("dve", wk_f32(gI + half), wk_f32(gI + half), 1.0, wk_f32(gXA + half), ALU.add, ALU.mult,
                    reads=wkk(gI + half) + wkk(gXA + half), writes=wkk(gI + half))
            for half in order:
                act(wk_f32(gR + half), wk_f32(gR + half), AF.Sqrt, bias=kf(0), scale=-1.0, reads=wkk(gR + half) + [("KF",)], writes=wkk(gR + half))
                stt("dve", wk_f32(gI + half), wk_f32(gI + half), 0.5, wk_f32(gR + half), ALU.mult, ALU.mult,
                    reads=wkk(gI + half) + wkk(gR + half), writes=wkk(gI + half))
            if mode == "A":
                gH = gH1 if h % 2 == 0 else gH2
                H_ = wk_f32(gH, 0, 2048)
                A_, I_ = wk_f32(gA, 0, 2048), wk_f32(gI, 0, 2048)
                ini = CB[:, h: h + 1]
                P.op("dve", lambda e: e.tensor_tensor_scan(out=H_[:, 2047:1023:-1], data0=A_[:, 2047:1023:-1], data1=I_[:, 2047:1023:-1], initial=ini, op0=ALU.mult, op1=ALU.add),
                     wkk(gA + 1) + wkk(gI + 1) + [("CB",)], wkk(gH + 1))
                ini2 = H_[:, 1024:1025]
                P.op("dve", lambda e: e.tensor_tensor_scan(out=H_[:, 1023::-1], data0=A_[:, 1023::-1], data1=I_[:, 1023::-1], initial=ini2, op0=ALU.mult, op1=ALU.add),
                     wkk(gA) + wkk(gI) + wkk(gH + 1), wkk(gH))
                P.dma("sp", hbd[s * 8 + h, :, :], H_, wkk(gH, 2), [("HBD", s, h)], ("HBW", h % 2))
                ts("dve", CB[:, h: h + 1], H_[:, 0:1], lkL(s), None, ALU.mult, reads=wkk(gH) + [("LINK",)], writes=[("CB",)])
                return
            H_ = wk_f32(gH1, 0, 2048)
            A_, I_ = wk_f32(gA, 0, 2048), wk_f32(gI, 0, 2048)
            ini = CF[:, h: h + 1]
            P.op("dve", lambda e: e.tensor_tensor_scan(out=H_[:, 0:1024], data0=A_[:, 0:1024], data1=I_[:, 0:1024], initial=ini, op0=ALU.mult, op1=ALU.add),
                 wkk(gA) + wkk(gI) + [("CF",)], wkk(gH1))
            ini2 = H_[:, 1023:1024]
            P.op("dve", lambda e: e.tensor_tensor_scan(out=H_[:, 1024:2048], data0=A_[:, 1024:2048], data1=I_[:, 1024:2048], initial=ini2, op0=ALU.mult, op1=ALU.add),
                 wkk(gA + 1) + wkk(gI + 1) + wkk(gH1), wkk(gH1 + 1))
            ts("dve", CF[:, h: h + 1], H_[:, T - 1: T], lkR(s), None, ALU.mult, reads=wkk(gH1 + 1) + [("LINK",)], writes=[("CF",)])
            for half in range(2):
                tt("dve", wk_f32(gH1 + half), wk_f32(gH1 + half), wk_f32(gH2 + half), ALU.add, reads=wkk(gH1 + half) + wkk(gH2 + half), writes=wkk(gH1 + half))
            for half in range(2):
                tt("dve", big_bf(h, half * 1024, 1024), wk_f32(gH1 + half), big_f32(6, half * 1024, 1024), ALU.mult,
                   reads=wkk(gH1 + half) + bgk(12 + half), writes=bgk(h))

        out_deps = []
        for si in range(NSEG):
            for half in range(2):
                for tb in range(8):
                    g = tb
                    tok0 = half * 1024 + tb * 128
                    xt = wk_f32(g)
                    P.dma("sp", xt, xin[si, tok0: tok0 + 128, :], [], wkk(g), ("XT", tb))
                    P.op("dve", lambda e, xt=xt: e.bn_stats(out=SM[:, 0:6], in_=xt[:, 0:512]), wkk(g), [("SM",)])
                    P.op("dve", lambda e, xt=xt: e.bn_stats(out=SM[:, 6:12], in_=xt[:, 512:1024]), wkk(g), [("SM",)])
                    P.op("dve", lambda e, tb=tb: e.bn_aggr(out=SM[:, 32 + 2 * tb: 34 + 2 * tb], in_=SM[:, 0:12]), [("SM",)], [("SM",), ("SM3",)])
                var8 = SM[:, 32:48].rearrange("p (i t) -> p i t", t=2)[:, :, 1:2]
                rs8 = SM[:, 48:56].rearrange("p (i t) -> p i t", t=1)
                act(rs8, var8, AF.Ln, bias=kf(1), reads=[("SM3",), ("KF",)], writes=[("SM4",)])
                act(SM[:, 48:56], SM[:, 48:56], AF.Exp, scale=-0.5, reads=[("SM4",)], writes=[("SM4",)])
                for tb in range(8):
                    xt = wk_f32(tb)
                    ts("dve", xt, xt, SM[:, 32 + 2 * tb: 33 + 2 * tb], SM[:, 48 + tb: 49 + tb], ALU.subtract, ALU.mult,
                       reads=wkk(tb) + [("SM3",), ("SM4",)], writes=wkk(tb))
                for c in range(8):
                    pg = next_ps()
                    for tb in range(8):
                        tr(ps(pg, tb * 128, 128), wk_f32(tb, c * 128, 128), IDF[:], reads=wkk(tb) + [("IDF",)], writes=[psk(pg)],
                           signal=(tb == 7))
                    yg = (8, 9, 12)[c % 3]
                    act(wk_f32(yg), ps(pg), AF.Identity, bias=LNIN[:, 8 + c: 9 + c], scale=LNIN[:, c: c + 1],
                        reads=[psk(pg), ("LNIN",)], writes=wkk(yg))
                    store_x(si, c, half, wk_f32(yg), wkk(yg), par=0, xh_write=False)

        for l in range(L):
            last = (l == L - 1)
            par = l % 2
            state["par"] = par
            memset("dve", CB[:], 0.0, [("CB",)])
            for si in reversed(range(NSEG)):
                load_xh(par, si)
                for h in range(8):
                    rg_head(l, si, h, "A")
            memset("dve", CF[:], 0.0, [("CF",)])
            for si in range(NSEG):
                load_xh(par, si)
                for h in range(8):
                    rg_head(l, si, h, "B")

                for ch in range(4):
                    su = load_unit(l, ("u", ch))
                    for half in range(2):
                        pg = next_ps()
                        for t2 in range(2):
                            for k in range(8):
                                mm(ps(pg, t2 * 512, 512), wt(su, k), xh(k, half * 1024 + t2 * 512, 512), k == 0, k == 7,
                                   reads=[("R", su), ("XH", k, half)], writes=[psk(pg)], signal=(k == 7 and t2 == 1))
                        act(big_bf(8 + ch, half * 1024, 1024), ps(pg), AF.Gelu_apprx_tanh, bias=cst(l, "bin", OFF_SG // 128 + ch),
                            reads=[psk(pg), ("CST",)], writes=bgk(8 + ch))
                sv = [load_unit(l, ("v", j)) for j in range(4)]
                SGO3 = BIG[:, 8 * 2048: 12 * 2048].rearrange("p (g t) -> p g t", g=4)
                for half in range(2):
                    for i in range(8):
                        tb = half * 8 + i
                        pg = next_ps()
                        for j in range(4):
                            for k in range(8):
                                mm(ps(pg, j * 128, 128), xh(k, tb * 128, 128), wt(sv[j], k), k == 0, False,
                                   reads=[("R", sv[j]), ("XH", k, half)], writes=[psk(pg)])
                            mm(ps(pg, j * 128, 128), ONEB[32 * l: 32 * l + 1, :], ROWPB[32 * l: 32 * l + 1, j * 128:(j + 1) * 128], False, False,
                               reads=[("ROWPB",)], writes=[psk(pg)])
                            mm(ps(pg, j * 128, 128), ONEB[32 * l: 32 * l + 1, :], ROWPB[32 * l: 32 * l + 1, 512 + j * 128: 512 + (j + 1) * 128], False, True,
                               reads=[("ROWPB",)], writes=[psk(pg)], signal=(j == 3))
                        V = wk_f32(i, 0, 512)
                        act(V, ps(pg, 0, 512), AF.Gelu_apprx_tanh, reads=[psk(pg)], writes=wkk(i))
                        P.op("dve", lambda e, V=V: e.bn_stats(out=SM[:, 16:22], in_=V), wkk(i), [("SM2",)])
                        P.op("dve", lambda e, i=i: e.bn_aggr(out=SM[:, 32 + 2 * i: 34 + 2 * i], in_=SM[:, 16:22]), [("SM2",)], [("SM2",), ("SM3",)])
                    var8 = SM[:, 32:48].rearrange("p (i t) -> p i t", t=2)[:, :, 1:2]
                    rs8 = SM[:, 48:56].rearrange("p (i t) -> p i t", t=1)
                    act(rs8, var8, AF.Ln, bias=kf(1), reads=[("SM3",), ("KF",)], writes=[("SM4",)])
                    act(SM[:, 48:56], SM[:, 48:56], AF.Exp, scale=-0.5, reads=[("SM4",)], writes=[("SM4",)])
                    for i in range(8):
                        tb = half * 8 + i
                        V = wk_f32(i, 0, 512)
                        gvn = 8 + i % 2
                        VN = wk_bf(gvn, 0, 512)
                        ts("dve", VN, V, SM[:, 32 + 2 * i: 33 + 2 * i], SM[:, 48 + i: 49 + i], ALU.subtract, ALU.mult,
                           reads=wkk(i) + [("SM3",), ("SM4",)], writes=wkk(gvn))
                        pg2 = next_ps()
                        for g in range(4):
                            mm(ps(pg2, g * 128, 128), VN[:, g * 128:(g + 1) * 128], SGWT[:, l * 512 + g * 128: l * 512 + (g + 1) * 128], True, True,
                               reads=wkk(gvn) + [("SGWT", l)], writes=[psk(pg2)], signal=(g == 3))
                        gt_ = 10 + i % 2
                        TT = wk_f32(gt_, 0, 512)
                        tt("dve", TT, ps(pg2, 0, 512), GMAT[:, l * 512:(l + 1) * 512], ALU.mult, reads=[psk(pg2), ("GMAT", l)], writes=wkk(gt_))
                        tt("dve", TT, TT, BIASM[:, l * 512:(l + 1) * 512], ALU.add, reads=wkk(gt_) + [("BIASM", l)], writes=wkk(gt_))
                        uview = SGO3[:, :, tb * 128:(tb + 1) * 128]
                        tt("dve", uview, TT.rearrange("p (g t) -> p g t", g=4), uview, ALU.mult, reads=wkk(gt_) + bgk(8, 4), writes=bgk(8, 4))

                for ch in range(4):
                    sc = load_unit(l, ("c", ch))
                    scg = load_unit(l, ("cg", ch))
                    for k in range(31):
                        ts("dve", DC[:, k * 128:(k + 1) * 128], IDB[:], cst(l, "ccw", k * 4 + ch), None, ALU.mult,
                           reads=[("IDB",), ("CST",)], writes=[("DC",)])
                    bc_, bcg_ = cst(l, "bin", OFF_CC // 128 + ch), cst(l, "bin", OFF_CC // 128 + 4 + ch)
                    for half in range(2):
                        pc, pgg = next_ps(), next_ps()
                        for (pp, ss) in ((pc, sc), (pgg, scg)):
                            for t2 in range(2):
                                for k in range(8):
                                    mm(ps(pp, t2 * 512, 512), wt(ss, k), xh(k, half * 1024 + t2 * 512, 512), k == 0, k == 7,
                                       reads=[("R", ss), ("XH", k, half)], writes=[psk(pp)], signal=(k == 7 and t2 == 1))
                        gs = 8 + half
                        act(wk_f32(gs), ps(pgg), AF.Sigmoid, bias=bcg_, reads=[psk(pgg), ("CST",)], writes=wkk(gs))
                        stt("dve", PAD[:, 15 + half * 1024: 15 + half * 1024 + 1024], ps(pc), bc_, wk_f32(gs),
                            ALU.add, ALU.mult, reads=[psk(pc), ("CST",)] + wkk(gs), writes=[("PAD",)])
                    pc, pgg = next_ps(), next_ps()
                    for (pp, ss) in ((pc, sc), (pgg, scg)):
                        for k in range(8):
                            mm(ps(pp, 0, 32), wt(ss, k), xh(k, T, 32), k == 0, k == 7, reads=[("R", ss), ("XHH",)], writes=[psk(pp)], signal=(k == 7))
                    act(HT[:, 32:64], ps(pgg, 0, 32), AF.Sigmoid, bias=bcg_, reads=[psk(pgg), ("CST",)], writes=[("HT",)])
                    stt("dve", HT[:, 0:32], ps(pc, 0, 32), bc_, HT[:, 32:64], ALU.add, ALU.mult, reads=[psk(pc), ("CST",), ("HT",)], writes=[("HT",)])
                    ts("dve", PAD[:, 0:15], HT[:, 1:16], lkL(si), None, ALU.mult, reads=[("HT",), ("LINK",)], writes=[("PAD",)])
                    ts("dve", PAD[:, 15 + T: 30 + T], HT[:, 16:31], lkR(si), None, ALU.mult, reads=[("HT",), ("LINK",)], writes=[("PAD",)])
                    for half in range(2):
                        pg = next_ps()
                        for t2 in range(2):
                            for k in range(31):
                                o = k + half * 1024 + t2 * 512
                                mm(ps(pg, t2 * 512, 512), DC[:, k * 128:(k + 1) * 128], PAD[:, o: o + 512], k == 0, k == 30,
                                   reads=[("DC",), ("PAD",)], writes=[psk(pg)], signal=(k == 30 and t2 == 1))
                        act(wk_f32(2 * ch + half), ps(pg), AF.Identity, bias=cst(l, "ccb", ch), reads=[psk(pg), ("CST",)], writes=wkk(2 * ch + half))
                def cc_cb(c, half, Tap, Tk):
                    act(big_bf(12 + c, half * 1024, 1024), Tap, AF.Silu, bias=cst(l, "cclb", c), scale=cst(l, "cclg", c),
                        reads=Tk + [("CST",)], writes=bgk(12 + c))
                layer_norm(4, 128, lambda c, lo, n: wk_f32(2 * c, lo, n), lambda c, half: wkk(2 * c + half), cc_cb, [8, 9, 10, 11])

                for oc in range(8):
                    sl = {nm: load_unit(l, (nm, oc), 1024 if nm in ("ga", "gb", "gc", "ba") else 512) for nm in ("ga", "gb", "gc", "ba", "bb", "bc")}
                    for half in range(2):
                        for bi, (gn, yn, nk, src0) in enumerate((("ga", "ba", 8, 0), ("gb", "bb", 4, 8), ("gc", "bc", 4, 12))):
                            pgt, py = next_ps(), next_ps()
                            for t2 in range(2):
                                for k in range(8):
                                    mm(ps(pgt, t2 * 512, 512), wt(sl[gn], k), xh(k, half * 1024 + t2 * 512, 512), k == 0, k == 7,
                                       reads=[("R", sl[gn]), ("XH", k, half)], writes=[psk(pgt)], signal=(k == 7 and t2 == 1))
                            for t2 in range(2):
                                for k in range(nk):
                                    mm(ps(py, t2 * 512, 512), wt(sl[yn], k), big_bf(src0 + k, half * 1024 + t2 * 512, 512), k == 0, k == nk - 1,
                                       reads=[("R", sl[yn])] + bgk(src0 + k), writes=[psk(py)], signal=(k == nk - 1 and t2 == 1))
                            gg = 8 + bi
                            act(wk_f32(gg), ps(pgt), AF.Sigmoid, bias=cst(l, "bin", OFF_GATE // 128 + bi * 8 + oc), reads=[psk(pgt), ("CST",)], writes=wkk(gg))
                            tt("dve", wk_f32(gg), ps(py), wk_f32(gg), ALU.mult, reads=[psk(py)] + wkk(gg), writes=wkk(gg))
                        tt("dve", wk_f32(8), wk_f32(8), wk_f32(9), ALU.add, reads=wkk(8) + wkk(9), writes=wkk(8))
                        tt("dve", wk_bf(oc, half * 1024, 1024), wk_f32(8), wk_f32(10), ALU.add, reads=wkk(8) + wkk(10), writes=wkk(oc))

                def resid(oc, half, pg, bname, zdst, zkeys_w):
                    gx = 9 + (2 * oc + half) % 2
                    ge = 11 + (2 * oc + half) % 2
                    P.dma("sp", wk_f32(gx), xd[si * 8 + oc, :, half * 1024: half * 1024 + 1024], [("XD", si, oc, half)], wkk(gx), ("XDR", gx))
                    act(wk_f32(ge), ps(pg), AF.Identity, bias=cst(l, bname, oc), reads=[psk(pg), ("CST",)], writes=wkk(ge))
                    stt("dve", zdst, wk_f32(gx), ALPHA, wk_f32(ge), ALU.mult, ALU.add, reads=wkk(gx) + wkk(ge), writes=zkeys_w)
                for oc in range(8):
                    so = load_unit(l, ("o", oc))
                    for half in range(2):
                        pg = next_ps()
                        for t2 in range(2):
                            for k in range(8):
                                mm(ps(pg, t2 * 512, 512), wt(so, k), wk_bf(k, half * 1024 + t2 * 512, 512), k == 0, k == 7,
                                   reads=[("R", so)] + wkk(k), writes=[psk(pg)], signal=(k == 7 and t2 == 1))
                        resid(oc, half, pg, "bo", big_f32(oc, half * 1024, 1024), bgk(2 * oc + half))

                def ln_x_cb(gname, bname, par_out):
                    def cb(c, half, Tap, Tk):
                        yg = 4 + (2 * c + half) % 4
                        act(wk_f32(yg), Tap, AF.Identity, bias=cst(l, bname, c), scale=cst(l, gname, c), reads=Tk + [("CST",)], writes=wkk(yg))
                        store_x(si, c, half, wk_f32(yg), wkk(yg), par=par_out, xh_write=(par_out is None))
                    return cb
                layer_norm(8, 0, lambda c, lo, n: big_f32(c, lo, n), lambda c, half: bgk(2 * c + half), ln_x_cb("l1g", "l1b", None), [0, 1, 2, 3, 8, 9])

                for J in range(4):
                    for jj in range(8):
                        j = J * 8 + jj
                        s1 = load_unit(l, ("f1", j))
                        for half in range(2):
                            pg = next_ps()
                            for t2 in range(2):
                                for k in range(8):
                                    mm(ps(pg, t2 * 512, 512), wt(s1, k), xh(k, half * 1024 + t2 * 512, 512), k == 0, k == 7,
                                       reads=[("R", s1), ("XH", k, half)], writes=[psk(pg)], signal=(k == 7 and t2 == 1))
                            gr = 8 + (2 * jj + half) % 2
                            act(wk_f32(gr), ps(pg), AF.Relu, bias=cst(l, "bf1", j), reads=[psk(pg), ("CST",)], writes=wkk(gr))
                            tt("dve", wk_bf(jj, half * 1024, 1024), wk_f32(gr), wk_f32(gr), ALU.mult, reads=wkk(gr), writes=wkk(jj))
                    if J == 3:
                        state["xh"] = None
                        if si < NSEG - 1:
                            load_xh(par, si + 1)
                    s2 = [load_unit(l, ("f2", J * 8 + jj)) for jj in range(8)]
                    for oc in range(8):
                        for half in range(2):
                            pg = next_ps()
                            for t2 in range(2):
                                for jj in range(8):
                                    mm(ps(pg, t2 * 512, 512), wt(s2[jj], oc), wk_bf(jj, half * 1024 + t2 * 512, 512), jj == 0, jj == 7,
                                       reads=[("R", s2[jj])] + wkk(jj), writes=[psk(pg)], signal=(jj == 7 and t2 == 1))
                            acc = big_f32(oc, half * 1024, 1024)
                            if J == 0:
                                cp("dve", acc, ps(pg), [psk(pg)], bgk(2 * oc + half))
                            else:
                                tt("dve", acc, ps(pg), acc, ALU.add, reads=[psk(pg)] + bgk(2 * oc + half), writes=bgk(2 * oc + half))
                for oc in range(8):
                    for half in range(2):
                        gx = 9 + (2 * oc + half) % 2
                        acc = big_f32(oc, half * 1024, 1024)
                        P.dma("sp", wk_f32(gx), xd[si * 8 + oc, :, half * 1024: half * 1024 + 1024], [("XD", si, oc, half)], wkk(gx), ("XDR", gx))
                        act(acc, acc, AF.Identity, bias=cst(l, "bf2", oc), reads=bgk(2 * oc + half) + [("CST",)], writes=bgk(2 * oc + half))
                        stt("dve", acc, wk_f32(gx), ALPHA, acc, ALU.mult, ALU.add, reads=wkk(gx) + bgk(2 * oc + half), writes=bgk(2 * oc + half))
                if not last:
                    layer_norm(8, 0, lambda c, lo, n: big_f32(c, lo, n), lambda c, half: bgk(2 * c + half), ln_x_cb("l2g", "l2b", 1 - par), [0, 1, 2, 3, 8, 9])
                else:
                    def fin_cb(c, half, Tap, Tk):
                        act(wk_f32(4 + c), Tap, AF.Identity, bias=cst(l, "l2b", c), scale=cst(l, "l2g", c), reads=Tk + [("CST",)], writes=wkk(4 + c))
                        if c == 7:
                            for tb in range(8):
                                pg = next_ps()
                                for cc in range(8):
                                    tr(ps(pg, cc * 128, 128), wk_f32(4 + cc, tb * 128, 128), IDF[:], reads=wkk(4 + cc) + [("IDF",)], writes=[psk(pg)],
                                       signal=(cc == 7))
                                go = 12 if tb % 2 == 0 else 3
                                if tb % 2 == 0:
                                    cp("dve", wk_f32(go), ps(pg), [psk(pg)], wkk(go))
                                else:
                                    act(wk_f32(go), ps(pg), AF.Identity, reads=[psk(pg)], writes=wkk(go))
                                tok0 = half * 1024 + tb * 128
                                dep = P.dma("sp", yout[si, tok0: tok0 + 128, :], wk_f32(go), wkk(go), [("YO", si, half, tb)], ("YO", tb % 2))
                                out_deps.append(dep)
                    layer_norm(8, 0, lambda c, lo, n: big_f32(c, lo, n), lambda c, half: bgk(2 * c + half), fin_cb, [0, 1, 2, 3])

        fin = {}
        for s, v in out_deps:
            fin[s] = max(fin.get(s, 0), v)
        P.final_wait("sp", list(fin.items()))
        with nc.Block() as block:
            P.emit(block)
    return nc


def _colchunk(W, j, nk):
    blk = W[:, j * 128:(j + 1) * 128].reshape(nk, 128, 128)
    out = np.zeros((128, 1024), np.float32)
    out[:, : nk * 128] = blk.transpose(1, 0, 2).reshape(128, nk * 128)
    return out


def _prep_weights(inp):
    wu = np.zeros((L, NU, 128, 1024), np.float32)
    cst = np.zeros((L, 128, NCST), np.float32)
    rows = np.zeros((L, 1, NROW), np.float32)
    for l in range(L):
        w_in = inp["w_in"][l]
        for h in range(8):
            wu[l, UNITS[("x", h)]] = _colchunk(w_in, OFF_RG_X // 128 + h, 8)
            wu[l, UNITS[("g", h)]] = _colchunk(w_in, OFF_RG_G // 128 + h, 8)
            gt = np.zeros((128, 1024), np.float32)
            gt[:, 0:128] = inp["rg_wa"][l, 0, h]
            gt[:, 128:256] = inp["rg_wx"][l, 0, h]
            gt[:, 256:384] = inp["rg_wa"][l, 1, h]
            gt[:, 384:512] = inp["rg_wx"][l, 1, h]
            wu[l, UNITS[("gt", h)]] = gt
        for ch in range(4):
            wu[l, UNITS[("u", ch)]] = _colchunk(w_in, OFF_SG // 128 + ch, 8)
            wu[l, UNITS[("v", ch)]] = _colchunk(w_in, OFF_SG // 128 + 4 + ch, 8)
            wu[l, UNITS[("c", ch)]] = _colchunk(w_in, OFF_CC // 128 + ch, 8)
            wu[l, UNITS[("cg", ch)]] = _colchunk(w_in, OFF_CC // 128 + 4 + ch, 8)
        for oc in range(8):
            wu[l, UNITS[("ga", oc)]] = _colchunk(w_in, OFF_GATE // 128 + oc, 8)
            wu[l, UNITS[("gb", oc)]] = _colchunk(w_in, OFF_GATE // 128 + 8 + oc, 8)
            wu[l, UNITS[("gc", oc)]] = _colchunk(w_in, OFF_GATE // 128 + 16 + oc, 8)
            wu[l, UNITS[("ba", oc)]] = _colchunk(inp["w_ba"][l], oc, 8)
            wu[l, UNITS[("bb", oc)]] = _colchunk(inp["w_bb"][l], oc, 4)
            wu[l, UNITS[("bc", oc)]] = _colchunk(inp["w_bc"][l], oc, 4)
            wu[l, UNITS[("o", oc)]] = _colchunk(inp["w_o"][l], oc, 8)
        for j in range(32):
            wu[l, UNITS[("f1", j)]] = _colchunk(inp["w_ff1"][l], j, 8)
            wu[l, UNITS[("f2", j)]] = inp["w_ff2"][l][j * 128:(j + 1) * 128, :]
        def put(name, vec, n):
            cst[l, :, _c[name]: _c[name] + n] = np.asarray(vec, np.float32).reshape(n, 128).T
        put("bin", inp["b_in"][l], 56)
        put("caw", inp["conv_a_w"][l].reshape(-1), 32)
        put("cab", inp["conv_a_b"][l], 8)
        put("rba", inp["rg_ba"][l].reshape(-1), 16)
        put("rbx", inp["rg_bx"][l].reshape(-1), 16)
        put("lam", inp["rg_lambda"][l].reshape(-1), 16)
        put("sglg", inp["sg_ln_g"][l], 4)
        put("ccw", inp["conv_c_w"][l].reshape(-1), 124)
        put("ccb", inp["conv_c_b"][l], 4)
        put("cclg", inp["cc_ln_g"][l], 4)
        put("cclb", inp["cc_ln_b"][l], 4)
        put("bo", inp["b_o"][l], 8)
        put("l1g", inp["ln1_g"][l], 8)
        put("l1b", inp["ln1_b"][l], 8)
        put("bf1", inp["b_ff1"][l], 32)
        put("bf2", inp["b_ff2"][l], 8)
        put("l2g", inp["ln2_g"][l], 8)
        put("l2b", inp["ln2_b"][l], 8)
        rows[l, 0, 0:512] = inp["sg_ln_b"][l]
        rows[l, 0, 512:1024] = inp["sg_b"][l].reshape(-1)
        rows[l, 0, 1024:1536] = inp["b_in"][l][OFF_SG + 512: OFF_SG + 1024]
    lnin = np.zeros((128, 16), np.float32)
    lnin[:, 0:8] = np.asarray(inp["ln_in_g"], np.float32).reshape(8, 128).T
    lnin[:, 8:16] = np.asarray(inp["ln_in_b"], np.float32).reshape(8, 128).T
    sgwt = np.ascontiguousarray(np.asarray(inp["sg_w"], np.float32).transpose(0, 1, 3, 2))
    return wu, cst, rows, lnin, sgwt


_NC_CACHE = {}
_SAMPLE_SLOTS = [(c, k) for c in range(4, 8) for k in range(4)]


def _core_inputs(inp):
    xp = inp["x_prompt"].astype(np.float32, copy=False)
    xs = inp["x_sample"].astype(np.float32, copy=False)
    xins = [np.zeros((NSEG, T, D), np.float32) for _ in range(NCORES)]
    links = [np.zeros((128, 16), np.float32) for _ in range(NCORES)]
    xins[0][:] = xp[0].reshape(NSEG, T, D)
    links[0][:, 1:8] = 1.0
    links[0][:, 8:15] = 1.0
    for i, (c, k) in enumerate(_SAMPLE_SLOTS):
        xins[c][k] = xs[i]
    return xins, links


def kernel(**inputs):
    inp = {k: np.asarray(v) for k, v in inputs.items()}
    wu, cst, rows, lnin, sgwt = _prep_weights(inp)
    xins, links = _core_inputs(inp)
    if "nc" not in _NC_CACHE:
        _NC_CACHE["nc"] = build(None)
    nc = _NC_CACHE["nc"]
    ident = np.eye(128, dtype=np.float32)
    used = {0} | {c for c, _ in _SAMPLE_SLOTS}
    wz = np.zeros_like(wu)
    in_maps = [{"xin": xins[c], "link": links[c], "wu": (wu if c in used else wz), "cst": cst, "rows": rows, "lnin": lnin,
                "sgwt": sgwt, "ident": ident} for c in range(NCORES)]
    res = run_bass_kernel_spmd(nc, in_maps, core_ids=list(range(NCORES)))
    y_prompt = np.ascontiguousarray(res.results[0]["yout"]).reshape(1, NSEG * T, D).astype(np.float32, copy=False)
    y_sample = np.zeros((16, T, D), np.float32)
    for i, (c, k) in enumerate(_SAMPLE_SLOTS):
        y_sample[i] = res.results[c]["yout"][k]
    return (y_prompt, y_sample)
```

```python
import numpy as np
import concourse.bass as bass
import concourse.mybir as mybir
from concourse.bass_utils import run_bass_kernel_spmd

F32 = mybir.dt.float32
BF16 = mybir.dt.bfloat16
AF = mybir.ActivationFunctionType
ALU = mybir.AluOpType

D = 1024
T = 2048
HALO = 16
NSEG = 8
L = 2
NCORES = 8
ALPHA = float((2 * L) ** 0.25)
EPS = 1e-5
OFF_RG_X, OFF_RG_G, OFF_SG, OFF_CC, OFF_GATE = 0, 1024, 2048, 3072, 4096
NSLOT = 10
NWK = 13

_c = {}
_o = 0
def _add(name, n):
    global _o
    _c[name] = _o
    _o += n
_add("bin", 56); _add("caw", 32); _add("cab", 8); _add("rba", 16); _add("rbx", 16); _add("lam", 16)
_add("sglg", 4); _add("ccw", 124); _add("ccb", 4); _add("cclg", 4); _add("cclb", 4)
_add("bo", 8); _add("l1g", 8); _add("l1b", 8); _add("bf1", 32); _add("bf2", 8); _add("l2g", 8); _add("l2b", 8)
NCST = _o
NROW = 1536

def _units():
    u = {}
    n = 0
    for h in range(8):
        u[("x", h)] = n; u[("g", h)] = n + 1; u[("gt", h)] = n + 2; n += 3
    for ch in range(4):
        u[("u", ch)] = n; n += 1
    for j in range(4):
        u[("v", j)] = n; n += 1
    for ch in range(4):
        u[("c", ch)] = n; u[("cg", ch)] = n + 1; n += 2
    for oc in range(8):
        for k, nm in enumerate(("ga", "gb", "gc", "ba", "bb", "bc")):
            u[(nm, oc)] = n + k
        n += 6
    for oc in range(8):
        u[("o", oc)] = n; n += 1
    for J in range(4):
        for jj in range(8):
            u[("f1", J * 8 + jj)] = n; n += 1
        for jj in range(8):
            u[("f2", J * 8 + jj)] = n; n += 1
    return u, n
UNITS, NU = _units()


class Prog:
    ENGS = ("pe", "act", "dve", "pool", "sp")

    def __init__(self, nc, es):
        self.nc = nc
        self.es = es
        self.streams = {e: [] for e in self.ENGS}
        self.cnt = {e: 0 for e in self.ENGS}
        self.known = {e: {} for e in self.ENGS}
        self.buf = {}
        self.sems = {}
        self.dcnt = {}
        for e in self.ENGS:
            self.sems[e] = es.enter_context(nc.semaphore("s_" + e))
        self.out_deps = []

    def dsem(self, key):
        if key not in self.sems:
            self.sems[key] = self.es.enter_context(self.nc.semaphore("d_%d" % len(self.sems)))
            self.dcnt[key] = 0
        return key

    def _deps(self, engine, reads, writes):
        deps = {}
        def add(d):
            if d is None:
                return
            s, v = d
            if deps.get(s, 0) < v:
                deps[s] = v
        for k in reads:
            st = self.buf.get(k)
            if st:
                add(st["w"])
        for k in writes:
            st = self.buf.get(k)
            if st:
                add(st["w"])
                for s, v in st["r"].items():
                    add((s, v))
        waits = []
        kn = self.known[engine]
        for s, v in deps.items():
            if engine == "pe" and s == "pe":
                continue
            if kn.get(s, 0) < v:
                waits.append((s, v))
                kn[s] = v
        return waits

    def _commit(self, dep, reads, writes):
        for k in writes:
            self.buf[k] = {"w": dep, "r": {}}
        for k in reads:
            st = self.buf.setdefault(k, {"w": None, "r": {}})
            if st["r"].get(dep[0], 0) < dep[1]:
                st["r"][dep[0]] = dep[1]

    def op(self, engine, fn, reads=(), writes=(), signal=True):
        waits = self._deps(engine, reads, writes)
        sems = self.sems
        if signal:
            self.cnt[engine] += 1
            dep = (engine, self.cnt[engine])
        else:
            dep = (engine, self.cnt[engine] + 1)
        semh = sems[engine]
        def thunk(eng, waits=waits, fn=fn, signal=signal):
            for s, v in waits:
                eng.wait_ge(sems[s], v)
            ins = fn(eng)
            if signal:
                ins.then_inc(semh, 1)
        self.streams[engine].append(thunk)
        self._commit(dep, reads, writes)

    def dma(self, engine, out, in_, reads, writes, semkey):
        self.dsem(semkey)
        waits = self._deps(engine, reads, writes)
        prev = self.dcnt[semkey]
        if prev > 0 and self.known[engine].get(semkey, 0) < prev:
            waits.append((semkey, prev))
            self.known[engine][semkey] = prev
        self.dcnt[semkey] += 16
        dep = (semkey, self.dcnt[semkey])
        sems = self.sems
        def thunk(eng, waits=waits):
            for s, v in waits:
                eng.wait_ge(sems[s], v)
            eng.dma_start(out=out, in_=in_).then_inc(sems[semkey], 16)
        self.streams[engine].append(thunk)
        self._commit(dep, reads, writes)
        return dep

    def final_wait(self, engine, deps):
        sems = self.sems
        def thunk(eng):
            for s, v in deps:
                eng.wait_ge(sems[s], v)
        self.streams[engine].append(thunk)

    def emit(self, block):
        P = self
        @block.tensor
        def _(e):
            for t in P.streams["pe"]:
                t(e)
        @block.scalar
        def _(e):
            for t in P.streams["act"]:
                t(e)
        @block.vector
        def _(e):
            for t in P.streams["dve"]:
                t(e)
        @block.gpsimd
        def _(e):
            for t in P.streams["pool"]:
                t(e)
        @block.sync
        def _(e):
            for t in P.streams["sp"]:
                t(e)


def build(seg_kinds, debug=False):
    import contextlib
    nc = bass.Bass("TRN2", target_bir_lowering=False)
    xin = nc.dram_tensor("xin", [NSEG, T, D], F32, kind="ExternalInput").ap()
    wu = nc.dram_tensor("wu", [L, NU, 128, 1024], F32, kind="ExternalInput").ap()
    cst_d = nc.dram_tensor("cst", [L, 128, NCST], F32, kind="ExternalInput").ap()
    row_d = nc.dram_tensor("rows", [L, 1, NROW], F32, kind="ExternalInput").ap()
    lnin_d = nc.dram_tensor("lnin", [128, 16], F32, kind="ExternalInput").ap()
    sgwt_d = nc.dram_tensor("sgwt", [L, 4, 128, 128], F32, kind="ExternalInput").ap()
    ident_d = nc.dram_tensor("ident", [128, 128], F32, kind="ExternalInput").ap()
    yout = nc.dram_tensor("yout", [NSEG, T, D], F32, kind="ExternalOutput").ap()
    link_d = nc.dram_tensor("link", [128, 16], F32, kind="ExternalInput").ap()
    xd = nc.dram_tensor("xd", [NSEG * 8, 128, T], F32, kind="Internal").ap()
    xhd = nc.dram_tensor("xhd", [2 * NSEG * 8, 128, T], BF16, kind="Internal").ap()
    hbd = nc.dram_tensor("hbd", [NSEG * 8, 128, T], F32, kind="Internal").ap()
    xad = nc.dram_tensor("xad", [NSEG * 8, 128, T], F32, kind="Internal").ap()

    es = contextlib.ExitStack()
    with es:
        P = Prog(nc, es)
        sb = lambda name, shape, dt: es.enter_context(nc.sbuf_tensor(name, shape, dt))
        XH = sb("XH", [128, 8 * (T + 2 * HALO)], BF16)
        BIG = sb("BIG", [128, 32768], BF16)
        WK = sb("WK", [128, NWK * 2048], BF16)
        PAD = sb("PAD", [128, 2176], BF16)
        DC = sb("DC", [128, 31 * 128], BF16)
        DA = sb("DA", [128, 2 * 4 * 128], BF16)
        RING = sb("RING", [128, NSLOT * 1024], BF16)
        CST = sb("CST", [128, L * NCST], F32)
        CX = sb("CX", [128, L * 64], F32)
        ROWP = sb("ROWP", [33, 512], F32)
        LNIN = sb("LNIN", [128, 16], F32)
        SGWT = sb("SGWT", [128, L * 512], BF16)
        BIASM = sb("BIASM", [128, L * 512], F32)
        GMAT = sb("GMAT", [128, L * 512], F32)
        IDB = sb("IDB", [128, 128], BF16)
        IDF = sb("IDF", [128, 128], F32)
        ONESD = sb("ONESD", [128, 256], F32)
        ONE = sb("ONE", [128, 128], F32)
        ONESB = sb("ONESB", [128, 256], BF16)
        ONEB = sb("ONEB", [33, 128], BF16)
        ROWPB = sb("ROWPB", [33, 1024], BF16)
        KF = sb("KF", [128, 8], F32)
        SM = sb("SM", [128, 64], F32)
        LINK = sb("LINK", [128, 16], F32)
        CF = sb("CF", [128, 8], F32)
        CB = sb("CB", [128, 8], F32)
        HT = sb("HT", [128, 64], F32)
        PS = es.enter_context(nc.psum_tensor("PS", [128, 4096], F32))

        XW = T + 2 * HALO
        def xh(c, lo, n):
            return XH[:, c * XW + lo: c * XW + lo + n]
        def big_bf(g, lo=0, n=2048):
            return BIG[:, g * 2048 + lo: g * 2048 + lo + n]
        def big_f32(c, lo=0, n=2048):
            return BIG[:, c * 4096: (c + 1) * 4096].bitcast(F32)[:, lo: lo + n]
        def wk_bf(g, lo=0, n=2048):
            return WK[:, g * 2048 + lo: g * 2048 + lo + n]
        def wk_f32(g, lo=0, n=1024):
            return WK[:, g * 2048: g * 2048 + 2 * (lo + n)].bitcast(F32)[:, lo: lo + n]
        def wkk(g, n=1):
            return [("W", g + i) for i in range(n)]
        def bgk(g, n=1):
            return [("B", g + i) for i in range(n)]
        def ps(g, lo=0, n=1024):
            return PS[:, g * 1024 + lo: g * 1024 + lo + n]
        psk = lambda g: ("PS", g)
        def cst(l, name, j=0):
            o = l * NCST + _c[name] + j
            return CST[:, o: o + 1]
        def cx(l, o):
            return CX[:, l * 64 + o: l * 64 + o + 1]
        kf = lambda j: KF[:, j: j + 1]

        state = {"psg": 0, "slot": 0, "da": [None, None]}
        def next_ps():
            g = state["psg"]
            state["psg"] = (g + 1) % 4
            return g

        def load_unit(l, key, n=1024):
            s = state["slot"]
            state["slot"] = (s + 1) % NSLOT
            u = UNITS[key]
            P.dma("pool", RING[:, s * 1024: s * 1024 + n], wu[l, u, :, 0:n], reads=[], writes=[("R", s)], semkey=("R", s))
            return s
        def wt(s, k, n=128, width=128):
            return RING[:, s * 1024 + k * width: s * 1024 + k * width + n]

        def act(out, in_, func, bias=None, scale=1.0, reads=(), writes=()):
            def fn(e):
                kw = {}
                if bias is not None:
                    kw["bias"] = bias
                return e.activation(out=out, in_=in_, func=func, scale=scale, **kw)
            P.op("act", fn, reads, writes)
        def tt(eng, out, in0, in1, op, reads=(), writes=()):
            P.op(eng, lambda e: e.tensor_tensor(out=out, in0=in0, in1=in1, op=op), reads, writes)
        def ts(eng, out, in0, s1, s2, op0, op1=None, reads=(), writes=()):
            if op1 is None:
                P.op(eng, lambda e: e.tensor_scalar(out=out, in0=in0, scalar1=s1, scalar2=None, op0=op0), reads, writes)
            else:
                P.op(eng, lambda e: e.tensor_scalar(out=out, in0=in0, scalar1=s1, scalar2=s2, op0=op0, op1=op1), reads, writes)
        def stt(eng, out, in0, scalar, in1, op0, op1, reads=(), writes=()):
            P.op(eng, lambda e: e.scalar_tensor_tensor(out=out, in0=in0, scalar=scalar, in1=in1, op0=op0, op1=op1), reads, writes)
        def cp(eng, out, in_, reads=(), writes=()):
            P.op(eng, lambda e: e.tensor_copy(out=out, in_=in_), reads, writes)
        def mm(out, lhsT, rhs, start, stop, reads=(), writes=(), signal=False):
            P.op("pe", lambda e: e.matmul(out, lhsT, rhs, start=start, stop=stop), reads, writes, signal=signal)
        def tr(out, in_, ident, reads=(), writes=(), signal=False):
            P.op("pe", lambda e: e.transpose(out, in_, ident), reads, writes, signal=signal)
        def memset(eng, ap, val, writes=()):
            P.op(eng, lambda e: e.memset(ap, val), (), writes)

        memset("dve", KF[:, 0:1], 1.0, [("KF",)])
        memset("dve", KF[:, 1:2], EPS, [("KF",)])
        memset("dve", KF[:, 2:3], 0.0, [("KF",)])
        memset("dve", KF[:, 3:4], -0.5, [("KF",)])
        memset("dve", KF[:, 4:5], 0.5, [("KF",)])
        memset("dve", ONE[:], 1.0, [("ONE",)])
        memset("dve", ONESD[:, 0:128], 1.0 / 1024.0, [("ONESD",)])
        memset("dve", ONESD[:, 128:256], 1.0 / 512.0, [("ONESD",)])
        memset("dve", ONESB[:, 0:128], 1.0 / 1024.0, [("ONESD",)])
        memset("dve", ONESB[:, 128:256], 1.0 / 512.0, [("ONESD",)])
        P.dma("sp", IDF[:], ident_d, [], [("IDF",)], ("IDF",))
        cp("dve", IDB[:], IDF[:], [("IDF",)], [("IDB",)])
        P.dma("sp", CST[:].rearrange("p (l n) -> p l n", l=L), cst_d.rearrange("l p n -> p l n"), [], [("CST",)], ("CST",))
        for l in range(L):
            P.dma("sp", ROWP[32 * l: 32 * l + 1, :], row_d[l, :, 1024:1536], [], [("ROWP",)], ("ROWP",))
        P.dma("sp", LNIN[:], lnin_d, [], [("LNIN",)], ("LNIN",))
        memset("dve", ONEB[:], 1.0, [("ONEB",)])
        for l in range(L):
            pr = slice(32 * l, 32 * l + 1)
            cp("dve", ROWPB[pr, 0:512], ROWP[pr, :], [("ROWP",)], [("ROWPB",)])
            tt("dve", ROWP[pr, :], ROWP[pr, :], ROWPB[pr, 0:512], ALU.subtract, reads=[("ROWP",), ("ROWPB",)], writes=[("ROWP",)])
            cp("dve", ROWPB[pr, 512:1024], ROWP[pr, :], [("ROWP",)], [("ROWPB",)])
        for l in range(L):
            lamv = CST[:, l * NCST + _c["lam"]: l * NCST + _c["lam"] + 16]
            c1 = CX[:, l * 64: l * 64 + 16]
            act(c1, lamv, AF.Exp, scale=-1.0, reads=[("CST",)], writes=[("CX",)])
            act(c1, c1, AF.Ln, bias=kf(0), reads=[("CX",), ("KF",)], writes=[("CX",)])
            ts("dve", c1, c1, -4.0, None, ALU.mult, reads=[("CX",)], writes=[("CX",)])
            ts("dve", CX[:, l * 64 + 48: l * 64 + 64], c1, 2.0, None, ALU.mult, reads=[("CX",)], writes=[("CX",)])
            for nm, o in (("rba", 16), ("rbx", 32)):
                src = CST[:, l * NCST + _c[nm]: l * NCST + _c[nm] + 16]
                ts("dve", CX[:, l * 64 + o: l * 64 + o + 16], src, 0.5, None, ALU.mult, reads=[("CST",)], writes=[("CX",)])
            P.dma("pool", SGWT[:, l * 512:(l + 1) * 512].rearrange("q (g p) -> q g p", g=4),
                  sgwt_d[l].rearrange("g q p -> q g p"), [], [("SGWT", l)], ("SGWT", l))
            SGWF = wk_f32(2, 0, 512)
            RSR = wk_f32(1, 0, 128)[0:1, :]
            ROWT = wk_f32(0, 0, 1024)[0:1, :]
            P.dma("sp", SGWF.rearrange("q (g p) -> q g p", g=4), sgwt_d[l].rearrange("g q p -> q g p"),
                  [], wkk(2), ("SGWF",))
            P.dma("sp", ROWT, row_d[l, :, 0:1024], [], wkk(0), ("ROWT",))
            for g in range(4):
                pg = next_ps()
                mm(ps(pg, 0, 128)[0:1, :], ONE[:, 0:1], SGWF[:, g * 128:(g + 1) * 128], True, True,
                   reads=[("ONE",)] + wkk(2), writes=[psk(pg)], signal=True)
                cp("dve", RSR, ps(pg, 0, 128)[0:1, :], [psk(pg)], wkk(1))
                pg2 = next_ps()
                mm(ps(pg2, 0, 128), ROWT[:, g * 128:(g + 1) * 128], RSR, True, False,
                   reads=wkk(0) + wkk(1), writes=[psk(pg2)])
                mm(ps(pg2, 0, 128), ONE[0:1, :], ROWT[:, 512 + g * 128: 512 + (g + 1) * 128], False, True,
                   reads=wkk(0) + [("ONE",)], writes=[psk(pg2)], signal=True)
                cp("dve", BIASM[:, l * 512 + g * 128: l * 512 + (g + 1) * 128], ps(pg2, 0, 128), [psk(pg2)], [("BIASM", l)])
                ts("dve", GMAT[:, l * 512 + g * 128: l * 512 + (g + 1) * 128], ONE[:], cst(l, "sglg", g), None, ALU.mult,
                   reads=[("ONE",), ("CST",)], writes=[("GMAT", l)])

        def ln_stats(c, half, nch, onescol, zc, zkeys, tg, pm, pq):
            t0, t1 = tg[2], tg[3]
            zb = tg[4:6] if len(tg) >= 6 else None
            lo = half * 1024
            tq = (t0, t1)[c % 2]
            act(wk_bf(tq, 0, 1024), zc(c, lo, 1024), AF.Square, reads=zkeys(c, half), writes=wkk(tq))
            if zb is not None:
                tz = zb[c % 2]
                cp("dve", wk_bf(tz, 0, 1024), zc(c, lo, 1024), zkeys(c, half), wkk(tz))
                for t2 in range(2):
                    mm(ps(pm, t2 * 512, 512), ONESB[:, onescol:onescol + 128], wk_bf(tz, t2 * 512, 512), c == 0, c == nch - 1,
                       reads=wkk(tz) + [("ONESD",)], writes=[psk(pm)], signal=(c == nch - 1 and t2 == 1))
            else:
                for t2 in range(2):
                    mm(ps(pm, t2 * 512, 512), ONESD[:, onescol:onescol + 128], zc(c, lo + t2 * 512, 512), c == 0, c == nch - 1,
                       reads=zkeys(c, half) + [("ONESD",)], writes=[psk(pm)], signal=(c == nch - 1 and t2 == 1))
            for t2 in range(2):
                mm(ps(pq, t2 * 512, 512), ONESB[:, onescol:onescol + 128], wk_bf(tq, t2 * 512, 512), c == 0, c == nch - 1,
                   reads=wkk(tq) + [("ONESD",)], writes=[psk(pq)], signal=(t2 == 1))

        def ln_finish(half, nch, zc, zkeys, cb, tg, pm, pq):
            gMR, gRS, t0, t1 = tg[0], tg[1], tg[2], tg[3]
            lo = half * 1024
            act(wk_f32(gMR), ps(pm), AF.Identity, reads=[psk(pm)], writes=wkk(gMR))
            tt("dve", wk_f32(t0), wk_f32(gMR), wk_f32(gMR), ALU.mult, reads=wkk(gMR), writes=wkk(t0))
            tt("dve", wk_f32(t0), ps(pq), wk_f32(t0), ALU.subtract, reads=[psk(pq)] + wkk(t0), writes=wkk(t0))
            act(wk_f32(t0), wk_f32(t0), AF.Ln, bias=kf(1), reads=wkk(t0) + [("KF",)], writes=wkk(t0))
            act(wk_f32(gRS), wk_f32(t0), AF.Exp, scale=-0.5, reads=wkk(t0), writes=wkk(gRS))
            tt("dve", wk_f32(gMR), wk_f32(gMR), wk_f32(gRS), ALU.mult, reads=wkk(gMR) + wkk(gRS), writes=wkk(gMR))
            for c in range(nch):
                tq = (t0, t1)[c % 2]
                tt("dve", wk_f32(tq), zc(c, lo, 1024), wk_f32(gRS), ALU.mult, reads=zkeys(c, half) + wkk(gRS), writes=wkk(tq))
                tt("dve", wk_f32(tq), wk_f32(tq), wk_f32(gMR), ALU.subtract, reads=wkk(tq) + wkk(gMR), writes=wkk(tq))
                cb(c, half, wk_f32(tq), wkk(tq))

        def layer_norm(nch, onescol, zc, zkeys, cb, tg):
            for half in range(2):
                pm, pq = next_ps(), next_ps()
                for c in range(nch):
                    ln_stats(c, half, nch, onescol, zc, zkeys, tg, pm, pq)
                ln_finish(half, nch, zc, zkeys, cb, tg, pm, pq)

        def store_x(s, c, half, y, ykeys, par=None, xh_write=True):
            lo = half * 1024
            P.dma("sp", xd[s * 8 + c, :, lo: lo + 1024], y, ykeys, [("XD", s, c, half)], ("XDW",) + tuple(ykeys[0]))
            if xh_write:
                act(xh(c, lo, 1024), y, AF.Identity, reads=ykeys, writes=[("XH", c, half)])
                src, skeys, sk = xh(c, lo, 1024), [("XH", c, half)], ("XHDW", c, half)
            else:
                gb = 10 + c % 2
                act(wk_bf(gb, 0, 1024), y, AF.Identity, reads=ykeys, writes=wkk(gb))
                src, skeys, sk = wk_bf(gb, 0, 1024), wkk(gb), ("XHDW", gb)
            if par is not None:
                P.dma("sp", xhd[(par * NSEG + s) * 8 + c, :, lo: lo + 1024], src, skeys,
                      [("XHD", par, s, c, half)], sk)

        XH3 = XH[:].rearrange("p (c w) -> p c w", c=8)
        def load_xh(par, s):
            if state.get("xh") == (par, s):
                return
            state["xh"] = (par, s)
            for c in range(8):
                P.dma("sp", xh(c, 0, T), xhd[(par * NSEG + s) * 8 + c, :, :], [("XHD", par, s, c, 0), ("XHD", par, s, c, 1)],
                      [("XH", c, 0), ("XH", c, 1)], ("XHL", c))
            sl, sr = max(s - 1, 0), min(s + 1, NSEG - 1)
            bl, br = (par * NSEG + sl) * 8, (par * NSEG + sr) * 8
            P.dma("sp", XH3[:, :, T: T + HALO], xhd[bl: bl + 8, :, T - HALO: T].rearrange("c p t -> p c t"),
                  [("XHD", par, sl, c, 1) for c in range(8)], [("XHH",)], ("XHH", 0))
            P.dma("sp", XH3[:, :, T + HALO: T + 2 * HALO], xhd[br: br + 8, :, 0: HALO].rearrange("c p t -> p c t"),
                  [("XHD", par, sr, c, 0) for c in range(8)], [("XHH",)], ("XHH", 1))
        lkL = lambda s: LINK[:, s: s + 1]
        lkR = lambda s: LINK[:, 8 + s: 9 + s]
        P.dma("sp", LINK[:], link_d, [], [("LINK",)], ("LINK",))

        gXA, gXAB, gR, gI, gA, gH1, gH2 = 0, 2, 3, 5, 7, 9, 11

        def rg_head(l, s, h, mode):
            sx = load_unit(l, ("x", h)) if mode == "A" else None
            sg_ = load_unit(l, ("g", h)) if mode == "B" else None
            sgt = load_unit(l, ("gt", h), 512)
            if mode == "B":
                P.dma("sp", wk_f32(gXA, 0, 2048), xad[s * 8 + h, :, :], [("XAD", s, h)], wkk(gXA, 2), ("XAR",))
                P.dma("sp", wk_f32(gH2, 0, 2048), hbd[s * 8 + h, :, :], [("HBD", s, h)], wkk(gH2, 2), ("HBR",))
                for half in range(2):
                    act(wk_bf(gXAB, half * 1024, 1024), wk_f32(gXA + half), AF.Identity, reads=wkk(gXA + half), writes=wkk(gXAB))
                for half in range(2):
                    pg = next_ps()
                    for t2 in range(2):
                        for k in range(8):
                            mm(ps(pg, t2 * 512, 512), wt(sg_, k), xh(k, half * 1024 + t2 * 512, 512), k == 0, k == 7,
                               reads=[("R", sg_), ("XH", k, half)], writes=[psk(pg)], signal=(k == 7 and t2 == 1))
                    act(big_f32(6, half * 1024, 1024), ps(pg), AF.Gelu_apprx_tanh, bias=cst(l, "bin", OFF_RG_G // 128 + h),
                        reads=[psk(pg), ("CST",)], writes=bgk(12 + half))
            else:
                def build_da(l_, h_, buf):
                    for k in range(4):
                        ts("dve", DA[:, buf * 512 + k * 128: buf * 512 + (k + 1) * 128], IDB[:], cst(l_, "caw", k * 8 + h_), None, ALU.mult,
                           reads=[("IDB",), ("CST",)], writes=[("DA", buf)])
                    state["da"][buf] = (l_, h_)
                if (l, h) not in state["da"]:
                    build_da(l, h, 0 if state["da"][0] != (l, (h - 1) % 8) else 1)
                dab = state["da"].index((l, h))
                build_da(l, (h + 1) % 8, 1 - dab)
                bx = cst(l, "bin", OFF_RG_X // 128 + h)
                for half in range(2):
                    pg = next_ps()
                    for t2 in range(2):
                        for k in range(8):
                            mm(ps(pg, t2 * 512, 512), wt(sx, k), xh(k, half * 1024 + t2 * 512, 512), k == 0, k == 7,
                               reads=[("R", sx), ("XH", k, half)], writes=[psk(pg)], signal=(k == 7 and t2 == 1))
                    act(PAD[:, 2 + half * 1024: 2 + half * 1024 + 1024], ps(pg), AF.Identity, bias=bx,
                        reads=[psk(pg), ("CST",)], writes=[("PAD",)])
                pg = next_ps()
                for k in range(8):
                    mm(ps(pg, 0, 32), wt(sx, k), xh(k, T, 32), k == 0, k == 7, reads=[("R", sx), ("XHH",)], writes=[psk(pg)], signal=(k == 7))
                act(HT[:, 0:32], ps(pg, 0, 32), AF.Identity, bias=bx, reads=[psk(pg), ("CST",)], writes=[("HT",)])
                if h == 7 and s > 0:
                    load_xh(state["par"], s - 1)
                act(PAD[:, 0:2], HT[:, 14:16], AF.Identity, scale=lkL(s), reads=[("HT",), ("LINK",)], writes=[("PAD",)])
                act(PAD[:, 2 + T: 3 + T], HT[:, 16:17], AF.Identity, scale=lkR(s), reads=[("HT",), ("LINK",)], writes=[("PAD",)])
                for half in range(2):
                    pg = next_ps()
                    for t2 in range(2):
                        for k in range(4):
                            o = k + half * 1024 + t2 * 512
                            mm(ps(pg, t2 * 512, 512), DA[:, dab * 512 + k * 128: dab * 512 + (k + 1) * 128], PAD[:, o: o + 512],
                               k == 0, k == 3, reads=[("DA", dab), ("PAD",)], writes=[psk(pg)], signal=(k == 3 and t2 == 1))
                    act(wk_f32(gXA + half), ps(pg), AF.Identity, bias=cst(l, "cab", h), reads=[psk(pg), ("CST",)], writes=wkk(gXA + half))
                    cp("dve", wk_bf(gXAB, half * 1024, 1024), wk_f32(gXA + half), wkk(gXA + half), wkk(gXAB))
                P.dma("sp", xad[s * 8 + h, :, :], wk_f32(gXA, 0, 2048), wkk(gXA, 2), [("XAD", s, h)], ("XAW", h % 2))
            d = 1 if mode == "A" else 0
            for gi, (gdst, hb_o) in enumerate(((gR, 16), (gI, 32))):
                for half in range(2):
                    pg = next_ps()
                    for t2 in range(2):
                        mm(ps(pg, t2 * 512, 512), wt(sgt, d * 2 + gi), wk_bf(gXAB, half * 1024 + t2 * 512, 512), True, True,
                           reads=[("R", sgt)] + wkk(gXAB), writes=[psk(pg)], signal=(t2 == 1))
                    act(wk_f32(gdst + half), ps(pg), AF.Tanh, bias=cx(l, hb_o + d * 8 + h), scale=0.5,
                        reads=[psk(pg), ("CX",)], writes=wkk(gdst + half))
            c1ap = cx(l, d * 8 + h)
            c2ap = cx(l, 48 + d * 8 + h)
            order = (1, 0) if mode == "A" else (0, 1)
            for half in order:
                act(wk_f32(gA + half), wk_f32(gR + half), AF.Exp, bias=c1ap, scale=c1ap, reads=wkk(gR + half) + [("CX",)], writes=wkk(gA + half))
                stt("dve", wk_f32(gR + half), wk_f32(gA + half), 0.999998, wk_f32(gA + half), ALU.min, ALU.mult, reads=wkk(gA + half), writes=wkk(gR + half))
            for half in order:
                stt("dve", wk_f32(gI + half), wk_f32(gI + half), 1.0, wk_f32(gXA + half), ALU.add, ALU.mult,
                    reads=wkk(gI + half) + wkk(gXA + half), writes=wkk(gI + half))
            for half in order:
                act(wk_f32(gR + half), wk_f32(gR + half), AF.Sqrt, bias=kf(0), scale=-1.0, reads=wkk(gR + half) + [("KF",)], writes=wkk(gR + half))
                stt("dve", wk_f32(gI + half), wk_f32(gI + half), 0.5, wk_f32(gR + half), ALU.mult, ALU.mult,
                    reads=wkk(gI + half) + wkk(gR + half), writes=wkk(gI + half))
            if mode == "A":
                gH = gH1 if h % 2 == 0 else gH2
                H_ = wk_f32(gH, 0, 2048)
                A_, I_ = wk_f32(gA, 0, 2048), wk_f32(gI, 0, 2048)
                ini = CB[:, h: h + 1]
                P.op("dve", lambda e: e.tensor_tensor_scan(out=H_[:, 2047:1023:-1], data0=A_[:, 2047:1023:-1], data1=I_[:, 2047:1023:-1], initial=ini, op0=ALU.mult, op1=ALU.add),
                     wkk(gA + 1) + wkk(gI + 1) + [("CB",)], wkk(gH + 1))
                ini2 = H_[:, 1024:1025]
                P.op("dve", lambda e: e.tensor_tensor_scan(out=H_[:, 1023::-1], data0=A_[:, 1023::-1], data1=I_[:, 1023::-1], initial=ini2, op0=ALU.mult, op1=ALU.add),
                     wkk(gA) + wkk(gI) + wkk(gH + 1), wkk(gH))
                P.dma("sp", hbd[s * 8 + h, :, :], H_, wkk(gH, 2), [("HBD", s, h)], ("HBW", h % 2))
                ts("dve", CB[:, h: h + 1], H_[:, 0:1], lkL(s), None, ALU.mult, reads=wkk(gH) + [("LINK",)], writes=[("CB",)])
                return
            H_ = wk_f32(gH1, 0, 2048)
            A_, I_ = wk_f32(gA, 0, 2048), wk_f32(gI, 0, 2048)
            ini = CF[:, h: h + 1]
            P.op("dve", lambda e: e.tensor_tensor_scan(out=H_[:, 0:1024], data0=A_[:, 0:1024], data1=I_[:, 0:1024], initial=ini, op0=ALU.mult, op1=ALU.add),
                 wkk(gA) + wkk(gI) + [("CF",)], wkk(gH1))
            ini2 = H_[:, 1023:1024]
            P.op("dve", lambda e: e.tensor_tensor_scan(out=H_[:, 1024:2048], data0=A_[:, 1024:2048], data1=I_[:, 1024:2048], initial=ini2, op0=ALU.mult, op1=ALU.add),
                 wkk(gA + 1) + wkk(gI + 1) + wkk(gH1), wkk(gH1 + 1))
            ts("dve", CF[:, h: h + 1], H_[:, T - 1: T], lkR(s), None, ALU.mult, reads=wkk(gH1 + 1) + [("LINK",)], writes=[("CF",)])
            for half in range(2):
                tt("dve", wk_f32(gH1 + half), wk_f32(gH1 + half), wk_f32(gH2 + half), ALU.add, reads=wkk(gH1 + half) + wkk(gH2 + half), writes=wkk(gH1 + half))
            for half in range(2):
                tt("dve", big_bf(h, half * 1024, 1024), wk_f32(gH1 + half), big_f32(6, half * 1024, 1024), ALU.mult,
                   reads=wkk(gH1 + half) + bgk(12 + half), writes=bgk(h))

        out_deps = []
        for si in range(NSEG):
            for half in range(2):
                for tb in range(8):
                    g = tb
                    tok0 = half * 1024 + tb * 128
                    xt = wk_f32(g)
                    P.dma("sp", xt, xin[si, tok0: tok0 + 128, :], [], wkk(g), ("XT", tb))
                    P.op("dve", lambda e, xt=xt: e.bn_stats(out=SM[:, 0:6], in_=xt[:, 0:512]), wkk(g), [("SM",)])
                    P.op("dve", lambda e, xt=xt: e.bn_stats(out=SM[:, 6:12], in_=xt[:, 512:1024]), wkk(g), [("SM",)])
                    P.op("dve", lambda e, tb=tb: e.bn_aggr(out=SM[:, 32 + 2 * tb: 34 + 2 * tb], in_=SM[:, 0:12]), [("SM",)], [("SM",), ("SM3",)])
                var8 = SM[:, 32:48].rearrange("p (i t) -> p i t", t=2)[:, :, 1:2]
                rs8 = SM[:, 48:56].rearrange("p (i t) -> p i t", t=1)
                act(rs8, var8, AF.Ln, bias=kf(1), reads=[("SM3",), ("KF",)], writes=[("SM4",)])
                act(SM[:, 48:56], SM[:, 48:56], AF.Exp, scale=-0.5, reads=[("SM4",)], writes=[("SM4",)])
                for tb in range(8):
                    xt = wk_f32(tb)
                    ts("dve", xt, xt, SM[:, 32 + 2 * tb: 33 + 2 * tb], SM[:, 48 + tb: 49 + tb], ALU.subtract, ALU.mult,
                       reads=wkk(tb) + [("SM3",), ("SM4",)], writes=wkk(tb))
                for c in range(8):
                    pg = next_ps()
                    for tb in range(8):
                        tr(ps(pg, tb * 128, 128), wk_f32(tb, c * 128, 128), IDF[:], reads=wkk(tb) + [("IDF",)], writes=[psk(pg)],
                           signal=(tb == 7))
                    yg = (8, 9, 12)[c % 3]
                    act(wk_f32(yg), ps(pg), AF.Identity, bias=LNIN[:, 8 + c: 9 + c], scale=LNIN[:, c: c + 1],
                        reads=[psk(pg), ("LNIN",)], writes=wkk(yg))
                    store_x(si, c, half, wk_f32(yg), wkk(yg), par=0, xh_write=False)

        for l in range(L):
            last = (l == L - 1)
            par = l % 2
            state["par"] = par
            memset("dve", CB[:], 0.0, [("CB",)])
            for si in reversed(range(NSEG)):
                load_xh(par, si)
                for h in range(8):
                    rg_head(l, si, h, "A")
            memset("dve", CF[:], 0.0, [("CF",)])
            for si in range(NSEG):
                load_xh(par, si)
                for h in range(8):
                    rg_head(l, si, h, "B")

                for ch in range(4):
                    su = load_unit(l, ("u", ch))
                    for half in range(2):
                        pg = next_ps()
                        for t2 in range(2):
                            for k in range(8):
                                mm(ps(pg, t2 * 512, 512), wt(su, k), xh(k, half * 1024 + t2 * 512, 512), k == 0, k == 7,
                                   reads=[("R", su), ("XH", k, half)], writes=[psk(pg)], signal=(k == 7 and t2 == 1))
                        act(big_bf(8 + ch, half * 1024, 1024), ps(pg), AF.Gelu_apprx_tanh, bias=cst(l, "bin", OFF_SG // 128 + ch),
                            reads=[psk(pg), ("CST",)], writes=bgk(8 + ch))
                sv = [load_unit(l, ("v", j)) for j in range(4)]
                SGO3 = BIG[:, 8 * 2048: 12 * 2048].rearrange("p (g t) -> p g t", g=4)
                for half in range(2):
                    for i in range(8):
                        tb = half * 8 + i
                        pg = next_ps()
                        for j in range(4):
                            for k in range(8):
                                mm(ps(pg, j * 128, 128), xh(k, tb * 128, 128), wt(sv[j], k), k == 0, False,
                                   reads=[("R", sv[j]), ("XH", k, half)], writes=[psk(pg)])
                            mm(ps(pg, j * 128, 128), ONEB[32 * l: 32 * l + 1, :], ROWPB[32 * l: 32 * l + 1, j * 128:(j + 1) * 128], False, False,
                               reads=[("ROWPB",)], writes=[psk(pg)])
                            mm(ps(pg, j * 128, 128), ONEB[32 * l: 32 * l + 1, :], ROWPB[32 * l: 32 * l + 1, 512 + j * 128: 512 + (j + 1) * 128], False, True,
                               reads=[("ROWPB",)], writes=[psk(pg)], signal=(j == 3))
                        V = wk_f32(i, 0, 512)
                        act(V, ps(pg, 0, 512), AF.Gelu_apprx_tanh, reads=[psk(pg)], writes=wkk(i))
                        P.op("dve", lambda e, V=V: e.bn_stats(out=SM[:, 16:22], in_=V), wkk(i), [("SM2",)])
                        P.op("dve", lambda e, i=i: e.bn_aggr(out=SM[:, 32 + 2 * i: 34 + 2 * i], in_=SM[:, 16:22]), [("SM2",)], [("SM2",), ("SM3",)])
                    var8 = SM[:, 32:48].rearrange("p (i t) -> p i t", t=2)[:, :, 1:2]
                    rs8 = SM[:, 48:56].rearrange("p (i t) -> p i t", t=1)
                    act(rs8, var8, AF.Ln, bias=kf(1), reads=[("SM3",), ("KF",)], writes=[("SM4",)])
                    act(SM[:, 48:56], SM[:, 48:56], AF.Exp, scale=-0.5, reads=[("SM4",)], writes=[("SM4",)])
                    for i in range(8):
                        tb = half * 8 + i
                        V = wk_f32(i, 0, 512)
                        gvn = 8 + i % 2
                        VN = wk_bf(gvn, 0, 512)
                        ts("dve", VN, V, SM[:, 32 + 2 * i: 33 + 2 * i], SM[:, 48 + i: 49 + i], ALU.subtract, ALU.mult,
                           reads=wkk(i) + [("SM3",), ("SM4",)], writes=wkk(gvn))
                        pg2 = next_ps()
                        for g in range(4):
                            mm(ps(pg2, g * 128, 128), VN[:, g * 128:(g + 1) * 128], SGWT[:, l * 512 + g * 128: l * 512 + (g + 1) * 128], True, True,
                               reads=wkk(gvn) + [("SGWT", l)], writes=[psk(pg2)], signal=(g == 3))
                        gt_ = 10 + i % 2
                        TT = wk_f32(gt_, 0, 512)
                        tt("dve", TT, ps(pg2, 0, 512), GMAT[:, l * 512:(l + 1) * 512], ALU.mult, reads=[psk(pg2), ("GMAT", l)], writes=wkk(gt_))
                        tt("dve", TT, TT, BIASM[:, l * 512:(l + 1) * 512], ALU.add, reads=wkk(gt_) + [("BIASM", l)], writes=wkk(gt_))
                        uview = SGO3[:, :, tb * 128:(tb + 1) * 128]
                        tt("dve", uview, TT.rearrange("p (g t) -> p g t", g=4), uview, ALU.mult, reads=wkk(gt_) + bgk(8, 4), writes=bgk(8, 4))

                for ch in range(4):
                    sc = load_unit(l, ("c", ch))
                    scg = load_unit(l, ("cg", ch))
                    for k in range(31):
                        ts("dve", DC[:, k * 128:(k + 1) * 128], IDB[:], cst(l, "ccw", k * 4 + ch), None, ALU.mult,
                           reads=[("IDB",), ("CST",)], writes=[("DC",)])
                    bc_, bcg_ = cst(l, "bin", OFF_CC // 128 + ch), cst(l, "bin", OFF_CC // 128 + 4 + ch)
                    for half in range(2):
                        pc, pgg = next_ps(), next_ps()
                        for (pp, ss) in ((pc, sc), (pgg, scg)):
                            for t2 in range(2):
                                for k in range(8):
                                    mm(ps(pp, t2 * 512, 512), wt(ss, k), xh(k, half * 1024 + t2 * 512, 512), k == 0, k == 7,
                                       reads=[("R", ss), ("XH", k, half)], writes=[psk(pp)], signal=(k == 7 and t2 == 1))
                        gs = 8 + half
                        act(wk_f32(gs), ps(pgg), AF.Sigmoid, bias=bcg_, reads=[psk(pgg), ("CST",)], writes=wkk(gs))
                        stt("dve", PAD[:, 15 + half * 1024: 15 + half * 1024 + 1024], ps(pc), bc_, wk_f32(gs),
                            ALU.add, ALU.mult, reads=[psk(pc), ("CST",)] + wkk(gs), writes=[("PAD",)])
                    pc, pgg = next_ps(), next_ps()
                    for (pp, ss) in ((pc, sc), (pgg, scg)):
                        for k in range(8):
                            mm(ps(pp, 0, 32), wt(ss, k), xh(k, T, 32), k == 0, k == 7, reads=[("R", ss), ("XHH",)], writes=[psk(pp)], signal=(k == 7))
                    act(HT[:, 32:64], ps(pgg, 0, 32), AF.Sigmoid, bias=bcg_, reads=[psk(pgg), ("CST",)], writes=[("HT",)])
                    stt("dve", HT[:, 0:32], ps(pc, 0, 32), bc_, HT[:, 32:64], ALU.add, ALU.mult, reads=[psk(pc), ("CST",), ("HT",)], writes=[("HT",)])
                    ts("dve", PAD[:, 0:15], HT[:, 1:16], lkL(si), None, ALU.mult, reads=[("HT",), ("LINK",)], writes=[("PAD",)])
                    ts("dve", PAD[:, 15 + T: 30 + T], HT[:, 16:31], lkR(si), None, ALU.mult, reads=[("HT",), ("LINK",)], writes=[("PAD",)])
                    for half in range(2):
                        pg = next_ps()
                        for t2 in range(2):
                            for k in range(31):
                                o = k + half * 1024 + t2 * 512
                                mm(ps(pg, t2 * 512, 512), DC[:, k * 128:(k + 1) * 128], PAD[:, o: o + 512], k == 0, k == 30,
                                   reads=[("DC",), ("PAD",)], writes=[psk(pg)], signal=(k == 30 and t2 == 1))
                        act(wk_f32(2 * ch + half), ps(pg), AF.Identity, bias=cst(l, "ccb", ch), reads=[psk(pg), ("CST",)], writes=wkk(2 * ch + half))
                def cc_cb(c, half, Tap, Tk):
                    act(big_bf(12 + c, half * 1024, 1024), Tap, AF.Silu, bias=cst(l, "cclb", c), scale=cst(l, "cclg", c),
                        reads=Tk + [("CST",)], writes=bgk(12 + c))
                layer_norm(4, 128, lambda c, lo, n: wk_f32(2 * c, lo, n), lambda c, half: wkk(2 * c + half), cc_cb, [8, 9, 10, 11])

                for oc in range(8):
                    sl = {nm: load_unit(l, (nm, oc), 1024 if nm in ("ga", "gb", "gc", "ba") else 512) for nm in ("ga", "gb", "gc", "ba", "bb", "bc")}
                    for half in range(2):
                        for bi, (gn, yn, nk, src0) in enumerate((("ga", "ba", 8, 0), ("gb", "bb", 4, 8), ("gc", "bc", 4, 12))):
                            pgt, py = next_ps(), next_ps()
                            for t2 in range(2):
                                for k in range(8):
                                    mm(ps(pgt, t2 * 512, 512), wt(sl[gn], k), xh(k, half * 1024 + t2 * 512, 512), k == 0, k == 7,
                                       reads=[("R", sl[gn]), ("XH", k, half)], writes=[psk(pgt)], signal=(k == 7 and t2 == 1))
                            for t2 in range(2):
                                for k in range(nk):
                                    mm(ps(py, t2 * 512, 512), wt(sl[yn], k), big_bf(src0 + k, half * 1024 + t2 * 512, 512), k == 0, k == nk - 1,
                                       reads=[("R", sl[yn])] + bgk(src0 + k), writes=[psk(py)], signal=(k == nk - 1 and t2 == 1))
                            gg = 8 + bi
                            act(wk_f32(gg), ps(pgt), AF.Sigmoid, bias=cst(l, "bin", OFF_GATE // 128 + bi * 8 + oc), reads=[psk(pgt), ("CST",)], writes=wkk(gg))
                            tt("dve", wk_f32(gg), ps(py), wk_f32(gg), ALU.mult, reads=[psk(py)] + wkk(gg), writes=wkk(gg))
                        tt("dve", wk_f32(8), wk_f32(8), wk_f32(9), ALU.add, reads=wkk(8) + wkk(9), writes=wkk(8))
                        tt("dve", wk_bf(oc, half * 1024, 1024), wk_f32(8), wk_f32(10), ALU.add, reads=wkk(8) + wkk(10), writes=wkk(oc))

                def resid(oc, half, pg, bname, zdst, zkeys_w):
                    gx = 9 + (2 * oc + half) % 2
                    ge = 11 + (2 * oc + half) % 2
                    P.dma("sp", wk_f32(gx), xd[si * 8 + oc, :, half * 1024: half * 1024 + 1024], [("XD", si, oc, half)], wkk(gx), ("XDR", gx))
                    act(wk_f32(ge), ps(pg), AF.Identity, bias=cst(l, bname, oc), reads=[psk(pg), ("CST",)], writes=wkk(ge))
                    stt("dve", zdst, wk_f32(gx), ALPHA, wk_f32(ge), ALU.mult, ALU.add, reads=wkk(gx) + wkk(ge), writes=zkeys_w)
                for oc in range(8):
                    so = load_unit(l, ("o", oc))
                    for half in range(2):
                        pg = next_ps()
                        for t2 in range(2):
                            for k in range(8):
                                mm(ps(pg, t2 * 512, 512), wt(so, k), wk_bf(k, half * 1024 + t2 * 512, 512), k == 0, k == 7,
                                   reads=[("R", so)] + wkk(k), writes=[psk(pg)], signal=(k == 7 and t2 == 1))
                        resid(oc, half, pg, "bo", big_f32(oc, half * 1024, 1024), bgk(2 * oc + half))

                def ln_x_cb(gname, bname, par_out):
                    def cb(c, half, Tap, Tk):
                        yg = 4 + (2 * c + half) % 4
                        act(wk_f32(yg), Tap, AF.Identity, bias=cst(l, bname, c), scale=cst(l, gname, c), reads=Tk + [("CST",)], writes=wkk(yg))
                        store_x(si, c, half, wk_f32(yg), wkk(yg), par=par_out, xh_write=(par_out is None))
                    return cb
                layer_norm(8, 0, lambda c, lo, n: big_f32(c, lo, n), lambda c, half: bgk(2 * c + half), ln_x_cb("l1g", "l1b", None), [0, 1, 2, 3, 8, 9])

                for J in range(4):
                    for jj in range(8):
                        j = J * 8 + jj
                        s1 = load_unit(l, ("f1", j))
                        for half in range(2):
                            pg = next_ps()
                            for t2 in range(2):
                                for k in range(8):
                                    mm(ps(pg, t2 * 512, 512), wt(s1, k), xh(k, half * 1024 + t2 * 512, 512), k == 0, k == 7,
                                       reads=[("R", s1), ("XH", k, half)], writes=[psk(pg)], signal=(k == 7 and t2 == 1))
                            gr = 8 + (2 * jj + half) % 2
                            act(wk_f32(gr), ps(pg), AF.Relu, bias=cst(l, "bf1", j), reads=[psk(pg), ("CST",)], writes=wkk(gr))
                            tt("dve", wk_bf(jj, half * 1024, 1024), wk_f32(gr), wk_f32(gr), ALU.mult, reads=wkk(gr), writes=wkk(jj))
                    if J == 3:
                        state["xh"] = None
                        if si < NSEG - 1:
                            load_xh(par, si + 1)
                    s2 = [load_unit(l, ("f2", J * 8 + jj)) for jj in range(8)]
                    for oc in range(8):
                        for half in range(2):
                            pg = next_ps()
                            for t2 in range(2):
                                for jj in range(8):
                                    mm(ps(pg, t2 * 512, 512), wt(s2[jj], oc), wk_bf(jj, half * 1024 + t2 * 512, 512), jj == 0, jj == 7,
                                       reads=[("R", s2[jj])] + wkk(jj), writes=[psk(pg)], signal=(jj == 7 and t2 == 1))
                            acc = big_f32(oc, half * 1024, 1024)
                            if J == 0:
                                cp("dve", acc, ps(pg), [psk(pg)], bgk(2 * oc + half))
                            else:
                                tt("dve", acc, ps(pg), acc, ALU.add, reads=[psk(pg)] + bgk(2 * oc + half), writes=bgk(2 * oc + half))
                for oc in range(8):
                    for half in range(2):
                        gx = 9 + (2 * oc + half) % 2
                        acc = big_f32(oc, half * 1024, 1024)
                        P.dma("sp", wk_f32(gx), xd[si * 8 + oc, :, half * 1024: half * 1024 + 1024], [("XD", si, oc, half)], wkk(gx), ("XDR", gx))
                        act(acc, acc, AF.Identity, bias=cst(l, "bf2", oc), reads=bgk(2 * oc + half) + [("CST",)], writes=bgk(2 * oc + half))
                        stt("dve", acc, wk_f32(gx), ALPHA, acc, ALU.mult, ALU.add, reads=wkk(gx) + bgk(2 * oc + half), writes=bgk(2 * oc + half))
                if not last:
                    layer_norm(8, 0, lambda c, lo, n: big_f32(c, lo, n), lambda c, half: bgk(2 * c + half), ln_x_cb("l2g", "l2b", 1 - par), [0, 1, 2, 3, 8, 9])
                else:
                    def fin_cb(c, half, Tap, Tk):
                        act(wk_f32(4 + c), Tap, AF.Identity, bias=cst(l, "l2b", c), scale=cst(l, "l2g", c), reads=Tk + [("CST",)], writes=wkk(4 + c))
                        if c == 7:
                            for tb in range(8):
                                pg = next_ps()
                                for cc in range(8):
                                    tr(ps(pg, cc * 128, 128), wk_f32(4 + cc, tb * 128, 128), IDF[:], reads=wkk(4 + cc) + [("IDF",)], writes=[psk(pg)],
                                       signal=(cc == 7))
                                go = 12 if tb % 2 == 0 else 3
                                if tb % 2 == 0:
                                    cp("dve", wk_f32(go), ps(pg), [psk(pg)], wkk(go))
                                else:
                                    act(wk_f32(go), ps(pg), AF.Identity, reads=[psk(pg)], writes=wkk(go))
                                tok0 = half * 1024 + tb * 128
                                dep = P.dma("sp", yout[si, tok0: tok0 + 128, :], wk_f32(go), wkk(go), [("YO", si, half, tb)], ("YO", tb % 2))
                                out_deps.append(dep)
                    layer_norm(8, 0, lambda c, lo, n: big_f32(c, lo, n), lambda c, half: bgk(2 * c + half), fin_cb, [0, 1, 2, 3])

        fin = {}
        for s, v in out_deps:
            fin[s] = max(fin.get(s, 0), v)
        P.final_wait("sp", list(fin.items()))
        with nc.Block() as block:
            P.emit(block)
    return nc


def _colchunk(W, j, nk):
    blk = W[:, j * 128:(j + 1) * 128].reshape(nk, 128, 128)
    out = np.zeros((128, 1024), np.float32)
    out[:, : nk * 128] = blk.transpose(1, 0, 2).reshape(128, nk * 128)
    return out


def _prep_weights(inp):
    wu = np.zeros((L, NU, 128, 1024), np.float32)
    cst = np.zeros((L, 128, NCST), np.float32)
    rows = np.zeros((L, 1, NROW), np.float32)
    for l in range(L):
        w_in = inp["w_in"][l]
        for h in range(8):
            wu[l, UNITS[("x", h)]] = _colchunk(w_in, OFF_RG_X // 128 + h, 8)
            wu[l, UNITS[("g", h)]] = _colchunk(w_in, OFF_RG_G // 128 + h, 8)
            gt = np.zeros((128, 1024), np.float32)
            gt[:, 0:128] = inp["rg_wa"][l, 0, h]
            gt[:, 128:256] = inp["rg_wx"][l, 0, h]
            gt[:, 256:384] = inp["rg_wa"][l, 1, h]
            gt[:, 384:512] = inp["rg_wx"][l, 1, h]
            wu[l, UNITS[("gt", h)]] = gt
        for ch in range(4):
            wu[l, UNITS[("u", ch)]] = _colchunk(w_in, OFF_SG // 128 + ch, 8)
            wu[l, UNITS[("v", ch)]] = _colchunk(w_in, OFF_SG // 128 + 4 + ch, 8)
            wu[l, UNITS[("c", ch)]] = _colchunk(w_in, OFF_CC // 128 + ch, 8)
            wu[l, UNITS[("cg", ch)]] = _colchunk(w_in, OFF_CC // 128 + 4 + ch, 8)
        for oc in range(8):
            wu[l, UNITS[("ga", oc)]] = _colchunk(w_in, OFF_GATE // 128 + oc, 8)
            wu[l, UNITS[("gb", oc)]] = _colchunk(w_in, OFF_GATE // 128 + 8 + oc, 8)
            wu[l, UNITS[("gc", oc)]] = _colchunk(w_in, OFF_GATE // 128 + 16 + oc, 8)
            wu[l, UNITS[("ba", oc)]] = _colchunk(inp["w_ba"][l], oc, 8)
            wu[l, UNITS[("bb", oc)]] = _colchunk(inp["w_bb"][l], oc, 4)
            wu[l, UNITS[("bc", oc)]] = _colchunk(inp["w_bc"][l], oc, 4)
            wu[l, UNITS[("o", oc)]] = _colchunk(inp["w_o"][l], oc, 8)
        for j in range(32):
            wu[l, UNITS[("f1", j)]] = _colchunk(inp["w_ff1"][l], j, 8)
            wu[l, UNITS[("f2", j)]] = inp["w_ff2"][l][j * 128:(j + 1) * 128, :]
        def put(name, vec, n):
            cst[l, :, _c[name]: _c[name] + n] = np.asarray(vec, np.float32).reshape(n, 128).T
        put("bin", inp["b_in"][l], 56)
        put("caw", inp["conv_a_w"][l].reshape(-1), 32)
        put("cab", inp["conv_a_b"][l], 8)
        put("rba", inp["rg_ba"][l].reshape(-1), 16)
        put("rbx", inp["rg_bx"][l].reshape(-1), 16)
        put("lam", inp["rg_lambda"][l].reshape(-1), 16)
        put("sglg", inp["sg_ln_g"][l], 4)
        put("ccw", inp["conv_c_w"][l].reshape(-1), 124)
        put("ccb", inp["conv_c_b"][l], 4)
        put("cclg", inp["cc_ln_g"][l], 4)
        put("cclb", inp["cc_ln_b"][l], 4)
        put("bo", inp["b_o"][l], 8)
        put("l1g", inp["ln1_g"][l], 8)
        put("l1b", inp["ln1_b"][l], 8)
        put("bf1", inp["b_ff1"][l], 32)
        put("bf2", inp["b_ff2"][l], 8)
        put("l2g", inp["ln2_g"][l], 8)
        put("l2b", inp["ln2_b"][l], 8)
        rows[l, 0, 0:512] = inp["sg_ln_b"][l]
        rows[l, 0, 512:1024] = inp["sg_b"][l].reshape(-1)
        rows[l, 0, 1024:1536] = inp["b_in"][l][OFF_SG + 512: OFF_SG + 1024]
    lnin = np.zeros((128, 16), np.float32)
    lnin[:, 0:8] = np.asarray(inp["ln_in_g"], np.float32).reshape(8, 128).T
    lnin[:, 8:16] = np.asarray(inp["ln_in_b"], np.float32).reshape(8, 128).T
    sgwt = np.ascontiguousarray(np.asarray(inp["sg_w"], np.float32).transpose(0, 1, 3, 2))
    return wu, cst, rows, lnin, sgwt


_NC_CACHE = {}
_SAMPLE_SLOTS = [(c, k) for c in range(4, 8) for k in range(4)]


def _core_inputs(inp):
    xp = inp["x_prompt"].astype(np.float32, copy=False)
    xs = inp["x_sample"].astype(np.float32, copy=False)
    xins = [np.zeros((NSEG, T, D), np.float32) for _ in range(NCORES)]
    links = [np.zeros((128, 16), np.float32) for _ in range(NCORES)]
    xins[0][:] = xp[0].reshape(NSEG, T, D)
    links[0][:, 1:8] = 1.0
    links[0][:, 8:15] = 1.0
    for i, (c, k) in enumerate(_SAMPLE_SLOTS):
        xins[c][k] = xs[i]
    return xins, links


def kernel(**inputs):
    inp = {k: np.asarray(v) for k, v in inputs.items()}
    wu, cst, rows, lnin, sgwt = _prep_weights(inp)
    xins, links = _core_inputs(inp)
    if "nc" not in _NC_CACHE:
        _NC_CACHE["nc"] = build(None)
    nc = _NC_CACHE["nc"]
    ident = np.eye(128, dtype=np.float32)
    used = {0} | {c for c, _ in _SAMPLE_SLOTS}
    wz = np.zeros_like(wu)
    in_maps = [{"xin": xins[c], "link": links[c], "wu": (wu if c in used else wz), "cst": cst, "rows": rows, "lnin": lnin,
                "sgwt": sgwt, "ident": ident} for c in range(NCORES)]
    res = run_bass_kernel_spmd(nc, in_maps, core_ids=list(range(NCORES)))
    y_prompt = np.ascontiguousarray(res.results[0]["yout"]).reshape(1, NSEG * T, D).astype(np.float32, copy=False)
    y_sample = np.zeros((16, T, D), np.float32)
    for i, (c, k) in enumerate(_SAMPLE_SLOTS):
        y_sample[i] = res.results[c]["yout"][k]
    return (y_prompt, y_sample)
```
